# Optimizing a Trainium2 kernel written in Bass

```python
import jax, jax.numpy as jnp
from jax import lax
import numpy as np

D_MODEL = 1024
BATCH = 4
SEQ = 4096
DEPTH = 4
DEC_BATCH = 128
DEC_SEQ = 1
PAST_LEN = 2048
PAGE_SIZE = 128

N_MIXERS = 2
N_A_LAYERS = (DEPTH + 1) // 2
N_B_LAYERS = DEPTH // 2
CHUNK = 128
D_SGU = D_MODEL
SGU_GROUPS = 8
SGU_GROUP_DIM = D_SGU // SGU_GROUPS
N_HEADS = 8
HEAD_DIM = D_MODEL // N_HEADS
N_KV_HEADS = 2
KV_GROUP = N_HEADS // N_KV_HEADS
N_IDX_HEADS = 8
IDX_DIM = 64
TOPK_MAX = 256
Q_BLOCK = 128
Q_COLS = N_HEADS * HEAD_DIM
KV_COLS = N_KV_HEADS * HEAD_DIM
QI_COLS = N_IDX_HEADS * IDX_DIM
PROJ_B = Q_COLS + 2 * KV_COLS + QI_COLS + IDX_DIM + N_IDX_HEADS
D_FF = 2816
CONV_W = 3
ALPHA = (2 * DEPTH) ** 0.25
BETA = (8 * DEPTH) ** -0.25
LN_EPS = 1e-5
N_MOD = 6

kernel_name = 'hybrid_sgu_dsa_convglu_step'


def layer_norm(x, g, b):
    xf = x.astype(jnp.float32)
    mu = jnp.mean(xf, axis=-1, keepdims=True)
    var = jnp.mean(jnp.square(xf - mu), axis=-1, keepdims=True)
    return ((xf - mu) * lax.rsqrt(var + LN_EPS)).astype(x.dtype) * g + b


def ada_params(c, w_ada, b_ada):
    return (jax.nn.silu(c) @ w_ada + b_ada).reshape(c.shape[0], N_MOD, D_MODEL)


def modulate(x, shift, scale):
    return x * (1 + scale[:, None, :]) + shift[:, None, :]


def post_norm(x, y, gate, g, b):
    return layer_norm(ALPHA * x + (1 + gate[:, None, :]) * y, g, b)


def sgu_mixer(h, w_in, b_in, norm_g, norm_b, w_s, b_s, w_out):
    z = jax.nn.gelu(h @ w_in + b_in)
    u, v = jnp.split(z, 2, axis=-1)
    v = layer_norm(v, norm_g, norm_b)
    bsz, t, _ = v.shape
    c = min(t, CHUNK)
    n = -(-t // c)
    pad = n * c - t
    vg = jnp.pad(v, ((0, 0), (0, pad), (0, 0))).reshape(bsz, n, c, SGU_GROUPS, SGU_GROUP_DIM)
    w = jnp.tril(w_s[:, :c, :c])
    mixed = jnp.einsum('gts,bnsgd->bntgd', w, vg) + b_s[:, :c].T[None, None, :, :, None]
    mixed = mixed.reshape(bsz, n * c, D_SGU)[:, :t]
    return (u * mixed) @ w_out, v


def dsa_project(h, w_in):
    bsz, t, _ = h.shape
    splits = [Q_COLS, Q_COLS + KV_COLS, Q_COLS + 2 * KV_COLS, Q_COLS + 2 * KV_COLS + QI_COLS,
              Q_COLS + 2 * KV_COLS + QI_COLS + IDX_DIM]
    q, k, v, qi, ki, wi = jnp.split(h @ w_in, splits, axis=-1)
    return (q.reshape(bsz, t, N_HEADS, HEAD_DIM),
            k.reshape(bsz, t, N_KV_HEADS, HEAD_DIM),
            v.reshape(bsz, t, N_KV_HEADS, HEAD_DIM),
            qi.reshape(bsz, t, N_IDX_HEADS, IDX_DIM),
            ki,
            wi * N_IDX_HEADS ** -0.5)


def dsa_attend(q, qi, wi, q_pos, k, v, ki, kv_pos, n_keep):
    bsz, t = q.shape[:2]
    qb = min(t, Q_BLOCK)
    nb = -(-t // qb)
    pad = nb * qb - t

    def blocks(a):
        a = jnp.pad(a, [(0, 0), (0, pad)] + [(0, 0)] * (a.ndim - 2))
        return jnp.moveaxis(a.reshape((bsz, nb, qb) + a.shape[2:]), 1, 0)

    qpos_b = jnp.pad(q_pos, (0, pad), mode='edge').reshape(nb, qb)
    ki32 = ki.astype(jnp.float32)

    def one_block(args):
        q_b, qi_b, wi_b, pos = args
        s_h = jnp.einsum('bqhd,bsd->bqhs', qi_b.astype(jnp.float32), ki32)
        s_idx = jnp.einsum('bqhs,bqh->bqs', jax.nn.relu(s_h) * IDX_DIM ** -0.5, wi_b.astype(jnp.float32))
        causal = kv_pos[None, :] <= pos[:, None]
        s_idx = jnp.where(causal[None], s_idx, -jnp.inf)
        _, sel = lax.top_k(s_idx, n_keep)
        kg = jax.vmap(lambda a, i: a[i])(k, sel)
        vg = jax.vmap(lambda a, i: a[i])(v, sel)
        sel_ok = kv_pos[sel] <= pos[None, :, None]
        qg = q_b.reshape(bsz, qb, N_KV_HEADS, KV_GROUP, HEAD_DIM)
        logits = jnp.einsum('bqhgd,bqkhd->bqhgk', qg, kg, preferred_element_type=jnp.float32) * HEAD_DIM ** -0.5
        logits = jnp.where(sel_ok[:, :, None, None, :], logits, -jnp.inf)
        p = jax.nn.softmax(logits, axis=-1).astype(vg.dtype)
        o = jnp.einsum('bqhgk,bqkhd->bqhgd', p, vg)
        return o.reshape(bsz, qb, N_HEADS * HEAD_DIM)

    out = lax.map(one_block, (blocks(q), blocks(qi), blocks(wi), qpos_b))
    return jnp.moveaxis(out, 0, 1).reshape(bsz, nb * qb, N_HEADS * HEAD_DIM)[:, :t]


def paged_rows(pool, page_table):
    g = pool[page_table]
    return g.reshape((g.shape[0], g.shape[1] * g.shape[2]) + g.shape[3:])


def conv_glu(h, w_up, conv_w, conv_b, w_down, past):
    a, u = jnp.split(h @ w_up, 2, axis=-1)
    t = a.shape[1]
    full = jnp.concatenate([past, a], axis=1)
    conv = sum(full[:, j:j + t] * conv_w[j] for j in range(CONV_W)) + conv_b
    y = (jax.nn.gelu(conv) * u) @ w_down
    return y, full[:, -(CONV_W - 1):]


def setup_inputs(seed: int = 0) -> dict:
    key = jax.random.key(seed)
    ks = list(jax.random.split(key, 40))
    cnt = [0]

    def nk():
        cnt[0] += 1
        return ks[cnt[0] - 1]

    def nrm(shape, s=1.0):
        return jax.random.normal(nk(), shape, jnp.float32) * s

    n_pages = PAST_LEN // PAGE_SIZE
    n_phys = (5 * DEC_BATCH * n_pages) // 4
    page_table = jax.random.permutation(nk(), n_phys)[:DEC_BATCH * n_pages].reshape(DEC_BATCH, n_pages).astype(jnp.int32)
    return {
        'x_prompt': nrm((BATCH, SEQ, D_MODEL)),
        'x_sample': nrm((DEC_BATCH, DEC_SEQ, D_MODEL)),
        'cache_k': nrm((N_B_LAYERS, n_phys, PAGE_SIZE, N_KV_HEADS, HEAD_DIM)),
        'cache_v': nrm((N_B_LAYERS, n_phys, PAGE_SIZE, N_KV_HEADS, HEAD_DIM)),
        'cache_kidx': nrm((N_B_LAYERS, n_phys, PAGE_SIZE, IDX_DIM)),
        'state_conv': nrm((DEPTH, DEC_BATCH, CONV_W - 1, D_FF)),
        'page_table': page_table,
        'c_prompt': nrm((BATCH, D_MODEL)),
        'c_sample': nrm((DEC_BATCH, D_MODEL)),
        'w_ada': nrm((DEPTH, D_MODEL, N_MOD * D_MODEL), 0.2 * D_MODEL ** -0.5),
        'b_ada': nrm((DEPTH, N_MOD * D_MODEL), 0.01),
        'ln_g': 1.0 + nrm((DEPTH, 2, D_MODEL), 0.01),
        'ln_b': nrm((DEPTH, 2, D_MODEL), 0.01),
        'sgu_w_in': nrm((N_A_LAYERS, D_MODEL, 2 * D_SGU), D_MODEL ** -0.5),
        'sgu_b_in': nrm((N_A_LAYERS, 2 * D_SGU), 0.01),
        'sgu_norm_g': 1.0 + nrm((N_A_LAYERS, D_SGU), 0.01),
        'sgu_norm_b': nrm((N_A_LAYERS, D_SGU), 0.01),
        'sgu_w_s': nrm((N_A_LAYERS, SGU_GROUPS, CHUNK, CHUNK), CHUNK ** -0.5),
        'sgu_b_s': 1.0 + nrm((N_A_LAYERS, SGU_GROUPS, CHUNK), 0.01),
        'sgu_w_out': nrm((N_A_LAYERS, D_SGU, D_MODEL), BETA * D_SGU ** -0.5),
        'dsa_w_in': nrm((N_B_LAYERS, D_MODEL, PROJ_B), D_MODEL ** -0.5),
        'dsa_w_out': nrm((N_B_LAYERS, Q_COLS, D_MODEL), BETA * Q_COLS ** -0.5),
        'ffn_w_up': nrm((DEPTH, D_MODEL, 2 * D_FF), D_MODEL ** -0.5),
        'ffn_conv_w': nrm((DEPTH, CONV_W, D_FF), CONV_W ** -0.5),
        'ffn_conv_b': nrm((DEPTH, D_FF), 0.01),
        'ffn_w_down': nrm((DEPTH, D_FF, D_MODEL), BETA * D_FF ** -0.5),
    }


def reference(x_prompt, x_sample, cache_k, cache_v, cache_kidx, state_conv, page_table, c_prompt, c_sample,
              w_ada, b_ada, ln_g, ln_b, sgu_w_in, sgu_b_in, sgu_norm_g, sgu_norm_b, sgu_w_s, sgu_b_s, sgu_w_out,
              dsa_w_in, dsa_w_out, ffn_w_up, ffn_conv_w, ffn_conv_b, ffn_w_down):
    t_p = x_prompt.shape[1]
    t_s = x_sample.shape[1]
    past_len = page_table.shape[1] * cache_k.shape[2]
    pos_p = jnp.arange(t_p, dtype=jnp.int32)
    pos_s = past_len + jnp.arange(t_s, dtype=jnp.int32)
    kvpos_s = jnp.arange(past_len + t_s, dtype=jnp.int32)
    keep_p = min(TOPK_MAX, t_p // 4)
    keep_s = min(TOPK_MAX, (past_len + t_s) // 4)
    conv_zero = jnp.zeros((x_prompt.shape[0], CONV_W - 1, D_FF), x_prompt.dtype)

    xp, xs = x_prompt, x_sample
    kp_l, vp_l, kip_l, ks_l, vs_l, kis_l, sgu_l, convp_l, convs_l = [], [], [], [], [], [], [], [], []
    for i in range(DEPTH):
        mp = ada_params(c_prompt, w_ada[i], b_ada[i])
        ms = ada_params(c_sample, w_ada[i], b_ada[i])
        hp = modulate(xp, mp[:, 0], mp[:, 1])
        hs = modulate(xs, ms[:, 0], ms[:, 1])
        j = i // N_MIXERS
        if i % N_MIXERS == 0:
            yp, _ = sgu_mixer(hp, sgu_w_in[j], sgu_b_in[j], sgu_norm_g[j], sgu_norm_b[j], sgu_w_s[j], sgu_b_s[j], sgu_w_out[j])
            ys, v_rows = sgu_mixer(hs, sgu_w_in[j], sgu_b_in[j], sgu_norm_g[j], sgu_norm_b[j], sgu_w_s[j], sgu_b_s[j], sgu_w_out[j])
            sgu_l.append(v_rows)
        else:
            qp, kp, vp, qip, kip, wip = dsa_project(hp, dsa_w_in[j])
            yp = dsa_attend(qp, qip, wip, pos_p, kp, vp, kip, pos_p, keep_p) @ dsa_w_out[j]
            qs, ks_new, vs_new, qis, kis_new, wis = dsa_project(hs, dsa_w_in[j])
            k_all = jnp.concatenate([paged_rows(cache_k[j], page_table), ks_new], axis=1)
            v_all = jnp.concatenate([paged_rows(cache_v[j], page_table), vs_new], axis=1)
            ki_all = jnp.concatenate([paged_rows(cache_kidx[j], page_table), kis_new], axis=1)
            ys = dsa_attend(qs, qis, wis, pos_s, k_all, v_all, ki_all, kvpos_s, keep_s) @ dsa_w_out[j]
            kp_l.append(kp)
            vp_l.append(vp)
            kip_l.append(kip)
            ks_l.append(ks_new)
            vs_l.append(vs_new)
            kis_l.append(kis_new)
        xp = post_norm(xp, yp, mp[:, 2], ln_g[i, 0], ln_b[i, 0])
        xs = post_norm(xs, ys, ms[:, 2], ln_g[i, 0], ln_b[i, 0])
        fp, cp = conv_glu(modulate(xp, mp[:, 3], mp[:, 4]), ffn_w_up[i], ffn_conv_w[i], ffn_conv_b[i], ffn_w_down[i], conv_zero)
        fs, cs = conv_glu(modulate(xs, ms[:, 3], ms[:, 4]), ffn_w_up[i], ffn_conv_w[i], ffn_conv_b[i], ffn_w_down[i], state_conv[i])
        xp = post_norm(xp, fp, mp[:, 5], ln_g[i, 1], ln_b[i, 1])
        xs = post_norm(xs, fs, ms[:, 5], ln_g[i, 1], ln_b[i, 1])
        convp_l.append(cp)
        convs_l.append(cs)

    new_k_prompt = jnp.stack(kp_l)
    new_v_prompt = jnp.stack(vp_l)
    new_kidx_prompt = jnp.stack(kip_l)
    new_k_sample = jnp.stack(ks_l)
    new_v_sample = jnp.stack(vs_l)
    new_kidx_sample = jnp.stack(kis_l)
    new_sgu_v_sample = jnp.stack(sgu_l)
    new_conv_prompt = jnp.stack(convp_l)
    new_conv_sample = jnp.stack(convs_l)
    return (xp, xs, new_k_prompt, new_v_prompt, new_kidx_prompt, new_k_sample, new_v_sample, new_kidx_sample,
            new_sgu_v_sample, new_conv_prompt, new_conv_sample)
```

```python
import contextlib
import numpy as np
import concourse.bass as bass
import concourse.mybir as mybir
from concourse.bass_utils import run_bass_kernel_spmd

F32 = mybir.dt.float32
BF16 = mybir.dt.bfloat16
I32 = mybir.dt.int32
AF = mybir.ActivationFunctionType
ALU = mybir.AluOpType

D = 1024
KC = 8
DFF = 2816
FC = 22
DEPTH = 4
ALPHA = (2 * DEPTH) ** 0.25
LN_EPS = 1e-5
EPS_A = LN_EPS / (ALPHA * ALPHA)
NSMP = 16
N_CORES = 8

ENG_ATTR = {"pe": "tensor", "act": "scalar", "dve": "vector", "pool": "gpsimd", "sp": "sync"}


class Sched:
    def __init__(self, nc, n_dma_ch=10, same_engine_wait=True):
        self.nc = nc
        self.engs = list(ENG_ATTR)
        self.ops = []
        self.last_w = {}
        self.readers = {}
        self.n_dma_ch = n_dma_ch
        self.same_engine_wait = same_engine_wait
        self.ch_next = {e: 0 for e in self.engs}
        self.ch_last = {e: [None] * n_dma_ch for e in self.engs}
        self.last_op = {e: None for e in self.engs}
        self.pending_bar = {e: set() for e in self.engs}
        self.cc_eng = "pool"

    def cc(self, fn, reads=(), writes=()):
        return self.op("pool", fn, reads, writes, dma=True, cc=True)

    def _needs_wait(self, prod, cons_eng):
        if prod["dma"]:
            return True
        if prod["eng"] != cons_eng:
            return True
        if cons_eng == "pe":
            return False
        return self.same_engine_wait

    def op(self, eng, fn, reads=(), writes=(), dma=False, cc=False):
        idx = len(self.ops)
        deps = set()
        for k in list(reads) + list(writes):
            if k in self.last_w:
                deps.add(self.last_w[k])
        for k in writes:
            deps.update(self.readers.get(k, ()))
        if self.pending_bar[eng]:
            deps.update(self.pending_bar[eng])
            self.pending_bar[eng] = set()
        o = dict(eng=eng, fn=fn, deps=deps, dma=dma, signal=False, ch=None, inc=(1 if cc else 16))
        if dma:
            if cc:
                c = self.n_dma_ch - 1
            else:
                nfree = self.n_dma_ch - (1 if self.cc_eng == eng else 0)
                c = self.ch_next[eng]
                self.ch_next[eng] = (c + 1) % nfree
            o["ch"] = c
            o["prev"] = self.ch_last[eng][c]
            self.ch_last[eng][c] = idx
            o["signal"] = True
        else:
            self.last_op[eng] = idx
        self.ops.append(o)
        for k in reads:
            self.readers.setdefault(k, []).append(idx)
        for k in writes:
            self.last_w[k] = idx
            self.readers[k] = []
        return idx

    def dma(self, fn, reads=(), writes=(), q="sp"):
        return self.op(q, fn, reads, writes, dma=True)

    def barrier(self):
        b = set()
        for e in self.engs:
            if self.last_op[e] is not None:
                b.add(self.last_op[e])
            for c in self.ch_last[e]:
                if c is not None:
                    b.add(c)
        for e in self.engs:
            self.pending_bar[e] = set(b)

    def emit(self, stack):
        nc, ops = self.nc, self.ops
        for o in ops:
            for d in o["deps"]:
                if self._needs_wait(ops[d], o["eng"]):
                    ops[d]["signal"] = True
        used = [e for e in self.engs if any(o["eng"] == e for o in ops)]
        sems = {e: stack.enter_context(nc.semaphore(f"sem_{e}")) for e in used}
        chs = {}
        for e in used:
            if any(o["dma"] and o["eng"] == e for o in ops):
                chs[e] = [stack.enter_context(nc.semaphore(f"dch_{e}_{i}")) for i in range(self.n_dma_ch)]
        ticket = {e: 0 for e in used}
        chcnt = {e: [0] * self.n_dma_ch for e in used}
        for o in ops:
            e = o["eng"]
            if o["dma"]:
                chcnt[e][o["ch"]] += o["inc"]
                o["sig"] = (f"dch_{e}_{o['ch']}", chs[e][o["ch"]], chcnt[e][o["ch"]])
            elif o["signal"]:
                ticket[e] += 1
                o["sig"] = (f"sem_{e}", sems[e], ticket[e])
            else:
                o["sig"] = None
        per_eng = {e: [i for i, o in enumerate(ops) if o["eng"] == e] for e in used}
        waited = {e: {} for e in used}
        self.n_wait = 0
        block = stack.enter_context(nc.Block())

        def make(e):
            def body(engh):
                for i in per_eng[e]:
                    o = ops[i]
                    need = {}
                    dl = set(o["deps"])
                    if o["dma"] and o["prev"] is not None:
                        dl.add(o["prev"])
                    for d in dl:
                        p = ops[d]
                        if not (o["dma"] and d == o.get("prev")) and not self._needs_wait(p, e):
                            continue
                        name, sem, val = p["sig"]
                        if need.get(name, (None, 0))[1] < val:
                            need[name] = (sem, val)
                    for name, (sem, val) in need.items():
                        if waited[e].get(name, 0) < val:
                            engh.wait_ge(sem, val)
                            waited[e][name] = val
                            self.n_wait += 1
                    ins = o["fn"](engh)
                    if o["sig"] is not None:
                        if o["dma"] and o["inc"] == 1:
                            ins.then_inc(o["sig"][1])
                        else:
                            ins.then_inc(o["sig"][1], 16 if o["dma"] else 1)
                if e in chs:
                    for c in range(self.n_dma_ch):
                        if chcnt[e][c] > 0:
                            engh.wait_ge(chs[e][c], chcnt[e][c])
            return body

        for e in used:
            getattr(block, ENG_ATTR[e])(make(e))


def vec_pk(ap1d, p=128):
    return ap1d.rearrange("(kc p) -> p kc", p=p)


def build_program(NBLK=17, n_layers=4, with_samples=True, KEEP=256, NIT=24, n_cores=N_CORES, dbg_stop=99, NPHYS=2560, NITS=26):
    NT = NBLK * 128
    HALF = NT - 128
    NKEY = 2 * HALF
    CH = 512
    BIG = 30000.0
    U8 = mybir.dt.uint8
    tiles = [(0, 128)]
    t = 128
    while t < NT:
        w = min(256, NT - t)
        tiles.append((t, w))
        t += w
    WMAX = max(max(w for _, w in tiles), 128 + NSMP)

    nc = bass.Bass("TRN2", target_bir_lowering=False)
    dt_in = lambda name, shape, dt=F32: nc.dram_tensor(name, list(shape), dt, kind="ExternalInput").ap()
    dt_out = lambda name, shape, dt=F32: nc.dram_tensor(name, list(shape), dt, kind="ExternalOutput").ap()

    xloc = dt_in("xloc", [NT, D])
    call = dt_in("call", [1 + NSMP, D])
    role = dt_in("role", [128, 2])
    w_ada = dt_in("w_ada", [DEPTH, D, 6 * D]); b_ada = dt_in("b_ada", [DEPTH, 6 * D])
    ln_g = dt_in("ln_g", [DEPTH, 2, D]); ln_b = dt_in("ln_b", [DEPTH, 2, D])
    sgu_w_in = dt_in("sgu_w_in", [2, D, 2 * D]); sgu_b_in = dt_in("sgu_b_in", [2, 2 * D])
    sgu_norm_g = dt_in("sgu_norm_g", [2, D]); sgu_norm_b = dt_in("sgu_norm_b", [2, D])
    sgu_w_s = dt_in("sgu_w_s", [2, 8, 128, 128]); sgu_b_s = dt_in("sgu_b_s", [2, 8, 128])
    sgu_w_out = dt_in("sgu_w_out", [2, D, D])
    ffn_w_up = dt_in("ffn_w_up", [DEPTH, D, 2 * DFF]); ffn_conv_w = dt_in("ffn_conv_w", [DEPTH, 3, DFF])
    ffn_conv_b = dt_in("ffn_conv_b", [DEPTH, DFF]); ffn_w_down = dt_in("ffn_w_down", [DEPTH, DFF, D])

    dsa_w_in = dt_in("dsa_w_in", [2, D, 2120]); dsa_w_out = dt_in("dsa_w_out", [2, D, D])
    knew = dt_out("knew", [2, HALF, 256]); vnew = dt_out("vnew", [2, HALF, 256]); kinew = dt_out("kinew", [2, HALF, 64])
    VSEG = min(2 * HALF, 2048)
    NVS = (2 * HALF) // VSEG
    SEGW = [HALF, HALF] + [VSEG] * NVS + [HALF]
    NSEG = len(SEGW)
    bounce = [[nc.dram_tensor(f"bounce{i}_{g}", [128, w], BF16, kind="Internal").ap() for g, w in enumerate(SEGW)] for i in range(2)]
    gath = [[nc.dram_tensor(f"gath{i}_{g}", [256, w], BF16, kind="Internal").ap() for g, w in enumerate(SEGW)] for i in range(2)]
    SMP = with_samples
    WS = NSMP if SMP else 0
    xs_in = dt_in("xs_in", [NSMP, D]); sconv = dt_in("sconv", [DEPTH, NSMP, 2, DFF])
    NPG = 16
    KEEP_S = 256
    cache_k = [dt_in(f"cache_k{i}", [NPHYS * 128, 256]) for i in range(2)]; cache_v = [dt_in(f"cache_v{i}", [NPHYS * 128, 256]) for i in range(2)]
    cache_ki = [dt_in(f"cache_ki{i}", [NPHYS * 128, 64]) for i in range(2)]; ptab = dt_in("ptab", [NSMP, NPG], I32)
    ksn = dt_out("ksn", [2, NSMP, 256]); vsn = dt_out("vsn", [2, NSMP, 256]); kisn = dt_out("kisn", [2, NSMP, 64])
    ys = dt_out("ys", [NSMP, D]); sguv = dt_out("sguv", [2, NSMP, D]); convs = dt_out("convs", [DEPTH, NSMP, 2, DFF])
    y_loc = dt_out("y_loc", [NT, D])
    convp = dt_out("convp", [DEPTH, 2, DFF])

    st = contextlib.ExitStack()
    with st:
        SB = lambda n, s, d=F32: st.enter_context(nc.sbuf_tensor(n, list(s), d))
        PS = lambda n: st.enter_context(nc.psum_tensor(n, [128, 512], F32))
        S = Sched(nc)

        xres = SB("xres", [128, KC, NT])
        ident = SB("ident", [128, 128]); ones_f = SB("ones_f", [128, 128])
        tri01 = SB("tri01", [128, 128])
        rolet = SB("rolet", [128, 2])
        cT = SB("cT", [128, KC, 1 + NSMP])
        modp = SB("modp", [128, 48])
        modp1 = SB("modp1", [128, 48])
        lng = SB("lng", [128, 2, KC]); lnb = SB("lnb", [128, 2, KC])
        xsT = SB("xsT", [128, KC, NSMP]); zs = SB("zs", [128, KC, NSMP]); mods1 = SB("mods1", [128, 48, NSMP])
        KTn = SB("KTn", [128, 2, NSMP], BF16); kiTn = SB("kiTn", [128, NSMP], BF16); Vn = SB("Vn", [NSMP, 256], BF16)
        idx_all = SB("idx_all", [128, NSMP * NPG], I32); piota_p = SB("piota_p", [128, 1])
        vTs = SB("vTs", [128, KC, NSMP]); w00c = SB("w00c", [128, 8]); bs0c = SB("bs0c", [128, 8]); tmp16 = SB("tmp16", [128, KC, NSMP])
        hT = SB("hT", [128, KC, WMAX], BF16)
        z = SB("z", [128, KC, WMAX])
        zsq = [SB(f"zsq{i}", [128, WMAX]) for i in range(2)]
        m_t = SB("m_t", [128, WMAX]); v_t = SB("v_t", [128, WMAX]); r_t = SB("r_t", [128, WMAX])
        NWB, WK = 3, 11
        wst = [SB(f"wst{i}", [128, WK, 128]) for i in range(NWB)]
        wbf = [SB(f"wbf{i}", [128, WK, 128], BF16) for i in range(NWB)]
        tokt = SB("tokt", [128, D])
        psA = [PS(f"psA{i}") for i in range(2)]
        psB = [PS(f"psB{i}") for i in range(2)]
        psT = [PS(f"psT{i}") for i in range(2)]
        psS = [PS(f"psS{i}") for i in range(2)]

        cnt = {"w": 0, "ps": 0, "pb": 0, "pt": 0, "zs": 0, "pq": 0}
        PAIRS = [[2 * i, 2 * i + 1] for i in range(n_cores // 2)]

        def slow_vec(dst, src):
            S.dma(lambda e: e.dma_start(out=dst, in_=src, allow_slow_non_contiguous=True), writes=[dst.tensor.name])

        def load_wchunk(w_ap2d, kcn, c0, ncol=128, k0=0):
            i = cnt["w"] % NWB
            cnt["w"] += 1
            src = w_ap2d[k0 * 128:(k0 + kcn) * 128, c0:c0 + ncol].rearrange("(kc p) n -> p kc n", p=128)
            S.dma(lambda e: e.dma_start(out=wst[i][:, :kcn, :ncol], in_=src), writes=[f"wst{i}"])
            S.op("pool", lambda e: e.tensor_copy(out=wbf[i][:, :kcn, :ncol], in_=wst[i][:, :kcn, :ncol]),
                 reads=[f"wst{i}"], writes=[f"wbf{i}"])
            return wbf[i], f"wbf{i}"

        def mm_group(ps, pskey, W, pairs, reads):
            def fn(e):
                n = len(pairs)
                for j, (l, r) in enumerate(pairs):
                    ins = e.matmul(ps, lhsT=l, rhs=r, start=(j == 0), stop=(j == n - 1))
                return ins
            S.op("pe", fn, reads=reads, writes=[pskey])

        def linear_chunk(w_ap2d, kcn, c0, src, srckeys, W, ps, pskey, off=0):
            pairs, keys = [], []
            for k0 in range(0, kcn, WK):
                kn = min(WK, kcn - k0)
                wt, wkey = load_wchunk(w_ap2d, kn, c0, k0=k0)
                pairs += [(wt[:, k, :], src[:, k0 + k, off:off + W]) for k in range(kn)]
                keys.append(wkey)
            mm_group(ps[:, :W], pskey, W, pairs, keys + srckeys)

        def next_ps(lst, name):
            i = cnt[name] % 2
            cnt[name] += 1
            return lst[i], "%s%d" % ({"pq": "psS"}.get(name, name), i)

        def tile_key(t):
            return "x%d" % [tt for tt, ww in tiles if tt <= t < tt + ww][0]

        def modulate(t0, W, jshift, jscale):
            xk = tile_key(t0)
            for k in range(KC):
                S.op("act", lambda e, k=k: e.activation(out=hT[:, k, :W], in_=xres[:, k, t0:t0 + W], func=AF.Identity,
                                                        scale=modp1[:, jscale * 8 + k:jscale * 8 + k + 1],
                                                        bias=modp[:, jshift * 8 + k:jshift * 8 + k + 1]),
                     reads=[xk, "modp", "modp1"], writes=["hT"])

        def postnorm(t0, W, sub, samples=False):
            if samples:
                return _postnorm(zs, "zs", NSMP, sub, lambda k: xsT[:, k, :], "xs")
            return _postnorm(z, "z", W, sub, lambda k: xres[:, k, t0:t0 + W], tile_key(t0))

        def _postnorm(z, zk, W, sub, xout, xk):
            s1, s1k = psS[0], "psS0"
            s2, s2k = psS[1], "psS1"
            for k in range(KC):
                i = cnt["zs"] % 2
                cnt["zs"] += 1
                S.op("act", lambda e, k=k, i=i: e.activation(out=zsq[i][:, :W], in_=z[:, k, :W], func=AF.Square),
                     reads=[zk], writes=[f"zsq{i}"])
                S.op("pe", lambda e, k=k: e.matmul(s1[:, :W], lhsT=ones_f[:], rhs=z[:, k, :W], start=(k == 0), stop=(k == KC - 1)),
                     reads=[zk, "ones_f"], writes=[s1k])
                S.op("pe", lambda e, k=k, i=i: e.matmul(s2[:, :W], lhsT=ones_f[:], rhs=zsq[i][:, :W], start=(k == 0), stop=(k == KC - 1)),
                     reads=[f"zsq{i}", "ones_f"], writes=[s2k])
            S.op("act", lambda e: e.activation(out=m_t[:, :W], in_=s1[:, :W], func=AF.Identity, scale=1.0 / D), reads=[s1k], writes=["m_t"])
            S.op("dve", lambda e: e.tensor_tensor(out=v_t[:, :W], in0=m_t[:, :W], in1=m_t[:, :W], op=ALU.mult), reads=["m_t"], writes=["v_t"])
            S.op("dve", lambda e: e.scalar_tensor_tensor(out=v_t[:, :W], in0=s2[:, :W], scalar=1.0 / D, in1=v_t[:, :W],
                                                         op0=ALU.mult, op1=ALU.subtract), reads=[s2k, "v_t"], writes=["v_t"])
            S.op("act", lambda e: e.activation(out=r_t[:, :W], in_=v_t[:, :W], func=AF.Sqrt, bias=epsA[:, 0:1]), reads=["v_t", "epsA"], writes=["r_t"])
            S.op("dve", lambda e: e.reciprocal(out=r_t[:, :W], in_=r_t[:, :W]), reads=["r_t"], writes=["r_t"])
            for k in range(KC):
                S.op("dve", lambda e, k=k: e.tensor_tensor(out=z[:, k, :W], in0=z[:, k, :W], in1=m_t[:, :W], op=ALU.subtract),
                     reads=[zk, "m_t"], writes=[zk])
                S.op("dve", lambda e, k=k: e.tensor_tensor(out=z[:, k, :W], in0=z[:, k, :W], in1=r_t[:, :W], op=ALU.mult),
                     reads=[zk, "r_t"], writes=[zk])
                S.op("act", lambda e, k=k: e.activation(out=xout(k), in_=z[:, k, :W], func=AF.Identity,
                                                        scale=lng[:, sub, k:k + 1], bias=lnb[:, sub, k:k + 1]),
                     reads=[zk, "lng", "lnb"], writes=[xk])

        def z_from_ps(ps, pskey, c, t0, W, jgate):
            S.op("dve", lambda e: e.scalar_tensor_tensor(out=z[:, c, :W], in0=ps[:, :W], scalar=modp1[:, jgate * 8 + c:jgate * 8 + c + 1],
                                                         in1=xres[:, c, t0:t0 + W], op0=ALU.mult, op1=ALU.add),
                 reads=[pskey, "modp1", tile_key(t0)], writes=["z"])

        def modulate_s(jshift, jscale):
            S.op("dve", lambda e: e.tensor_tensor(out=tmp16[:], in0=xsT[:], in1=mods1[:, jscale * 8:jscale * 8 + 8, :], op=ALU.mult),
                 reads=["xs", "mods1"], writes=["tmp16"])
            S.op("dve", lambda e: e.tensor_tensor(out=hT[:, :, 128:128 + NSMP], in0=tmp16[:], in1=modall[:, jshift * 8:jshift * 8 + 8, 1:1 + NSMP],
                                                  op=ALU.add), reads=["tmp16", "modall"], writes=["hT"])

        def zs_from_ps(ps, pskey, c, jgate, off=128):
            S.op("dve", lambda e: e.tensor_tensor(out=zs[:, c, :], in0=ps[:, off:off + NSMP], in1=mods1[:, jgate * 8 + c, :], op=ALU.mult),
                 reads=[pskey, "mods1"], writes=["zs"])
            S.op("dve", lambda e: e.tensor_tensor(out=zs[:, c, :], in0=zs[:, c, :], in1=xsT[:, c, :], op=ALU.add), reads=["zs", "xs"], writes=["zs"])

        epsA = SB("epsA", [128, 1]); epsL = SB("epsL", [128, 1]); onesb = SB("onesb", [128, 128], BF16)
        S.op("pool", lambda e: e.memset(epsA[:], EPS_A), writes=["epsA"])
        S.op("pool", lambda e: e.memset(epsL[:], LN_EPS), writes=["epsL"])
        S.op("pool", lambda e: e.memset(ones_f[:], 1.0), writes=["ones_f"])
        identb = SB("identb", [128, 128], BF16)
        S.op("pool", lambda e: e.memset(ident[:], 1.0), writes=["ident"])
        S.op("pool", lambda e: e.affine_select(out=ident[:], in_=ident[:], pattern=[[1, 128]], compare_op=ALU.is_equal,
                                               fill=0.0, base=0, channel_multiplier=-1), reads=["ident"], writes=["ident"])
        S.op("pool", lambda e: e.memset(tri01[:], 1.0), writes=["tri01"])
        S.op("pool", lambda e: e.affine_select(out=tri01[:], in_=tri01[:], pattern=[[1, 128]], compare_op=ALU.is_ge,
                                               fill=0.0, base=0, channel_multiplier=-1), reads=["tri01"], writes=["tri01"])
        S.dma(lambda e: e.dma_start(out=rolet[:], in_=role[:, :]), writes=["rolet"])
        S.op("pool", lambda e: e.tensor_copy(out=identb[:], in_=ident[:]), reads=["ident"], writes=["identb"])
        S.op("pool", lambda e: e.tensor_copy(out=onesb[:], in_=ones_f[:]), reads=["ones_f"], writes=["onesb"])

        def transpose_rows(src_tile, nrows, ncols_chunks, consume):
            for c0 in range(0, ncols_chunks, 4):
                pt, ptk = next_ps(psT, "pt")
                cs = list(range(c0, min(c0 + 4, ncols_chunks)))

                def fn(e, cs=cs, pt=pt):
                    for c in cs:
                        ins = e.transpose(out=pt[:, (c - cs[0]) * 128:(c - cs[0]) * 128 + nrows],
                                          in_=src_tile[:nrows, c * 128:(c + 1) * 128], identity=ident[:nrows, :nrows])
                    return ins
                S.op("pe", fn, reads=["tokt", "ident"], writes=[ptk])
                for c in cs:
                    consume(c, pt[:, (c - cs[0]) * 128:(c - cs[0]) * 128 + nrows], ptk)

        S.dma(lambda e: e.dma_start(out=tokt[:1 + NSMP, :], in_=call[:, :]), writes=["tokt"])
        transpose_rows(tokt, 1 + NSMP, KC,
                       lambda c, p, k: S.op("act", lambda e: e.activation(out=cT[:, c, :], in_=p, func=AF.Silu), reads=[k], writes=["cT"]))

        S.op("pool", lambda e: e.iota(piota_p[:], pattern=[[0, 1]], base=0, channel_multiplier=1, allow_small_or_imprecise_dtypes=True), writes=["piota_p"])
        if SMP:
            ptf = SB("ptf", [128, NSMP * NPG])
            S.dma(lambda e: e.dma_start(out=idx_all[:], in_=ptab.rearrange("s g -> (s g)").partition_broadcast(128)), writes=["idx_all"])
            S.op("dve", lambda e: e.tensor_copy(out=ptf[:], in_=idx_all[:]), reads=["idx_all"], writes=["ptf"])
            S.op("dve", lambda e: e.tensor_scalar(out=ptf[:], in0=ptf[:], scalar1=128.0, scalar2=piota_p[:, 0:1], op0=ALU.mult, op1=ALU.add),
                 reads=["ptf", "piota_p"], writes=["ptf"])
            S.op("dve", lambda e: e.tensor_copy(out=idx_all[:], in_=ptf[:]), reads=["ptf"], writes=["idx_all"])
            S.dma(lambda e: e.dma_start(out=tokt[:NSMP, :], in_=xs_in[:, :]), writes=["tokt"])
            transpose_rows(tokt, NSMP, KC,
                           lambda c, p, k: S.op("dve", lambda e: e.tensor_copy(out=xsT[:, c, :], in_=p), reads=[k], writes=["xs"]))
        for b in range(NBLK):
            S.dma(lambda e, b=b: e.dma_start(out=tokt[:], in_=xloc[b * 128:(b + 1) * 128, :]), writes=["tokt"])
            tk = [tt for tt, ww in tiles if tt <= b * 128 < tt + ww][0]
            transpose_rows(tokt, 128, KC,
                           lambda c, p, k, b=b, tk=tk: S.op("dve", lambda e: e.tensor_copy(out=xres[:, c, b * 128:(b + 1) * 128], in_=p),
                                                            reads=[k], writes=[f"x{tk}"]))

        bada_t = SB("bada_t", [128, 48])
        modall = SB("modall", [128, 48, 1 + NSMP])
        halo = SB("halo", [128, FC, 2])
        cw = SB("cw", [128, 3, FC]); cb = SB("cb", [128, FC])
        b_u = SB("b_u", [128, KC]); ngp = SB("ngp", [128, KC]); nbp = SB("nbp", [128, KC])
        bst = SB("bst", [128, 2, 6]); mv = SB("mv", [128, 2]); rs_t = SB("rs_t", [128, 1])

        ARENA_F32 = 19712
        arena = SB("arena", [128, ARENA_F32])
        ar = {"off": 0, "phase": 0}

        def new_phase():
            S.barrier()
            ar["off"] = 0
            ar["phase"] += 1

        def carve(shape, dt=F32):
            n = 1
            for d_ in shape[1:]:
                n *= d_
            nf = n if dt == F32 else (n + 1) // 2
            a = arena[:, ar["off"]:ar["off"] + nf]
            ar["off"] += nf
            assert ar["off"] <= ARENA_F32, ("arena overflow", ar["off"])
            if dt != F32:
                a = a.bitcast(dt)
            if len(shape) == 3:
                a = a.rearrange("p (a b) -> p a b", a=shape[1])
            return a

        def load_resident(dst, dkey, w_ap2d, c0, ncols):
            for cc in range(0, ncols, 128):
                wt, wkey = load_wchunk(w_ap2d, KC, c0 + cc)
                S.op("pool", lambda e, cc=cc, wt=wt: e.tensor_copy(out=dst[:, :, cc:cc + 128], in_=wt[:, :KC, :]),
                     reads=[wkey], writes=[dkey])

        def do_layer(li):
            j = li // 2
            slow_vec(bada_t[:], vec_pk(b_ada[li]))
            slow_vec(lng[:, 0, :], vec_pk(ln_g[li, 0])); slow_vec(lng[:, 1, :], vec_pk(ln_g[li, 1]))
            slow_vec(lnb[:, 0, :], vec_pk(ln_b[li, 0])); slow_vec(lnb[:, 1, :], vec_pk(ln_b[li, 1]))
            for jj in range(48):
                i = cnt["w"] % NWB
                cnt["w"] += 1
                S.dma(lambda e, i=i, jj=jj: e.dma_start(out=wst[i][:, :KC, :],
                                                         in_=w_ada[li][:, jj * 128:(jj + 1) * 128].rearrange("(kc p) n -> p kc n", p=128)),
                      writes=[f"wst{i}"])
                ps, pk = next_ps(psA, "ps")
                mm_group(ps[:, :1 + NSMP], pk, 1 + NSMP, [(wst[i][:, k, :], cT[:, k, :]) for k in range(KC)], [f"wst{i}", "cT"])
                S.op("act", lambda e, ps=ps, jj=jj: e.activation(out=modall[:, jj, :], in_=ps[:, :1 + NSMP], func=AF.Identity,
                                                                 bias=bada_t[:, jj:jj + 1]), reads=[pk, "bada_t"], writes=["modall"])
            S.op("dve", lambda e: e.tensor_copy(out=modp[:], in_=modall[:, :, 0]), reads=["modall"], writes=["modp"])
            S.op("dve", lambda e: e.tensor_scalar(out=modp1[:], in0=modp[:], scalar1=1.0, scalar2=None, op0=ALU.add),
                 reads=["modp"], writes=["modp1"])
            for jg in (2, 5):
                S.op("dve", lambda e, jg=jg: e.tensor_scalar(out=modp1[:, jg * 8:jg * 8 + 8], in0=modp1[:, jg * 8:jg * 8 + 8],
                                                             scalar1=1.0 / ALPHA, scalar2=None, op0=ALU.mult),
                     reads=["modp1"], writes=["modp1"])

            if SMP:
                S.op("dve", lambda e: e.tensor_scalar(out=mods1[:], in0=modall[:, :, 1:1 + NSMP], scalar1=1.0, scalar2=None, op0=ALU.add),
                     reads=["modall"], writes=["mods1"])
                for jg in (2, 5):
                    S.op("dve", lambda e, jg=jg: e.tensor_scalar(out=mods1[:, jg * 8:jg * 8 + 8, :], in0=mods1[:, jg * 8:jg * 8 + 8, :],
                                                                 scalar1=1.0 / ALPHA, scalar2=None, op0=ALU.mult), reads=["mods1"], writes=["mods1"])
            if li % 2 == 0:
                new_phase()
                w_u = carve([128, KC, D], BF16); w_v = carve([128, KC, D], BF16); w_o = carve([128, KC, D], BF16)
                bvb = carve([128, D]); WsT = carve([128, 8, 128], BF16); C2 = carve([128, 8, 128])
                uT = carve([128, KC, WMAX]); umT = carve([128, KC, WMAX], BF16)
                vtok = carve([128, D]); vhat = carve([128, D], BF16); mix = carve([128, 128])
                load_resident(w_u, "w_u", sgu_w_in[j], 0, D)
                load_resident(w_v, "w_v", sgu_w_in[j], D, D)
                load_resident(w_o, "w_o", sgu_w_out[j], 0, D)
                slow_vec(b_u[:], vec_pk(sgu_b_in[j, 0:D]))
                slow_vec(ngp[:], vec_pk(sgu_norm_g[j])); slow_vec(nbp[:], vec_pk(sgu_norm_b[j]))
                S.dma(lambda e: e.dma_start(out=bvb, in_=sgu_b_in[j, D:2 * D].partition_broadcast(128)), writes=["bvb"])
                S.dma(lambda e: e.dma_start(out=C2, in_=sgu_b_s[j].partition_broadcast(128)), writes=["C2"])
                for g in range(8):
                    S.dma(lambda e, g=g: e.dma_start(out=tokt[:, g * 128:(g + 1) * 128], in_=sgu_w_s[j, g]), writes=["tokt"])
                transpose_rows(tokt, 128, 8,
                               lambda c, p, k: S.op("dve", lambda e: e.tensor_tensor(out=WsT[:, c, :], in0=p, in1=tri01[:], op=ALU.mult),
                                                    reads=[k, "tri01"], writes=["WsT"]))
                S.op("pool", lambda e: e.tensor_copy(out=onesb[:], in_=ones_f[:]), reads=["ones_f"], writes=["onesb"])
                for g in range(8):
                    ps, pk = next_ps(psA, "ps")
                    mm_group(ps[:, :128], pk, 128, [(onesb[:], WsT[:, g, :])], ["onesb", "WsT"])
                    S.op("dve", lambda e, g=g, ps=ps: e.scalar_tensor_tensor(out=C2[:, g, :], in0=ps[:, :128], scalar=nbp[:, g:g + 1],
                                                                             in1=C2[:, g, :], op0=ALU.mult, op1=ALU.add),
                         reads=[pk, "nbp", "C2"], writes=["C2"])
                if SMP:
                    S.dma(lambda e: e.dma_start(out=w00c[:], in_=sgu_w_s[j, :, 0, 0].partition_broadcast(128), allow_slow_non_contiguous=True), writes=["w00c"])
                    S.dma(lambda e: e.dma_start(out=bs0c[:], in_=sgu_b_s[j, :, 0].partition_broadcast(128), allow_slow_non_contiguous=True), writes=["bs0c"])

                def sgu_tile(t0, W):
                    smp = SMP and t0 == 0
                    Wx = W + (NSMP if smp else 0)
                    modulate(t0, W, 0, 1)
                    if smp:
                        modulate_s(0, 1)
                    for c in range(KC):
                        ps, pk = next_ps(psA, "ps")
                        mm_group(ps[:, :Wx], pk, Wx, [(w_u[:, k, c * 128:(c + 1) * 128], hT[:, k, :Wx]) for k in range(KC)], ["w_u", "hT"])
                        S.op("act", lambda e, c=c, ps=ps: e.activation(out=uT[:, c, :Wx], in_=ps[:, :Wx], func=AF.Gelu_apprx_tanh,
                                                                      bias=b_u[:, c:c + 1]), reads=[pk, "b_u"], writes=["uT"])
                    if smp:
                        for half in range(2):
                            ps, pk = next_ps(psB, "pb")
                            mm_group(ps[:NSMP, :512], pk, 512,
                                     [(hT[:, k, 128:128 + NSMP], w_v[:, k, half * 512:(half + 1) * 512]) for k in range(KC)], ["w_v", "hT"])
                            S.op("dve", lambda e, ps=ps, half=half: e.tensor_tensor(out=vtok[:NSMP, half * 512:(half + 1) * 512], in0=ps[:NSMP, :512],
                                                                                    in1=bvb[:NSMP, half * 512:(half + 1) * 512], op=ALU.add),
                                 reads=[pk, "bvb"], writes=["vtok"])
                        S.op("act", lambda e: e.activation(out=vtok[:NSMP, :], in_=vtok[:NSMP, :], func=AF.Gelu_apprx_tanh), reads=["vtok"], writes=["vtok"])
                        for half in range(2):
                            S.op("dve", lambda e, half=half: e.bn_stats(out=bst[:NSMP, half, :], in_=vtok[:NSMP, half * 512:(half + 1) * 512]),
                                 reads=["vtok"], writes=["bst"])
                        S.op("dve", lambda e: e.bn_aggr(out=mv[:NSMP], in_=bst[:NSMP]), reads=["bst"], writes=["mv"])
                        S.op("act", lambda e: e.activation(out=rs_t[:NSMP], in_=mv[:NSMP, 1:2], func=AF.Sqrt, bias=epsL[:NSMP, 0:1]),
                             reads=["mv", "epsL"], writes=["rs_t"])
                        S.op("dve", lambda e: e.reciprocal(out=rs_t[:NSMP], in_=rs_t[:NSMP]), reads=["rs_t"], writes=["rs_t"])
                        S.op("dve", lambda e: e.tensor_scalar(out=vtok[:NSMP, :], in0=vtok[:NSMP, :], scalar1=mv[:NSMP, 0:1], scalar2=rs_t[:NSMP, 0:1],
                                                              op0=ALU.subtract, op1=ALU.mult), reads=["vtok", "mv", "rs_t"], writes=["vtok"])
                        for c0 in (0, 4):
                            pt, ptk = next_ps(psT, "pt")

                            def fn(e, c0=c0, pt=pt):
                                for c in range(c0, c0 + 4):
                                    ins = e.transpose(out=pt[:, (c - c0) * 128:(c - c0) * 128 + NSMP], in_=vtok[:NSMP, c * 128:(c + 1) * 128],
                                                      identity=ident[:NSMP, :NSMP])
                                return ins
                            S.op("pe", fn, reads=["vtok", "ident"], writes=[ptk])
                            for c in range(c0, c0 + 4):
                                S.op("act", lambda e, c=c, c0=c0, pt=pt: e.activation(out=vTs[:, c, :], in_=pt[:, (c - c0) * 128:(c - c0) * 128 + NSMP],
                                                                                    func=AF.Identity, scale=ngp[:, c:c + 1], bias=nbp[:, c:c + 1]),
                                     reads=[ptk, "ngp", "nbp"], writes=["vTs"])
                        for c0 in (0, 4):
                            pt, ptk = next_ps(psT, "pt")

                            def fn2(e, c0=c0, pt=pt):
                                for c in range(c0, c0 + 4):
                                    ins = e.transpose(out=pt[:NSMP, (c - c0) * 128:(c - c0 + 1) * 128], in_=vTs[:, c, :], identity=ident[:])
                                return ins
                            S.op("pe", fn2, reads=["vTs", "ident"], writes=[ptk])
                            S.op("dve", lambda e, c0=c0, pt=pt: e.tensor_copy(out=tokt[:NSMP, c0 * 128:(c0 + 4) * 128], in_=pt[:NSMP, :512]),
                                 reads=[ptk], writes=["tokt"])
                        S.dma(lambda e: e.dma_start(out=sguv[j], in_=tokt[:NSMP, :]), reads=["tokt"])
                        for g in range(8):
                            S.op("dve", lambda e, g=g: e.tensor_scalar(out=tmp16[:, g, :], in0=vTs[:, g, :], scalar1=w00c[:, g:g + 1], scalar2=bs0c[:, g:g + 1],
                                                                      op0=ALU.mult, op1=ALU.add), reads=["vTs", "w00c", "bs0c"], writes=["tmp16"])
                        S.op("dve", lambda e: e.tensor_tensor(out=umT[:, :, 128:128 + NSMP], in0=tmp16[:], in1=uT[:, :, 128:128 + NSMP], op=ALU.mult),
                             reads=["tmp16", "uT"], writes=["umT"])
                    for bb in range(W // 128):
                        for half in range(2):
                            ps, pk = next_ps(psB, "pb")
                            mm_group(ps[:, :512], pk, 512,
                                     [(hT[:, k, bb * 128:(bb + 1) * 128], w_v[:, k, half * 512:(half + 1) * 512]) for k in range(KC)], ["w_v", "hT"])
                            S.op("dve", lambda e, ps=ps, half=half: e.tensor_tensor(out=vtok[:, half * 512:(half + 1) * 512], in0=ps[:, :512],
                                                                                    in1=bvb[:, half * 512:(half + 1) * 512], op=ALU.add),
                                 reads=[pk, "bvb"], writes=["vtok"])
                        S.op("act", lambda e: e.activation(out=vtok, in_=vtok, func=AF.Gelu_apprx_tanh), reads=["vtok"], writes=["vtok"])
                        for half in range(2):
                            S.op("dve", lambda e, half=half: e.bn_stats(out=bst[:, half, :], in_=vtok[:, half * 512:(half + 1) * 512]),
                                 reads=["vtok"], writes=["bst"])
                        S.op("dve", lambda e: e.bn_aggr(out=mv[:], in_=bst[:]), reads=["bst"], writes=["mv"])
                        S.op("act", lambda e: e.activation(out=rs_t[:], in_=mv[:, 1:2], func=AF.Sqrt, bias=epsL[:, 0:1]),
                             reads=["mv", "epsL"], writes=["rs_t"])
                        S.op("dve", lambda e: e.reciprocal(out=rs_t[:], in_=rs_t[:]), reads=["rs_t"], writes=["rs_t"])
                        S.op("dve", lambda e: e.tensor_scalar(out=vhat, in0=vtok, scalar1=mv[:, 0:1], scalar2=rs_t[:, 0:1],
                                                              op0=ALU.subtract, op1=ALU.mult), reads=["vtok", "mv", "rs_t"], writes=["vhat"])
                        for gh in range(2):
                            ps, pk = next_ps(psB, "pb")

                            def fn(e, ps=ps, gh=gh):
                                for g4 in range(4):
                                    g = gh * 4 + g4
                                    ins = e.matmul(ps[:, g4 * 128:(g4 + 1) * 128], lhsT=vhat[:, g * 128:(g + 1) * 128], rhs=WsT[:, g, :],
                                                   start=True, stop=True)
                                return ins
                            S.op("pe", fn, reads=["vhat", "WsT"], writes=[pk])
                            for g4 in range(4):
                                g = gh * 4 + g4
                                S.op("dve", lambda e, ps=ps, g=g, g4=g4: e.scalar_tensor_tensor(
                                    out=mix, in0=ps[:, g4 * 128:(g4 + 1) * 128], scalar=ngp[:, g:g + 1], in1=C2[:, g, :],
                                    op0=ALU.mult, op1=ALU.add), reads=[pk, "ngp", "C2"], writes=["mix"])
                                S.op("dve", lambda e, g=g, bb=bb: e.tensor_tensor(out=umT[:, g, bb * 128:(bb + 1) * 128], in0=mix,
                                                                                  in1=uT[:, g, bb * 128:(bb + 1) * 128], op=ALU.mult),
                                     reads=["mix", "uT"], writes=["umT"])
                    for c in range(KC):
                        ps, pk = next_ps(psA, "ps")
                        mm_group(ps[:, :Wx], pk, Wx, [(w_o[:, k, c * 128:(c + 1) * 128], umT[:, k, :Wx]) for k in range(KC)], ["w_o", "umT"])
                        z_from_ps(ps, pk, c, t0, W, 2)
                        if smp:
                            zs_from_ps(ps, pk, c, 2)
                    postnorm(t0, W, 0)
                    if smp:
                        postnorm(0, NSMP, 0, samples=True)
                for (t0, W) in tiles:
                    sgu_tile(t0, W)
            else:

                jd = j
                assert HALF % CH == 0
                WIN = dsa_w_in[jd]
                new_phase()
                wkv = carve([128, KC, 512], BF16); wki = carve([128, KC, 64], BF16)
                KTl = carve([128, 2, NT], BF16); kiTl = carve([128, NT], BF16); Vl = carve([128, NBLK, 256], BF16)
                kvst = [carve([128, 576]) for _ in range(2)]
                load_resident(wkv, "wkv", WIN, 1024, 512)
                wt, wkey = load_wchunk(WIN, KC, 2048, ncol=64)
                S.op("pool", lambda e, wt=wt: e.tensor_copy(out=wki, in_=wt[:, :KC, :64]), reads=[wkey], writes=["wki"])
                wki2 = carve([128, KC, 128], BF16)
                for hh in range(2):
                    S.op("pool", lambda e, hh=hh: e.tensor_copy(out=wki2[:, :, hh * 64:(hh + 1) * 64], in_=wki), reads=["wki"], writes=["wki2"])
                S.op("pool", lambda e: e.memset(kiTl, 0.0), writes=["kiTl"])
                def dsap_tile(t0, W):
                    smp = SMP and t0 == 0
                    Wx = W + (NSMP if smp else 0)
                    modulate(t0, W, 0, 1)
                    if smp:
                        modulate_s(0, 1)
                    for c in range(2):
                        ps, pk = next_ps(psA, "ps")
                        linear_chunk(WIN, KC, 1024 + c * 128, hT, ["hT"], Wx, ps, pk)
                        S.op("act", lambda e, ps=ps, c=c: e.activation(out=KTl[:, c, t0:t0 + W], in_=ps[:, :W], func=AF.Identity),
                             reads=[pk], writes=["KTl"])
                        if smp:
                            S.op("act", lambda e, ps=ps, c=c: e.activation(out=KTn[:, c, :], in_=ps[:, 128:128 + NSMP], func=AF.Identity), reads=[pk], writes=["KTn"])
                    ps, pk = next_ps(psA, "ps")
                    wt, wkey = load_wchunk(WIN, KC, 2048, ncol=64)
                    mm_group(ps[:64, :W], pk, W, [(wt[:, k, :64], hT[:, k, :W]) for k in range(KC)], [wkey, "hT"])
                    S.op("act", lambda e, ps=ps: e.activation(out=kiTl[:64, t0:t0 + W], in_=ps[:64, :W], func=AF.Identity),
                         reads=[pk], writes=["kiTl"])
                    if smp:
                        ps, pk = next_ps(psA, "ps")
                        mm_group(ps[:, :NSMP], pk, NSMP, [(wki2[:, k, :], hT[:, k, 128:128 + NSMP]) for k in range(KC)], ["wki2", "hT"])
                        S.op("act", lambda e, ps=ps: e.activation(out=kiTn[:], in_=ps[:, :NSMP], func=AF.Identity), reads=[pk], writes=["kiTn"])
                        stg = kvst[0]
                        ps, pk = next_ps(psB, "pb")
                        mm_group(ps[:NSMP, :512], pk, 512, [(hT[:, k, 128:128 + NSMP], wkv[:, k, :]) for k in range(KC)], ["wkv", "hT"])
                        S.op("act", lambda e, ps=ps: e.activation(out=stg[:NSMP, 0:512], in_=ps[:NSMP, :512], func=AF.Identity), reads=[pk], writes=["kvst0"])
                        ps2, pk2 = next_ps(psB, "pb")
                        mm_group(ps2[:NSMP, :64], pk2, 64, [(hT[:, k, 128:128 + NSMP], wki[:, k, :]) for k in range(KC)], ["wki", "hT"])
                        S.op("dve", lambda e, ps2=ps2: e.tensor_copy(out=stg[:NSMP, 512:576], in_=ps2[:NSMP, :64]), reads=[pk2], writes=["kvst0"])
                        S.op("pool", lambda e: e.tensor_copy(out=Vn[:], in_=stg[:NSMP, 256:512]), reads=["kvst0"], writes=["Vn"])
                        S.dma(lambda e: e.dma_start(out=ksn[jd], in_=stg[:NSMP, 0:256]), reads=["kvst0"])
                        S.dma(lambda e: e.dma_start(out=vsn[jd], in_=stg[:NSMP, 256:512]), reads=["kvst0"])
                        S.dma(lambda e: e.dma_start(out=kisn[jd], in_=stg[:NSMP, 512:576]), reads=["kvst0"])
                    for bb in range(W // 128):
                        blk = t0 // 128 + bb
                        stg = kvst[blk % 2]; sk = f"kvst{blk % 2}"
                        ps, pk = next_ps(psB, "pb")
                        mm_group(ps[:, :512], pk, 512, [(hT[:, k, bb * 128:(bb + 1) * 128], wkv[:, k, :]) for k in range(KC)], ["wkv", "hT"])
                        S.op("act", lambda e, ps=ps, stg=stg: e.activation(out=stg[:, 0:512], in_=ps[:, :512], func=AF.Identity),
                             reads=[pk], writes=[sk])
                        ps2, pk2 = next_ps(psB, "pb")
                        mm_group(ps2[:, :64], pk2, 64, [(hT[:, k, bb * 128:(bb + 1) * 128], wki[:, k, :]) for k in range(KC)], ["wki", "hT"])
                        S.op("dve", lambda e, ps2=ps2, stg=stg: e.tensor_copy(out=stg[:, 512:576], in_=ps2[:, :64]), reads=[pk2], writes=[sk])
                        S.op("pool", lambda e, stg=stg, blk=blk: e.tensor_copy(out=Vl[:, blk, :], in_=stg[:, 256:512]), reads=[sk], writes=["Vl"])
                        if blk >= 1:
                            r0 = (blk - 1) * 128
                            S.dma(lambda e, stg=stg, r0=r0: e.dma_start(out=knew[jd, r0:r0 + 128, :], in_=stg[:, 0:256]), reads=[sk])
                            S.dma(lambda e, stg=stg, r0=r0: e.dma_start(out=vnew[jd, r0:r0 + 128, :], in_=stg[:, 256:512]), reads=[sk])
                            S.dma(lambda e, stg=stg, r0=r0: e.dma_start(out=kinew[jd, r0:r0 + 128, :], in_=stg[:, 512:576]), reads=[sk])
                for (t0, W) in tiles:
                    dsap_tile(t0, W)
                if dbg_stop <= 1:
                    return ffn_phase(li)
                bn, gt_ = bounce[jd], gath[jd]
                for c in range(2):
                    S.dma(lambda e, c=c: e.dma_start(out=bn[c][:, :], in_=KTl[:, c, 128:NT]), reads=["KTl"], writes=[f"bounce{jd}_{c}"])
                bpv = VSEG // 256
                for g in range(NVS):
                    S.dma(lambda e, g=g: e.dma_start(out=bn[2 + g].rearrange("p (b v) -> p b v", v=256), in_=Vl[:, 1 + g * bpv:1 + (g + 1) * bpv, :]),
                          reads=["Vl"], writes=[f"bounce{jd}_{2 + g}"])
                S.dma(lambda e: e.dma_start(out=bn[NSEG - 1][:, :], in_=kiTl[:, 128:NT]), reads=["kiTl"], writes=[f"bounce{jd}_{NSEG - 1}"])
                for g in range(NSEG):
                    S.cc(lambda e, g=g: e.collective_compute("AllGather", ALU.bypass, replica_groups=PAIRS, ins=[bn[g][:, :]], outs=[gt_[g][:, :]]),
                         reads=[f"bounce{jd}_{g}"], writes=[f"gath{jd}_{g}"])
                gk = [f"gath{jd}_{g}" for g in range(NSEG)]

                if dbg_stop <= 2:
                    return ffn_phase(li)

                if SMP:
                    new_phase()
                    NK1 = NPG + 1
                    KTs = carve([128, 2, NK1 * 128], BF16); kiTs = carve([128, NK1 * 128], BF16); Vs = carve([128, NK1, 256], BF16)
                    Kst = [carve([128, 256]) for _ in range(2)]; Vst = [carve([128, 256]) for _ in range(2)]; kist = [carve([128, 128]) for _ in range(2)]
                    qTs = carve([128, 8, NSMP], BF16); qiTs = carve([128, 4, NSMP], BF16); oTs = carve([128, 8, NSMP], BF16)
                    wwi_s = carve([128, KC, 8], BF16); E_all = carve([128, NSMP, 128])
                    rS = carve([128, NK1, 8]); eS = carve([128, NK1, 8]); pTs = carve([128, NK1, 8], BF16)
                    scT = carve([128, NK1]); junk17 = carve([128, NK1]); mk17 = carve([128, NK1]); nb16 = carve([128, NK1])
                    ss = carve([128, 32])
                    wit, witp, wib, cntp, lo_s, w0_s, mid_s, gew_s, negM, rec8, m11 = (
                        ss[:, 0:8], ss[:, 8:16], ss[:, 16:24], ss[:, 24:25], ss[:, 25:26], ss[:, 26:27], ss[:, 27:28], ss[:, 28:29],
                        ss[:, 29:30], ss[:, 16:24], ss[:, 30:31])
                    rec8 = carve([128, 8])
                    wt, wkey = load_wchunk(WIN, KC, 2112, ncol=8)
                    S.op("pool", lambda e, wt=wt: e.tensor_copy(out=wwi_s, in_=wt[:, :KC, :8]), reads=[wkey], writes=["wwi_s"])
                    S.op("pool", lambda e: e.memset(KTs, 0.0), writes=["KTs"])
                    S.op("pool", lambda e: e.memset(kiTs, 0.0), writes=["kiTs"])
                    S.op("pool", lambda e: e.memset(Vs, 0.0), writes=["Vs"])
                    S.op("pool", lambda e: e.memset(nb16, 0.0), writes=["nb16"])
                    S.op("pool", lambda e: e.memset(nb16[:, NPG:NPG + 1], -BIG), writes=["nb16"])
                    S.op("pool", lambda e: e.affine_select(out=nb16[:, NPG:NPG + 1], in_=nb16[:, NPG:NPG + 1], pattern=[[0, 1]], compare_op=ALU.is_gt,
                                                           fill=0.0, base=0, channel_multiplier=1), reads=["nb16"], writes=["nb16"])
                    S.op("dve", lambda e: e.tensor_copy(out=E_all[:NSMP], in_=ident[:NSMP, :NSMP].unsqueeze(2).to_broadcast([NSMP, NSMP, 128])),
                         reads=["ident"], writes=["E_all"])
                    modulate_s(0, 1)
                    for h in range(8):
                        ps, pk = next_ps(psA, "ps")
                        linear_chunk(WIN, KC, h * 128, hT, ["hT"], NSMP, ps, pk, off=128)
                        S.op("act", lambda e, ps=ps, h=h: e.activation(out=qTs[:, h, :], in_=ps[:, :NSMP], func=AF.Identity, scale=128 ** -0.5), reads=[pk], writes=["qTs"])
                    for c in range(4):
                        ps, pk = next_ps(psA, "ps")
                        linear_chunk(WIN, KC, 1536 + c * 128, hT, ["hT"], NSMP, ps, pk, off=128)
                        S.op("act", lambda e, ps=ps, c=c: e.activation(out=qiTs[:, c, :], in_=ps[:, :NSMP], func=AF.Identity), reads=[pk], writes=["qiTs"])
                    ps, pk = next_ps(psA, "ps")
                    mm_group(ps[:NSMP, :8], pk, 8, [(hT[:, k, 128:128 + NSMP], wwi_s[:, k, :]) for k in range(KC)], ["wwi_s", "hT"])
                    S.op("act", lambda e, ps=ps: e.activation(out=wit[:NSMP], in_=ps[:NSMP, :8], func=AF.Identity, scale=(8 ** -0.5) * (64 ** -0.5)), reads=[pk], writes=["wit"])
                    S.op("dve", lambda e: e.tensor_copy(out=witp[:NSMP, 0:4], in_=wit[:NSMP, 0:8:2]), reads=["wit"], writes=["witp"])
                    S.op("dve", lambda e: e.tensor_copy(out=witp[:NSMP, 4:8], in_=wit[:NSMP, 1:8:2]), reads=["wit"], writes=["witp"])
                    CK = cache_k[jd]; CV = cache_v[jd]; CI = cache_ki[jd]

                    def sample_attend(si):
                        for pg in range(NPG):
                            i = pg % 2
                            icol = idx_all[:, si * NPG + pg:si * NPG + pg + 1]
                            S.dma(lambda e, i=i, icol=icol: e.indirect_dma_start(out=Kst[i], out_offset=None, in_=CK[:, :],
                                                                                in_offset=bass.IndirectOffsetOnAxis(ap=icol, axis=0)),
                                  reads=["idx_all"], writes=[f"Kst{i}"], q="pool")
                            if dbg_stop <= 4.1:
                                continue
                            S.dma(lambda e, i=i, icol=icol: e.indirect_dma_start(out=Vst[i], out_offset=None, in_=CV[:, :],
                                                                                in_offset=bass.IndirectOffsetOnAxis(ap=icol, axis=0)),
                                  reads=["idx_all"], writes=[f"Vst{i}"], q="pool")
                            S.dma(lambda e, i=i, icol=icol: e.indirect_dma_start(out=kist[i][:, 0:64], out_offset=None, in_=CI[:, :],
                                                                                in_offset=bass.IndirectOffsetOnAxis(ap=icol, axis=0)),
                                  reads=["idx_all"], writes=[f"kist{i}"], q="pool")
                            if dbg_stop <= 4.2:
                                continue
                            S.op("dve", lambda e, i=i: e.tensor_copy(out=kist[i][:, 64:128], in_=kist[i][:, 0:64]), reads=[f"kist{i}"], writes=[f"kist{i}"])
                            S.op("act", lambda e, i=i, pg=pg: e.activation(out=Vs[:, pg, :], in_=Vst[i], func=AF.Identity), reads=[f"Vst{i}"], writes=["Vs"])
                            if dbg_stop <= 4.3:
                                continue
                            pt, ptk = next_ps(psT, "pt")

                            def fnt(e, i=i, pt=pt):
                                e.transpose(out=pt[:, 0:128], in_=Kst[i][:, 0:128], identity=ident[:])
                                e.transpose(out=pt[:, 128:256], in_=Kst[i][:, 128:256], identity=ident[:])
                                return e.transpose(out=pt[:, 256:384], in_=kist[i][:, :], identity=ident[:])
                            S.op("pe", fnt, reads=[f"Kst{i}", f"kist{i}", "ident"], writes=[ptk])
                            if dbg_stop <= 4.4:
                                continue
                            S.op("act", lambda e, pt=pt, pg=pg: e.activation(out=KTs[:, :, pg * 128:(pg + 1) * 128],
                                                                             in_=pt[:, 0:256].rearrange("p (c s) -> p c s", c=2), func=AF.Identity),
                                 reads=[ptk], writes=["KTs"])
                            if dbg_stop <= 4.45:
                                continue
                            S.op("act", lambda e, pt=pt, pg=pg: e.activation(out=kiTs[:, pg * 128:(pg + 1) * 128], in_=pt[:, 256:384], func=AF.Identity),
                                 reads=[ptk], writes=["kiTs"])
                        if dbg_stop <= 4.5:
                            return
                        S.op("dve", lambda e: e.tensor_copy(out=KTs[:, :, NPG * 128:NPG * 128 + 1], in_=KTn[:, :, si:si + 1]), reads=["KTn"], writes=["KTs"])
                        S.op("dve", lambda e: e.tensor_copy(out=kiTs[:, NPG * 128:NPG * 128 + 1], in_=kiTn[:, si:si + 1]), reads=["kiTn"], writes=["kiTs"])
                        S.dma(lambda e: e.dma_start(out=Vs[0:1, NPG, :], in_=Vn[si:si + 1, :]), reads=["Vn"], writes=["Vs"])
                        if dbg_stop <= 5.1:
                            return
                        pse, pek = next_ps(psA, "ps")
                        pso_, pok = next_ps(psA, "ps")
                        pscs = [pse[:, :NK1 * 4].rearrange("p (g h) -> p g h", h=4), pso_[:, :NK1 * 4].rearrange("p (g h) -> p g h", h=4)]
                        for hh in range(2):
                            def fns(e, hh=hh):
                                pb = hh * 64
                                for pg in range(NK1):
                                    ins = e.matmul(pscs[hh][:, pg, :], lhsT=kiTs[pb:pb + 64, pg * 128:(pg + 1) * 128], rhs=qiTs[pb:pb + 64, :, si],
                                                   start=True, stop=True)
                                return ins
                            S.op("pe", fns, reads=["kiTs", "qiTs"], writes=[(pek, pok)[hh]])
                            S.op("act", lambda e, hh=hh: e.activation(out=rS[:, :, hh * 4:(hh + 1) * 4], in_=pscs[hh], func=AF.Relu),
                                 reads=[(pek, pok)[hh]], writes=["rS"])
                        if dbg_stop <= 5.2:
                            return
                        ps2, pk2 = next_ps(psA, "ps")
                        mm_group(ps2[:, :8], pk2, 8, [(E_all[:NSMP, si, :], witp[:NSMP, :])], ["E_all", "witp"])
                        S.op("dve", lambda e, ps2=ps2: e.tensor_copy(out=wib, in_=ps2[:, :8]), reads=[pk2], writes=["wib"])
                        S.op("dve", lambda e: e.tensor_tensor(out=rS, in0=rS, in1=wib.unsqueeze(1).to_broadcast([128, NK1, 8]), op=ALU.mult),
                             reads=["rS", "wib"], writes=["rS"])
                        S.op("dve", lambda e: e.tensor_reduce(out=scT, in_=rS, axis=mybir.AxisListType.X, op=ALU.add), reads=["rS"], writes=["scT"])
                        if dbg_stop <= 5.3:
                            return
                        S.op("act", lambda e: e.activation(out=junk17, in_=scT, func=AF.Square, accum_out=cntp), reads=["scT"], writes=["junk17", "cntp"])
                        pq, pqk = next_ps(psS, "pq")
                        mm_group(pq[:, :1], pqk, 1, [(ones_f[:], cntp)], ["ones_f", "cntp"])
                        S.op("act", lambda e, pq=pq: e.activation(out=w0_s, in_=pq[:, :1], func=AF.Sqrt), reads=[pqk], writes=["w0_s"])
                        S.op("dve", lambda e: e.tensor_scalar(out=lo_s, in0=w0_s, scalar1=-1.0, scalar2=-1.0, op0=ALU.mult, op1=ALU.add), reads=["w0_s"], writes=["lo_s"])
                        S.op("dve", lambda e: e.tensor_scalar(out=w0_s, in0=w0_s, scalar1=2.0, scalar2=2.0, op0=ALU.mult, op1=ALU.add), reads=["w0_s"], writes=["w0_s"])
                        S.op("dve", lambda e: e.tensor_tensor(out=scT, in0=scT, in1=nb16, op=ALU.add), reads=["scT", "nb16"], writes=["scT"])
                        if dbg_stop <= 5.4:
                            return
                        for it in range(NITS):
                            ck = 0.5 ** (it + 1)
                            S.op("dve", lambda e, ck=ck: e.scalar_tensor_tensor(out=mid_s, in0=w0_s, scalar=ck, in1=lo_s, op0=ALU.mult, op1=ALU.add),
                                 reads=["w0_s", "lo_s"], writes=["mid_s"])
                            S.op("dve", lambda e: e.tensor_scalar(out=junk17, in0=scT, scalar1=mid_s, scalar2=0.0, op0=ALU.is_ge, op1=ALU.add, accum_out=cntp),
                                 reads=["scT", "mid_s"], writes=["junk17", "cntp"])
                            pq, pqk = next_ps(psS, "pq")
                            mm_group(pq[:, :1], pqk, 1, [(ones_f[:], cntp)], ["ones_f", "cntp"])
                            S.op("dve", lambda e, pq=pq: e.tensor_scalar(out=gew_s, in0=pq[:, :1], scalar1=KEEP_S - 0.5, scalar2=w0_s, op0=ALU.is_ge, op1=ALU.mult),
                                 reads=[pqk, "w0_s"], writes=["gew_s"])
                            S.op("dve", lambda e, ck=ck: e.scalar_tensor_tensor(out=lo_s, in0=gew_s, scalar=ck, in1=lo_s, op0=ALU.mult, op1=ALU.add),
                                 reads=["gew_s", "lo_s"], writes=["lo_s"])
                        S.op("dve", lambda e: e.tensor_scalar(out=mk17, in0=scT, scalar1=lo_s, scalar2=None, op0=ALU.is_ge), reads=["scT", "lo_s"], writes=["mk17"])
                        if dbg_stop <= 5:
                            return
                        ps, pk = next_ps(psA, "ps")
                        psl = ps[:, :NK1 * 8].rearrange("p (g h) -> p g h", h=8)

                        def fnl(e, psl=psl):
                            for pg in range(NK1):
                                for kvh in range(2):
                                    ins = e.matmul(psl[:, pg, kvh * 4:(kvh + 1) * 4], lhsT=KTs[:, kvh, pg * 128:(pg + 1) * 128], rhs=qTs[:, kvh * 4:(kvh + 1) * 4, si],
                                                   start=True, stop=True)
                            return ins
                        S.op("pe", fnl, reads=["KTs", "qTs"], writes=[pk])
                        S.op("dve", lambda e, ps=ps: e.tensor_reduce(out=cntp, in_=ps[:, :NK1 * 8], axis=mybir.AxisListType.X, op=ALU.max), reads=[pk], writes=["cntp"])
                        pt, ptk = next_ps(psT, "pt")
                        S.op("pe", lambda e, pt=pt: e.transpose(out=pt[:1, 0:128], in_=cntp, identity=ident[:]), reads=["cntp", "ident"], writes=[ptk])
                        S.op("dve", lambda e, pt=pt: e.tensor_reduce(out=m11[0:1], in_=pt[:1, 0:128], axis=mybir.AxisListType.X, op=ALU.max), reads=[ptk], writes=["m11"])
                        pq, pqk = next_ps(psS, "pq")
                        mm_group(pq[:, :1], pqk, 1, [(ones_f[0:1, :], m11[0:1])], ["ones_f", "m11"])
                        S.op("dve", lambda e, pq=pq: e.tensor_scalar(out=negM, in0=pq[:, :1], scalar1=-1.0, scalar2=None, op0=ALU.mult), reads=[pqk], writes=["negM"])
                        S.op("act", lambda e, psl=psl: e.activation(out=eS, in_=psl, func=AF.Exp, bias=negM), reads=[pk, "negM"], writes=["eS"])
                        S.op("dve", lambda e: e.tensor_tensor(out=pTs, in0=eS, in1=mk17.unsqueeze(2).to_broadcast([128, NK1, 8]), op=ALU.mult),
                             reads=["eS", "mk17"], writes=["pTs"])
                        if dbg_stop <= 6:
                            return
                        pso, psok = psB[0], "pb0"
                        pss, pssk = psB[1], "pb1"

                        def fno(e):
                            for kvh in range(2):
                                for pg in range(NK1):
                                    ins = e.matmul(pso[:, kvh * 4:(kvh + 1) * 4], lhsT=Vs[:, pg, kvh * 128:(kvh + 1) * 128], rhs=pTs[:, pg, kvh * 4:(kvh + 1) * 4],
                                                   start=(pg == 0), stop=(pg == NK1 - 1))
                            return ins
                        S.op("pe", fno, reads=["Vs", "pTs"], writes=[psok])

                        def fnsum(e):
                            for pg in range(NK1):
                                ins = e.matmul(pss[:, 0:8], lhsT=onesb[:], rhs=pTs[:, pg, :], start=(pg == 0), stop=(pg == NK1 - 1))
                            return ins
                        S.op("pe", fnsum, reads=["onesb", "pTs"], writes=[pssk])
                        S.op("dve", lambda e: e.tensor_scalar(out=rec8, in0=pss[:, 0:8], scalar1=1e-30, scalar2=None, op0=ALU.max), reads=[pssk], writes=["rec8"])
                        S.op("dve", lambda e: e.reciprocal(out=rec8, in_=rec8), reads=["rec8"], writes=["rec8"])
                        S.op("dve", lambda e: e.tensor_tensor(out=oTs[:, :, si], in0=pso[:, 0:8], in1=rec8, op=ALU.mult), reads=[psok, "rec8"], writes=["oTs"])
                    for si in range(NSMP if dbg_stop >= 99 else (0 if dbg_stop <= 3 else 1)):
                        sample_attend(si)
                    for c in range(KC):
                        ps, pk = next_ps(psA, "ps")
                        linear_chunk(dsa_w_out[jd], KC, c * 128, oTs, ["oTs"], NSMP, ps, pk)
                        zs_from_ps(ps, pk, c, 2, off=0)
                    postnorm(0, NSMP, 0, samples=True)
                new_phase()
                NKBM = 2 * (NBLK - 1)
                wwi = carve([128, KC, 8], BF16)
                scores = carve([128, NKEY]); junk = carve([128, NKEY], U8); maskT = carve([128, NKBM, 128], BF16)
                qT = carve([128, 8, 128], BF16); qiT = carve([128, 4, 128], BF16); oT = carve([128, 8, 128], BF16)
                KTc = [carve([128, 2, CH], BF16) for _ in range(2)]; Vc = [carve([128, CH // 128, 256], BF16) for _ in range(2)]
                kic = [carve([128, CH], BF16) for _ in range(2)]
                rbuf = [carve([128, CH]) for _ in range(2)]; mrow = [carve([128, CH], BF16) for _ in range(2)]
                e_t = [carve([128, 4, 128], BF16) for _ in range(2)]; pmt = [carve([128, 4, 128], BF16) for _ in range(2)]
                cbt = carve([128, CH]); kposb = carve([128, CH]); rec = carve([128, CH]); qsq = carve([128, 8, 128], BF16)
                sm = carve([128, 32])
                wi_t, qpos, lo, w0, mid, cntt, gew, qn2, kn2, negC, mins, qrel, piota = (
                    sm[:, 0:8], sm[:, 8:9], sm[:, 9:10], sm[:, 10:11], sm[:, 11:12], sm[:, 12:13], sm[:, 13:14], sm[:, 14:15],
                    sm[:, 15:16], sm[:, 16:17], sm[:, 17:25], sm[:, 25:26], sm[:, 26:27])
                wt, wkey = load_wchunk(WIN, KC, 2112, ncol=8)
                S.op("pool", lambda e, wt=wt: e.tensor_copy(out=wwi, in_=wt[:, :KC, :8]), reads=[wkey], writes=["wwi"])
                S.op("pool", lambda e: e.iota(kposb, pattern=[[1, CH]], base=0, channel_multiplier=0, allow_small_or_imprecise_dtypes=True),
                     writes=["kposb"])
                S.op("pool", lambda e: e.iota(piota, pattern=[[0, 1]], base=0, channel_multiplier=1, allow_small_or_imprecise_dtypes=True),
                     writes=["piota"])
                ccnt = {"k": 0}

                def load_kv_chunk(kc, want):
                    i = ccnt["k"] % 2
                    ccnt["k"] += 1
                    r, cl = divmod(kc * CH, HALF)
                    rr = slice(r * 128, (r + 1) * 128)
                    if "ki" in want:
                        for hh in range(2):
                            S.dma(lambda e, i=i, hh=hh, r=r, cl=cl: e.dma_start(out=kic[i][hh * 64:(hh + 1) * 64, :],
                                                                               in_=gt_[NSEG - 1][r * 128:r * 128 + 64, cl:cl + CH]),
                                  reads=[gk[NSEG - 1]], writes=[f"kic{i}"])
                    if "kv" in want:
                        for c in range(2):
                            S.dma(lambda e, i=i, c=c, rr=rr, cl=cl: e.dma_start(out=KTc[i][:, c, :], in_=gt_[c][rr, cl:cl + CH]),
                                  reads=[gk[c]], writes=[f"KTc{i}"])
                        vg, vo = divmod((cl // 128) * 256, VSEG)
                        S.dma(lambda e, i=i, rr=rr, vg=vg, vo=vo: e.dma_start(
                            out=Vc[i], in_=gt_[2 + vg][rr, vo:vo + (CH // 128) * 256].rearrange("p (b v) -> p b v", v=256)),
                            reads=[gk[2 + vg]], writes=[f"Vc{i}"])
                    return i

                S.op("dve", lambda e: e.memset(kn2, 0.0), writes=["kn2"])
                for kc in range(NKEY // CH):
                    i = load_kv_chunk(kc, ("kv",))
                    for c in range(2):
                        S.op("act", lambda e, i=i, c=c: e.activation(out=mrow[c], in_=KTc[i][:, c, :], func=AF.Square), reads=[f"KTc{i}"], writes=[f"mrow{c}"])
                        ps, pk = next_ps(psA, "ps")
                        mm_group(ps[:, :CH], pk, CH, [(onesb[:], mrow[c])], ["onesb", f"mrow{c}"])
                        S.op("dve", lambda e, ps=ps: e.tensor_reduce(out=cntt, in_=ps[:, :CH], axis=mybir.AxisListType.X, op=ALU.max), reads=[pk], writes=["cntt"])
                        S.op("dve", lambda e: e.tensor_tensor(out=kn2, in0=kn2, in1=cntt, op=ALU.max), reads=["kn2", "cntt"], writes=["kn2"])

                def att_block(qb):
                    t0 = qb * 128
                    nkb = (NBLK - 1) + qb
                    nch = -(-nkb // (CH // 128))
                    S_ = nch * CH
                    modulate(t0, 128, 0, 1)
                    for h in range(8):
                        ps, pk = next_ps(psA, "ps")
                        linear_chunk(WIN, KC, h * 128, hT, ["hT"], 128, ps, pk)
                        S.op("act", lambda e, ps=ps, h=h: e.activation(out=qT[:, h, :], in_=ps[:, :128], func=AF.Identity, scale=128 ** -0.5),
                             reads=[pk], writes=["qT"])
                    for c in range(4):
                        ps, pk = next_ps(psA, "ps")
                        linear_chunk(WIN, KC, 1536 + c * 128, hT, ["hT"], 128, ps, pk)
                        S.op("act", lambda e, ps=ps, c=c: e.activation(out=qiT[:, c, :], in_=ps[:, :128], func=AF.Identity), reads=[pk], writes=["qiT"])
                    ps, pk = next_ps(psA, "ps")
                    mm_group(ps[:, :8], pk, 8, [(hT[:, k, :128], wwi[:, k, :]) for k in range(KC)], ["wwi", "hT"])
                    S.op("act", lambda e, ps=ps: e.activation(out=wi_t, in_=ps[:, :8], func=AF.Identity, scale=(8 ** -0.5) * (64 ** -0.5)),
                         reads=[pk], writes=["wi_t"])
                    S.op("dve", lambda e, qb=qb: e.tensor_scalar(out=qpos, in0=piota, scalar1=rolet[:, 0:1], scalar2=float((qb - 1) * 128),
                                                                op0=ALU.add, op1=ALU.add), reads=["piota", "rolet"], writes=["qpos"])
                    for kc in range(nch):
                        i = load_kv_chunk(kc, ("ki",))
                        sc = scores[:, kc * CH:(kc + 1) * CH]
                        for h in range(8):
                            ps, pk = next_ps(psA, "ps")
                            pb = (h % 2) * 64
                            mm_group(ps[:, :CH], pk, CH, [(qiT[pb:pb + 64, h // 2, :], kic[i][pb:pb + 64, :])], ["qiT", f"kic{i}"])
                            rb = rbuf[h % 2]
                            S.op("act", lambda e, ps=ps, rb=rb: e.activation(out=rb, in_=ps[:, :CH], func=AF.Relu), reads=[pk], writes=[f"rbuf{h % 2}"])
                            if h == 0:
                                S.op("dve", lambda e, rb=rb, sc=sc: e.tensor_scalar(out=sc, in0=rb, scalar1=wi_t[:, 0:1], scalar2=None, op0=ALU.mult),
                                     reads=[f"rbuf{h % 2}", "wi_t"], writes=["scores"])
                            else:
                                S.op("dve", lambda e, rb=rb, sc=sc, h=h: e.scalar_tensor_tensor(out=sc, in0=rb, scalar=wi_t[:, h:h + 1], in1=sc,
                                                                                              op0=ALU.mult, op1=ALU.add),
                                     reads=[f"rbuf{h % 2}", "wi_t", "scores"], writes=["scores"])
                        S.op("dve", lambda e, sc=sc, kc=kc: e.tensor_reduce(out=mins[:, kc:kc + 1], in_=sc, axis=mybir.AxisListType.X, op=ALU.min),
                             reads=["scores"], writes=["mins"])
                        S.op("dve", lambda e, kc=kc: e.tensor_scalar(out=qrel, in0=qpos, scalar1=float(-kc * CH), scalar2=None, op0=ALU.add),
                             reads=["qpos"], writes=["qrel"])
                        S.op("dve", lambda e: e.tensor_scalar(out=cbt, in0=kposb, scalar1=qrel, scalar2=-BIG, op0=ALU.is_gt, op1=ALU.mult),
                             reads=["kposb", "qrel"], writes=["cbt"])
                        S.op("dve", lambda e, sc=sc: e.tensor_tensor(out=sc, in0=sc, in1=cbt, op=ALU.add), reads=["scores", "cbt"], writes=["scores"])
                    S.op("dve", lambda e, nch=nch: e.tensor_reduce(out=lo, in_=mins[:, :nch], axis=mybir.AxisListType.X, op=ALU.min), reads=["mins"], writes=["lo"])
                    S.op("dve", lambda e: e.tensor_scalar(out=lo, in0=lo, scalar1=-1.0, scalar2=None, op0=ALU.add), reads=["lo"], writes=["lo"])
                    S.op("dve", lambda e, S_=S_: e.tensor_reduce(out=w0, in_=scores[:, :S_], axis=mybir.AxisListType.X, op=ALU.max), reads=["scores"], writes=["w0"])
                    S.op("dve", lambda e: e.scalar_tensor_tensor(out=w0, in0=w0, scalar=1.0, in1=lo, op0=ALU.add, op1=ALU.subtract), reads=["w0", "lo"], writes=["w0"])
                    S.op("dve", lambda e: e.tensor_scalar(out=w0, in0=w0, scalar1=1.0, scalar2=None, op0=ALU.max), reads=["w0"], writes=["w0"])
                    for it in range(NIT):
                        ck = 0.5 ** (it + 1)
                        S.op("dve", lambda e, ck=ck: e.scalar_tensor_tensor(out=mid, in0=w0, scalar=ck, in1=lo, op0=ALU.mult, op1=ALU.add),
                             reads=["w0", "lo"], writes=["mid"])
                        S.op("dve", lambda e, S_=S_: e.tensor_scalar(out=junk[:, :S_], in0=scores[:, :S_], scalar1=mid, scalar2=0.0,
                                                                     op0=ALU.is_ge, op1=ALU.add, accum_out=cntt),
                             reads=["scores", "mid"], writes=["junk", "cntt"])
                        S.op("dve", lambda e: e.tensor_scalar(out=gew, in0=cntt, scalar1=KEEP - 0.5, scalar2=w0, op0=ALU.is_ge, op1=ALU.mult),
                             reads=["cntt", "w0"], writes=["gew"])
                        S.op("dve", lambda e, ck=ck: e.scalar_tensor_tensor(out=lo, in0=gew, scalar=ck, in1=lo, op0=ALU.mult, op1=ALU.add),
                             reads=["gew", "lo"], writes=["lo"])
                    for kc in range(nch):
                        mr = mrow[kc % 2]
                        S.op("dve", lambda e, mr=mr, kc=kc: e.tensor_scalar(out=mr, in0=scores[:, kc * CH:(kc + 1) * CH], scalar1=lo, scalar2=None, op0=ALU.is_ge),
                             reads=["scores", "lo"], writes=[f"mrow{kc % 2}"])
                        pt, ptk = next_ps(psT, "pt")
                        ptb = pt[:, :].bitcast(BF16)

                        def fn(e, mr=mr, ptb=ptb):
                            for b4 in range(CH // 128):
                                ins = e.transpose(out=ptb[:, b4 * 128:(b4 + 1) * 128], in_=mr[:, b4 * 128:(b4 + 1) * 128], identity=identb[:])
                            return ins
                        S.op("pe", fn, reads=[f"mrow{kc % 2}", "identb"], writes=[ptk])
                        S.op("act", lambda e, ptb=ptb, kc=kc: e.activation(out=maskT[:, kc * 4:(kc + 1) * 4, :],
                                                                          in_=ptb[:, :CH].rearrange("p (b q) -> p b q", q=128), func=AF.Identity),
                             reads=[ptk], writes=["maskT"])
                    S.op("act", lambda e: e.activation(out=qsq, in_=qT, func=AF.Square), reads=["qT"], writes=["qsq"])
                    S.op("dve", lambda e: e.memset(qn2, 0.0), writes=["qn2"])
                    for hf in range(2):
                        ps, pk = next_ps(psA, "ps")
                        mm_group(ps[:, :CH], pk, CH, [(onesb[:], qsq[:, hf * 4:(hf + 1) * 4, :].rearrange("p h q -> p (h q)"))], ["onesb", "qsq"])
                        S.op("dve", lambda e, ps=ps: e.tensor_reduce(out=cntt, in_=ps[:, :CH], axis=mybir.AxisListType.X, op=ALU.max), reads=[pk], writes=["cntt"])
                        S.op("dve", lambda e: e.tensor_tensor(out=qn2, in0=qn2, in1=cntt, op=ALU.max), reads=["qn2", "cntt"], writes=["qn2"])
                    S.op("dve", lambda e: e.tensor_tensor(out=negC, in0=qn2, in1=kn2, op=ALU.mult), reads=["qn2", "kn2"], writes=["negC"])
                    S.op("act", lambda e: e.activation(out=negC, in_=negC, func=AF.Sqrt), reads=["negC"], writes=["negC"])
                    S.op("dve", lambda e: e.tensor_scalar(out=negC, in0=negC, scalar1=-1.0, scalar2=None, op0=ALU.mult), reads=["negC"], writes=["negC"])
                    nblk_proc = nch * (CH // 128)
                    for kvh in range(2):
                        pso, psok = psB[0], "pb0"
                        pss, pssk = psB[1], "pb1"
                        for kc in range(nch):
                            i = load_kv_chunk(kc, ("kv",))
                            for b4 in range(CH // 128):
                                kb = kc * 4 + b4
                                pl, plk = next_ps(psA, "ps")
                                mm_group(pl[:, :512], plk, 512, [(KTc[i][:, kvh, b4 * 128:(b4 + 1) * 128],
                                                                   qT[:, kvh * 4:(kvh + 1) * 4, :].rearrange("p h q -> p (h q)"))], [f"KTc{i}", "qT"])
                                et = e_t[kb % 2]; pm = pmt[kb % 2]
                                S.op("act", lambda e, pl=pl, et=et: e.activation(out=et, in_=pl[:, :512].rearrange("p (h q) -> p h q", h=4), func=AF.Exp,
                                                                               bias=negC), reads=[plk, "negC"], writes=[f"e_t{kb % 2}"])
                                S.op("pool", lambda e, et=et, pm=pm, kb=kb: e.tensor_tensor(out=pm, in0=et, in1=maskT[:, kb:kb + 1, :].to_broadcast([128, 4, 128]),
                                                                                          op=ALU.mult), reads=[f"e_t{kb % 2}", "maskT"], writes=[f"pm{kb % 2}"])
                                first, last = (kb == 0), (kb == nblk_proc - 1)
                                pmf = pm.rearrange("p h q -> p (h q)")
                                S.op("pe", lambda e, i=i, b4=b4, pmf=pmf, first=first, last=last, kvh=kvh: e.matmul(
                                    pso[:, :512], lhsT=Vc[i][:, b4, kvh * 128:(kvh + 1) * 128], rhs=pmf, start=first, stop=last),
                                    reads=[f"Vc{i}", f"pm{kb % 2}"], writes=[psok])
                                S.op("pe", lambda e, pmf=pmf, first=first, last=last: e.matmul(pss[:, :512], lhsT=onesb[:], rhs=pmf, start=first, stop=last),
                                     reads=["onesb", f"pm{kb % 2}"], writes=[pssk])
                        S.op("dve", lambda e: e.tensor_scalar(out=rec, in0=pss[:, :512], scalar1=1e-30, scalar2=None, op0=ALU.max), reads=[pssk], writes=["rec"])
                        S.op("dve", lambda e: e.reciprocal(out=rec, in_=rec), reads=["rec"], writes=["rec"])
                        S.op("dve", lambda e, kvh=kvh: e.tensor_tensor(out=oT[:, kvh * 4:(kvh + 1) * 4, :], in0=pso[:, :512].rearrange("p (h q) -> p h q", h=4),
                                                                      in1=rec.rearrange("p (h q) -> p h q", h=4), op=ALU.mult), reads=[psok, "rec"], writes=["oT"])
                    for c in range(KC):
                        ps, pk = next_ps(psA, "ps")
                        linear_chunk(dsa_w_out[jd], KC, c * 128, oT, ["oT"], 128, ps, pk)
                        z_from_ps(ps, pk, c, t0, 128, 2)
                    postnorm(t0, 128, 0)

                for qb in range(NBLK):
                    att_block(qb)
            ffn_phase(li)

        def ffn_phase(li):
            new_phase()
            aT = [carve([128, 2 + WMAX]) for _ in range(2)]
            cvt = carve([128, WMAX]); gt = carve([128, WMAX]); guT = carve([128, FC, WMAX], BF16)
            for jc in range(3):
                slow_vec(cw[:, jc, :], vec_pk(ffn_conv_w[li, jc]))
            slow_vec(cb[:], vec_pk(ffn_conv_b[li]))
            S.op("pool", lambda e: e.memset(halo[:], 0.0), writes=["halo"])
            if SMP:
                aS = carve([128, FC, NSMP]); uS = carve([128, FC, NSMP]); cvS = carve([128, FC, NSMP]); t2S = carve([128, FC, NSMP])
                Pst = carve([128, FC, 2 * NSMP])
                sstg = carve([128, DFF])
                S.dma(lambda e: e.dma_start(out=sstg[:2 * NSMP, :], in_=sconv[li].rearrange("s j f -> (s j) f")), writes=["sstg"])
                for c0 in range(0, FC, 4):
                    pt, ptk = next_ps(psT, "pt")
                    cs = list(range(c0, min(c0 + 4, FC)))

                    def fnp(e, cs=cs, pt=pt):
                        for c in cs:
                            ins = e.transpose(out=pt[:, (c - cs[0]) * 128:(c - cs[0]) * 128 + 2 * NSMP], in_=sstg[:2 * NSMP, c * 128:(c + 1) * 128],
                                              identity=ident[:2 * NSMP, :2 * NSMP])
                        return ins
                    S.op("pe", fnp, reads=["sstg", "ident"], writes=[ptk])
                    for c in cs:
                        S.op("dve", lambda e, c=c, cs=cs, pt=pt: e.tensor_copy(out=Pst[:, c, :], in_=pt[:, (c - cs[0]) * 128:(c - cs[0]) * 128 + 2 * NSMP]),
                             reads=[ptk], writes=["Pst"])
                S.dma(lambda e: e.dma_start(out=convs[li, :, 0, :], in_=sconv[li, :, 1, :]))

            def ffn_tile(ti, t0, W):
                smp = SMP and ti == 0
                Wx = W + (NSMP if smp else 0)
                modulate(t0, W, 3, 4)
                if smp:
                    modulate_s(3, 4)
                for fc in range(FC):
                    pa, pak = next_ps(psA, "ps")
                    linear_chunk(ffn_w_up[li], KC, fc * 128, hT, ["hT"], Wx, pa, pak)
                    pu, puk = next_ps(psB, "pb")
                    linear_chunk(ffn_w_up[li], KC, DFF + fc * 128, hT, ["hT"], Wx, pu, puk)
                    if smp:
                        S.op("act", lambda e, pa=pa, fc=fc: e.activation(out=aS[:, fc, :], in_=pa[:, 128:128 + NSMP], func=AF.Identity), reads=[pak], writes=["aS"])
                        S.op("act", lambda e, pu=pu, fc=fc: e.activation(out=uS[:, fc, :], in_=pu[:, 128:128 + NSMP], func=AF.Identity), reads=[puk], writes=["uS"])
                    ai = fc % 2
                    a_t = aT[ai]
                    S.op("pool", lambda e, a_t=a_t, fc=fc: e.tensor_copy(out=a_t[:, 0:2], in_=halo[:, fc, :]), reads=["halo"], writes=[f"aT{ai}"])
                    S.op("act", lambda e, a_t=a_t, pa=pa: e.activation(out=a_t[:, 2:2 + W], in_=pa[:, :W], func=AF.Identity),
                         reads=[pak], writes=[f"aT{ai}"])
                    S.op("pool", lambda e, a_t=a_t, fc=fc: e.tensor_copy(out=halo[:, fc, :], in_=a_t[:, W:W + 2]), reads=[f"aT{ai}"], writes=["halo"])
                    S.op("dve", lambda e, a_t=a_t, fc=fc: e.tensor_scalar(out=cvt[:, :W], in0=a_t[:, 0:W], scalar1=cw[:, 0, fc:fc + 1],
                                                                          scalar2=cb[:, fc:fc + 1], op0=ALU.mult, op1=ALU.add),
                         reads=[f"aT{ai}", "cw", "cb"], writes=["cvt"])
                    for jc in (1, 2):
                        S.op("dve", lambda e, a_t=a_t, fc=fc, jc=jc: e.scalar_tensor_tensor(
                            out=cvt[:, :W], in0=a_t[:, jc:jc + W], scalar=cw[:, jc, fc:fc + 1], in1=cvt[:, :W], op0=ALU.mult, op1=ALU.add),
                            reads=[f"aT{ai}", "cw", "cvt"], writes=["cvt"])
                    S.op("act", lambda e: e.activation(out=gt[:, :W], in_=cvt[:, :W], func=AF.Gelu_apprx_tanh), reads=["cvt"], writes=["gt"])
                    S.op("dve", lambda e, pu=pu, fc=fc: e.tensor_tensor(out=guT[:, fc, :W], in0=gt[:, :W], in1=pu[:, :W], op=ALU.mult),
                         reads=["gt", puk], writes=["guT"])
                if ti == 0:
                    S.op("dve", lambda e: e.tensor_scalar(out=halo[:], in0=halo[:], scalar1=rolet[:, 1:2], scalar2=None, op0=ALU.mult),
                         reads=["halo", "rolet"], writes=["halo"])
                if smp:
                    Pv = Pst.rearrange("p f (s j) -> p f s j", j=2)
                    bc = lambda col: col.unsqueeze(2).to_broadcast([128, FC, NSMP])
                    S.op("dve", lambda e: e.tensor_tensor(out=cvS, in0=Pv[:, :, :, 0], in1=bc(cw[:, 0, :]), op=ALU.mult), reads=["Pst", "cw"], writes=["cvS"])
                    S.op("dve", lambda e: e.tensor_tensor(out=t2S, in0=Pv[:, :, :, 1], in1=bc(cw[:, 1, :]), op=ALU.mult), reads=["Pst", "cw"], writes=["t2S"])
                    S.op("dve", lambda e: e.tensor_tensor(out=cvS, in0=cvS, in1=t2S, op=ALU.add), reads=["cvS", "t2S"], writes=["cvS"])
                    S.op("dve", lambda e: e.tensor_tensor(out=t2S, in0=aS, in1=bc(cw[:, 2, :]), op=ALU.mult), reads=["aS", "cw"], writes=["t2S"])
                    S.op("dve", lambda e: e.tensor_tensor(out=cvS, in0=cvS, in1=t2S, op=ALU.add), reads=["cvS", "t2S"], writes=["cvS"])
                    S.op("dve", lambda e: e.tensor_tensor(out=cvS, in0=cvS, in1=bc(cb[:, :]), op=ALU.add), reads=["cvS", "cb"], writes=["cvS"])
                    S.op("act", lambda e: e.activation(out=cvS, in_=cvS, func=AF.Gelu_apprx_tanh), reads=["cvS"], writes=["cvS"])
                    S.op("dve", lambda e: e.tensor_tensor(out=guT[:, :, 128:128 + NSMP], in0=cvS, in1=uS, op=ALU.mult), reads=["cvS", "uS"], writes=["guT"])
                    for c0 in range(0, FC, 4):
                        pt, ptk = next_ps(psT, "pt")
                        cs = list(range(c0, min(c0 + 4, FC)))

                        def fna(e, cs=cs, pt=pt):
                            for c in cs:
                                ins = e.transpose(out=pt[:NSMP, (c - cs[0]) * 128:(c - cs[0] + 1) * 128], in_=aS[:, c, :], identity=ident[:])
                            return ins
                        S.op("pe", fna, reads=["aS", "ident"], writes=[ptk])
                        S.op("dve", lambda e, cs=cs, pt=pt: e.tensor_copy(out=sstg[:NSMP, cs[0] * 128:(cs[-1] + 1) * 128], in_=pt[:NSMP, :len(cs) * 128]),
                             reads=[ptk], writes=["sstg"])
                    S.dma(lambda e: e.dma_start(out=convs[li, :, 1, :], in_=sstg[:NSMP, :]), reads=["sstg"])
                for c in range(KC):
                    ps, pk = next_ps(psA, "ps")
                    linear_chunk(ffn_w_down[li], FC, c * 128, guT, ["guT"], Wx, ps, pk)
                    z_from_ps(ps, pk, c, t0, W, 5)
                    if smp:
                        zs_from_ps(ps, pk, c, 5)
                postnorm(t0, W, 1)
                if smp:
                    postnorm(0, NSMP, 1, samples=True)
            for ti, (t0, W) in enumerate(tiles):
                ffn_tile(ti, t0, W)
            for jr in range(2):
                S.dma(lambda e, li=li, jr=jr: e.dma_start(out=vec_pk(convp[li, jr]), in_=halo[:, :, jr],
                                                          allow_slow_non_contiguous=True), reads=["halo"])

        for li in range(n_layers):
            do_layer(li)
        new_phase()
        if SMP:
            for c0 in (0, 4):
                pt, ptk = next_ps(psT, "pt")

                def fny(e, c0=c0, pt=pt):
                    for c in range(c0, c0 + 4):
                        ins = e.transpose(out=pt[:NSMP, (c - c0) * 128:(c - c0 + 1) * 128], in_=xsT[:, c, :], identity=ident[:])
                    return ins
                S.op("pe", fny, reads=["xs", "ident"], writes=[ptk])
                S.op("dve", lambda e, c0=c0, pt=pt: e.tensor_copy(out=tokt[:NSMP, c0 * 128:(c0 + 4) * 128], in_=pt[:NSMP, :512]), reads=[ptk], writes=["tokt"])
            S.dma(lambda e: e.dma_start(out=ys[:, :], in_=tokt[:NSMP, :]), reads=["tokt"])
        for b in range(NBLK):
            tk = [tt for tt, ww in tiles if tt <= b * 128 < tt + ww][0]
            for c0 in range(0, KC, 4):
                pt, ptk = next_ps(psT, "pt")

                def fn(e, b=b, c0=c0, pt=pt):
                    for c in range(c0, c0 + 4):
                        ins = e.transpose(out=pt[:, (c - c0) * 128:(c - c0 + 1) * 128], in_=xres[:, c, b * 128:(b + 1) * 128], identity=ident[:])
                    return ins
                S.op("pe", fn, reads=[f"x{tk}", "ident"], writes=[ptk])
                S.op("dve", lambda e, c0=c0, pt=pt: e.tensor_copy(out=tokt[:, c0 * 128:(c0 + 4) * 128], in_=pt[:, :512]), reads=[ptk], writes=["tokt"])
            S.dma(lambda e, b=b: e.dma_start(out=y_loc[b * 128:(b + 1) * 128, :], in_=tokt[:]), reads=["tokt"])

        S.emit(st)
    return nc


_WEIGHT_KEYS = ["w_ada", "b_ada", "ln_g", "ln_b", "sgu_w_in", "sgu_b_in", "sgu_norm_g", "sgu_norm_b", "sgu_w_s", "sgu_b_s",
                "sgu_w_out", "dsa_w_in", "dsa_w_out", "ffn_w_up", "ffn_conv_w", "ffn_conv_b", "ffn_w_down"]
_NC_CACHE = {}


def kernel(**inp):
    f32 = lambda a: np.ascontiguousarray(np.asarray(a), dtype=np.float32)
    x_prompt = f32(inp["x_prompt"]); c_prompt = f32(inp["c_prompt"]); c_sample = f32(inp["c_sample"])
    B, T, _ = x_prompt.shape
    half = T // 2
    n_cores = 2 * B
    if "nc" not in _NC_CACHE:
        _NC_CACHE["nc"] = build_program(NBLK=1 + half // 128, n_layers=DEPTH)
    nc = _NC_CACHE["nc"]
    weights = {k: f32(inp[k]) for k in _WEIGHT_KEYS}
    x_sample = f32(inp["x_sample"]); state_conv = f32(inp["state_conv"])
    page_table = np.ascontiguousarray(np.asarray(inp["page_table"]), dtype=np.int32)
    ck_, cv_, ci_ = f32(inp["cache_k"]), f32(inp["cache_v"]), f32(inp["cache_kidx"])
    shared = {}
    for i in range(2):
        shared[f"cache_k{i}"] = ck_[i].reshape(-1, 256); shared[f"cache_v{i}"] = cv_[i].reshape(-1, 256); shared[f"cache_ki{i}"] = ci_[i].reshape(-1, 64)
    in_maps = []
    for c in range(n_cores):
        seq, role = c // 2, c % 2
        if role == 0:
            xloc = np.concatenate([np.zeros((128, D), np.float32), x_prompt[seq, :half]], axis=0)
        else:
            xloc = x_prompt[seq, half - 128:]
        m = dict(weights)
        m.update(shared)
        sl_s = slice(NSMP * c, NSMP * (c + 1))
        m["xs_in"] = np.ascontiguousarray(x_sample[sl_s, 0, :])
        m["sconv"] = np.ascontiguousarray(state_conv[:, sl_s])
        m["ptab"] = np.ascontiguousarray(page_table[sl_s])
        m["xloc"] = np.ascontiguousarray(xloc)
        m["call"] = np.ascontiguousarray(np.concatenate([c_prompt[seq:seq + 1], c_sample[NSMP * c:NSMP * (c + 1)]], axis=0))
        m["role"] = np.tile(np.array([[float(half * role), float(role)]], np.float32), (128, 1))
        in_maps.append(m)
    res = run_bass_kernel_spmd(nc, in_maps, core_ids=list(range(n_cores))).results

    y_prompt = np.zeros((B, T, D), np.float32)
    new_conv_prompt = np.zeros((DEPTH, B, 2, DFF), np.float32)
    new_k_prompt = np.zeros((2, B, T, 2, 128), np.float32); new_v_prompt = np.zeros((2, B, T, 2, 128), np.float32)
    new_kidx_prompt = np.zeros((2, B, T, 64), np.float32)
    for c in range(n_cores):
        seq, role = c // 2, c % 2
        sl = slice(role * half, (role + 1) * half)
        y_prompt[seq, sl] = res[c]["y_loc"][128:]
        new_k_prompt[:, seq, sl] = res[c]["knew"].reshape(2, half, 2, 128)
        new_v_prompt[:, seq, sl] = res[c]["vnew"].reshape(2, half, 2, 128)
        new_kidx_prompt[:, seq, sl] = res[c]["kinew"]
        if role == 1:
            new_conv_prompt[:, seq] = res[c]["convp"]
    DB = c_sample.shape[0]
    y_sample = np.zeros((DB, 1, D), np.float32)
    new_k_sample = np.zeros((2, DB, 1, 2, 128), np.float32); new_v_sample = np.zeros((2, DB, 1, 2, 128), np.float32)
    new_kidx_sample = np.zeros((2, DB, 1, 64), np.float32)
    new_sgu_v_sample = np.zeros((2, DB, 1, D), np.float32)
    new_conv_sample = np.zeros((DEPTH, DB, 2, DFF), np.float32)
    for c in range(n_cores):
        sl_s = slice(NSMP * c, NSMP * (c + 1))
        y_sample[sl_s, 0] = res[c]["ys"]
        new_k_sample[:, sl_s, 0] = res[c]["ksn"].reshape(2, NSMP, 2, 128)
        new_v_sample[:, sl_s, 0] = res[c]["vsn"].reshape(2, NSMP, 2, 128)
        new_kidx_sample[:, sl_s, 0] = res[c]["kisn"]
        new_sgu_v_sample[:, sl_s, 0] = res[c]["sguv"]
        new_conv_sample[:, sl_s] = res[c]["convs"]
    return (y_prompt, y_sample, new_k_prompt, new_v_prompt, new_kidx_prompt, new_k_sample, new_v_sample, new_kidx_sample,
            new_sgu_v_sample, new_conv_prompt, new_conv_sample)


def extra_sample_inputs(d):
    ck, cv, ci = d["cache_k"], d["cache_v"], d["cache_ki"]
    out = {"ptab": np.ascontiguousarray(d["pt"].astype(np.int32))}
    for i in range(2):
        out[f"cache_k{i}"] = np.ascontiguousarray(ck[i].reshape(-1, 256)); out[f"cache_v{i}"] = np.ascontiguousarray(cv[i].reshape(-1, 256))
        out[f"cache_ki{i}"] = np.ascontiguousarray(ci[i].reshape(-1, 64))
    return out
```

```python
import contextlib
import numpy as np
import concourse.bass as bass
import concourse.mybir as mybir
from concourse.bass_utils import run_bass_kernel_spmd

F32 = mybir.dt.float32
BF16 = mybir.dt.bfloat16
I32 = mybir.dt.int32
AF = mybir.ActivationFunctionType
ALU = mybir.AluOpType

D = 1024
KC = 8
DFF = 2816
FC = 22
DEPTH = 4
ALPHA = (2 * DEPTH) ** 0.25
LN_EPS = 1e-5
EPS_A = LN_EPS / (ALPHA * ALPHA)
NSMP = 16
N_CORES = 8

ENG_ATTR = {"pe": "tensor", "act": "scalar", "dve": "vector", "pool": "gpsimd", "sp": "sync"}


class Sched:
    def __init__(self, nc, n_dma_ch=10, same_engine_wait=True):
        self.nc = nc
        self.engs = list(ENG_ATTR)
        self.ops = []
        self.last_w = {}
        self.readers = {}
        self.n_dma_ch = n_dma_ch
        self.same_engine_wait = same_engine_wait
        self.ch_next = {e: 0 for e in self.engs}
        self.ch_last = {e: [None] * n_dma_ch for e in self.engs}
        self.last_op = {e: None for e in self.engs}
        self.pending_bar = {e: set() for e in self.engs}
        self.cc_eng = "pool"

    def cc(self, fn, reads=(), writes=()):
        return self.op("pool", fn, reads, writes, dma=True, cc=True)

    def _needs_wait(self, prod, cons_eng):
        if prod["dma"]:
            return True
        if prod["eng"] != cons_eng:
            return True
        if cons_eng == "pe":
            return False
        return self.same_engine_wait

    def op(self, eng, fn, reads=(), writes=(), dma=False, cc=False):
        idx = len(self.ops)
        deps = set()
        for k in list(reads) + list(writes):
            if k in self.last_w:
                deps.add(self.last_w[k])
        for k in writes:
            deps.update(self.readers.get(k, ()))
        if self.pending_bar[eng]:
            deps.update(self.pending_bar[eng])
            self.pending_bar[eng] = set()
        o = dict(eng=eng, fn=fn, deps=deps, dma=dma, signal=False, ch=None, inc=(1 if cc else 16))
        if dma:
            if cc:
                c = self.n_dma_ch - 1
            else:
                nfree = self.n_dma_ch - (1 if self.cc_eng == eng else 0)
                c = self.ch_next[eng]
                self.ch_next[eng] = (c + 1) % nfree
            o["ch"] = c
            o["prev"] = self.ch_last[eng][c]
            self.ch_last[eng][c] = idx
            o["signal"] = True
        else:
            self.last_op[eng] = idx
        self.ops.append(o)
        for k in reads:
            self.readers.setdefault(k, []).append(idx)
        for k in writes:
            self.last_w[k] = idx
            self.readers[k] = []
        return idx

    def dma(self, fn, reads=(), writes=(), q="sp"):
        return self.op(q, fn, reads, writes, dma=True)

    def barrier(self):
        b = set()
        for e in self.engs:
            if self.last_op[e] is not None:
                b.add(self.last_op[e])
            for c in self.ch_last[e]:
                if c is not None:
                    b.add(c)
        for e in self.engs:
            self.pending_bar[e] = set(b)

    def emit(self, stack):
        nc, ops = self.nc, self.ops
        for o in ops:
            for d in o["deps"]:
                if self._needs_wait(ops[d], o["eng"]):
                    ops[d]["signal"] = True
        used = [e for e in self.engs if any(o["eng"] == e for o in ops)]
        sems = {e: stack.enter_context(nc.semaphore(f"sem_{e}")) for e in used}
        chs = {}
        for e in used:
            if any(o["dma"] and o["eng"] == e for o in ops):
                chs[e] = [stack.enter_context(nc.semaphore(f"dch_{e}_{i}")) for i in range(self.n_dma_ch)]
        ticket = {e: 0 for e in used}
        chcnt = {e: [0] * self.n_dma_ch for e in used}
        for o in ops:
            e = o["eng"]
            if o["dma"]:
                chcnt[e][o["ch"]] += o["inc"]
                o["sig"] = (f"dch_{e}_{o['ch']}", chs[e][o["ch"]], chcnt[e][o["ch"]])
            elif o["signal"]:
                ticket[e] += 1
                o["sig"] = (f"sem_{e}", sems[e], ticket[e])
            else:
                o["sig"] = None
        per_eng = {e: [i for i, o in enumerate(ops) if o["eng"] == e] for e in used}
        waited = {e: {} for e in used}
        self.n_wait = 0
        block = stack.enter_context(nc.Block())

        def make(e):
            def body(engh):
                for i in per_eng[e]:
                    o = ops[i]
                    need = {}
                    dl = set(o["deps"])
                    if o["dma"] and o["prev"] is not None:
                        dl.add(o["prev"])
                    for d in dl:
                        p = ops[d]
                        if not (o["dma"] and d == o.get("prev")) and not self._needs_wait(p, e):
                            continue
                        name, sem, val = p["sig"]
                        if need.get(name, (None, 0))[1] < val:
                            need[name] = (sem, val)
                    for name, (sem, val) in need.items():
                        if waited[e].get(name, 0) < val:
                            engh.wait_ge(sem, val)
                            waited[e][name] = val
                            self.n_wait += 1
                    ins = o["fn"](engh)
                    if o["sig"] is not None:
                        if o["dma"] and o["inc"] == 1:
                            ins.then_inc(o["sig"][1])
                        else:
                            ins.then_inc(o["sig"][1], 16 if o["dma"] else 1)
                if e in chs:
                    for c in range(self.n_dma_ch):
                        if chcnt[e][c] > 0:
                            engh.wait_ge(chs[e][c], chcnt[e][c])
            return body

        for e in used:
            getattr(block, ENG_ATTR[e])(make(e))


def vec_pk(ap1d, p=128):
    return ap1d.rearrange("(kc p) -> p kc", p=p)


def build_program(NBLK=17, n_layers=4, with_samples=True, KEEP=256, NIT=24, n_cores=N_CORES, dbg_stop=99, NPHYS=2560, NITS=26):
    NT = NBLK * 128
    HALF = NT - 128
    NKEY = 2 * HALF
    CH = 512
    BIG = 30000.0
    U8 = mybir.dt.uint8
    tiles = [(0, 128)]
    t = 128
    while t < NT:
        w = min(256, NT - t)
        tiles.append((t, w))
        t += w
    WMAX = max(max(w for _, w in tiles), 128 + NSMP)

    nc = bass.Bass("TRN2", target_bir_lowering=False)
    dt_in = lambda name, shape, dt=F32: nc.dram_tensor(name, list(shape), dt, kind="ExternalInput").ap()
    dt_out = lambda name, shape, dt=F32: nc.dram_tensor(name, list(shape), dt, kind="ExternalOutput").ap()

    xloc = dt_in("xloc", [NT, D])
    call = dt_in("call", [1 + NSMP, D])
    role = dt_in("role", [128, 2])
    w_ada = dt_in("w_ada", [DEPTH, D, 6 * D]); b_ada = dt_in("b_ada", [DEPTH, 6 * D])
    ln_g = dt_in("ln_g", [DEPTH, 2, D]); ln_b = dt_in("ln_b", [DEPTH, 2, D])
    sgu_w_in = dt_in("sgu_w_in", [2, D, 2 * D]); sgu_b_in = dt_in("sgu_b_in", [2, 2 * D])
    sgu_norm_g = dt_in("sgu_norm_g", [2, D]); sgu_norm_b = dt_in("sgu_norm_b", [2, D])
    sgu_w_s = dt_in("sgu_w_s", [2, 8, 128, 128]); sgu_b_s = dt_in("sgu_b_s", [2, 8, 128])
    sgu_w_out = dt_in("sgu_w_out", [2, D, D])
    ffn_w_up = dt_in("ffn_w_up", [DEPTH, D, 2 * DFF]); ffn_conv_w = dt_in("ffn_conv_w", [DEPTH, 3, DFF])
    ffn_conv_b = dt_in("ffn_conv_b", [DEPTH, DFF]); ffn_w_down = dt_in("ffn_w_down", [DEPTH, DFF, D])

    dsa_w_in = dt_in("dsa_w_in", [2, D, 2120]); dsa_w_out = dt_in("dsa_w_out", [2, D, D])
    knew = dt_out("knew", [2, HALF, 256]); vnew = dt_out("vnew", [2, HALF, 256]); kinew = dt_out("kinew", [2, HALF, 64])
    VSEG = min(2 * HALF, 2048)
    NVS = (2 * HALF) // VSEG
    SEGW = [HALF, HALF] + [VSEG] * NVS + [HALF]
    NSEG = len(SEGW)
    bounce = [[nc.dram_tensor(f"bounce{i}_{g}", [128, w], BF16, kind="Internal").ap() for g, w in enumerate(SEGW)] for i in range(2)]
    gath = [[nc.dram_tensor(f"gath{i}_{g}", [256, w], BF16, kind="Internal").ap() for g, w in enumerate(SEGW)] for i in range(2)]
    SMP = with_samples
    WS = NSMP if SMP else 0
    xs_in = dt_in("xs_in", [NSMP, D]); sconv = dt_in("sconv", [DEPTH, NSMP, 2, DFF])
    NPG = 16
    KEEP_S = 256
    cache_k = [dt_in(f"cache_k{i}", [NPHYS * 128, 256]) for i in range(2)]; cache_v = [dt_in(f"cache_v{i}", [NPHYS * 128, 256]) for i in range(2)]
    cache_ki = [dt_in(f"cache_ki{i}", [NPHYS * 128, 64]) for i in range(2)]; ptab = dt_in("ptab", [NSMP, NPG], I32)
    ksn = dt_out("ksn", [2, NSMP, 256]); vsn = dt_out("vsn", [2, NSMP, 256]); kisn = dt_out("kisn", [2, NSMP, 64])
    ys = dt_out("ys", [NSMP, D]); sguv = dt_out("sguv", [2, NSMP, D]); convs = dt_out("convs", [DEPTH, NSMP, 2, DFF])
    y_loc = dt_out("y_loc", [NT, D])
    convp = dt_out("convp", [DEPTH, 2, DFF])

    st = contextlib.ExitStack()
    with st:
        SB = lambda n, s, d=F32: st.enter_context(nc.sbuf_tensor(n, list(s), d))
        PS = lambda n: st.enter_context(nc.psum_tensor(n, [128, 512], F32))
        S = Sched(nc)

        xres = SB("xres", [128, KC, NT])
        ident = SB("ident", [128, 128]); ones_f = SB("ones_f", [128, 128])
        tri01 = SB("tri01", [128, 128])
        rolet = SB("rolet", [128, 2])
        cT = SB("cT", [128, KC, 1 + NSMP])
        modp = SB("modp", [128, 48])
        modp1 = SB("modp1", [128, 48])
        lng = SB("lng", [128, 2, KC]); lnb = SB("lnb", [128, 2, KC])
        xsT = SB("xsT", [128, KC, NSMP]); zs = SB("zs", [128, KC, NSMP]); mods1 = SB("mods1", [128, 48, NSMP])
        KTn = SB("KTn", [128, 2, NSMP], BF16); kiTn = SB("kiTn", [128, NSMP], BF16); Vn = SB("Vn", [NSMP, 256], BF16)
        idx_all = SB("idx_all", [128, NSMP * NPG], I32); piota_p = SB("piota_p", [128, 1])
        vTs = SB("vTs", [128, KC, NSMP]); w00c = SB("w00c", [128, 8]); bs0c = SB("bs0c", [128, 8]); tmp16 = SB("tmp16", [128, KC, NSMP])
        hT = SB("hT", [128, KC, WMAX], BF16)
        z = SB("z", [128, KC, WMAX])
        zsq = [SB(f"zsq{i}", [128, WMAX]) for i in range(2)]
        m_t = SB("m_t", [128, WMAX]); v_t = SB("v_t", [128, WMAX]); r_t = SB("r_t", [128, WMAX])
        NWB, WK = 3, 11
        wst = [SB(f"wst{i}", [128, WK, 128]) for i in range(NWB)]
        wbf = [SB(f"wbf{i}", [128, WK, 128], BF16) for i in range(NWB)]
        tokt = SB("tokt", [128, D])
        psA = [PS(f"psA{i}") for i in range(2)]
        psB = [PS(f"psB{i}") for i in range(2)]
        psT = [PS(f"psT{i}") for i in range(2)]
        psS = [PS(f"psS{i}") for i in range(2)]

        cnt = {"w": 0, "ps": 0, "pb": 0, "pt": 0, "zs": 0, "pq": 0}
        PAIRS = [[2 * i, 2 * i + 1] for i in range(n_cores // 2)]

        def slow_vec(dst, src):
            S.dma(lambda e: e.dma_start(out=dst, in_=src, allow_slow_non_contiguous=True), writes=[dst.tensor.name])

        def load_wchunk(w_ap2d, kcn, c0, ncol=128, k0=0):
            i = cnt["w"] % NWB
            cnt["w"] += 1
            src = w_ap2d[k0 * 128:(k0 + kcn) * 128, c0:c0 + ncol].rearrange("(kc p) n -> p kc n", p=128)
            S.dma(lambda e: e.dma_start(out=wst[i][:, :kcn, :ncol], in_=src), writes=[f"wst{i}"])
            if cnt["w"] % 2 == 0:
                S.op("act", lambda e: e.activation(out=wbf[i][:, :kcn, :ncol], in_=wst[i][:, :kcn, :ncol], func=AF.Identity),
                     reads=[f"wst{i}"], writes=[f"wbf{i}"])
            else:
                S.op("dve", lambda e: e.tensor_copy(out=wbf[i][:, :kcn, :ncol], in_=wst[i][:, :kcn, :ncol]),
                     reads=[f"wst{i}"], writes=[f"wbf{i}"])
            return wbf[i], f"wbf{i}"

        def mm_group(ps, pskey, W, pairs, reads):
            def fn(e):
                n = len(pairs)
                for j, (l, r) in enumerate(pairs):
                    ins = e.matmul(ps, lhsT=l, rhs=r, start=(j == 0), stop=(j == n - 1))
                return ins
            S.op("pe", fn, reads=reads, writes=[pskey])

        def linear_chunk(w_ap2d, kcn, c0, src, srckeys, W, ps, pskey, off=0):
            pairs, keys = [], []
            for k0 in range(0, kcn, WK):
                kn = min(WK, kcn - k0)
                wt, wkey = load_wchunk(w_ap2d, kn, c0, k0=k0)
                pairs += [(wt[:, k, :], src[:, k0 + k, off:off + W]) for k in range(kn)]
                keys.append(wkey)
            mm_group(ps[:, :W], pskey, W, pairs, keys + srckeys)

        def next_ps(lst, name):
            i = cnt[name] % 2
            cnt[name] += 1
            return lst[i], "%s%d" % ({"pq": "psS"}.get(name, name), i)

        def tile_key(t):
            return "x%d" % [tt for tt, ww in tiles if tt <= t < tt + ww][0]

        def modulate(t0, W, jshift, jscale):
            xk = tile_key(t0)
            for k in range(KC):
                S.op("act", lambda e, k=k: e.activation(out=hT[:, k, :W], in_=xres[:, k, t0:t0 + W], func=AF.Identity,
                                                        scale=modp1[:, jscale * 8 + k:jscale * 8 + k + 1],
                                                        bias=modp[:, jshift * 8 + k:jshift * 8 + k + 1]),
                     reads=[xk, "modp", "modp1"], writes=["hT"])

        def postnorm(t0, W, sub, samples=False):
            if samples:
                return _postnorm(zs, "zs", NSMP, sub, lambda k: xsT[:, k, :], "xs")
            return _postnorm(z, "z", W, sub, lambda k: xres[:, k, t0:t0 + W], tile_key(t0))

        def _postnorm(z, zk, W, sub, xout, xk):
            s1, s1k = psS[0], "psS0"
            s2, s2k = psS[1], "psS1"
            for k in range(KC):
                i = cnt["zs"] % 2
                cnt["zs"] += 1
                S.op("act", lambda e, k=k, i=i: e.activation(out=zsq[i][:, :W], in_=z[:, k, :W], func=AF.Square),
                     reads=[zk], writes=[f"zsq{i}"])
                S.op("pe", lambda e, k=k: e.matmul(s1[:, :W], lhsT=ones_f[:], rhs=z[:, k, :W], start=(k == 0), stop=(k == KC - 1)),
                     reads=[zk, "ones_f"], writes=[s1k])
                S.op("pe", lambda e, k=k, i=i: e.matmul(s2[:, :W], lhsT=ones_f[:], rhs=zsq[i][:, :W], start=(k == 0), stop=(k == KC - 1)),
                     reads=[f"zsq{i}", "ones_f"], writes=[s2k])
            S.op("act", lambda e: e.activation(out=m_t[:, :W], in_=s1[:, :W], func=AF.Identity, scale=1.0 / D), reads=[s1k], writes=["m_t"])
            S.op("dve", lambda e: e.tensor_tensor(out=v_t[:, :W], in0=m_t[:, :W], in1=m_t[:, :W], op=ALU.mult), reads=["m_t"], writes=["v_t"])
            S.op("dve", lambda e: e.scalar_tensor_tensor(out=v_t[:, :W], in0=s2[:, :W], scalar=1.0 / D, in1=v_t[:, :W],
                                                         op0=ALU.mult, op1=ALU.subtract), reads=[s2k, "v_t"], writes=["v_t"])
            S.op("act", lambda e: e.activation(out=r_t[:, :W], in_=v_t[:, :W], func=AF.Sqrt, bias=epsA[:, 0:1]), reads=["v_t", "epsA"], writes=["r_t"])
            S.op("dve", lambda e: e.reciprocal(out=r_t[:, :W], in_=r_t[:, :W]), reads=["r_t"], writes=["r_t"])
            for k in range(KC):
                S.op("dve", lambda e, k=k: e.tensor_tensor(out=z[:, k, :W], in0=z[:, k, :W], in1=m_t[:, :W], op=ALU.subtract),
                     reads=[zk, "m_t"], writes=[zk])
                S.op("dve", lambda e, k=k: e.tensor_tensor(out=z[:, k, :W], in0=z[:, k, :W], in1=r_t[:, :W], op=ALU.mult),
                     reads=[zk, "r_t"], writes=[zk])
                S.op("act", lambda e, k=k: e.activation(out=xout(k), in_=z[:, k, :W], func=AF.Identity,
                                                        scale=lng[:, sub, k:k + 1], bias=lnb[:, sub, k:k + 1]),
                     reads=[zk, "lng", "lnb"], writes=[xk])

        def z_from_ps(ps, pskey, c, t0, W, jgate):
            S.op("dve", lambda e: e.scalar_tensor_tensor(out=z[:, c, :W], in0=ps[:, :W], scalar=modp1[:, jgate * 8 + c:jgate * 8 + c + 1],
                                                         in1=xres[:, c, t0:t0 + W], op0=ALU.mult, op1=ALU.add),
                 reads=[pskey, "modp1", tile_key(t0)], writes=["z"])

        def modulate_s(jshift, jscale):
            S.op("dve", lambda e: e.tensor_tensor(out=tmp16[:], in0=xsT[:], in1=mods1[:, jscale * 8:jscale * 8 + 8, :], op=ALU.mult),
                 reads=["xs", "mods1"], writes=["tmp16"])
            S.op("dve", lambda e: e.tensor_tensor(out=hT[:, :, 128:128 + NSMP], in0=tmp16[:], in1=modall[:, jshift * 8:jshift * 8 + 8, 1:1 + NSMP],
                                                  op=ALU.add), reads=["tmp16", "modall"], writes=["hT"])

        def zs_from_ps(ps, pskey, c, jgate, off=128):
            S.op("dve", lambda e: e.tensor_tensor(out=zs[:, c, :], in0=ps[:, off:off + NSMP], in1=mods1[:, jgate * 8 + c, :], op=ALU.mult),
                 reads=[pskey, "mods1"], writes=["zs"])
            S.op("dve", lambda e: e.tensor_tensor(out=zs[:, c, :], in0=zs[:, c, :], in1=xsT[:, c, :], op=ALU.add), reads=["zs", "xs"], writes=["zs"])

        epsA = SB("epsA", [128, 1]); epsL = SB("epsL", [128, 1]); onesb = SB("onesb", [128, 128], BF16)
        S.op("pool", lambda e: e.memset(epsA[:], EPS_A), writes=["epsA"])
        S.op("pool", lambda e: e.memset(epsL[:], LN_EPS), writes=["epsL"])
        S.op("pool", lambda e: e.memset(ones_f[:], 1.0), writes=["ones_f"])
        identb = SB("identb", [128, 128], BF16)
        S.op("pool", lambda e: e.memset(ident[:], 1.0), writes=["ident"])
        S.op("pool", lambda e: e.affine_select(out=ident[:], in_=ident[:], pattern=[[1, 128]], compare_op=ALU.is_equal,
                                               fill=0.0, base=0, channel_multiplier=-1), reads=["ident"], writes=["ident"])
        S.op("pool", lambda e: e.memset(tri01[:], 1.0), writes=["tri01"])
        S.op("pool", lambda e: e.affine_select(out=tri01[:], in_=tri01[:], pattern=[[1, 128]], compare_op=ALU.is_ge,
                                               fill=0.0, base=0, channel_multiplier=-1), reads=["tri01"], writes=["tri01"])
        S.dma(lambda e: e.dma_start(out=rolet[:], in_=role[:, :]), writes=["rolet"])
        S.op("pool", lambda e: e.tensor_copy(out=identb[:], in_=ident[:]), reads=["ident"], writes=["identb"])
        S.op("pool", lambda e: e.tensor_copy(out=onesb[:], in_=ones_f[:]), reads=["ones_f"], writes=["onesb"])

        def transpose_rows(src_tile, nrows, ncols_chunks, consume):
            for c0 in range(0, ncols_chunks, 4):
                pt, ptk = next_ps(psT, "pt")
                cs = list(range(c0, min(c0 + 4, ncols_chunks)))

                def fn(e, cs=cs, pt=pt):
                    for c in cs:
                        ins = e.transpose(out=pt[:, (c - cs[0]) * 128:(c - cs[0]) * 128 + nrows],
                                          in_=src_tile[:nrows, c * 128:(c + 1) * 128], identity=ident[:nrows, :nrows])
                    return ins
                S.op("pe", fn, reads=["tokt", "ident"], writes=[ptk])
                for c in cs:
                    consume(c, pt[:, (c - cs[0]) * 128:(c - cs[0]) * 128 + nrows], ptk)

        S.dma(lambda e: e.dma_start(out=tokt[:1 + NSMP, :], in_=call[:, :]), writes=["tokt"])
        transpose_rows(tokt, 1 + NSMP, KC,
                       lambda c, p, k: S.op("act", lambda e: e.activation(out=cT[:, c, :], in_=p, func=AF.Silu), reads=[k], writes=["cT"]))

        S.op("pool", lambda e: e.iota(piota_p[:], pattern=[[0, 1]], base=0, channel_multiplier=1, allow_small_or_imprecise_dtypes=True), writes=["piota_p"])
        if SMP:
            ptf = SB("ptf", [128, NSMP * NPG])
            S.dma(lambda e: e.dma_start(out=idx_all[:], in_=ptab.rearrange("s g -> (s g)").partition_broadcast(128)), writes=["idx_all"])
            S.op("dve", lambda e: e.tensor_copy(out=ptf[:], in_=idx_all[:]), reads=["idx_all"], writes=["ptf"])
            S.op("dve", lambda e: e.tensor_scalar(out=ptf[:], in0=ptf[:], scalar1=128.0, scalar2=piota_p[:, 0:1], op0=ALU.mult, op1=ALU.add),
                 reads=["ptf", "piota_p"], writes=["ptf"])
            S.op("dve", lambda e: e.tensor_copy(out=idx_all[:], in_=ptf[:]), reads=["ptf"], writes=["idx_all"])
            S.dma(lambda e: e.dma_start(out=tokt[:NSMP, :], in_=xs_in[:, :]), writes=["tokt"])
            transpose_rows(tokt, NSMP, KC,
                           lambda c, p, k: S.op("dve", lambda e: e.tensor_copy(out=xsT[:, c, :], in_=p), reads=[k], writes=["xs"]))
        for b in range(NBLK):
            S.dma(lambda e, b=b: e.dma_start(out=tokt[:], in_=xloc[b * 128:(b + 1) * 128, :]), writes=["tokt"])
            tk = [tt for tt, ww in tiles if tt <= b * 128 < tt + ww][0]
            transpose_rows(tokt, 128, KC,
                           lambda c, p, k, b=b, tk=tk: S.op("dve", lambda e: e.tensor_copy(out=xres[:, c, b * 128:(b + 1) * 128], in_=p),
                                                            reads=[k], writes=[f"x{tk}"]))

        bada_t = SB("bada_t", [128, 48])
        modall = SB("modall", [128, 48, 1 + NSMP])
        halo = SB("halo", [128, FC, 2])
        cw = SB("cw", [128, 3, FC]); cb = SB("cb", [128, FC])
        b_u = SB("b_u", [128, KC]); ngp = SB("ngp", [128, KC]); nbp = SB("nbp", [128, KC])
        bst = SB("bst", [128, 2, 6]); mv = SB("mv", [128, 2]); rs_t = SB("rs_t", [128, 1])

        ARENA_F32 = 19712
        arena = SB("arena", [128, ARENA_F32])
        ar = {"off": 0, "phase": 0}

        def new_phase():
            S.barrier()
            ar["off"] = 0
            ar["phase"] += 1

        def carve(shape, dt=F32):
            n = 1
            for d_ in shape[1:]:
                n *= d_
            nf = n if dt == F32 else (n + 1) // 2
            a = arena[:, ar["off"]:ar["off"] + nf]
            ar["off"] += nf
            assert ar["off"] <= ARENA_F32, ("arena overflow", ar["off"])
            if dt != F32:
                a = a.bitcast(dt)
            if len(shape) == 3:
                a = a.rearrange("p (a b) -> p a b", a=shape[1])
            return a

        def load_resident(dst, dkey, w_ap2d, c0, ncols):
            for cc in range(0, ncols, 128):
                wt, wkey = load_wchunk(w_ap2d, KC, c0 + cc)
                S.op("act", lambda e, cc=cc, wt=wt: e.activation(out=dst[:, :, cc:cc + 128], in_=wt[:, :KC, :], func=AF.Identity),
                     reads=[wkey], writes=[dkey])

        def do_layer(li):
            j = li // 2
            slow_vec(bada_t[:], vec_pk(b_ada[li]))
            slow_vec(lng[:, 0, :], vec_pk(ln_g[li, 0])); slow_vec(lng[:, 1, :], vec_pk(ln_g[li, 1]))
            slow_vec(lnb[:, 0, :], vec_pk(ln_b[li, 0])); slow_vec(lnb[:, 1, :], vec_pk(ln_b[li, 1]))
            for jj in range(48):
                i = cnt["w"] % NWB
                cnt["w"] += 1
                S.dma(lambda e, i=i, jj=jj: e.dma_start(out=wst[i][:, :KC, :],
                                                         in_=w_ada[li][:, jj * 128:(jj + 1) * 128].rearrange("(kc p) n -> p kc n", p=128)),
                      writes=[f"wst{i}"])
                ps, pk = next_ps(psA, "ps")
                mm_group(ps[:, :1 + NSMP], pk, 1 + NSMP, [(wst[i][:, k, :], cT[:, k, :]) for k in range(KC)], [f"wst{i}", "cT"])
                S.op("act", lambda e, ps=ps, jj=jj: e.activation(out=modall[:, jj, :], in_=ps[:, :1 + NSMP], func=AF.Identity,
                                                                 bias=bada_t[:, jj:jj + 1]), reads=[pk, "bada_t"], writes=["modall"])
            S.op("dve", lambda e: e.tensor_copy(out=modp[:], in_=modall[:, :, 0]), reads=["modall"], writes=["modp"])
            S.op("dve", lambda e: e.tensor_scalar(out=modp1[:], in0=modp[:], scalar1=1.0, scalar2=None, op0=ALU.add),
                 reads=["modp"], writes=["modp1"])
            for jg in (2, 5):
                S.op("dve", lambda e, jg=jg: e.tensor_scalar(out=modp1[:, jg * 8:jg * 8 + 8], in0=modp1[:, jg * 8:jg * 8 + 8],
                                                             scalar1=1.0 / ALPHA, scalar2=None, op0=ALU.mult),
                     reads=["modp1"], writes=["modp1"])

            if SMP:
                S.op("dve", lambda e: e.tensor_scalar(out=mods1[:], in0=modall[:, :, 1:1 + NSMP], scalar1=1.0, scalar2=None, op0=ALU.add),
                     reads=["modall"], writes=["mods1"])
                for jg in (2, 5):
                    S.op("dve", lambda e, jg=jg: e.tensor_scalar(out=mods1[:, jg * 8:jg * 8 + 8, :], in0=mods1[:, jg * 8:jg * 8 + 8, :],
                                                                 scalar1=1.0 / ALPHA, scalar2=None, op0=ALU.mult), reads=["mods1"], writes=["mods1"])
            if li % 2 == 0:
                new_phase()
                w_u = carve([128, KC, D], BF16); w_v = carve([128, KC, D], BF16); w_o = carve([128, KC, D], BF16)
                bvb = carve([128, D]); WsT = carve([128, 8, 128], BF16); C2 = carve([128, 8, 128])
                uT = carve([128, KC, WMAX]); umT = carve([128, KC, WMAX], BF16)
                vtok = carve([128, D]); vhat = carve([128, D], BF16); mix = carve([128, 128])
                load_resident(w_u, "w_u", sgu_w_in[j], 0, D)
                load_resident(w_v, "w_v", sgu_w_in[j], D, D)
                load_resident(w_o, "w_o", sgu_w_out[j], 0, D)
                slow_vec(b_u[:], vec_pk(sgu_b_in[j, 0:D]))
                slow_vec(ngp[:], vec_pk(sgu_norm_g[j])); slow_vec(nbp[:], vec_pk(sgu_norm_b[j]))
                S.dma(lambda e: e.dma_start(out=bvb, in_=sgu_b_in[j, D:2 * D].partition_broadcast(128)), writes=["bvb"])
                S.dma(lambda e: e.dma_start(out=C2, in_=sgu_b_s[j].partition_broadcast(128)), writes=["C2"])
                for g in range(8):
                    S.dma(lambda e, g=g: e.dma_start(out=tokt[:, g * 128:(g + 1) * 128], in_=sgu_w_s[j, g]), writes=["tokt"])
                transpose_rows(tokt, 128, 8,
                               lambda c, p, k: S.op("dve", lambda e: e.tensor_tensor(out=WsT[:, c, :], in0=p, in1=tri01[:], op=ALU.mult),
                                                    reads=[k, "tri01"], writes=["WsT"]))
                S.op("pool", lambda e: e.tensor_copy(out=onesb[:], in_=ones_f[:]), reads=["ones_f"], writes=["onesb"])
                for g in range(8):
                    ps, pk = next_ps(psA, "ps")
                    mm_group(ps[:, :128], pk, 128, [(onesb[:], WsT[:, g, :])], ["onesb", "WsT"])
                    S.op("dve", lambda e, g=g, ps=ps: e.scalar_tensor_tensor(out=C2[:, g, :], in0=ps[:, :128], scalar=nbp[:, g:g + 1],
                                                                             in1=C2[:, g, :], op0=ALU.mult, op1=ALU.add),
                         reads=[pk, "nbp", "C2"], writes=["C2"])
                if SMP:
                    S.dma(lambda e: e.dma_start(out=w00c[:], in_=sgu_w_s[j, :, 0, 0].partition_broadcast(128), allow_slow_non_contiguous=True), writes=["w00c"])
                    S.dma(lambda e: e.dma_start(out=bs0c[:], in_=sgu_b_s[j, :, 0].partition_broadcast(128), allow_slow_non_contiguous=True), writes=["bs0c"])

                def sgu_tile(t0, W):
                    smp = SMP and t0 == 0
                    Wx = W + (NSMP if smp else 0)
                    modulate(t0, W, 0, 1)
                    if smp:
                        modulate_s(0, 1)
                    for c in range(KC):
                        ps, pk = next_ps(psA, "ps")
                        mm_group(ps[:, :Wx], pk, Wx, [(w_u[:, k, c * 128:(c + 1) * 128], hT[:, k, :Wx]) for k in range(KC)], ["w_u", "hT"])
                        S.op("act", lambda e, c=c, ps=ps: e.activation(out=uT[:, c, :Wx], in_=ps[:, :Wx], func=AF.Gelu_apprx_tanh,
                                                                      bias=b_u[:, c:c + 1]), reads=[pk, "b_u"], writes=["uT"])
                    if smp:
                        for half in range(2):
                            ps, pk = next_ps(psB, "pb")
                            mm_group(ps[:NSMP, :512], pk, 512,
                                     [(hT[:, k, 128:128 + NSMP], w_v[:, k, half * 512:(half + 1) * 512]) for k in range(KC)], ["w_v", "hT"])
                            S.op("dve", lambda e, ps=ps, half=half: e.tensor_tensor(out=vtok[:NSMP, half * 512:(half + 1) * 512], in0=ps[:NSMP, :512],
                                                                                    in1=bvb[:NSMP, half * 512:(half + 1) * 512], op=ALU.add),
                                 reads=[pk, "bvb"], writes=["vtok"])
                        S.op("act", lambda e: e.activation(out=vtok[:NSMP, :], in_=vtok[:NSMP, :], func=AF.Gelu_apprx_tanh), reads=["vtok"], writes=["vtok"])
                        for half in range(2):
                            S.op("dve", lambda e, half=half: e.bn_stats(out=bst[:NSMP, half, :], in_=vtok[:NSMP, half * 512:(half + 1) * 512]),
                                 reads=["vtok"], writes=["bst"])
                        S.op("dve", lambda e: e.bn_aggr(out=mv[:NSMP], in_=bst[:NSMP]), reads=["bst"], writes=["mv"])
                        S.op("act", lambda e: e.activation(out=rs_t[:NSMP], in_=mv[:NSMP, 1:2], func=AF.Sqrt, bias=epsL[:NSMP, 0:1]),
                             reads=["mv", "epsL"], writes=["rs_t"])
                        S.op("dve", lambda e: e.reciprocal(out=rs_t[:NSMP], in_=rs_t[:NSMP]), reads=["rs_t"], writes=["rs_t"])
                        S.op("dve", lambda e: e.tensor_scalar(out=vtok[:NSMP, :], in0=vtok[:NSMP, :], scalar1=mv[:NSMP, 0:1], scalar2=rs_t[:NSMP, 0:1],
                                                              op0=ALU.subtract, op1=ALU.mult), reads=["vtok", "mv", "rs_t"], writes=["vtok"])
                        for c0 in (0, 4):
                            pt, ptk = next_ps(psT, "pt")

                            def fn(e, c0=c0, pt=pt):
                                for c in range(c0, c0 + 4):
                                    ins = e.transpose(out=pt[:, (c - c0) * 128:(c - c0) * 128 + NSMP], in_=vtok[:NSMP, c * 128:(c + 1) * 128],
                                                      identity=ident[:NSMP, :NSMP])
                                return ins
                            S.op("pe", fn, reads=["vtok", "ident"], writes=[ptk])
                            for c in range(c0, c0 + 4):
                                S.op("act", lambda e, c=c, c0=c0, pt=pt: e.activation(out=vTs[:, c, :], in_=pt[:, (c - c0) * 128:(c - c0) * 128 + NSMP],
                                                                                    func=AF.Identity, scale=ngp[:, c:c + 1], bias=nbp[:, c:c + 1]),
                                     reads=[ptk, "ngp", "nbp"], writes=["vTs"])
                        for c0 in (0, 4):
                            pt, ptk = next_ps(psT, "pt")

                            def fn2(e, c0=c0, pt=pt):
                                for c in range(c0, c0 + 4):
                                    ins = e.transpose(out=pt[:NSMP, (c - c0) * 128:(c - c0 + 1) * 128], in_=vTs[:, c, :], identity=ident[:])
                                return ins
                            S.op("pe", fn2, reads=["vTs", "ident"], writes=[ptk])
                            S.op("dve", lambda e, c0=c0, pt=pt: e.tensor_copy(out=tokt[:NSMP, c0 * 128:(c0 + 4) * 128], in_=pt[:NSMP, :512]),
                                 reads=[ptk], writes=["tokt"])
                        S.dma(lambda e: e.dma_start(out=sguv[j], in_=tokt[:NSMP, :]), reads=["tokt"])
                        for g in range(8):
                            S.op("dve", lambda e, g=g: e.tensor_scalar(out=tmp16[:, g, :], in0=vTs[:, g, :], scalar1=w00c[:, g:g + 1], scalar2=bs0c[:, g:g + 1],
                                                                      op0=ALU.mult, op1=ALU.add), reads=["vTs", "w00c", "bs0c"], writes=["tmp16"])
                        S.op("dve", lambda e: e.tensor_tensor(out=umT[:, :, 128:128 + NSMP], in0=tmp16[:], in1=uT[:, :, 128:128 + NSMP], op=ALU.mult),
                             reads=["tmp16", "uT"], writes=["umT"])
                    for bb in range(W // 128):
                        for half in range(2):
                            ps, pk = next_ps(psB, "pb")
                            mm_group(ps[:, :512], pk, 512,
                                     [(hT[:, k, bb * 128:(bb + 1) * 128], w_v[:, k, half * 512:(half + 1) * 512]) for k in range(KC)], ["w_v", "hT"])
                            S.op("dve", lambda e, ps=ps, half=half: e.tensor_tensor(out=vtok[:, half * 512:(half + 1) * 512], in0=ps[:, :512],
                                                                                    in1=bvb[:, half * 512:(half + 1) * 512], op=ALU.add),
                                 reads=[pk, "bvb"], writes=["vtok"])
                        S.op("act", lambda e: e.activation(out=vtok, in_=vtok, func=AF.Gelu_apprx_tanh), reads=["vtok"], writes=["vtok"])
                        for half in range(2):
                            S.op("dve", lambda e, half=half: e.bn_stats(out=bst[:, half, :], in_=vtok[:, half * 512:(half + 1) * 512]),
                                 reads=["vtok"], writes=["bst"])
                        S.op("dve", lambda e: e.bn_aggr(out=mv[:], in_=bst[:]), reads=["bst"], writes=["mv"])
                        S.op("act", lambda e: e.activation(out=rs_t[:], in_=mv[:, 1:2], func=AF.Sqrt, bias=epsL[:, 0:1]),
                             reads=["mv", "epsL"], writes=["rs_t"])
                        S.op("dve", lambda e: e.reciprocal(out=rs_t[:], in_=rs_t[:]), reads=["rs_t"], writes=["rs_t"])
                        S.op("dve", lambda e: e.tensor_scalar(out=vhat, in0=vtok, scalar1=mv[:, 0:1], scalar2=rs_t[:, 0:1],
                                                              op0=ALU.subtract, op1=ALU.mult), reads=["vtok", "mv", "rs_t"], writes=["vhat"])
                        for gh in range(2):
                            ps, pk = next_ps(psB, "pb")

                            def fn(e, ps=ps, gh=gh):
                                for g4 in range(4):
                                    g = gh * 4 + g4
                                    ins = e.matmul(ps[:, g4 * 128:(g4 + 1) * 128], lhsT=vhat[:, g * 128:(g + 1) * 128], rhs=WsT[:, g, :],
                                                   start=True, stop=True)
                                return ins
                            S.op("pe", fn, reads=["vhat", "WsT"], writes=[pk])
                            for g4 in range(4):
                                g = gh * 4 + g4
                                S.op("dve", lambda e, ps=ps, g=g, g4=g4: e.scalar_tensor_tensor(
                                    out=mix, in0=ps[:, g4 * 128:(g4 + 1) * 128], scalar=ngp[:, g:g + 1], in1=C2[:, g, :],
                                    op0=ALU.mult, op1=ALU.add), reads=[pk, "ngp", "C2"], writes=["mix"])
                                S.op("dve", lambda e, g=g, bb=bb: e.tensor_tensor(out=umT[:, g, bb * 128:(bb + 1) * 128], in0=mix,
                                                                                  in1=uT[:, g, bb * 128:(bb + 1) * 128], op=ALU.mult),
                                     reads=["mix", "uT"], writes=["umT"])
                    for c in range(KC):
                        ps, pk = next_ps(psA, "ps")
                        mm_group(ps[:, :Wx], pk, Wx, [(w_o[:, k, c * 128:(c + 1) * 128], umT[:, k, :Wx]) for k in range(KC)], ["w_o", "umT"])
                        z_from_ps(ps, pk, c, t0, W, 2)
                        if smp:
                            zs_from_ps(ps, pk, c, 2)
                    postnorm(t0, W, 0)
                    if smp:
                        postnorm(0, NSMP, 0, samples=True)
                for (t0, W) in tiles:
                    sgu_tile(t0, W)
            else:

                jd = j
                assert HALF % CH == 0
                WIN = dsa_w_in[jd]
                new_phase()
                wkv = carve([128, KC, 512], BF16); wki = carve([128, KC, 64], BF16)
                KTl = carve([128, 2, NT], BF16); kiTl = carve([128, NT], BF16); Vl = carve([128, NBLK, 256], BF16)
                kvst = [carve([128, 576]) for _ in range(2)]
                load_resident(wkv, "wkv", WIN, 1024, 512)
                wt, wkey = load_wchunk(WIN, KC, 2048, ncol=64)
                S.op("pool", lambda e, wt=wt: e.tensor_copy(out=wki, in_=wt[:, :KC, :64]), reads=[wkey], writes=["wki"])
                wki2 = carve([128, KC, 128], BF16)
                for hh in range(2):
                    S.op("pool", lambda e, hh=hh: e.tensor_copy(out=wki2[:, :, hh * 64:(hh + 1) * 64], in_=wki), reads=["wki"], writes=["wki2"])
                S.op("pool", lambda e: e.memset(kiTl, 0.0), writes=["kiTl"])
                def dsap_tile(t0, W):
                    smp = SMP and t0 == 0
                    Wx = W + (NSMP if smp else 0)
                    modulate(t0, W, 0, 1)
                    if smp:
                        modulate_s(0, 1)
                    for c in range(2):
                        ps, pk = next_ps(psA, "ps")
                        linear_chunk(WIN, KC, 1024 + c * 128, hT, ["hT"], Wx, ps, pk)
                        S.op("act", lambda e, ps=ps, c=c: e.activation(out=KTl[:, c, t0:t0 + W], in_=ps[:, :W], func=AF.Identity),
                             reads=[pk], writes=["KTl"])
                        if smp:
                            S.op("act", lambda e, ps=ps, c=c: e.activation(out=KTn[:, c, :], in_=ps[:, 128:128 + NSMP], func=AF.Identity), reads=[pk], writes=["KTn"])
                    ps, pk = next_ps(psA, "ps")
                    wt, wkey = load_wchunk(WIN, KC, 2048, ncol=64)
                    mm_group(ps[:64, :W], pk, W, [(wt[:, k, :64], hT[:, k, :W]) for k in range(KC)], [wkey, "hT"])
                    S.op("act", lambda e, ps=ps: e.activation(out=kiTl[:64, t0:t0 + W], in_=ps[:64, :W], func=AF.Identity),
                         reads=[pk], writes=["kiTl"])
                    if smp:
                        ps, pk = next_ps(psA, "ps")
                        mm_group(ps[:, :NSMP], pk, NSMP, [(wki2[:, k, :], hT[:, k, 128:128 + NSMP]) for k in range(KC)], ["wki2", "hT"])
                        S.op("act", lambda e, ps=ps: e.activation(out=kiTn[:], in_=ps[:, :NSMP], func=AF.Identity), reads=[pk], writes=["kiTn"])
                        stg = kvst[0]
                        ps, pk = next_ps(psB, "pb")
                        mm_group(ps[:NSMP, :512], pk, 512, [(hT[:, k, 128:128 + NSMP], wkv[:, k, :]) for k in range(KC)], ["wkv", "hT"])
                        S.op("act", lambda e, ps=ps: e.activation(out=stg[:NSMP, 0:512], in_=ps[:NSMP, :512], func=AF.Identity), reads=[pk], writes=["kvst0"])
                        ps2, pk2 = next_ps(psB, "pb")
                        mm_group(ps2[:NSMP, :64], pk2, 64, [(hT[:, k, 128:128 + NSMP], wki[:, k, :]) for k in range(KC)], ["wki", "hT"])
                        S.op("dve", lambda e, ps2=ps2: e.tensor_copy(out=stg[:NSMP, 512:576], in_=ps2[:NSMP, :64]), reads=[pk2], writes=["kvst0"])
                        S.op("pool", lambda e: e.tensor_copy(out=Vn[:], in_=stg[:NSMP, 256:512]), reads=["kvst0"], writes=["Vn"])
                        S.dma(lambda e: e.dma_start(out=ksn[jd], in_=stg[:NSMP, 0:256]), reads=["kvst0"])
                        S.dma(lambda e: e.dma_start(out=vsn[jd], in_=stg[:NSMP, 256:512]), reads=["kvst0"])
                        S.dma(lambda e: e.dma_start(out=kisn[jd], in_=stg[:NSMP, 512:576]), reads=["kvst0"])
                    for bb in range(W // 128):
                        blk = t0 // 128 + bb
                        stg = kvst[blk % 2]; sk = f"kvst{blk % 2}"
                        ps, pk = next_ps(psB, "pb")
                        mm_group(ps[:, :512], pk, 512, [(hT[:, k, bb * 128:(bb + 1) * 128], wkv[:, k, :]) for k in range(KC)], ["wkv", "hT"])
                        S.op("act", lambda e, ps=ps, stg=stg: e.activation(out=stg[:, 0:512], in_=ps[:, :512], func=AF.Identity),
                             reads=[pk], writes=[sk])
                        ps2, pk2 = next_ps(psB, "pb")
                        mm_group(ps2[:, :64], pk2, 64, [(hT[:, k, bb * 128:(bb + 1) * 128], wki[:, k, :]) for k in range(KC)], ["wki", "hT"])
                        S.op("dve", lambda e, ps2=ps2, stg=stg: e.tensor_copy(out=stg[:, 512:576], in_=ps2[:, :64]), reads=[pk2], writes=[sk])
                        S.op("pool", lambda e, stg=stg, blk=blk: e.tensor_copy(out=Vl[:, blk, :], in_=stg[:, 256:512]), reads=[sk], writes=["Vl"])
                        if blk >= 1:
                            r0 = (blk - 1) * 128
                            S.dma(lambda e, stg=stg, r0=r0: e.dma_start(out=knew[jd, r0:r0 + 128, :], in_=stg[:, 0:256]), reads=[sk])
                            S.dma(lambda e, stg=stg, r0=r0: e.dma_start(out=vnew[jd, r0:r0 + 128, :], in_=stg[:, 256:512]), reads=[sk])
                            S.dma(lambda e, stg=stg, r0=r0: e.dma_start(out=kinew[jd, r0:r0 + 128, :], in_=stg[:, 512:576]), reads=[sk])
                for (t0, W) in tiles:
                    dsap_tile(t0, W)
                if dbg_stop <= 1:
                    return ffn_phase(li)
                bn, gt_ = bounce[jd], gath[jd]
                for c in range(2):
                    S.dma(lambda e, c=c: e.dma_start(out=bn[c][:, :], in_=KTl[:, c, 128:NT]), reads=["KTl"], writes=[f"bounce{jd}_{c}"])
                bpv = VSEG // 256
                for g in range(NVS):
                    S.dma(lambda e, g=g: e.dma_start(out=bn[2 + g].rearrange("p (b v) -> p b v", v=256), in_=Vl[:, 1 + g * bpv:1 + (g + 1) * bpv, :]),
                          reads=["Vl"], writes=[f"bounce{jd}_{2 + g}"])
                S.dma(lambda e: e.dma_start(out=bn[NSEG - 1][:, :], in_=kiTl[:, 128:NT]), reads=["kiTl"], writes=[f"bounce{jd}_{NSEG - 1}"])
                for g in range(NSEG):
                    S.cc(lambda e, g=g: e.collective_compute("AllGather", ALU.bypass, replica_groups=PAIRS, ins=[bn[g][:, :]], outs=[gt_[g][:, :]]),
                         reads=[f"bounce{jd}_{g}"], writes=[f"gath{jd}_{g}"])
                gk = [f"gath{jd}_{g}" for g in range(NSEG)]

                if dbg_stop <= 2:
                    return ffn_phase(li)

                if SMP:
                    new_phase()
                    NK1 = NPG + 1
                    KTs2 = [carve([128, 2, NK1 * 128], BF16) for _ in range(2)]; kiTs2 = [carve([128, NK1 * 128], BF16) for _ in range(2)]
                    Vs2 = [carve([128, NK1, 256], BF16) for _ in range(2)]
                    Kst = [carve([128, 256]) for _ in range(2)]; Vst = [carve([128, 256]) for _ in range(2)]; kist = [carve([128, 128]) for _ in range(2)]
                    qTs = carve([128, 8, NSMP], BF16); qiTs = carve([128, 4, NSMP], BF16); oTs = carve([128, 8, NSMP], BF16)
                    wwi_s = carve([128, KC, 8], BF16); E_all = carve([128, NSMP, 128])
                    rS = carve([128, NK1, 8]); eS = carve([128, NK1, 8]); pTs = carve([128, NK1, 8], BF16)
                    scT = carve([128, NK1]); junk17 = carve([128, NK1]); mk17 = carve([128, NK1]); nb16 = carve([128, NK1])
                    ss = carve([128, 32])
                    wit, witp, wib, cntp, lo_s, w0_s, mid_s, gew_s, negM, rec8, m11 = (
                        ss[:, 0:8], ss[:, 8:16], ss[:, 16:24], ss[:, 24:25], ss[:, 25:26], ss[:, 26:27], ss[:, 27:28], ss[:, 28:29],
                        ss[:, 29:30], ss[:, 16:24], ss[:, 30:31])
                    rec8 = carve([128, 8])
                    wt, wkey = load_wchunk(WIN, KC, 2112, ncol=8)
                    S.op("pool", lambda e, wt=wt: e.tensor_copy(out=wwi_s, in_=wt[:, :KC, :8]), reads=[wkey], writes=["wwi_s"])
                    for bi in range(2):
                        S.op("pool", lambda e, bi=bi: e.memset(KTs2[bi], 0.0), writes=[f"KTs{bi}"])
                        S.op("pool", lambda e, bi=bi: e.memset(kiTs2[bi], 0.0), writes=[f"kiTs{bi}"])
                        S.op("pool", lambda e, bi=bi: e.memset(Vs2[bi], 0.0), writes=[f"Vs{bi}"])
                    S.op("pool", lambda e: e.memset(nb16, 0.0), writes=["nb16"])
                    S.op("pool", lambda e: e.memset(nb16[:, NPG:NPG + 1], -BIG), writes=["nb16"])
                    S.op("pool", lambda e: e.affine_select(out=nb16[:, NPG:NPG + 1], in_=nb16[:, NPG:NPG + 1], pattern=[[0, 1]], compare_op=ALU.is_gt,
                                                           fill=0.0, base=0, channel_multiplier=1), reads=["nb16"], writes=["nb16"])
                    S.op("dve", lambda e: e.tensor_copy(out=E_all[:NSMP], in_=ident[:NSMP, :NSMP].unsqueeze(2).to_broadcast([NSMP, NSMP, 128])),
                         reads=["ident"], writes=["E_all"])
                    modulate_s(0, 1)
                    for h in range(8):
                        ps, pk = next_ps(psA, "ps")
                        linear_chunk(WIN, KC, h * 128, hT, ["hT"], NSMP, ps, pk, off=128)
                        S.op("act", lambda e, ps=ps, h=h: e.activation(out=qTs[:, h, :], in_=ps[:, :NSMP], func=AF.Identity, scale=128 ** -0.5), reads=[pk], writes=["qTs"])
                    for c in range(4):
                        ps, pk = next_ps(psA, "ps")
                        linear_chunk(WIN, KC, 1536 + c * 128, hT, ["hT"], NSMP, ps, pk, off=128)
                        S.op("act", lambda e, ps=ps, c=c: e.activation(out=qiTs[:, c, :], in_=ps[:, :NSMP], func=AF.Identity), reads=[pk], writes=["qiTs"])
                    ps, pk = next_ps(psA, "ps")
                    mm_group(ps[:NSMP, :8], pk, 8, [(hT[:, k, 128:128 + NSMP], wwi_s[:, k, :]) for k in range(KC)], ["wwi_s", "hT"])
                    S.op("act", lambda e, ps=ps: e.activation(out=wit[:NSMP], in_=ps[:NSMP, :8], func=AF.Identity, scale=(8 ** -0.5) * (64 ** -0.5)), reads=[pk], writes=["wit"])
                    S.op("dve", lambda e: e.tensor_copy(out=witp[:NSMP, 0:4], in_=wit[:NSMP, 0:8:2]), reads=["wit"], writes=["witp"])
                    S.op("dve", lambda e: e.tensor_copy(out=witp[:NSMP, 4:8], in_=wit[:NSMP, 1:8:2]), reads=["wit"], writes=["witp"])
                    CK = cache_k[jd]; CV = cache_v[jd]; CI = cache_ki[jd]

                    def sample_attend(si):
                        bi = si % 2
                        KTs, kiTs, Vs = KTs2[bi], kiTs2[bi], Vs2[bi]
                        kKT, kki, kV = f"KTs{bi}", f"kiTs{bi}", f"Vs{bi}"
                        for pg in range(NPG):
                            i = pg % 2
                            icol = idx_all[:, si * NPG + pg:si * NPG + pg + 1]
                            S.dma(lambda e, i=i, icol=icol: e.indirect_dma_start(out=Kst[i], out_offset=None, in_=CK[:, :],
                                                                                in_offset=bass.IndirectOffsetOnAxis(ap=icol, axis=0)),
                                  reads=["idx_all"], writes=[f"Kst{i}"], q="pool")
                            if dbg_stop <= 4.1:
                                continue
                            S.dma(lambda e, i=i, icol=icol: e.indirect_dma_start(out=Vst[i], out_offset=None, in_=CV[:, :],
                                                                                in_offset=bass.IndirectOffsetOnAxis(ap=icol, axis=0)),
                                  reads=["idx_all"], writes=[f"Vst{i}"], q="pool")
                            S.dma(lambda e, i=i, icol=icol: e.indirect_dma_start(out=kist[i][:, 0:64], out_offset=None, in_=CI[:, :],
                                                                                in_offset=bass.IndirectOffsetOnAxis(ap=icol, axis=0)),
                                  reads=["idx_all"], writes=[f"kist{i}"], q="pool")
                            if dbg_stop <= 4.2:
                                continue
                            S.op("dve", lambda e, i=i: e.tensor_copy(out=kist[i][:, 64:128], in_=kist[i][:, 0:64]), reads=[f"kist{i}"], writes=[f"kist{i}"])
                            S.op("act", lambda e, i=i, pg=pg: e.activation(out=Vs[:, pg, :], in_=Vst[i], func=AF.Identity), reads=[f"Vst{i}"], writes=[kV])
                            if dbg_stop <= 4.3:
                                continue
                            pt, ptk = next_ps(psT, "pt")

                            def fnt(e, i=i, pt=pt):
                                e.transpose(out=pt[:, 0:128], in_=Kst[i][:, 0:128], identity=ident[:])
                                e.transpose(out=pt[:, 128:256], in_=Kst[i][:, 128:256], identity=ident[:])
                                return e.transpose(out=pt[:, 256:384], in_=kist[i][:, :], identity=ident[:])
                            S.op("pe", fnt, reads=[f"Kst{i}", f"kist{i}", "ident"], writes=[ptk])
                            if dbg_stop <= 4.4:
                                continue
                            S.op("act", lambda e, pt=pt, pg=pg: e.activation(out=KTs[:, :, pg * 128:(pg + 1) * 128],
                                                                             in_=pt[:, 0:256].rearrange("p (c s) -> p c s", c=2), func=AF.Identity),
                                 reads=[ptk], writes=[kKT])
                            if dbg_stop <= 4.45:
                                continue
                            S.op("act", lambda e, pt=pt, pg=pg: e.activation(out=kiTs[:, pg * 128:(pg + 1) * 128], in_=pt[:, 256:384], func=AF.Identity),
                                 reads=[ptk], writes=[kki])
                        if dbg_stop <= 4.5:
                            return
                        S.op("dve", lambda e: e.tensor_copy(out=KTs[:, :, NPG * 128:NPG * 128 + 1], in_=KTn[:, :, si:si + 1]), reads=["KTn"], writes=[kKT])
                        S.op("dve", lambda e: e.tensor_copy(out=kiTs[:, NPG * 128:NPG * 128 + 1], in_=kiTn[:, si:si + 1]), reads=["kiTn"], writes=[kki])
                        S.dma(lambda e: e.dma_start(out=Vs[0:1, NPG, :], in_=Vn[si:si + 1, :]), reads=["Vn"], writes=[kV])
                        if dbg_stop <= 5.1:
                            return
                        pse, pek = next_ps(psA, "ps")
                        pso_, pok = next_ps(psA, "ps")
                        pscs = [pse[:, :NK1 * 4].rearrange("p (g h) -> p g h", h=4), pso_[:, :NK1 * 4].rearrange("p (g h) -> p g h", h=4)]
                        for hh in range(2):
                            def fns(e, hh=hh):
                                pb = hh * 64
                                for pg in range(NK1):
                                    ins = e.matmul(pscs[hh][:, pg, :], lhsT=kiTs[pb:pb + 64, pg * 128:(pg + 1) * 128], rhs=qiTs[pb:pb + 64, :, si],
                                                   start=True, stop=True)
                                return ins
                            S.op("pe", fns, reads=[kki, "qiTs"], writes=[(pek, pok)[hh]])
                            S.op("act", lambda e, hh=hh: e.activation(out=rS[:, :, hh * 4:(hh + 1) * 4], in_=pscs[hh], func=AF.Relu),
                                 reads=[(pek, pok)[hh]], writes=["rS"])
                        if dbg_stop <= 5.2:
                            return
                        ps2, pk2 = next_ps(psA, "ps")
                        mm_group(ps2[:, :8], pk2, 8, [(E_all[:NSMP, si, :], witp[:NSMP, :])], ["E_all", "witp"])
                        S.op("dve", lambda e, ps2=ps2: e.tensor_copy(out=wib, in_=ps2[:, :8]), reads=[pk2], writes=["wib"])
                        S.op("dve", lambda e: e.tensor_tensor(out=rS, in0=rS, in1=wib.unsqueeze(1).to_broadcast([128, NK1, 8]), op=ALU.mult),
                             reads=["rS", "wib"], writes=["rS"])
                        S.op("dve", lambda e: e.tensor_reduce(out=scT, in_=rS, axis=mybir.AxisListType.X, op=ALU.add), reads=["rS"], writes=["scT"])
                        if dbg_stop <= 5.3:
                            return
                        S.op("act", lambda e: e.activation(out=junk17, in_=scT, func=AF.Square, accum_out=cntp), reads=["scT"], writes=["junk17", "cntp"])
                        pq, pqk = next_ps(psS, "pq")
                        mm_group(pq[:, :1], pqk, 1, [(ones_f[:], cntp)], ["ones_f", "cntp"])
                        S.op("act", lambda e, pq=pq: e.activation(out=w0_s, in_=pq[:, :1], func=AF.Sqrt), reads=[pqk], writes=["w0_s"])
                        S.op("dve", lambda e: e.tensor_scalar(out=lo_s, in0=w0_s, scalar1=-1.0, scalar2=-1.0, op0=ALU.mult, op1=ALU.add), reads=["w0_s"], writes=["lo_s"])
                        S.op("dve", lambda e: e.tensor_scalar(out=w0_s, in0=w0_s, scalar1=2.0, scalar2=2.0, op0=ALU.mult, op1=ALU.add), reads=["w0_s"], writes=["w0_s"])
                        S.op("dve", lambda e: e.tensor_tensor(out=scT, in0=scT, in1=nb16, op=ALU.add), reads=["scT", "nb16"], writes=["scT"])
                        if dbg_stop <= 5.4:
                            return
                        for it in range(NITS):
                            ck = 0.5 ** (it + 1)
                            S.op("dve", lambda e, ck=ck: e.scalar_tensor_tensor(out=mid_s, in0=w0_s, scalar=ck, in1=lo_s, op0=ALU.mult, op1=ALU.add),
                                 reads=["w0_s", "lo_s"], writes=["mid_s"])
                            S.op("dve", lambda e: e.tensor_scalar(out=junk17, in0=scT, scalar1=mid_s, scalar2=0.0, op0=ALU.is_ge, op1=ALU.add, accum_out=cntp),
                                 reads=["scT", "mid_s"], writes=["junk17", "cntp"])
                            pq, pqk = next_ps(psS, "pq")
                            mm_group(pq[:, :1], pqk, 1, [(ones_f[:], cntp)], ["ones_f", "cntp"])
                            S.op("dve", lambda e, pq=pq: e.tensor_scalar(out=gew_s, in0=pq[:, :1], scalar1=KEEP_S - 0.5, scalar2=w0_s, op0=ALU.is_ge, op1=ALU.mult),
                                 reads=[pqk, "w0_s"], writes=["gew_s"])
                            S.op("dve", lambda e, ck=ck: e.scalar_tensor_tensor(out=lo_s, in0=gew_s, scalar=ck, in1=lo_s, op0=ALU.mult, op1=ALU.add),
                                 reads=["gew_s", "lo_s"], writes=["lo_s"])
                        S.op("dve", lambda e: e.tensor_scalar(out=mk17, in0=scT, scalar1=lo_s, scalar2=None, op0=ALU.is_ge), reads=["scT", "lo_s"], writes=["mk17"])
                        if dbg_stop <= 5:
                            return
                        ps, pk = next_ps(psA, "ps")
                        psl = ps[:, :NK1 * 8].rearrange("p (g h) -> p g h", h=8)

                        def fnl(e, psl=psl):
                            for pg in range(NK1):
                                for kvh in range(2):
                                    ins = e.matmul(psl[:, pg, kvh * 4:(kvh + 1) * 4], lhsT=KTs[:, kvh, pg * 128:(pg + 1) * 128], rhs=qTs[:, kvh * 4:(kvh + 1) * 4, si],
                                                   start=True, stop=True)
                            return ins
                        S.op("pe", fnl, reads=[kKT, "qTs"], writes=[pk])
                        S.op("dve", lambda e, ps=ps: e.tensor_reduce(out=cntp, in_=ps[:, :NK1 * 8], axis=mybir.AxisListType.X, op=ALU.max), reads=[pk], writes=["cntp"])
                        pt, ptk = next_ps(psT, "pt")
                        S.op("pe", lambda e, pt=pt: e.transpose(out=pt[:1, 0:128], in_=cntp, identity=ident[:]), reads=["cntp", "ident"], writes=[ptk])
                        S.op("dve", lambda e, pt=pt: e.tensor_reduce(out=m11[0:1], in_=pt[:1, 0:128], axis=mybir.AxisListType.X, op=ALU.max), reads=[ptk], writes=["m11"])
                        pq, pqk = next_ps(psS, "pq")
                        mm_group(pq[:, :1], pqk, 1, [(ones_f[0:1, :], m11[0:1])], ["ones_f", "m11"])
                        S.op("dve", lambda e, pq=pq: e.tensor_scalar(out=negM, in0=pq[:, :1], scalar1=-1.0, scalar2=None, op0=ALU.mult), reads=[pqk], writes=["negM"])
                        S.op("act", lambda e, psl=psl: e.activation(out=eS, in_=psl, func=AF.Exp, bias=negM), reads=[pk, "negM"], writes=["eS"])
                        S.op("dve", lambda e: e.tensor_tensor(out=pTs, in0=eS, in1=mk17.unsqueeze(2).to_broadcast([128, NK1, 8]), op=ALU.mult),
                             reads=["eS", "mk17"], writes=["pTs"])
                        if dbg_stop <= 6:
                            return
                        pso, psok = psB[0], "pb0"
                        pss, pssk = psB[1], "pb1"

                        def fno(e):
                            for kvh in range(2):
                                for pg in range(NK1):
                                    ins = e.matmul(pso[:, kvh * 4:(kvh + 1) * 4], lhsT=Vs[:, pg, kvh * 128:(kvh + 1) * 128], rhs=pTs[:, pg, kvh * 4:(kvh + 1) * 4],
                                                   start=(pg == 0), stop=(pg == NK1 - 1))
                            return ins
                        S.op("pe", fno, reads=[kV, "pTs"], writes=[psok])

                        def fnsum(e):
                            for pg in range(NK1):
                                ins = e.matmul(pss[:, 0:8], lhsT=onesb[:], rhs=pTs[:, pg, :], start=(pg == 0), stop=(pg == NK1 - 1))
                            return ins
                        S.op("pe", fnsum, reads=["onesb", "pTs"], writes=[pssk])
                        S.op("dve", lambda e: e.tensor_scalar(out=rec8, in0=pss[:, 0:8], scalar1=1e-30, scalar2=None, op0=ALU.max), reads=[pssk], writes=["rec8"])
                        S.op("dve", lambda e: e.reciprocal(out=rec8, in_=rec8), reads=["rec8"], writes=["rec8"])
                        S.op("dve", lambda e: e.tensor_tensor(out=oTs[:, :, si], in0=pso[:, 0:8], in1=rec8, op=ALU.mult), reads=[psok, "rec8"], writes=["oTs"])
                    for si in range(NSMP if dbg_stop >= 99 else (0 if dbg_stop <= 3 else 1)):
                        sample_attend(si)
                    for c in range(KC):
                        ps, pk = next_ps(psA, "ps")
                        linear_chunk(dsa_w_out[jd], KC, c * 128, oTs, ["oTs"], NSMP, ps, pk)
                        zs_from_ps(ps, pk, c, 2, off=0)
                    postnorm(0, NSMP, 0, samples=True)
                new_phase()
                NKBM = 2 * (NBLK - 1)
                wwi = carve([128, KC, 8], BF16)
                scores = carve([128, NKEY]); junk = carve([128, NKEY], U8); maskT = carve([128, NKBM, 128], BF16)
                qT = carve([128, 8, 128], BF16); qiT = carve([128, 4, 128], BF16); oT = carve([128, 8, 128], BF16)
                KTc = [carve([128, 2, CH], BF16) for _ in range(2)]; Vc = [carve([128, CH // 128, 256], BF16) for _ in range(2)]
                kic = [carve([128, CH], BF16) for _ in range(2)]
                rbuf = [carve([128, CH]) for _ in range(2)]; mrow = [carve([128, CH], BF16) for _ in range(2)]
                e_t = [carve([128, 4, 128], BF16) for _ in range(2)]; pmt = [carve([128, 4, 128], BF16) for _ in range(2)]
                cbt = carve([128, CH]); kposb = carve([128, CH]); rec = carve([128, CH]); qsq = carve([128, 8, 128], BF16)
                sm = carve([128, 32])
                wi_t, qpos, lo, w0, mid, cntt, gew, qn2, kn2, negC, mins, qrel, piota = (
                    sm[:, 0:8], sm[:, 8:9], sm[:, 9:10], sm[:, 10:11], sm[:, 11:12], sm[:, 12:13], sm[:, 13:14], sm[:, 14:15],
                    sm[:, 15:16], sm[:, 16:17], sm[:, 17:25], sm[:, 25:26], sm[:, 26:27])
                wt, wkey = load_wchunk(WIN, KC, 2112, ncol=8)
                S.op("pool", lambda e, wt=wt: e.tensor_copy(out=wwi, in_=wt[:, :KC, :8]), reads=[wkey], writes=["wwi"])
                S.op("pool", lambda e: e.iota(kposb, pattern=[[1, CH]], base=0, channel_multiplier=0, allow_small_or_imprecise_dtypes=True),
                     writes=["kposb"])
                S.op("pool", lambda e: e.iota(piota, pattern=[[0, 1]], base=0, channel_multiplier=1, allow_small_or_imprecise_dtypes=True),
                     writes=["piota"])
                ccnt = {"k": 0}

                def load_kv_chunk(kc, want):
                    i = ccnt["k"] % 2
                    ccnt["k"] += 1
                    r, cl = divmod(kc * CH, HALF)
                    rr = slice(r * 128, (r + 1) * 128)
                    if "ki" in want:
                        for hh in range(2):
                            S.dma(lambda e, i=i, hh=hh, r=r, cl=cl: e.dma_start(out=kic[i][hh * 64:(hh + 1) * 64, :],
                                                                               in_=gt_[NSEG - 1][r * 128:r * 128 + 64, cl:cl + CH]),
                                  reads=[gk[NSEG - 1]], writes=[f"kic{i}"])
                    if "kv" in want:
                        for c in range(2):
                            S.dma(lambda e, i=i, c=c, rr=rr, cl=cl: e.dma_start(out=KTc[i][:, c, :], in_=gt_[c][rr, cl:cl + CH]),
                                  reads=[gk[c]], writes=[f"KTc{i}"])
                        vg, vo = divmod((cl // 128) * 256, VSEG)
                        S.dma(lambda e, i=i, rr=rr, vg=vg, vo=vo: e.dma_start(
                            out=Vc[i], in_=gt_[2 + vg][rr, vo:vo + (CH // 128) * 256].rearrange("p (b v) -> p b v", v=256)),
                            reads=[gk[2 + vg]], writes=[f"Vc{i}"])
                    return i

                S.op("dve", lambda e: e.memset(kn2, 0.0), writes=["kn2"])
                for kc in range(NKEY // CH):
                    i = load_kv_chunk(kc, ("kv",))
                    for c in range(2):
                        S.op("act", lambda e, i=i, c=c: e.activation(out=mrow[c], in_=KTc[i][:, c, :], func=AF.Square), reads=[f"KTc{i}"], writes=[f"mrow{c}"])
                        ps, pk = next_ps(psA, "ps")
                        mm_group(ps[:, :CH], pk, CH, [(onesb[:], mrow[c])], ["onesb", f"mrow{c}"])
                        S.op("dve", lambda e, ps=ps: e.tensor_reduce(out=cntt, in_=ps[:, :CH], axis=mybir.AxisListType.X, op=ALU.max), reads=[pk], writes=["cntt"])
                        S.op("dve", lambda e: e.tensor_tensor(out=kn2, in0=kn2, in1=cntt, op=ALU.max), reads=["kn2", "cntt"], writes=["kn2"])

                def att_block(qb):
                    t0 = qb * 128
                    nkb = (NBLK - 1) + qb
                    nch = -(-nkb // (CH // 128))
                    S_ = nch * CH
                    modulate(t0, 128, 0, 1)
                    for h in range(8):
                        ps, pk = next_ps(psA, "ps")
                        linear_chunk(WIN, KC, h * 128, hT, ["hT"], 128, ps, pk)
                        S.op("act", lambda e, ps=ps, h=h: e.activation(out=qT[:, h, :], in_=ps[:, :128], func=AF.Identity, scale=128 ** -0.5),
                             reads=[pk], writes=["qT"])
                    for c in range(4):
                        ps, pk = next_ps(psA, "ps")
                        linear_chunk(WIN, KC, 1536 + c * 128, hT, ["hT"], 128, ps, pk)
                        S.op("act", lambda e, ps=ps, c=c: e.activation(out=qiT[:, c, :], in_=ps[:, :128], func=AF.Identity), reads=[pk], writes=["qiT"])
                    ps, pk = next_ps(psA, "ps")
                    mm_group(ps[:, :8], pk, 8, [(hT[:, k, :128], wwi[:, k, :]) for k in range(KC)], ["wwi", "hT"])
                    S.op("act", lambda e, ps=ps: e.activation(out=wi_t, in_=ps[:, :8], func=AF.Identity, scale=(8 ** -0.5) * (64 ** -0.5)),
                         reads=[pk], writes=["wi_t"])
                    S.op("dve", lambda e, qb=qb: e.tensor_scalar(out=qpos, in0=piota, scalar1=rolet[:, 0:1], scalar2=float((qb - 1) * 128),
                                                                op0=ALU.add, op1=ALU.add), reads=["piota", "rolet"], writes=["qpos"])
                    for kc in range(nch):
                        i = load_kv_chunk(kc, ("ki",))
                        sc = scores[:, kc * CH:(kc + 1) * CH]
                        for h in range(8):
                            ps, pk = next_ps(psA, "ps")
                            pb = (h % 2) * 64
                            mm_group(ps[:, :CH], pk, CH, [(qiT[pb:pb + 64, h // 2, :], kic[i][pb:pb + 64, :])], ["qiT", f"kic{i}"])
                            rb = rbuf[h % 2]
                            S.op("act", lambda e, ps=ps, rb=rb: e.activation(out=rb, in_=ps[:, :CH], func=AF.Relu), reads=[pk], writes=[f"rbuf{h % 2}"])
                            if h == 0:
                                S.op("dve", lambda e, rb=rb, sc=sc: e.tensor_scalar(out=sc, in0=rb, scalar1=wi_t[:, 0:1], scalar2=None, op0=ALU.mult),
                                     reads=[f"rbuf{h % 2}", "wi_t"], writes=["scores"])
                            else:
                                S.op("dve", lambda e, rb=rb, sc=sc, h=h: e.scalar_tensor_tensor(out=sc, in0=rb, scalar=wi_t[:, h:h + 1], in1=sc,
                                                                                              op0=ALU.mult, op1=ALU.add),
                                     reads=[f"rbuf{h % 2}", "wi_t", "scores"], writes=["scores"])
                        S.op("dve", lambda e, sc=sc, kc=kc: e.tensor_reduce(out=mins[:, kc:kc + 1], in_=sc, axis=mybir.AxisListType.X, op=ALU.min),
                             reads=["scores"], writes=["mins"])
                        S.op("dve", lambda e, kc=kc: e.tensor_scalar(out=qrel, in0=qpos, scalar1=float(-kc * CH), scalar2=None, op0=ALU.add),
                             reads=["qpos"], writes=["qrel"])
                        S.op("dve", lambda e: e.tensor_scalar(out=cbt, in0=kposb, scalar1=qrel, scalar2=-BIG, op0=ALU.is_gt, op1=ALU.mult),
                             reads=["kposb", "qrel"], writes=["cbt"])
                        S.op("dve", lambda e, sc=sc: e.tensor_tensor(out=sc, in0=sc, in1=cbt, op=ALU.add), reads=["scores", "cbt"], writes=["scores"])
                    S.op("dve", lambda e, nch=nch: e.tensor_reduce(out=lo, in_=mins[:, :nch], axis=mybir.AxisListType.X, op=ALU.min), reads=["mins"], writes=["lo"])
                    S.op("dve", lambda e: e.tensor_scalar(out=lo, in0=lo, scalar1=-1.0, scalar2=None, op0=ALU.add), reads=["lo"], writes=["lo"])
                    S.op("dve", lambda e, S_=S_: e.tensor_reduce(out=w0, in_=scores[:, :S_], axis=mybir.AxisListType.X, op=ALU.max), reads=["scores"], writes=["w0"])
                    S.op("dve", lambda e: e.scalar_tensor_tensor(out=w0, in0=w0, scalar=1.0, in1=lo, op0=ALU.add, op1=ALU.subtract), reads=["w0", "lo"], writes=["w0"])
                    S.op("dve", lambda e: e.tensor_scalar(out=w0, in0=w0, scalar1=1.0, scalar2=None, op0=ALU.max), reads=["w0"], writes=["w0"])
                    for it in range(NIT):
                        ck = 0.5 ** (it + 1)
                        S.op("dve", lambda e, ck=ck: e.scalar_tensor_tensor(out=mid, in0=w0, scalar=ck, in1=lo, op0=ALU.mult, op1=ALU.add),
                             reads=["w0", "lo"], writes=["mid"])
                        S.op("dve", lambda e, S_=S_: e.tensor_scalar(out=junk[:, :S_], in0=scores[:, :S_], scalar1=mid, scalar2=0.0,
                                                                     op0=ALU.is_ge, op1=ALU.add, accum_out=cntt),
                             reads=["scores", "mid"], writes=["junk", "cntt"])
                        S.op("dve", lambda e: e.tensor_scalar(out=gew, in0=cntt, scalar1=KEEP - 0.5, scalar2=w0, op0=ALU.is_ge, op1=ALU.mult),
                             reads=["cntt", "w0"], writes=["gew"])
                        S.op("dve", lambda e, ck=ck: e.scalar_tensor_tensor(out=lo, in0=gew, scalar=ck, in1=lo, op0=ALU.mult, op1=ALU.add),
                             reads=["gew", "lo"], writes=["lo"])
                    for kc in range(nch):
                        mr = mrow[kc % 2]
                        S.op("dve", lambda e, mr=mr, kc=kc: e.tensor_scalar(out=mr, in0=scores[:, kc * CH:(kc + 1) * CH], scalar1=lo, scalar2=None, op0=ALU.is_ge),
                             reads=["scores", "lo"], writes=[f"mrow{kc % 2}"])
                        pt, ptk = next_ps(psT, "pt")
                        ptb = pt[:, :].bitcast(BF16)

                        def fn(e, mr=mr, ptb=ptb):
                            for b4 in range(CH // 128):
                                ins = e.transpose(out=ptb[:, b4 * 128:(b4 + 1) * 128], in_=mr[:, b4 * 128:(b4 + 1) * 128], identity=identb[:])
                            return ins
                        S.op("pe", fn, reads=[f"mrow{kc % 2}", "identb"], writes=[ptk])
                        S.op("act", lambda e, ptb=ptb, kc=kc: e.activation(out=maskT[:, kc * 4:(kc + 1) * 4, :],
                                                                          in_=ptb[:, :CH].rearrange("p (b q) -> p b q", q=128), func=AF.Identity),
                             reads=[ptk], writes=["maskT"])
                    S.op("act", lambda e: e.activation(out=qsq, in_=qT, func=AF.Square), reads=["qT"], writes=["qsq"])
                    S.op("dve", lambda e: e.memset(qn2, 0.0), writes=["qn2"])
                    for hf in range(2):
                        ps, pk = next_ps(psA, "ps")
                        mm_group(ps[:, :CH], pk, CH, [(onesb[:], qsq[:, hf * 4:(hf + 1) * 4, :].rearrange("p h q -> p (h q)"))], ["onesb", "qsq"])
                        S.op("dve", lambda e, ps=ps: e.tensor_reduce(out=cntt, in_=ps[:, :CH], axis=mybir.AxisListType.X, op=ALU.max), reads=[pk], writes=["cntt"])
                        S.op("dve", lambda e: e.tensor_tensor(out=qn2, in0=qn2, in1=cntt, op=ALU.max), reads=["qn2", "cntt"], writes=["qn2"])
                    S.op("dve", lambda e: e.tensor_tensor(out=negC, in0=qn2, in1=kn2, op=ALU.mult), reads=["qn2", "kn2"], writes=["negC"])
                    S.op("act", lambda e: e.activation(out=negC, in_=negC, func=AF.Sqrt), reads=["negC"], writes=["negC"])
                    S.op("dve", lambda e: e.tensor_scalar(out=negC, in0=negC, scalar1=-1.0, scalar2=None, op0=ALU.mult), reads=["negC"], writes=["negC"])
                    nblk_proc = nch * (CH // 128)
                    for kvh in range(2):
                        pso, psok = psB[0], "pb0"
                        pss, pssk = psB[1], "pb1"
                        for kc in range(nch):
                            i = load_kv_chunk(kc, ("kv",))
                            for b4 in range(CH // 128):
                                kb = kc * 4 + b4
                                pl, plk = next_ps(psA, "ps")
                                mm_group(pl[:, :512], plk, 512, [(KTc[i][:, kvh, b4 * 128:(b4 + 1) * 128],
                                                                   qT[:, kvh * 4:(kvh + 1) * 4, :].rearrange("p h q -> p (h q)"))], [f"KTc{i}", "qT"])
                                et = e_t[kb % 2]; pm = pmt[kb % 2]
                                S.op("act", lambda e, pl=pl, et=et: e.activation(out=et, in_=pl[:, :512].rearrange("p (h q) -> p h q", h=4), func=AF.Exp,
                                                                               bias=negC), reads=[plk, "negC"], writes=[f"e_t{kb % 2}"])
                                S.op("pool", lambda e, et=et, pm=pm, kb=kb: e.tensor_tensor(out=pm, in0=et, in1=maskT[:, kb:kb + 1, :].to_broadcast([128, 4, 128]),
                                                                                          op=ALU.mult), reads=[f"e_t{kb % 2}", "maskT"], writes=[f"pm{kb % 2}"])
                                first, last = (kb == 0), (kb == nblk_proc - 1)
                                pmf = pm.rearrange("p h q -> p (h q)")
                                S.op("pe", lambda e, i=i, b4=b4, pmf=pmf, first=first, last=last, kvh=kvh: e.matmul(
                                    pso[:, :512], lhsT=Vc[i][:, b4, kvh * 128:(kvh + 1) * 128], rhs=pmf, start=first, stop=last),
                                    reads=[f"Vc{i}", f"pm{kb % 2}"], writes=[psok])
                                S.op("pe", lambda e, pmf=pmf, first=first, last=last: e.matmul(pss[:, :512], lhsT=onesb[:], rhs=pmf, start=first, stop=last),
                                     reads=["onesb", f"pm{kb % 2}"], writes=[pssk])
                        S.op("dve", lambda e: e.tensor_scalar(out=rec, in0=pss[:, :512], scalar1=1e-30, scalar2=None, op0=ALU.max), reads=[pssk], writes=["rec"])
                        S.op("dve", lambda e: e.reciprocal(out=rec, in_=rec), reads=["rec"], writes=["rec"])
                        S.op("dve", lambda e, kvh=kvh: e.tensor_tensor(out=oT[:, kvh * 4:(kvh + 1) * 4, :], in0=pso[:, :512].rearrange("p (h q) -> p h q", h=4),
                                                                      in1=rec.rearrange("p (h q) -> p h q", h=4), op=ALU.mult), reads=[psok, "rec"], writes=["oT"])
                    for c in range(KC):
                        ps, pk = next_ps(psA, "ps")
                        linear_chunk(dsa_w_out[jd], KC, c * 128, oT, ["oT"], 128, ps, pk)
                        z_from_ps(ps, pk, c, t0, 128, 2)
                    postnorm(t0, 128, 0)

                for qb in range(NBLK):
                    att_block(qb)
            ffn_phase(li)

        def ffn_phase(li):
            new_phase()
            aT = [carve([128, 2 + WMAX]) for _ in range(2)]
            cvt = carve([128, WMAX]); gt = carve([128, WMAX]); guT = carve([128, FC, WMAX], BF16)
            for jc in range(3):
                slow_vec(cw[:, jc, :], vec_pk(ffn_conv_w[li, jc]))
            slow_vec(cb[:], vec_pk(ffn_conv_b[li]))
            S.op("pool", lambda e: e.memset(halo[:], 0.0), writes=["halo"])
            if SMP:
                aS = carve([128, FC, NSMP]); uS = carve([128, FC, NSMP]); cvS = carve([128, FC, NSMP]); t2S = carve([128, FC, NSMP])
                Pst = carve([128, FC, 2 * NSMP])
                sstg = carve([128, DFF])
                S.dma(lambda e: e.dma_start(out=sstg[:2 * NSMP, :], in_=sconv[li].rearrange("s j f -> (s j) f")), writes=["sstg"])
                for c0 in range(0, FC, 4):
                    pt, ptk = next_ps(psT, "pt")
                    cs = list(range(c0, min(c0 + 4, FC)))

                    def fnp(e, cs=cs, pt=pt):
                        for c in cs:
                            ins = e.transpose(out=pt[:, (c - cs[0]) * 128:(c - cs[0]) * 128 + 2 * NSMP], in_=sstg[:2 * NSMP, c * 128:(c + 1) * 128],
                                              identity=ident[:2 * NSMP, :2 * NSMP])
                        return ins
                    S.op("pe", fnp, reads=["sstg", "ident"], writes=[ptk])
                    for c in cs:
                        S.op("dve", lambda e, c=c, cs=cs, pt=pt: e.tensor_copy(out=Pst[:, c, :], in_=pt[:, (c - cs[0]) * 128:(c - cs[0]) * 128 + 2 * NSMP]),
                             reads=[ptk], writes=["Pst"])
                S.dma(lambda e: e.dma_start(out=convs[li, :, 0, :], in_=sconv[li, :, 1, :]))

            def ffn_tile(ti, t0, W):
                smp = SMP and ti == 0
                Wx = W + (NSMP if smp else 0)
                modulate(t0, W, 3, 4)
                if smp:
                    modulate_s(3, 4)
                for fc in range(FC):
                    pa, pak = next_ps(psA, "ps")
                    linear_chunk(ffn_w_up[li], KC, fc * 128, hT, ["hT"], Wx, pa, pak)
                    pu, puk = next_ps(psB, "pb")
                    linear_chunk(ffn_w_up[li], KC, DFF + fc * 128, hT, ["hT"], Wx, pu, puk)
                    if smp:
                        S.op("act", lambda e, pa=pa, fc=fc: e.activation(out=aS[:, fc, :], in_=pa[:, 128:128 + NSMP], func=AF.Identity), reads=[pak], writes=["aS"])
                        S.op("act", lambda e, pu=pu, fc=fc: e.activation(out=uS[:, fc, :], in_=pu[:, 128:128 + NSMP], func=AF.Identity), reads=[puk], writes=["uS"])
                    ai = fc % 2
                    a_t = aT[ai]
                    S.op("pool", lambda e, a_t=a_t, fc=fc: e.tensor_copy(out=a_t[:, 0:2], in_=halo[:, fc, :]), reads=["halo"], writes=[f"aT{ai}"])
                    S.op("act", lambda e, a_t=a_t, pa=pa: e.activation(out=a_t[:, 2:2 + W], in_=pa[:, :W], func=AF.Identity),
                         reads=[pak], writes=[f"aT{ai}"])
                    S.op("pool", lambda e, a_t=a_t, fc=fc: e.tensor_copy(out=halo[:, fc, :], in_=a_t[:, W:W + 2]), reads=[f"aT{ai}"], writes=["halo"])
                    S.op("dve", lambda e, a_t=a_t, fc=fc: e.tensor_scalar(out=cvt[:, :W], in0=a_t[:, 0:W], scalar1=cw[:, 0, fc:fc + 1],
                                                                          scalar2=cb[:, fc:fc + 1], op0=ALU.mult, op1=ALU.add),
                         reads=[f"aT{ai}", "cw", "cb"], writes=["cvt"])
                    for jc in (1, 2):
                        S.op("dve", lambda e, a_t=a_t, fc=fc, jc=jc: e.scalar_tensor_tensor(
                            out=cvt[:, :W], in0=a_t[:, jc:jc + W], scalar=cw[:, jc, fc:fc + 1], in1=cvt[:, :W], op0=ALU.mult, op1=ALU.add),
                            reads=[f"aT{ai}", "cw", "cvt"], writes=["cvt"])
                    S.op("act", lambda e: e.activation(out=gt[:, :W], in_=cvt[:, :W], func=AF.Gelu_apprx_tanh), reads=["cvt"], writes=["gt"])
                    S.op("dve", lambda e, pu=pu, fc=fc: e.tensor_tensor(out=guT[:, fc, :W], in0=gt[:, :W], in1=pu[:, :W], op=ALU.mult),
                         reads=["gt", puk], writes=["guT"])
                if ti == 0:
                    S.op("dve", lambda e: e.tensor_scalar(out=halo[:], in0=halo[:], scalar1=rolet[:, 1:2], scalar2=None, op0=ALU.mult),
                         reads=["halo", "rolet"], writes=["halo"])
                if smp:
                    Pv = Pst.rearrange("p f (s j) -> p f s j", j=2)
                    bc = lambda col: col.unsqueeze(2).to_broadcast([128, FC, NSMP])
                    S.op("dve", lambda e: e.tensor_tensor(out=cvS, in0=Pv[:, :, :, 0], in1=bc(cw[:, 0, :]), op=ALU.mult), reads=["Pst", "cw"], writes=["cvS"])
                    S.op("dve", lambda e: e.tensor_tensor(out=t2S, in0=Pv[:, :, :, 1], in1=bc(cw[:, 1, :]), op=ALU.mult), reads=["Pst", "cw"], writes=["t2S"])
                    S.op("dve", lambda e: e.tensor_tensor(out=cvS, in0=cvS, in1=t2S, op=ALU.add), reads=["cvS", "t2S"], writes=["cvS"])
                    S.op("dve", lambda e: e.tensor_tensor(out=t2S, in0=aS, in1=bc(cw[:, 2, :]), op=ALU.mult), reads=["aS", "cw"], writes=["t2S"])
                    S.op("dve", lambda e: e.tensor_tensor(out=cvS, in0=cvS, in1=t2S, op=ALU.add), reads=["cvS", "t2S"], writes=["cvS"])
                    S.op("dve", lambda e: e.tensor_tensor(out=cvS, in0=cvS, in1=bc(cb[:, :]), op=ALU.add), reads=["cvS", "cb"], writes=["cvS"])
                    S.op("act", lambda e: e.activation(out=cvS, in_=cvS, func=AF.Gelu_apprx_tanh), reads=["cvS"], writes=["cvS"])
                    S.op("dve", lambda e: e.tensor_tensor(out=guT[:, :, 128:128 + NSMP], in0=cvS, in1=uS, op=ALU.mult), reads=["cvS", "uS"], writes=["guT"])
                    for c0 in range(0, FC, 4):
                        pt, ptk = next_ps(psT, "pt")
                        cs = list(range(c0, min(c0 + 4, FC)))

                        def fna(e, cs=cs, pt=pt):
                            for c in cs:
                                ins = e.transpose(out=pt[:NSMP, (c - cs[0]) * 128:(c - cs[0] + 1) * 128], in_=aS[:, c, :], identity=ident[:])
                            return ins
                        S.op("pe", fna, reads=["aS", "ident"], writes=[ptk])
                        S.op("dve", lambda e, cs=cs, pt=pt: e.tensor_copy(out=sstg[:NSMP, cs[0] * 128:(cs[-1] + 1) * 128], in_=pt[:NSMP, :len(cs) * 128]),
                             reads=[ptk], writes=["sstg"])
                    S.dma(lambda e: e.dma_start(out=convs[li, :, 1, :], in_=sstg[:NSMP, :]), reads=["sstg"])
                for c in range(KC):
                    ps, pk = next_ps(psA, "ps")
                    linear_chunk(ffn_w_down[li], FC, c * 128, guT, ["guT"], Wx, ps, pk)
                    z_from_ps(ps, pk, c, t0, W, 5)
                    if smp:
                        zs_from_ps(ps, pk, c, 5)
                postnorm(t0, W, 1)
                if smp:
                    postnorm(0, NSMP, 1, samples=True)
            for ti, (t0, W) in enumerate(tiles):
                ffn_tile(ti, t0, W)
            for jr in range(2):
                S.dma(lambda e, li=li, jr=jr: e.dma_start(out=vec_pk(convp[li, jr]), in_=halo[:, :, jr],
                                                          allow_slow_non_contiguous=True), reads=["halo"])

        for li in range(n_layers):
            do_layer(li)
        new_phase()
        if SMP:
            for c0 in (0, 4):
                pt, ptk = next_ps(psT, "pt")

                def fny(e, c0=c0, pt=pt):
                    for c in range(c0, c0 + 4):
                        ins = e.transpose(out=pt[:NSMP, (c - c0) * 128:(c - c0 + 1) * 128], in_=xsT[:, c, :], identity=ident[:])
                    return ins
                S.op("pe", fny, reads=["xs", "ident"], writes=[ptk])
                S.op("dve", lambda e, c0=c0, pt=pt: e.tensor_copy(out=tokt[:NSMP, c0 * 128:(c0 + 4) * 128], in_=pt[:NSMP, :512]), reads=[ptk], writes=["tokt"])
            S.dma(lambda e: e.dma_start(out=ys[:, :], in_=tokt[:NSMP, :]), reads=["tokt"])
        for b in range(NBLK):
            tk = [tt for tt, ww in tiles if tt <= b * 128 < tt + ww][0]
            for c0 in range(0, KC, 4):
                pt, ptk = next_ps(psT, "pt")

                def fn(e, b=b, c0=c0, pt=pt):
                    for c in range(c0, c0 + 4):
                        ins = e.transpose(out=pt[:, (c - c0) * 128:(c - c0 + 1) * 128], in_=xres[:, c, b * 128:(b + 1) * 128], identity=ident[:])
                    return ins
                S.op("pe", fn, reads=[f"x{tk}", "ident"], writes=[ptk])
                S.op("dve", lambda e, c0=c0, pt=pt: e.tensor_copy(out=tokt[:, c0 * 128:(c0 + 4) * 128], in_=pt[:, :512]), reads=[ptk], writes=["tokt"])
            S.dma(lambda e, b=b: e.dma_start(out=y_loc[b * 128:(b + 1) * 128, :], in_=tokt[:]), reads=["tokt"])

        S.emit(st)
    return nc


_WEIGHT_KEYS = ["w_ada", "b_ada", "ln_g", "ln_b", "sgu_w_in", "sgu_b_in", "sgu_norm_g", "sgu_norm_b", "sgu_w_s", "sgu_b_s",
                "sgu_w_out", "dsa_w_in", "dsa_w_out", "ffn_w_up", "ffn_conv_w", "ffn_conv_b", "ffn_w_down"]
_NC_CACHE = {}


def kernel(**inp):
    f32 = lambda a: np.ascontiguousarray(np.asarray(a), dtype=np.float32)
    x_prompt = f32(inp["x_prompt"]); c_prompt = f32(inp["c_prompt"]); c_sample = f32(inp["c_sample"])
    B, T, _ = x_prompt.shape
    half = T // 2
    n_cores = 2 * B
    if "nc" not in _NC_CACHE:
        _NC_CACHE["nc"] = build_program(NBLK=1 + half // 128, n_layers=DEPTH)
    nc = _NC_CACHE["nc"]
    weights = {k: f32(inp[k]) for k in _WEIGHT_KEYS}
    x_sample = f32(inp["x_sample"]); state_conv = f32(inp["state_conv"])
    page_table = np.ascontiguousarray(np.asarray(inp["page_table"]), dtype=np.int32)
    ck_, cv_, ci_ = f32(inp["cache_k"]), f32(inp["cache_v"]), f32(inp["cache_kidx"])
    shared = {}
    for i in range(2):
        shared[f"cache_k{i}"] = ck_[i].reshape(-1, 256); shared[f"cache_v{i}"] = cv_[i].reshape(-1, 256); shared[f"cache_ki{i}"] = ci_[i].reshape(-1, 64)
    in_maps = []
    for c in range(n_cores):
        seq, role = c // 2, c % 2
        if role == 0:
            xloc = np.concatenate([np.zeros((128, D), np.float32), x_prompt[seq, :half]], axis=0)
        else:
            xloc = x_prompt[seq, half - 128:]
        m = dict(weights)
        m.update(shared)
        sl_s = slice(NSMP * c, NSMP * (c + 1))
        m["xs_in"] = np.ascontiguousarray(x_sample[sl_s, 0, :])
        m["sconv"] = np.ascontiguousarray(state_conv[:, sl_s])
        m["ptab"] = np.ascontiguousarray(page_table[sl_s])
        m["xloc"] = np.ascontiguousarray(xloc)
        m["call"] = np.ascontiguousarray(np.concatenate([c_prompt[seq:seq + 1], c_sample[NSMP * c:NSMP * (c + 1)]], axis=0))
        m["role"] = np.tile(np.array([[float(half * role), float(role)]], np.float32), (128, 1))
        in_maps.append(m)
    res = run_bass_kernel_spmd(nc, in_maps, core_ids=list(range(n_cores))).results

    y_prompt = np.zeros((B, T, D), np.float32)
    new_conv_prompt = np.zeros((DEPTH, B, 2, DFF), np.float32)
    new_k_prompt = np.zeros((2, B, T, 2, 128), np.float32); new_v_prompt = np.zeros((2, B, T, 2, 128), np.float32)
    new_kidx_prompt = np.zeros((2, B, T, 64), np.float32)
    for c in range(n_cores):
        seq, role = c // 2, c % 2
        sl = slice(role * half, (role + 1) * half)
        y_prompt[seq, sl] = res[c]["y_loc"][128:]
        new_k_prompt[:, seq, sl] = res[c]["knew"].reshape(2, half, 2, 128)
        new_v_prompt[:, seq, sl] = res[c]["vnew"].reshape(2, half, 2, 128)
        new_kidx_prompt[:, seq, sl] = res[c]["kinew"]
        if role == 1:
            new_conv_prompt[:, seq] = res[c]["convp"]
    DB = c_sample.shape[0]
    y_sample = np.zeros((DB, 1, D), np.float32)
    new_k_sample = np.zeros((2, DB, 1, 2, 128), np.float32); new_v_sample = np.zeros((2, DB, 1, 2, 128), np.float32)
    new_kidx_sample = np.zeros((2, DB, 1, 64), np.float32)
    new_sgu_v_sample = np.zeros((2, DB, 1, D), np.float32)
    new_conv_sample = np.zeros((DEPTH, DB, 2, DFF), np.float32)
    for c in range(n_cores):
        sl_s = slice(NSMP * c, NSMP * (c + 1))
        y_sample[sl_s, 0] = res[c]["ys"]
        new_k_sample[:, sl_s, 0] = res[c]["ksn"].reshape(2, NSMP, 2, 128)
        new_v_sample[:, sl_s, 0] = res[c]["vsn"].reshape(2, NSMP, 2, 128)
        new_kidx_sample[:, sl_s, 0] = res[c]["kisn"]
        new_sgu_v_sample[:, sl_s, 0] = res[c]["sguv"]
        new_conv_sample[:, sl_s] = res[c]["convs"]
    return (y_prompt, y_sample, new_k_prompt, new_v_prompt, new_kidx_prompt, new_k_sample, new_v_sample, new_kidx_sample,
            new_sgu_v_sample, new_conv_prompt, new_conv_sample)


def extra_sample_inputs(d):
    ck, cv, ci = d["cache_k"], d["cache_v"], d["cache_ki"]
    out = {"ptab": np.ascontiguousarray(d["pt"].astype(np.int32))}
    for i in range(2):
        out[f"cache_k{i}"] = np.ascontiguousarray(ck[i].reshape(-1, 256)); out[f"cache_v{i}"] = np.ascontiguousarray(cv[i].reshape(-1, 256))
        out[f"cache_ki{i}"] = np.ascontiguousarray(ci[i].reshape(-1, 64))
    return out
```

```python
import contextlib
import numpy as np
import concourse.bass as bass
import concourse.mybir as mybir
from concourse.bass_utils import run_bass_kernel_spmd

F32 = mybir.dt.float32
BF16 = mybir.dt.bfloat16
I32 = mybir.dt.int32
AF = mybir.ActivationFunctionType
ALU = mybir.AluOpType

D = 1024
KC = 8
DFF = 2816
FC = 22
DEPTH = 4
ALPHA = (2 * DEPTH) ** 0.25
LN_EPS = 1e-5
EPS_A = LN_EPS / (ALPHA * ALPHA)
NSMP = 16
N_CORES = 8

ENG_ATTR = {"pe": "tensor", "act": "scalar", "dve": "vector", "pool": "gpsimd", "sp": "sync"}


class Sched:
    def __init__(self, nc, n_dma_ch=10, same_engine_wait=True):
        self.nc = nc
        self.engs = list(ENG_ATTR)
        self.ops = []
        self.last_w = {}
        self.readers = {}
        self.n_dma_ch = n_dma_ch
        self.same_engine_wait = same_engine_wait
        self.ch_next = {e: 0 for e in self.engs}
        self.ch_last = {e: [None] * n_dma_ch for e in self.engs}
        self.last_op = {e: None for e in self.engs}
        self.pending_bar = {e: set() for e in self.engs}
        self.cc_eng = "pool"

    def cc(self, fn, reads=(), writes=()):
        return self.op("pool", fn, reads, writes, dma=True, cc=True)

    def _needs_wait(self, prod, cons_eng):
        if prod["dma"]:
            return True
        if prod["eng"] != cons_eng:
            return True
        if cons_eng == "pe":
            return False
        return self.same_engine_wait

    def op(self, eng, fn, reads=(), writes=(), dma=False, cc=False):
        idx = len(self.ops)
        deps = set()
        for k in list(reads) + list(writes):
            if k in self.last_w:
                deps.add(self.last_w[k])
        for k in writes:
            deps.update(self.readers.get(k, ()))
        if self.pending_bar[eng]:
            deps.update(self.pending_bar[eng])
            self.pending_bar[eng] = set()
        o = dict(eng=eng, fn=fn, deps=deps, dma=dma, signal=False, ch=None, inc=(1 if cc else 16))
        if dma:
            if cc:
                c = self.n_dma_ch - 1
            else:
                nfree = self.n_dma_ch - (1 if self.cc_eng == eng else 0)
                c = self.ch_next[eng]
                self.ch_next[eng] = (c + 1) % nfree
            o["ch"] = c
            o["prev"] = self.ch_last[eng][c]
            self.ch_last[eng][c] = idx
            o["signal"] = True
        else:
            self.last_op[eng] = idx
        self.ops.append(o)
        for k in reads:
            self.readers.setdefault(k, []).append(idx)
        for k in writes:
            self.last_w[k] = idx
            self.readers[k] = []
        return idx

    def dma(self, fn, reads=(), writes=(), q="sp"):
        return self.op(q, fn, reads, writes, dma=True)

    def barrier(self):
        b = set()
        for e in self.engs:
            if self.last_op[e] is not None:
                b.add(self.last_op[e])
            for c in self.ch_last[e]:
                if c is not None:
                    b.add(c)
        for e in self.engs:
            self.pending_bar[e] = set(b)

    def emit(self, stack):
        nc, ops = self.nc, self.ops
        for o in ops:
            for d in o["deps"]:
                if self._needs_wait(ops[d], o["eng"]):
                    ops[d]["signal"] = True
        used = [e for e in self.engs if any(o["eng"] == e for o in ops)]
        sems = {e: stack.enter_context(nc.semaphore(f"sem_{e}")) for e in used}
        chs = {}
        for e in used:
            if any(o["dma"] and o["eng"] == e for o in ops):
                chs[e] = [stack.enter_context(nc.semaphore(f"dch_{e}_{i}")) for i in range(self.n_dma_ch)]
        ticket = {e: 0 for e in used}
        chcnt = {e: [0] * self.n_dma_ch for e in used}
        for o in ops:
            e = o["eng"]
            if o["dma"]:
                chcnt[e][o["ch"]] += o["inc"]
                o["sig"] = (f"dch_{e}_{o['ch']}", chs[e][o["ch"]], chcnt[e][o["ch"]])
            elif o["signal"]:
                ticket[e] += 1
                o["sig"] = (f"sem_{e}", sems[e], ticket[e])
            else:
                o["sig"] = None
        per_eng = {e: [i for i, o in enumerate(ops) if o["eng"] == e] for e in used}
        waited = {e: {} for e in used}
        self.n_wait = 0
        block = stack.enter_context(nc.Block())

        def make(e):
            def body(engh):
                for i in per_eng[e]:
                    o = ops[i]
                    need = {}
                    dl = set(o["deps"])
                    if o["dma"] and o["prev"] is not None:
                        dl.add(o["prev"])
                    for d in dl:
                        p = ops[d]
                        if not (o["dma"] and d == o.get("prev")) and not self._needs_wait(p, e):
                            continue
                        name, sem, val = p["sig"]
                        if need.get(name, (None, 0))[1] < val:
                            need[name] = (sem, val)
                    for name, (sem, val) in need.items():
                        if waited[e].get(name, 0) < val:
                            engh.wait_ge(sem, val)
                            waited[e][name] = val
                            self.n_wait += 1
                    ins = o["fn"](engh)
                    if o["sig"] is not None:
                        if o["dma"] and o["inc"] == 1:
                            ins.then_inc(o["sig"][1])
                        else:
                            ins.then_inc(o["sig"][1], 16 if o["dma"] else 1)
                if e in chs:
                    for c in range(self.n_dma_ch):
                        if chcnt[e][c] > 0:
                            engh.wait_ge(chs[e][c], chcnt[e][c])
            return body

        for e in used:
            getattr(block, ENG_ATTR[e])(make(e))


def vec_pk(ap1d, p=128):
    return ap1d.rearrange("(kc p) -> p kc", p=p)


def build_program(NBLK=17, n_layers=4, with_samples=True, KEEP=256, NIT=20, n_cores=N_CORES, dbg_stop=99, NPHYS=2560, NITS=22):
    NT = NBLK * 128
    HALF = NT - 128
    NKEY = 2 * HALF
    CH = 512
    BIG = 30000.0
    U8 = mybir.dt.uint8
    tiles = [(0, 128)]
    t = 128
    while t < NT:
        w = min(256, NT - t)
        tiles.append((t, w))
        t += w
    WMAX = max(max(w for _, w in tiles), 128 + NSMP)

    nc = bass.Bass("TRN2", target_bir_lowering=False)
    dt_in = lambda name, shape, dt=F32: nc.dram_tensor(name, list(shape), dt, kind="ExternalInput").ap()
    dt_out = lambda name, shape, dt=F32: nc.dram_tensor(name, list(shape), dt, kind="ExternalOutput").ap()

    xloc = dt_in("xloc", [NT, D])
    call = dt_in("call", [1 + NSMP, D])
    role = dt_in("role", [128, 2])
    w_ada = dt_in("w_ada", [DEPTH, D, 6 * D]); b_ada = dt_in("b_ada", [DEPTH, 6 * D])
    ln_g = dt_in("ln_g", [DEPTH, 2, D]); ln_b = dt_in("ln_b", [DEPTH, 2, D])
    sgu_w_in = dt_in("sgu_w_in", [2, D, 2 * D]); sgu_b_in = dt_in("sgu_b_in", [2, 2 * D])
    sgu_norm_g = dt_in("sgu_norm_g", [2, D]); sgu_norm_b = dt_in("sgu_norm_b", [2, D])
    sgu_w_s = dt_in("sgu_w_s", [2, 8, 128, 128]); sgu_b_s = dt_in("sgu_b_s", [2, 8, 128])
    sgu_w_out = dt_in("sgu_w_out", [2, D, D])
    ffn_w_up = dt_in("ffn_w_up", [DEPTH, D, 2 * DFF]); ffn_conv_w = dt_in("ffn_conv_w", [DEPTH, 3, DFF])
    ffn_conv_b = dt_in("ffn_conv_b", [DEPTH, DFF]); ffn_w_down = dt_in("ffn_w_down", [DEPTH, DFF, D])

    dsa_w_in = dt_in("dsa_w_in", [2, D, 2120]); dsa_w_out = dt_in("dsa_w_out", [2, D, D])
    knew = dt_out("knew", [2, HALF, 256]); vnew = dt_out("vnew", [2, HALF, 256]); kinew = dt_out("kinew", [2, HALF, 64])
    VSEG = min(2 * HALF, 2048)
    NVS = (2 * HALF) // VSEG
    SEGW = [HALF, HALF] + [VSEG] * NVS + [HALF]
    NSEG = len(SEGW)
    bounce = [[nc.dram_tensor(f"bounce{i}_{g}", [128, w], BF16, kind="Internal").ap() for g, w in enumerate(SEGW)] for i in range(2)]
    gath = [[nc.dram_tensor(f"gath{i}_{g}", [256, w], BF16, kind="Internal").ap() for g, w in enumerate(SEGW)] for i in range(2)]
    SMP = with_samples
    WS = NSMP if SMP else 0
    xs_in = dt_in("xs_in", [NSMP, D]); sconv = dt_in("sconv", [DEPTH, NSMP, 2, DFF])
    NPG = 16
    KEEP_S = 256
    cache_k = [dt_in(f"cache_k{i}", [NPHYS * 128, 256]) for i in range(2)]; cache_v = [dt_in(f"cache_v{i}", [NPHYS * 128, 256]) for i in range(2)]
    cache_ki = [dt_in(f"cache_ki{i}", [NPHYS * 128, 64]) for i in range(2)]; ptab = dt_in("ptab", [NSMP, NPG], I32)
    ksn = dt_out("ksn", [2, NSMP, 256]); vsn = dt_out("vsn", [2, NSMP, 256]); kisn = dt_out("kisn", [2, NSMP, 64])
    ys = dt_out("ys", [NSMP, D]); sguv = dt_out("sguv", [2, NSMP, D]); convs = dt_out("convs", [DEPTH, NSMP, 2, DFF])
    y_loc = dt_out("y_loc", [NT, D])
    convp = dt_out("convp", [DEPTH, 2, DFF])

    st = contextlib.ExitStack()
    with st:
        SB = lambda n, s, d=F32: st.enter_context(nc.sbuf_tensor(n, list(s), d))
        PS = lambda n: st.enter_context(nc.psum_tensor(n, [128, 512], F32))
        S = Sched(nc)

        xres = SB("xres", [128, KC, NT])
        ident = SB("ident", [128, 128]); ones_f = SB("ones_f", [128, 128])
        tri01 = SB("tri01", [128, 128])
        rolet = SB("rolet", [128, 2])
        cT = SB("cT", [128, KC, 1 + NSMP])
        modp = SB("modp", [128, 48])
        modp1 = SB("modp1", [128, 48])
        lng = SB("lng", [128, 2, KC]); lnb = SB("lnb", [128, 2, KC])
        xsT = SB("xsT", [128, KC, NSMP]); zs = SB("zs", [128, KC, NSMP]); mods1 = SB("mods1", [128, 48, NSMP])
        KTn = SB("KTn", [128, 2, NSMP], BF16); kiTn = SB("kiTn", [128, NSMP], BF16); Vn = SB("Vn", [NSMP, 256], BF16)
        idx_all = SB("idx_all", [128, NSMP * NPG], I32); piota_p = SB("piota_p", [128, 1])
        vTs = SB("vTs", [128, KC, NSMP]); w00c = SB("w00c", [128, 8]); bs0c = SB("bs0c", [128, 8]); tmp16 = SB("tmp16", [128, KC, NSMP])
        hT = SB("hT", [128, KC, WMAX], BF16)
        z = SB("z", [128, KC, WMAX])
        zsq = [SB(f"zsq{i}", [128, WMAX]) for i in range(2)]
        m_t = SB("m_t", [128, WMAX]); v_t = SB("v_t", [128, WMAX]); r_t = SB("r_t", [128, WMAX])
        NWB, WK = 3, 11
        wst = [SB(f"wst{i}", [128, WK, 128]) for i in range(NWB)]
        wbf = [SB(f"wbf{i}", [128, WK, 128], BF16) for i in range(NWB)]
        tokt = SB("tokt", [128, D])
        psA = [PS(f"psA{i}") for i in range(2)]
        psB = [PS(f"psB{i}") for i in range(2)]
        psT = [PS(f"psT{i}") for i in range(2)]
        psS = [PS(f"psS{i}") for i in range(2)]

        cnt = {"w": 0, "ps": 0, "pb": 0, "pt": 0, "zs": 0, "pq": 0}
        PAIRS = [[2 * i, 2 * i + 1] for i in range(n_cores // 2)]

        def slow_vec(dst, src):
            S.dma(lambda e: e.dma_start(out=dst, in_=src, allow_slow_non_contiguous=True), writes=[dst.tensor.name])

        def load_wchunk(w_ap2d, kcn, c0, ncol=128, k0=0):
            i = cnt["w"] % NWB
            cnt["w"] += 1
            src = w_ap2d[k0 * 128:(k0 + kcn) * 128, c0:c0 + ncol].rearrange("(kc p) n -> p kc n", p=128)
            S.dma(lambda e: e.dma_start(out=wst[i][:, :kcn, :ncol], in_=src), writes=[f"wst{i}"])
            if cnt["w"] % 2 == 0:
                S.op("act", lambda e: e.activation(out=wbf[i][:, :kcn, :ncol], in_=wst[i][:, :kcn, :ncol], func=AF.Identity),
                     reads=[f"wst{i}"], writes=[f"wbf{i}"])
            else:
                S.op("dve", lambda e: e.tensor_copy(out=wbf[i][:, :kcn, :ncol], in_=wst[i][:, :kcn, :ncol]),
                     reads=[f"wst{i}"], writes=[f"wbf{i}"])
            return wbf[i], f"wbf{i}"

        def mm_group(ps, pskey, W, pairs, reads):
            def fn(e):
                n = len(pairs)
                for j, (l, r) in enumerate(pairs):
                    ins = e.matmul(ps, lhsT=l, rhs=r, start=(j == 0), stop=(j == n - 1))
                return ins
            S.op("pe", fn, reads=reads, writes=[pskey])

        def linear_chunk(w_ap2d, kcn, c0, src, srckeys, W, ps, pskey, off=0):
            pairs, keys = [], []
            for k0 in range(0, kcn, WK):
                kn = min(WK, kcn - k0)
                wt, wkey = load_wchunk(w_ap2d, kn, c0, k0=k0)
                pairs += [(wt[:, k, :], src[:, k0 + k, off:off + W]) for k in range(kn)]
                keys.append(wkey)
            mm_group(ps[:, :W], pskey, W, pairs, keys + srckeys)

        def next_ps(lst, name):
            i = cnt[name] % 2
            cnt[name] += 1
            return lst[i], "%s%d" % ({"pq": "psS"}.get(name, name), i)

        def tile_key(t):
            return "x%d" % [tt for tt, ww in tiles if tt <= t < tt + ww][0]

        def modulate(t0, W, jshift, jscale):
            xk = tile_key(t0)
            for k in range(KC):
                S.op("act", lambda e, k=k: e.activation(out=hT[:, k, :W], in_=xres[:, k, t0:t0 + W], func=AF.Identity,
                                                        scale=modp1[:, jscale * 8 + k:jscale * 8 + k + 1],
                                                        bias=modp[:, jshift * 8 + k:jshift * 8 + k + 1]),
                     reads=[xk, "modp", "modp1"], writes=["hT"])

        def postnorm(t0, W, sub, samples=False):
            if samples:
                return _postnorm(zs, "zs", NSMP, sub, lambda k: xsT[:, k, :], "xs")
            return _postnorm(z, "z", W, sub, lambda k: xres[:, k, t0:t0 + W], tile_key(t0))

        def _postnorm(z, zk, W, sub, xout, xk):
            s1, s1k = psS[0], "psS0"
            s2, s2k = psS[1], "psS1"
            for k in range(KC):
                i = cnt["zs"] % 2
                cnt["zs"] += 1
                S.op("act", lambda e, k=k, i=i: e.activation(out=zsq[i][:, :W], in_=z[:, k, :W], func=AF.Square),
                     reads=[zk], writes=[f"zsq{i}"])
                S.op("pe", lambda e, k=k: e.matmul(s1[:, :W], lhsT=ones_f[:], rhs=z[:, k, :W], start=(k == 0), stop=(k == KC - 1)),
                     reads=[zk, "ones_f"], writes=[s1k])
                S.op("pe", lambda e, k=k, i=i: e.matmul(s2[:, :W], lhsT=ones_f[:], rhs=zsq[i][:, :W], start=(k == 0), stop=(k == KC - 1)),
                     reads=[f"zsq{i}", "ones_f"], writes=[s2k])
            S.op("act", lambda e: e.activation(out=m_t[:, :W], in_=s1[:, :W], func=AF.Identity, scale=1.0 / D), reads=[s1k], writes=["m_t"])
            S.op("dve", lambda e: e.tensor_tensor(out=v_t[:, :W], in0=m_t[:, :W], in1=m_t[:, :W], op=ALU.mult), reads=["m_t"], writes=["v_t"])
            S.op("dve", lambda e: e.scalar_tensor_tensor(out=v_t[:, :W], in0=s2[:, :W], scalar=1.0 / D, in1=v_t[:, :W],
                                                         op0=ALU.mult, op1=ALU.subtract), reads=[s2k, "v_t"], writes=["v_t"])
            S.op("act", lambda e: e.activation(out=r_t[:, :W], in_=v_t[:, :W], func=AF.Sqrt, bias=epsA[:, 0:1]), reads=["v_t", "epsA"], writes=["r_t"])
            S.op("dve", lambda e: e.reciprocal(out=r_t[:, :W], in_=r_t[:, :W]), reads=["r_t"], writes=["r_t"])
            for k in range(KC):
                S.op("dve", lambda e, k=k: e.tensor_tensor(out=z[:, k, :W], in0=z[:, k, :W], in1=m_t[:, :W], op=ALU.subtract),
                     reads=[zk, "m_t"], writes=[zk])
                S.op("dve", lambda e, k=k: e.tensor_tensor(out=z[:, k, :W], in0=z[:, k, :W], in1=r_t[:, :W], op=ALU.mult),
                     reads=[zk, "r_t"], writes=[zk])
                S.op("act", lambda e, k=k: e.activation(out=xout(k), in_=z[:, k, :W], func=AF.Identity,
                                                        scale=lng[:, sub, k:k + 1], bias=lnb[:, sub, k:k + 1]),
                     reads=[zk, "lng", "lnb"], writes=[xk])

        def z_from_ps(ps, pskey, c, t0, W, jgate):
            S.op("dve", lambda e: e.scalar_tensor_tensor(out=z[:, c, :W], in0=ps[:, :W], scalar=modp1[:, jgate * 8 + c:jgate * 8 + c + 1],
                                                         in1=xres[:, c, t0:t0 + W], op0=ALU.mult, op1=ALU.add),
                 reads=[pskey, "modp1", tile_key(t0)], writes=["z"])

        def modulate_s(jshift, jscale):
            S.op("dve", lambda e: e.tensor_tensor(out=tmp16[:], in0=xsT[:], in1=mods1[:, jscale * 8:jscale * 8 + 8, :], op=ALU.mult),
                 reads=["xs", "mods1"], writes=["tmp16"])
            S.op("dve", lambda e: e.tensor_tensor(out=hT[:, :, 128:128 + NSMP], in0=tmp16[:], in1=modall[:, jshift * 8:jshift * 8 + 8, 1:1 + NSMP],
                                                  op=ALU.add), reads=["tmp16", "modall"], writes=["hT"])

        def zs_from_ps(ps, pskey, c, jgate, off=128):
            S.op("dve", lambda e: e.tensor_tensor(out=zs[:, c, :], in0=ps[:, off:off + NSMP], in1=mods1[:, jgate * 8 + c, :], op=ALU.mult),
                 reads=[pskey, "mods1"], writes=["zs"])
            S.op("dve", lambda e: e.tensor_tensor(out=zs[:, c, :], in0=zs[:, c, :], in1=xsT[:, c, :], op=ALU.add), reads=["zs", "xs"], writes=["zs"])

        epsA = SB("epsA", [128, 1]); epsL = SB("epsL", [128, 1]); onesb = SB("onesb", [128, 128], BF16)
        S.op("pool", lambda e: e.memset(epsA[:], EPS_A), writes=["epsA"])
        S.op("pool", lambda e: e.memset(epsL[:], LN_EPS), writes=["epsL"])
        S.op("pool", lambda e: e.memset(ones_f[:], 1.0), writes=["ones_f"])
        identb = SB("identb", [128, 128], BF16)
        S.op("pool", lambda e: e.memset(ident[:], 1.0), writes=["ident"])
        S.op("pool", lambda e: e.affine_select(out=ident[:], in_=ident[:], pattern=[[1, 128]], compare_op=ALU.is_equal,
                                               fill=0.0, base=0, channel_multiplier=-1), reads=["ident"], writes=["ident"])
        S.op("pool", lambda e: e.memset(tri01[:], 1.0), writes=["tri01"])
        S.op("pool", lambda e: e.affine_select(out=tri01[:], in_=tri01[:], pattern=[[1, 128]], compare_op=ALU.is_ge,
                                               fill=0.0, base=0, channel_multiplier=-1), reads=["tri01"], writes=["tri01"])
        S.dma(lambda e: e.dma_start(out=rolet[:], in_=role[:, :]), writes=["rolet"])
        S.op("pool", lambda e: e.tensor_copy(out=identb[:], in_=ident[:]), reads=["ident"], writes=["identb"])
        S.op("pool", lambda e: e.tensor_copy(out=onesb[:], in_=ones_f[:]), reads=["ones_f"], writes=["onesb"])

        def transpose_rows(src_tile, nrows, ncols_chunks, consume):
            for c0 in range(0, ncols_chunks, 4):
                pt, ptk = next_ps(psT, "pt")
                cs = list(range(c0, min(c0 + 4, ncols_chunks)))

                def fn(e, cs=cs, pt=pt):
                    for c in cs:
                        ins = e.transpose(out=pt[:, (c - cs[0]) * 128:(c - cs[0]) * 128 + nrows],
                                          in_=src_tile[:nrows, c * 128:(c + 1) * 128], identity=ident[:nrows, :nrows])
                    return ins
                S.op("pe", fn, reads=["tokt", "ident"], writes=[ptk])
                for c in cs:
                    consume(c, pt[:, (c - cs[0]) * 128:(c - cs[0]) * 128 + nrows], ptk)

        S.dma(lambda e: e.dma_start(out=tokt[:1 + NSMP, :], in_=call[:, :]), writes=["tokt"])
        transpose_rows(tokt, 1 + NSMP, KC,
                       lambda c, p, k: S.op("act", lambda e: e.activation(out=cT[:, c, :], in_=p, func=AF.Silu), reads=[k], writes=["cT"]))

        S.op("pool", lambda e: e.iota(piota_p[:], pattern=[[0, 1]], base=0, channel_multiplier=1, allow_small_or_imprecise_dtypes=True), writes=["piota_p"])
        if SMP:
            ptf = SB("ptf", [128, NSMP * NPG])
            S.dma(lambda e: e.dma_start(out=idx_all[:], in_=ptab.rearrange("s g -> (s g)").partition_broadcast(128)), writes=["idx_all"])
            S.op("dve", lambda e: e.tensor_copy(out=ptf[:], in_=idx_all[:]), reads=["idx_all"], writes=["ptf"])
            S.op("dve", lambda e: e.tensor_scalar(out=ptf[:], in0=ptf[:], scalar1=128.0, scalar2=piota_p[:, 0:1], op0=ALU.mult, op1=ALU.add),
                 reads=["ptf", "piota_p"], writes=["ptf"])
            S.op("dve", lambda e: e.tensor_copy(out=idx_all[:], in_=ptf[:]), reads=["ptf"], writes=["idx_all"])
            S.dma(lambda e: e.dma_start(out=tokt[:NSMP, :], in_=xs_in[:, :]), writes=["tokt"])
            transpose_rows(tokt, NSMP, KC,
                           lambda c, p, k: S.op("dve", lambda e: e.tensor_copy(out=xsT[:, c, :], in_=p), reads=[k], writes=["xs"]))
        for b in range(NBLK):
            S.dma(lambda e, b=b: e.dma_start(out=tokt[:], in_=xloc[b * 128:(b + 1) * 128, :]), writes=["tokt"])
            tk = [tt for tt, ww in tiles if tt <= b * 128 < tt + ww][0]
            transpose_rows(tokt, 128, KC,
                           lambda c, p, k, b=b, tk=tk: S.op("dve", lambda e: e.tensor_copy(out=xres[:, c, b * 128:(b + 1) * 128], in_=p),
                                                            reads=[k], writes=[f"x{tk}"]))

        bada_t = SB("bada_t", [128, 48])
        modall = SB("modall", [128, 48, 1 + NSMP])
        halo = SB("halo", [128, FC, 2])
        cw = SB("cw", [128, 3, FC]); cb = SB("cb", [128, FC])
        b_u = SB("b_u", [128, KC]); ngp = SB("ngp", [128, KC]); nbp = SB("nbp", [128, KC])
        bst = SB("bst", [128, 2, 6]); mv = SB("mv", [128, 2]); rs_t = SB("rs_t", [128, 1])

        ARENA_F32 = 19712
        arena = SB("arena", [128, ARENA_F32])
        ar = {"off": 0, "phase": 0}

        def new_phase():
            S.barrier()
            ar["off"] = 0
            ar["phase"] += 1

        def carve(shape, dt=F32):
            n = 1
            for d_ in shape[1:]:
                n *= d_
            nf = n if dt == F32 else (n + 1) // 2
            a = arena[:, ar["off"]:ar["off"] + nf]
            ar["off"] += nf
            assert ar["off"] <= ARENA_F32, ("arena overflow", ar["off"])
            if dt != F32:
                a = a.bitcast(dt)
            if len(shape) == 3:
                a = a.rearrange("p (a b) -> p a b", a=shape[1])
            return a

        def load_resident(dst, dkey, w_ap2d, c0, ncols):
            for cc in range(0, ncols, 128):
                wt, wkey = load_wchunk(w_ap2d, KC, c0 + cc)
                S.op("act", lambda e, cc=cc, wt=wt: e.activation(out=dst[:, :, cc:cc + 128], in_=wt[:, :KC, :], func=AF.Identity),
                     reads=[wkey], writes=[dkey])

        def do_layer(li):
            j = li // 2
            slow_vec(bada_t[:], vec_pk(b_ada[li]))
            slow_vec(lng[:, 0, :], vec_pk(ln_g[li, 0])); slow_vec(lng[:, 1, :], vec_pk(ln_g[li, 1]))
            slow_vec(lnb[:, 0, :], vec_pk(ln_b[li, 0])); slow_vec(lnb[:, 1, :], vec_pk(ln_b[li, 1]))
            for jj in range(48):
                i = cnt["w"] % NWB
                cnt["w"] += 1
                S.dma(lambda e, i=i, jj=jj: e.dma_start(out=wst[i][:, :KC, :],
                                                         in_=w_ada[li][:, jj * 128:(jj + 1) * 128].rearrange("(kc p) n -> p kc n", p=128)),
                      writes=[f"wst{i}"])
                ps, pk = next_ps(psA, "ps")
                mm_group(ps[:, :1 + NSMP], pk, 1 + NSMP, [(wst[i][:, k, :], cT[:, k, :]) for k in range(KC)], [f"wst{i}", "cT"])
                S.op("act", lambda e, ps=ps, jj=jj: e.activation(out=modall[:, jj, :], in_=ps[:, :1 + NSMP], func=AF.Identity,
                                                                 bias=bada_t[:, jj:jj + 1]), reads=[pk, "bada_t"], writes=["modall"])
            S.op("dve", lambda e: e.tensor_copy(out=modp[:], in_=modall[:, :, 0]), reads=["modall"], writes=["modp"])
            S.op("dve", lambda e: e.tensor_scalar(out=modp1[:], in0=modp[:], scalar1=1.0, scalar2=None, op0=ALU.add),
                 reads=["modp"], writes=["modp1"])
            for jg in (2, 5):
                S.op("dve", lambda e, jg=jg: e.tensor_scalar(out=modp1[:, jg * 8:jg * 8 + 8], in0=modp1[:, jg * 8:jg * 8 + 8],
                                                             scalar1=1.0 / ALPHA, scalar2=None, op0=ALU.mult),
                     reads=["modp1"], writes=["modp1"])

            if SMP:
                S.op("dve", lambda e: e.tensor_scalar(out=mods1[:], in0=modall[:, :, 1:1 + NSMP], scalar1=1.0, scalar2=None, op0=ALU.add),
                     reads=["modall"], writes=["mods1"])
                for jg in (2, 5):
                    S.op("dve", lambda e, jg=jg: e.tensor_scalar(out=mods1[:, jg * 8:jg * 8 + 8, :], in0=mods1[:, jg * 8:jg * 8 + 8, :],
                                                                 scalar1=1.0 / ALPHA, scalar2=None, op0=ALU.mult), reads=["mods1"], writes=["mods1"])
            if li % 2 == 0:
                new_phase()
                w_u = carve([128, KC, D], BF16); w_v = carve([128, KC, D], BF16); w_o = carve([128, KC, D], BF16)
                bvb = carve([128, D]); WsT = carve([128, 8, 128], BF16); C2 = carve([128, 8, 128])
                uT = carve([128, KC, WMAX]); umT = carve([128, KC, WMAX], BF16)
                vtok = carve([128, D]); vhat = carve([128, D], BF16); mix = carve([128, 128])
                load_resident(w_u, "w_u", sgu_w_in[j], 0, D)
                load_resident(w_v, "w_v", sgu_w_in[j], D, D)
                load_resident(w_o, "w_o", sgu_w_out[j], 0, D)
                slow_vec(b_u[:], vec_pk(sgu_b_in[j, 0:D]))
                slow_vec(ngp[:], vec_pk(sgu_norm_g[j])); slow_vec(nbp[:], vec_pk(sgu_norm_b[j]))
                S.dma(lambda e: e.dma_start(out=bvb, in_=sgu_b_in[j, D:2 * D].partition_broadcast(128)), writes=["bvb"])
                S.dma(lambda e: e.dma_start(out=C2, in_=sgu_b_s[j].partition_broadcast(128)), writes=["C2"])
                for g in range(8):
                    S.dma(lambda e, g=g: e.dma_start(out=tokt[:, g * 128:(g + 1) * 128], in_=sgu_w_s[j, g]), writes=["tokt"])
                transpose_rows(tokt, 128, 8,
                               lambda c, p, k: S.op("dve", lambda e: e.tensor_tensor(out=WsT[:, c, :], in0=p, in1=tri01[:], op=ALU.mult),
                                                    reads=[k, "tri01"], writes=["WsT"]))
                S.op("pool", lambda e: e.tensor_copy(out=onesb[:], in_=ones_f[:]), reads=["ones_f"], writes=["onesb"])
                for g in range(8):
                    ps, pk = next_ps(psA, "ps")
                    mm_group(ps[:, :128], pk, 128, [(onesb[:], WsT[:, g, :])], ["onesb", "WsT"])
                    S.op("dve", lambda e, g=g, ps=ps: e.scalar_tensor_tensor(out=C2[:, g, :], in0=ps[:, :128], scalar=nbp[:, g:g + 1],
                                                                             in1=C2[:, g, :], op0=ALU.mult, op1=ALU.add),
                         reads=[pk, "nbp", "C2"], writes=["C2"])
                if SMP:
                    S.dma(lambda e: e.dma_start(out=w00c[:], in_=sgu_w_s[j, :, 0, 0].partition_broadcast(128), allow_slow_non_contiguous=True), writes=["w00c"])
                    S.dma(lambda e: e.dma_start(out=bs0c[:], in_=sgu_b_s[j, :, 0].partition_broadcast(128), allow_slow_non_contiguous=True), writes=["bs0c"])

                def sgu_tile(t0, W):
                    smp = SMP and t0 == 0
                    Wx = W + (NSMP if smp else 0)
                    modulate(t0, W, 0, 1)
                    if smp:
                        modulate_s(0, 1)
                    for c in range(KC):
                        ps, pk = next_ps(psA, "ps")
                        mm_group(ps[:, :Wx], pk, Wx, [(w_u[:, k, c * 128:(c + 1) * 128], hT[:, k, :Wx]) for k in range(KC)], ["w_u", "hT"])
                        S.op("act", lambda e, c=c, ps=ps: e.activation(out=uT[:, c, :Wx], in_=ps[:, :Wx], func=AF.Gelu_apprx_tanh,
                                                                      bias=b_u[:, c:c + 1]), reads=[pk, "b_u"], writes=["uT"])
                    if smp:
                        for half in range(2):
                            ps, pk = next_ps(psB, "pb")
                            mm_group(ps[:NSMP, :512], pk, 512,
                                     [(hT[:, k, 128:128 + NSMP], w_v[:, k, half * 512:(half + 1) * 512]) for k in range(KC)], ["w_v", "hT"])
                            S.op("dve", lambda e, ps=ps, half=half: e.tensor_tensor(out=vtok[:NSMP, half * 512:(half + 1) * 512], in0=ps[:NSMP, :512],
                                                                                    in1=bvb[:NSMP, half * 512:(half + 1) * 512], op=ALU.add),
                                 reads=[pk, "bvb"], writes=["vtok"])
                        S.op("act", lambda e: e.activation(out=vtok[:NSMP, :], in_=vtok[:NSMP, :], func=AF.Gelu_apprx_tanh), reads=["vtok"], writes=["vtok"])
                        for half in range(2):
                            S.op("dve", lambda e, half=half: e.bn_stats(out=bst[:NSMP, half, :], in_=vtok[:NSMP, half * 512:(half + 1) * 512]),
                                 reads=["vtok"], writes=["bst"])
                        S.op("dve", lambda e: e.bn_aggr(out=mv[:NSMP], in_=bst[:NSMP]), reads=["bst"], writes=["mv"])
                        S.op("act", lambda e: e.activation(out=rs_t[:NSMP], in_=mv[:NSMP, 1:2], func=AF.Sqrt, bias=epsL[:NSMP, 0:1]),
                             reads=["mv", "epsL"], writes=["rs_t"])
                        S.op("dve", lambda e: e.reciprocal(out=rs_t[:NSMP], in_=rs_t[:NSMP]), reads=["rs_t"], writes=["rs_t"])
                        S.op("dve", lambda e: e.tensor_scalar(out=vtok[:NSMP, :], in0=vtok[:NSMP, :], scalar1=mv[:NSMP, 0:1], scalar2=rs_t[:NSMP, 0:1],
                                                              op0=ALU.subtract, op1=ALU.mult), reads=["vtok", "mv", "rs_t"], writes=["vtok"])
                        for c0 in (0, 4):
                            pt, ptk = next_ps(psT, "pt")

                            def fn(e, c0=c0, pt=pt):
                                for c in range(c0, c0 + 4):
                                    ins = e.transpose(out=pt[:, (c - c0) * 128:(c - c0) * 128 + NSMP], in_=vtok[:NSMP, c * 128:(c + 1) * 128],
                                                      identity=ident[:NSMP, :NSMP])
                                return ins
                            S.op("pe", fn, reads=["vtok", "ident"], writes=[ptk])
                            for c in range(c0, c0 + 4):
                                S.op("act", lambda e, c=c, c0=c0, pt=pt: e.activation(out=vTs[:, c, :], in_=pt[:, (c - c0) * 128:(c - c0) * 128 + NSMP],
                                                                                    func=AF.Identity, scale=ngp[:, c:c + 1], bias=nbp[:, c:c + 1]),
                                     reads=[ptk, "ngp", "nbp"], writes=["vTs"])
                        for c0 in (0, 4):
                            pt, ptk = next_ps(psT, "pt")

                            def fn2(e, c0=c0, pt=pt):
                                for c in range(c0, c0 + 4):
                                    ins = e.transpose(out=pt[:NSMP, (c - c0) * 128:(c - c0 + 1) * 128], in_=vTs[:, c, :], identity=ident[:])
                                return ins
                            S.op("pe", fn2, reads=["vTs", "ident"], writes=[ptk])
                            S.op("dve", lambda e, c0=c0, pt=pt: e.tensor_copy(out=tokt[:NSMP, c0 * 128:(c0 + 4) * 128], in_=pt[:NSMP, :512]),
                                 reads=[ptk], writes=["tokt"])
                        S.dma(lambda e: e.dma_start(out=sguv[j], in_=tokt[:NSMP, :]), reads=["tokt"])
                        for g in range(8):
                            S.op("dve", lambda e, g=g: e.tensor_scalar(out=tmp16[:, g, :], in0=vTs[:, g, :], scalar1=w00c[:, g:g + 1], scalar2=bs0c[:, g:g + 1],
                                                                      op0=ALU.mult, op1=ALU.add), reads=["vTs", "w00c", "bs0c"], writes=["tmp16"])
                        S.op("dve", lambda e: e.tensor_tensor(out=umT[:, :, 128:128 + NSMP], in0=tmp16[:], in1=uT[:, :, 128:128 + NSMP], op=ALU.mult),
                             reads=["tmp16", "uT"], writes=["umT"])
                    for bb in range(W // 128):
                        for half in range(2):
                            ps, pk = next_ps(psB, "pb")
                            mm_group(ps[:, :512], pk, 512,
                                     [(hT[:, k, bb * 128:(bb + 1) * 128], w_v[:, k, half * 512:(half + 1) * 512]) for k in range(KC)], ["w_v", "hT"])
                            S.op("dve", lambda e, ps=ps, half=half: e.tensor_tensor(out=vtok[:, half * 512:(half + 1) * 512], in0=ps[:, :512],
                                                                                    in1=bvb[:, half * 512:(half + 1) * 512], op=ALU.add),
                                 reads=[pk, "bvb"], writes=["vtok"])
                        S.op("act", lambda e: e.activation(out=vtok, in_=vtok, func=AF.Gelu_apprx_tanh), reads=["vtok"], writes=["vtok"])
                        for half in range(2):
                            S.op("dve", lambda e, half=half: e.bn_stats(out=bst[:, half, :], in_=vtok[:, half * 512:(half + 1) * 512]),
                                 reads=["vtok"], writes=["bst"])
                        S.op("dve", lambda e: e.bn_aggr(out=mv[:], in_=bst[:]), reads=["bst"], writes=["mv"])
                        S.op("act", lambda e: e.activation(out=rs_t[:], in_=mv[:, 1:2], func=AF.Sqrt, bias=epsL[:, 0:1]),
                             reads=["mv", "epsL"], writes=["rs_t"])
                        S.op("dve", lambda e: e.reciprocal(out=rs_t[:], in_=rs_t[:]), reads=["rs_t"], writes=["rs_t"])
                        S.op("dve", lambda e: e.tensor_scalar(out=vhat, in0=vtok, scalar1=mv[:, 0:1], scalar2=rs_t[:, 0:1],
                                                              op0=ALU.subtract, op1=ALU.mult), reads=["vtok", "mv", "rs_t"], writes=["vhat"])
                        for gh in range(2):
                            ps, pk = next_ps(psB, "pb")

                            def fn(e, ps=ps, gh=gh):
                                for g4 in range(4):
                                    g = gh * 4 + g4
                                    ins = e.matmul(ps[:, g4 * 128:(g4 + 1) * 128], lhsT=vhat[:, g * 128:(g + 1) * 128], rhs=WsT[:, g, :],
                                                   start=True, stop=True)
                                return ins
                            S.op("pe", fn, reads=["vhat", "WsT"], writes=[pk])
                            for g4 in range(4):
                                g = gh * 4 + g4
                                S.op("dve", lambda e, ps=ps, g=g, g4=g4: e.scalar_tensor_tensor(
                                    out=mix, in0=ps[:, g4 * 128:(g4 + 1) * 128], scalar=ngp[:, g:g + 1], in1=C2[:, g, :],
                                    op0=ALU.mult, op1=ALU.add), reads=[pk, "ngp", "C2"], writes=["mix"])
                                S.op("dve", lambda e, g=g, bb=bb: e.tensor_tensor(out=umT[:, g, bb * 128:(bb + 1) * 128], in0=mix,
                                                                                  in1=uT[:, g, bb * 128:(bb + 1) * 128], op=ALU.mult),
                                     reads=["mix", "uT"], writes=["umT"])
                    for c in range(KC):
                        ps, pk = next_ps(psA, "ps")
                        mm_group(ps[:, :Wx], pk, Wx, [(w_o[:, k, c * 128:(c + 1) * 128], umT[:, k, :Wx]) for k in range(KC)], ["w_o", "umT"])
                        z_from_ps(ps, pk, c, t0, W, 2)
                        if smp:
                            zs_from_ps(ps, pk, c, 2)
                    postnorm(t0, W, 0)
                    if smp:
                        postnorm(0, NSMP, 0, samples=True)
                for (t0, W) in tiles:
                    sgu_tile(t0, W)
            else:

                jd = j
                assert HALF % CH == 0
                WIN = dsa_w_in[jd]
                new_phase()
                wkv = carve([128, KC, 512], BF16); wki = carve([128, KC, 64], BF16)
                KTl = carve([128, 2, NT], BF16); kiTl = carve([128, NT], BF16); Vl = carve([128, NBLK, 256], BF16)
                kvst = [carve([128, 576]) for _ in range(2)]
                load_resident(wkv, "wkv", WIN, 1024, 512)
                wt, wkey = load_wchunk(WIN, KC, 2048, ncol=64)
                S.op("pool", lambda e, wt=wt: e.tensor_copy(out=wki, in_=wt[:, :KC, :64]), reads=[wkey], writes=["wki"])
                wki2 = carve([128, KC, 128], BF16)
                for hh in range(2):
                    S.op("pool", lambda e, hh=hh: e.tensor_copy(out=wki2[:, :, hh * 64:(hh + 1) * 64], in_=wki), reads=["wki"], writes=["wki2"])
                S.op("pool", lambda e: e.memset(kiTl, 0.0), writes=["kiTl"])
                def dsap_tile(t0, W):
                    smp = SMP and t0 == 0
                    Wx = W + (NSMP if smp else 0)
                    modulate(t0, W, 0, 1)
                    if smp:
                        modulate_s(0, 1)
                    for c in range(2):
                        ps, pk = next_ps(psA, "ps")
                        linear_chunk(WIN, KC, 1024 + c * 128, hT, ["hT"], Wx, ps, pk)
                        S.op("act", lambda e, ps=ps, c=c: e.activation(out=KTl[:, c, t0:t0 + W], in_=ps[:, :W], func=AF.Identity),
                             reads=[pk], writes=["KTl"])
                        if smp:
                            S.op("act", lambda e, ps=ps, c=c: e.activation(out=KTn[:, c, :], in_=ps[:, 128:128 + NSMP], func=AF.Identity), reads=[pk], writes=["KTn"])
                    ps, pk = next_ps(psA, "ps")
                    wt, wkey = load_wchunk(WIN, KC, 2048, ncol=64)
                    mm_group(ps[:64, :W], pk, W, [(wt[:, k, :64], hT[:, k, :W]) for k in range(KC)], [wkey, "hT"])
                    S.op("act", lambda e, ps=ps: e.activation(out=kiTl[:64, t0:t0 + W], in_=ps[:64, :W], func=AF.Identity),
                         reads=[pk], writes=["kiTl"])
                    if smp:
                        ps, pk = next_ps(psA, "ps")
                        mm_group(ps[:, :NSMP], pk, NSMP, [(wki2[:, k, :], hT[:, k, 128:128 + NSMP]) for k in range(KC)], ["wki2", "hT"])
                        S.op("act", lambda e, ps=ps: e.activation(out=kiTn[:], in_=ps[:, :NSMP], func=AF.Identity), reads=[pk], writes=["kiTn"])
                        stg = kvst[0]
                        ps, pk = next_ps(psB, "pb")
                        mm_group(ps[:NSMP, :512], pk, 512, [(hT[:, k, 128:128 + NSMP], wkv[:, k, :]) for k in range(KC)], ["wkv", "hT"])
                        S.op("act", lambda e, ps=ps: e.activation(out=stg[:NSMP, 0:512], in_=ps[:NSMP, :512], func=AF.Identity), reads=[pk], writes=["kvst0"])
                        ps2, pk2 = next_ps(psB, "pb")
                        mm_group(ps2[:NSMP, :64], pk2, 64, [(hT[:, k, 128:128 + NSMP], wki[:, k, :]) for k in range(KC)], ["wki", "hT"])
                        S.op("dve", lambda e, ps2=ps2: e.tensor_copy(out=stg[:NSMP, 512:576], in_=ps2[:NSMP, :64]), reads=[pk2], writes=["kvst0"])
                        S.op("pool", lambda e: e.tensor_copy(out=Vn[:], in_=stg[:NSMP, 256:512]), reads=["kvst0"], writes=["Vn"])
                        S.dma(lambda e: e.dma_start(out=ksn[jd], in_=stg[:NSMP, 0:256]), reads=["kvst0"])
                        S.dma(lambda e: e.dma_start(out=vsn[jd], in_=stg[:NSMP, 256:512]), reads=["kvst0"])
                        S.dma(lambda e: e.dma_start(out=kisn[jd], in_=stg[:NSMP, 512:576]), reads=["kvst0"])
                    for bb in range(W // 128):
                        blk = t0 // 128 + bb
                        stg = kvst[blk % 2]; sk = f"kvst{blk % 2}"
                        ps, pk = next_ps(psB, "pb")
                        mm_group(ps[:, :512], pk, 512, [(hT[:, k, bb * 128:(bb + 1) * 128], wkv[:, k, :]) for k in range(KC)], ["wkv", "hT"])
                        S.op("act", lambda e, ps=ps, stg=stg: e.activation(out=stg[:, 0:512], in_=ps[:, :512], func=AF.Identity),
                             reads=[pk], writes=[sk])
                        ps2, pk2 = next_ps(psB, "pb")
                        mm_group(ps2[:, :64], pk2, 64, [(hT[:, k, bb * 128:(bb + 1) * 128], wki[:, k, :]) for k in range(KC)], ["wki", "hT"])
                        S.op("dve", lambda e, ps2=ps2, stg=stg: e.tensor_copy(out=stg[:, 512:576], in_=ps2[:, :64]), reads=[pk2], writes=[sk])
                        S.op("pool", lambda e, stg=stg, blk=blk: e.tensor_copy(out=Vl[:, blk, :], in_=stg[:, 256:512]), reads=[sk], writes=["Vl"])
                        if blk >= 1:
                            r0 = (blk - 1) * 128
                            S.dma(lambda e, stg=stg, r0=r0: e.dma_start(out=knew[jd, r0:r0 + 128, :], in_=stg[:, 0:256]), reads=[sk])
                            S.dma(lambda e, stg=stg, r0=r0: e.dma_start(out=vnew[jd, r0:r0 + 128, :], in_=stg[:, 256:512]), reads=[sk])
                            S.dma(lambda e, stg=stg, r0=r0: e.dma_start(out=kinew[jd, r0:r0 + 128, :], in_=stg[:, 512:576]), reads=[sk])
                for (t0, W) in tiles:
                    dsap_tile(t0, W)
                if dbg_stop <= 1:
                    return ffn_phase(li)
                bn, gt_ = bounce[jd], gath[jd]
                for c in range(2):
                    S.dma(lambda e, c=c: e.dma_start(out=bn[c][:, :], in_=KTl[:, c, 128:NT]), reads=["KTl"], writes=[f"bounce{jd}_{c}"])
                bpv = VSEG // 256
                for g in range(NVS):
                    S.dma(lambda e, g=g: e.dma_start(out=bn[2 + g].rearrange("p (b v) -> p b v", v=256), in_=Vl[:, 1 + g * bpv:1 + (g + 1) * bpv, :]),
                          reads=["Vl"], writes=[f"bounce{jd}_{2 + g}"])
                S.dma(lambda e: e.dma_start(out=bn[NSEG - 1][:, :], in_=kiTl[:, 128:NT]), reads=["kiTl"], writes=[f"bounce{jd}_{NSEG - 1}"])
                for g in range(NSEG):
                    S.cc(lambda e, g=g: e.collective_compute("AllGather", ALU.bypass, replica_groups=PAIRS, ins=[bn[g][:, :]], outs=[gt_[g][:, :]]),
                         reads=[f"bounce{jd}_{g}"], writes=[f"gath{jd}_{g}"])
                gk = [f"gath{jd}_{g}" for g in range(NSEG)]

                if dbg_stop <= 2:
                    return ffn_phase(li)

                if SMP:
                    new_phase()
                    NK1 = NPG + 1
                    KTs2 = [carve([128, 2, NK1 * 128], BF16) for _ in range(2)]; kiTs2 = [carve([128, NK1 * 128], BF16) for _ in range(2)]
                    Vs2 = [carve([128, NK1, 256], BF16) for _ in range(2)]
                    Kst = [carve([128, 256]) for _ in range(2)]; Vst = [carve([128, 256]) for _ in range(2)]; kist = [carve([128, 128]) for _ in range(2)]
                    qTs = carve([128, 8, NSMP], BF16); qiTs = carve([128, 4, NSMP], BF16); oTs = carve([128, 8, NSMP], BF16)
                    wwi_s = carve([128, KC, 8], BF16); E_all = carve([128, NSMP, 128])
                    rS = carve([128, NK1, 8]); eS = carve([128, NK1, 8]); pTs = carve([128, NK1, 8], BF16)
                    scT = carve([128, NK1]); junk17 = carve([128, NK1]); mk17 = carve([128, NK1]); nb16 = carve([128, NK1])
                    ss = carve([128, 32])
                    wit, witp, wib, cntp, lo_s, w0_s, mid_s, gew_s, negM, rec8, m11 = (
                        ss[:, 0:8], ss[:, 8:16], ss[:, 16:24], ss[:, 24:25], ss[:, 25:26], ss[:, 26:27], ss[:, 27:28], ss[:, 28:29],
                        ss[:, 29:30], ss[:, 16:24], ss[:, 30:31])
                    rec8 = carve([128, 8])
                    wt, wkey = load_wchunk(WIN, KC, 2112, ncol=8)
                    S.op("pool", lambda e, wt=wt: e.tensor_copy(out=wwi_s, in_=wt[:, :KC, :8]), reads=[wkey], writes=["wwi_s"])
                    for bi in range(2):
                        S.op("pool", lambda e, bi=bi: e.memset(KTs2[bi], 0.0), writes=[f"KTs{bi}"])
                        S.op("pool", lambda e, bi=bi: e.memset(kiTs2[bi], 0.0), writes=[f"kiTs{bi}"])
                        S.op("pool", lambda e, bi=bi: e.memset(Vs2[bi], 0.0), writes=[f"Vs{bi}"])
                    S.op("pool", lambda e: e.memset(nb16, 0.0), writes=["nb16"])
                    S.op("pool", lambda e: e.memset(nb16[:, NPG:NPG + 1], -BIG), writes=["nb16"])
                    S.op("pool", lambda e: e.affine_select(out=nb16[:, NPG:NPG + 1], in_=nb16[:, NPG:NPG + 1], pattern=[[0, 1]], compare_op=ALU.is_gt,
                                                           fill=0.0, base=0, channel_multiplier=1), reads=["nb16"], writes=["nb16"])
                    S.op("dve", lambda e: e.tensor_copy(out=E_all[:NSMP], in_=ident[:NSMP, :NSMP].unsqueeze(2).to_broadcast([NSMP, NSMP, 128])),
                         reads=["ident"], writes=["E_all"])
                    modulate_s(0, 1)
                    for h in range(8):
                        ps, pk = next_ps(psA, "ps")
                        linear_chunk(WIN, KC, h * 128, hT, ["hT"], NSMP, ps, pk, off=128)
                        S.op("act", lambda e, ps=ps, h=h: e.activation(out=qTs[:, h, :], in_=ps[:, :NSMP], func=AF.Identity, scale=128 ** -0.5), reads=[pk], writes=["qTs"])
                    for c in range(4):
                        ps, pk = next_ps(psA, "ps")
                        linear_chunk(WIN, KC, 1536 + c * 128, hT, ["hT"], NSMP, ps, pk, off=128)
                        S.op("act", lambda e, ps=ps, c=c: e.activation(out=qiTs[:, c, :], in_=ps[:, :NSMP], func=AF.Identity), reads=[pk], writes=["qiTs"])
                    ps, pk = next_ps(psA, "ps")
                    mm_group(ps[:NSMP, :8], pk, 8, [(hT[:, k, 128:128 + NSMP], wwi_s[:, k, :]) for k in range(KC)], ["wwi_s", "hT"])
                    S.op("act", lambda e, ps=ps: e.activation(out=wit[:NSMP], in_=ps[:NSMP, :8], func=AF.Identity, scale=(8 ** -0.5) * (64 ** -0.5)), reads=[pk], writes=["wit"])
                    S.op("dve", lambda e: e.tensor_copy(out=witp[:NSMP, 0:4], in_=wit[:NSMP, 0:8:2]), reads=["wit"], writes=["witp"])
                    S.op("dve", lambda e: e.tensor_copy(out=witp[:NSMP, 4:8], in_=wit[:NSMP, 1:8:2]), reads=["wit"], writes=["witp"])
                    CK = cache_k[jd]; CV = cache_v[jd]; CI = cache_ki[jd]

                    def sample_attend(si):
                        bi = si % 2
                        KTs, kiTs, Vs = KTs2[bi], kiTs2[bi], Vs2[bi]
                        kKT, kki, kV = f"KTs{bi}", f"kiTs{bi}", f"Vs{bi}"
                        for pg in range(NPG):
                            i = pg % 2
                            icol = idx_all[:, si * NPG + pg:si * NPG + pg + 1]
                            S.dma(lambda e, i=i, icol=icol: e.indirect_dma_start(out=Kst[i], out_offset=None, in_=CK[:, :],
                                                                                in_offset=bass.IndirectOffsetOnAxis(ap=icol, axis=0)),
                                  reads=["idx_all"], writes=[f"Kst{i}"], q="pool")
                            if dbg_stop <= 4.1:
                                continue
                            S.dma(lambda e, i=i, icol=icol: e.indirect_dma_start(out=Vst[i], out_offset=None, in_=CV[:, :],
                                                                                in_offset=bass.IndirectOffsetOnAxis(ap=icol, axis=0)),
                                  reads=["idx_all"], writes=[f"Vst{i}"], q="pool")
                            S.dma(lambda e, i=i, icol=icol: e.indirect_dma_start(out=kist[i][:, 0:64], out_offset=None, in_=CI[:, :],
                                                                                in_offset=bass.IndirectOffsetOnAxis(ap=icol, axis=0)),
                                  reads=["idx_all"], writes=[f"kist{i}"], q="pool")
                            if dbg_stop <= 4.2:
                                continue
                            S.op("dve", lambda e, i=i: e.tensor_copy(out=kist[i][:, 64:128], in_=kist[i][:, 0:64]), reads=[f"kist{i}"], writes=[f"kist{i}"])
                            S.op("act", lambda e, i=i, pg=pg: e.activation(out=Vs[:, pg, :], in_=Vst[i], func=AF.Identity), reads=[f"Vst{i}"], writes=[kV])
                            if dbg_stop <= 4.3:
                                continue
                            pt, ptk = next_ps(psT, "pt")

                            def fnt(e, i=i, pt=pt):
                                e.transpose(out=pt[:, 0:128], in_=Kst[i][:, 0:128], identity=ident[:])
                                e.transpose(out=pt[:, 128:256], in_=Kst[i][:, 128:256], identity=ident[:])
                                return e.transpose(out=pt[:, 256:384], in_=kist[i][:, :], identity=ident[:])
                            S.op("pe", fnt, reads=[f"Kst{i}", f"kist{i}", "ident"], writes=[ptk])
                            if dbg_stop <= 4.4:
                                continue
                            S.op("act", lambda e, pt=pt, pg=pg: e.activation(out=KTs[:, :, pg * 128:(pg + 1) * 128],
                                                                             in_=pt[:, 0:256].rearrange("p (c s) -> p c s", c=2), func=AF.Identity),
                                 reads=[ptk], writes=[kKT])
                            if dbg_stop <= 4.45:
                                continue
                            S.op("act", lambda e, pt=pt, pg=pg: e.activation(out=kiTs[:, pg * 128:(pg + 1) * 128], in_=pt[:, 256:384], func=AF.Identity),
                                 reads=[ptk], writes=[kki])
                        if dbg_stop <= 4.5:
                            return
                        S.op("dve", lambda e: e.tensor_copy(out=KTs[:, :, NPG * 128:NPG * 128 + 1], in_=KTn[:, :, si:si + 1]), reads=["KTn"], writes=[kKT])
                        S.op("dve", lambda e: e.tensor_copy(out=kiTs[:, NPG * 128:NPG * 128 + 1], in_=kiTn[:, si:si + 1]), reads=["kiTn"], writes=[kki])
                        S.dma(lambda e: e.dma_start(out=Vs[0:1, NPG, :], in_=Vn[si:si + 1, :]), reads=["Vn"], writes=[kV])
                        if dbg_stop <= 5.1:
                            return
                        pse, pek = next_ps(psA, "ps")
                        pso_, pok = next_ps(psA, "ps")
                        pscs = [pse[:, :NK1 * 4].rearrange("p (g h) -> p g h", h=4), pso_[:, :NK1 * 4].rearrange("p (g h) -> p g h", h=4)]
                        for hh in range(2):
                            def fns(e, hh=hh):
                                pb = hh * 64
                                for pg in range(NK1):
                                    ins = e.matmul(pscs[hh][:, pg, :], lhsT=kiTs[pb:pb + 64, pg * 128:(pg + 1) * 128], rhs=qiTs[pb:pb + 64, :, si],
                                                   start=True, stop=True)
                                return ins
                            S.op("pe", fns, reads=[kki, "qiTs"], writes=[(pek, pok)[hh]])
                            S.op("act", lambda e, hh=hh: e.activation(out=rS[:, :, hh * 4:(hh + 1) * 4], in_=pscs[hh], func=AF.Relu),
                                 reads=[(pek, pok)[hh]], writes=["rS"])
                        if dbg_stop <= 5.2:
                            return
                        ps2, pk2 = next_ps(psA, "ps")
                        mm_group(ps2[:, :8], pk2, 8, [(E_all[:NSMP, si, :], witp[:NSMP, :])], ["E_all", "witp"])
                        S.op("dve", lambda e, ps2=ps2: e.tensor_copy(out=wib, in_=ps2[:, :8]), reads=[pk2], writes=["wib"])
                        S.op("dve", lambda e: e.tensor_tensor(out=rS, in0=rS, in1=wib.unsqueeze(1).to_broadcast([128, NK1, 8]), op=ALU.mult),
                             reads=["rS", "wib"], writes=["rS"])
                        S.op("dve", lambda e: e.tensor_reduce(out=scT, in_=rS, axis=mybir.AxisListType.X, op=ALU.add), reads=["rS"], writes=["scT"])
                        if dbg_stop <= 5.3:
                            return
                        S.op("act", lambda e: e.activation(out=junk17, in_=scT, func=AF.Square, accum_out=cntp), reads=["scT"], writes=["junk17", "cntp"])
                        pq, pqk = next_ps(psS, "pq")
                        mm_group(pq[:, :1], pqk, 1, [(ones_f[:], cntp)], ["ones_f", "cntp"])
                        S.op("act", lambda e, pq=pq: e.activation(out=w0_s, in_=pq[:, :1], func=AF.Sqrt), reads=[pqk], writes=["w0_s"])
                        S.op("dve", lambda e: e.tensor_scalar(out=lo_s, in0=w0_s, scalar1=-1.0, scalar2=-1.0, op0=ALU.mult, op1=ALU.add), reads=["w0_s"], writes=["lo_s"])
                        S.op("dve", lambda e: e.tensor_scalar(out=w0_s, in0=w0_s, scalar1=2.0, scalar2=2.0, op0=ALU.mult, op1=ALU.add), reads=["w0_s"], writes=["w0_s"])
                        S.op("dve", lambda e: e.tensor_tensor(out=scT, in0=scT, in1=nb16, op=ALU.add), reads=["scT", "nb16"], writes=["scT"])
                        if dbg_stop <= 5.4:
                            return
                        for it in range(NITS):
                            ck = 0.5 ** (it + 1)
                            S.op("dve", lambda e, ck=ck: e.scalar_tensor_tensor(out=mid_s, in0=w0_s, scalar=ck, in1=lo_s, op0=ALU.mult, op1=ALU.add),
                                 reads=["w0_s", "lo_s"], writes=["mid_s"])
                            S.op("dve", lambda e: e.tensor_scalar(out=junk17, in0=scT, scalar1=mid_s, scalar2=0.0, op0=ALU.is_ge, op1=ALU.add, accum_out=cntp),
                                 reads=["scT", "mid_s"], writes=["junk17", "cntp"])
                            pq, pqk = next_ps(psS, "pq")
                            mm_group(pq[:, :1], pqk, 1, [(ones_f[:], cntp)], ["ones_f", "cntp"])
                            S.op("dve", lambda e, pq=pq: e.tensor_scalar(out=gew_s, in0=pq[:, :1], scalar1=KEEP_S - 0.5, scalar2=w0_s, op0=ALU.is_ge, op1=ALU.mult),
                                 reads=[pqk, "w0_s"], writes=["gew_s"])
                            S.op("dve", lambda e, ck=ck: e.scalar_tensor_tensor(out=lo_s, in0=gew_s, scalar=ck, in1=lo_s, op0=ALU.mult, op1=ALU.add),
                                 reads=["gew_s", "lo_s"], writes=["lo_s"])
                        S.op("dve", lambda e: e.tensor_scalar(out=mk17, in0=scT, scalar1=lo_s, scalar2=None, op0=ALU.is_ge), reads=["scT", "lo_s"], writes=["mk17"])
                        if dbg_stop <= 5:
                            return
                        ps, pk = next_ps(psA, "ps")
                        psl = ps[:, :NK1 * 8].rearrange("p (g h) -> p g h", h=8)

                        def fnl(e, psl=psl):
                            for pg in range(NK1):
                                for kvh in range(2):
                                    ins = e.matmul(psl[:, pg, kvh * 4:(kvh + 1) * 4], lhsT=KTs[:, kvh, pg * 128:(pg + 1) * 128], rhs=qTs[:, kvh * 4:(kvh + 1) * 4, si],
                                                   start=True, stop=True)
                            return ins
                        S.op("pe", fnl, reads=[kKT, "qTs"], writes=[pk])
                        S.op("dve", lambda e, ps=ps: e.tensor_reduce(out=cntp, in_=ps[:, :NK1 * 8], axis=mybir.AxisListType.X, op=ALU.max), reads=[pk], writes=["cntp"])
                        pt, ptk = next_ps(psT, "pt")
                        S.op("pe", lambda e, pt=pt: e.transpose(out=pt[:1, 0:128], in_=cntp, identity=ident[:]), reads=["cntp", "ident"], writes=[ptk])
                        S.op("dve", lambda e, pt=pt: e.tensor_reduce(out=m11[0:1], in_=pt[:1, 0:128], axis=mybir.AxisListType.X, op=ALU.max), reads=[ptk], writes=["m11"])
                        pq, pqk = next_ps(psS, "pq")
                        mm_group(pq[:, :1], pqk, 1, [(ones_f[0:1, :], m11[0:1])], ["ones_f", "m11"])
                        S.op("dve", lambda e, pq=pq: e.tensor_scalar(out=negM, in0=pq[:, :1], scalar1=-1.0, scalar2=None, op0=ALU.mult), reads=[pqk], writes=["negM"])
                        S.op("act", lambda e, psl=psl: e.activation(out=eS, in_=psl, func=AF.Exp, bias=negM), reads=[pk, "negM"], writes=["eS"])
                        S.op("dve", lambda e: e.tensor_tensor(out=pTs, in0=eS, in1=mk17.unsqueeze(2).to_broadcast([128, NK1, 8]), op=ALU.mult),
                             reads=["eS", "mk17"], writes=["pTs"])
                        if dbg_stop <= 6:
                            return
                        pso, psok = psB[0], "pb0"
                        pss, pssk = psB[1], "pb1"

                        def fno(e):
                            for kvh in range(2):
                                for pg in range(NK1):
                                    ins = e.matmul(pso[:, kvh * 4:(kvh + 1) * 4], lhsT=Vs[:, pg, kvh * 128:(kvh + 1) * 128], rhs=pTs[:, pg, kvh * 4:(kvh + 1) * 4],
                                                   start=(pg == 0), stop=(pg == NK1 - 1))
                            return ins
                        S.op("pe", fno, reads=[kV, "pTs"], writes=[psok])

                        def fnsum(e):
                            for pg in range(NK1):
                                ins = e.matmul(pss[:, 0:8], lhsT=onesb[:], rhs=pTs[:, pg, :], start=(pg == 0), stop=(pg == NK1 - 1))
                            return ins
                        S.op("pe", fnsum, reads=["onesb", "pTs"], writes=[pssk])
                        S.op("dve", lambda e: e.tensor_scalar(out=rec8, in0=pss[:, 0:8], scalar1=1e-30, scalar2=None, op0=ALU.max), reads=[pssk], writes=["rec8"])
                        S.op("dve", lambda e: e.reciprocal(out=rec8, in_=rec8), reads=["rec8"], writes=["rec8"])
                        S.op("dve", lambda e: e.tensor_tensor(out=oTs[:, :, si], in0=pso[:, 0:8], in1=rec8, op=ALU.mult), reads=[psok, "rec8"], writes=["oTs"])
                    for si in range(NSMP if dbg_stop >= 99 else (0 if dbg_stop <= 3 else 1)):
                        sample_attend(si)
                    for c in range(KC):
                        ps, pk = next_ps(psA, "ps")
                        linear_chunk(dsa_w_out[jd], KC, c * 128, oTs, ["oTs"], NSMP, ps, pk)
                        zs_from_ps(ps, pk, c, 2, off=0)
                    postnorm(0, NSMP, 0, samples=True)
                new_phase()
                NKBM = 2 * (NBLK - 1)
                wwi = carve([128, KC, 8], BF16)
                scores = carve([128, NKEY]); junk = carve([128, NKEY], U8); maskT2 = [carve([128, NKBM, 128], BF16) for _ in range(2)]
                qT2 = [carve([128, 8, 128], BF16) for _ in range(2)]; qiT = carve([128, 4, 128], BF16); oT = carve([128, 8, 128], BF16)
                negC2 = [carve([128, 1]) for _ in range(2)]
                KTc = [carve([128, 2, CH], BF16) for _ in range(2)]; Vc = [carve([128, CH // 128, 256], BF16) for _ in range(2)]
                kic = [carve([128, CH], BF16) for _ in range(2)]
                rbuf = [carve([128, CH]) for _ in range(2)]; mrow = [carve([128, CH], BF16) for _ in range(2)]
                e_t = [carve([128, 4, 128], BF16) for _ in range(2)]; pmt = [carve([128, 4, 128], BF16) for _ in range(2)]
                cbt = carve([128, CH]); kposb = carve([128, CH]); rec = carve([128, CH]); qsq = carve([128, 8, 128], BF16)
                sm = carve([128, 32])
                wi_t, qpos, lo, w0, mid, cntt, gew, qn2, kn2, negC, mins, qrel, piota = (
                    sm[:, 0:8], sm[:, 8:9], sm[:, 9:10], sm[:, 10:11], sm[:, 11:12], sm[:, 12:13], sm[:, 13:14], sm[:, 14:15],
                    sm[:, 15:16], sm[:, 16:17], sm[:, 17:25], sm[:, 25:26], sm[:, 26:27])
                wt, wkey = load_wchunk(WIN, KC, 2112, ncol=8)
                S.op("pool", lambda e, wt=wt: e.tensor_copy(out=wwi, in_=wt[:, :KC, :8]), reads=[wkey], writes=["wwi"])
                S.op("pool", lambda e: e.iota(kposb, pattern=[[1, CH]], base=0, channel_multiplier=0, allow_small_or_imprecise_dtypes=True),
                     writes=["kposb"])
                S.op("pool", lambda e: e.iota(piota, pattern=[[0, 1]], base=0, channel_multiplier=1, allow_small_or_imprecise_dtypes=True),
                     writes=["piota"])
                ccnt = {"k": 0}

                def load_kv_chunk(kc, want):
                    i = ccnt["k"] % 2
                    ccnt["k"] += 1
                    r, cl = divmod(kc * CH, HALF)
                    rr = slice(r * 128, (r + 1) * 128)
                    if "ki" in want:
                        for hh in range(2):
                            S.dma(lambda e, i=i, hh=hh, r=r, cl=cl: e.dma_start(out=kic[i][hh * 64:(hh + 1) * 64, :],
                                                                               in_=gt_[NSEG - 1][r * 128:r * 128 + 64, cl:cl + CH]),
                                  reads=[gk[NSEG - 1]], writes=[f"kic{i}"])
                    if "kv" in want:
                        for c in range(2):
                            S.dma(lambda e, i=i, c=c, rr=rr, cl=cl: e.dma_start(out=KTc[i][:, c, :], in_=gt_[c][rr, cl:cl + CH]),
                                  reads=[gk[c]], writes=[f"KTc{i}"])
                        vg, vo = divmod((cl // 128) * 256, VSEG)
                        S.dma(lambda e, i=i, rr=rr, vg=vg, vo=vo: e.dma_start(
                            out=Vc[i], in_=gt_[2 + vg][rr, vo:vo + (CH // 128) * 256].rearrange("p (b v) -> p b v", v=256)),
                            reads=[gk[2 + vg]], writes=[f"Vc{i}"])
                    return i

                S.op("dve", lambda e: e.memset(kn2, 0.0), writes=["kn2"])
                for kc in range(NKEY // CH):
                    i = load_kv_chunk(kc, ("kv",))
                    for c in range(2):
                        S.op("act", lambda e, i=i, c=c: e.activation(out=mrow[c], in_=KTc[i][:, c, :], func=AF.Square), reads=[f"KTc{i}"], writes=[f"mrow{c}"])
                        ps, pk = next_ps(psA, "ps")
                        mm_group(ps[:, :CH], pk, CH, [(onesb[:], mrow[c])], ["onesb", f"mrow{c}"])
                        S.op("dve", lambda e, ps=ps: e.tensor_reduce(out=cntt, in_=ps[:, :CH], axis=mybir.AxisListType.X, op=ALU.max), reads=[pk], writes=["cntt"])
                        S.op("dve", lambda e: e.tensor_tensor(out=kn2, in0=kn2, in1=cntt, op=ALU.max), reads=["kn2", "cntt"], writes=["kn2"])

                def att_A(qb):
                    bq = qb % 2
                    qT, maskT, negC = qT2[bq], maskT2[bq], negC2[bq]
                    kq, km, kn = f"qT{bq}", f"maskT{bq}", f"negC{bq}"
                    t0 = qb * 128
                    nkb = (NBLK - 1) + qb
                    nch = -(-nkb // (CH // 128))
                    S_ = nch * CH
                    modulate(t0, 128, 0, 1)
                    for h in range(8):
                        ps, pk = next_ps(psA, "ps")
                        linear_chunk(WIN, KC, h * 128, hT, ["hT"], 128, ps, pk)
                        S.op("act", lambda e, ps=ps, h=h: e.activation(out=qT[:, h, :], in_=ps[:, :128], func=AF.Identity, scale=128 ** -0.5),
                             reads=[pk], writes=[kq])
                    for c in range(4):
                        ps, pk = next_ps(psA, "ps")
                        linear_chunk(WIN, KC, 1536 + c * 128, hT, ["hT"], 128, ps, pk)
                        S.op("act", lambda e, ps=ps, c=c: e.activation(out=qiT[:, c, :], in_=ps[:, :128], func=AF.Identity), reads=[pk], writes=["qiT"])
                    ps, pk = next_ps(psA, "ps")
                    mm_group(ps[:, :8], pk, 8, [(hT[:, k, :128], wwi[:, k, :]) for k in range(KC)], ["wwi", "hT"])
                    S.op("act", lambda e, ps=ps: e.activation(out=wi_t, in_=ps[:, :8], func=AF.Identity, scale=(8 ** -0.5) * (64 ** -0.5)),
                         reads=[pk], writes=["wi_t"])
                    S.op("dve", lambda e, qb=qb: e.tensor_scalar(out=qpos, in0=piota, scalar1=rolet[:, 0:1], scalar2=float((qb - 1) * 128),
                                                                op0=ALU.add, op1=ALU.add), reads=["piota", "rolet"], writes=["qpos"])
                    for kc in range(nch):
                        i = load_kv_chunk(kc, ("ki",))
                        sc = scores[:, kc * CH:(kc + 1) * CH]
                        for h in range(8):
                            ps, pk = next_ps(psA, "ps")
                            pb = (h % 2) * 64
                            mm_group(ps[:, :CH], pk, CH, [(qiT[pb:pb + 64, h // 2, :], kic[i][pb:pb + 64, :])], ["qiT", f"kic{i}"])
                            rb = rbuf[h % 2]
                            S.op("act", lambda e, ps=ps, rb=rb: e.activation(out=rb, in_=ps[:, :CH], func=AF.Relu), reads=[pk], writes=[f"rbuf{h % 2}"])
                            if h == 0:
                                S.op("dve", lambda e, rb=rb, sc=sc: e.tensor_scalar(out=sc, in0=rb, scalar1=wi_t[:, 0:1], scalar2=None, op0=ALU.mult),
                                     reads=[f"rbuf{h % 2}", "wi_t"], writes=["scores"])
                            else:
                                S.op("dve", lambda e, rb=rb, sc=sc, h=h: e.scalar_tensor_tensor(out=sc, in0=rb, scalar=wi_t[:, h:h + 1], in1=sc,
                                                                                              op0=ALU.mult, op1=ALU.add),
                                     reads=[f"rbuf{h % 2}", "wi_t", "scores"], writes=["scores"])
                        S.op("dve", lambda e, sc=sc, kc=kc: e.tensor_reduce(out=mins[:, kc:kc + 1], in_=sc, axis=mybir.AxisListType.X, op=ALU.min),
                             reads=["scores"], writes=["mins"])
                        S.op("dve", lambda e, kc=kc: e.tensor_scalar(out=qrel, in0=qpos, scalar1=float(-kc * CH), scalar2=None, op0=ALU.add),
                             reads=["qpos"], writes=["qrel"])
                        S.op("dve", lambda e: e.tensor_scalar(out=cbt, in0=kposb, scalar1=qrel, scalar2=-BIG, op0=ALU.is_gt, op1=ALU.mult),
                             reads=["kposb", "qrel"], writes=["cbt"])
                        S.op("dve", lambda e, sc=sc: e.tensor_tensor(out=sc, in0=sc, in1=cbt, op=ALU.add), reads=["scores", "cbt"], writes=["scores"])
                    S.op("dve", lambda e, nch=nch: e.tensor_reduce(out=lo, in_=mins[:, :nch], axis=mybir.AxisListType.X, op=ALU.min), reads=["mins"], writes=["lo"])
                    S.op("dve", lambda e: e.tensor_scalar(out=lo, in0=lo, scalar1=-1.0, scalar2=None, op0=ALU.add), reads=["lo"], writes=["lo"])
                    S.op("dve", lambda e, S_=S_: e.tensor_reduce(out=w0, in_=scores[:, :S_], axis=mybir.AxisListType.X, op=ALU.max), reads=["scores"], writes=["w0"])
                    S.op("dve", lambda e: e.scalar_tensor_tensor(out=w0, in0=w0, scalar=1.0, in1=lo, op0=ALU.add, op1=ALU.subtract), reads=["w0", "lo"], writes=["w0"])
                    S.op("dve", lambda e: e.tensor_scalar(out=w0, in0=w0, scalar1=1.0, scalar2=None, op0=ALU.max), reads=["w0"], writes=["w0"])
                    for it in range(NIT):
                        ck = 0.5 ** (it + 1)
                        S.op("dve", lambda e, ck=ck: e.scalar_tensor_tensor(out=mid, in0=w0, scalar=ck, in1=lo, op0=ALU.mult, op1=ALU.add),
                             reads=["w0", "lo"], writes=["mid"])
                        S.op("dve", lambda e, S_=S_: e.tensor_scalar(out=junk[:, :S_], in0=scores[:, :S_], scalar1=mid, scalar2=0.0,
                                                                     op0=ALU.is_ge, op1=ALU.add, accum_out=cntt),
                             reads=["scores", "mid"], writes=["junk", "cntt"])
                        S.op("dve", lambda e: e.tensor_scalar(out=gew, in0=cntt, scalar1=KEEP - 0.5, scalar2=w0, op0=ALU.is_ge, op1=ALU.mult),
                             reads=["cntt", "w0"], writes=["gew"])
                        S.op("dve", lambda e, ck=ck: e.scalar_tensor_tensor(out=lo, in0=gew, scalar=ck, in1=lo, op0=ALU.mult, op1=ALU.add),
                             reads=["gew", "lo"], writes=["lo"])
                def att_B(qb):
                    bq = qb % 2
                    qT, maskT, negC = qT2[bq], maskT2[bq], negC2[bq]
                    kq, km, kn = f"qT{bq}", f"maskT{bq}", f"negC{bq}"
                    t0 = qb * 128
                    nkb = (NBLK - 1) + qb
                    nch = -(-nkb // (CH // 128))
                    S_ = nch * CH
                    for kc in range(nch):
                        mr = mrow[kc % 2]
                        S.op("dve", lambda e, mr=mr, kc=kc: e.tensor_scalar(out=mr, in0=scores[:, kc * CH:(kc + 1) * CH], scalar1=lo, scalar2=None, op0=ALU.is_ge),
                             reads=["scores", "lo"], writes=[f"mrow{kc % 2}"])
                        pt, ptk = next_ps(psT, "pt")
                        ptb = pt[:, :].bitcast(BF16)

                        def fn(e, mr=mr, ptb=ptb):
                            for b4 in range(CH // 128):
                                ins = e.transpose(out=ptb[:, b4 * 128:(b4 + 1) * 128], in_=mr[:, b4 * 128:(b4 + 1) * 128], identity=identb[:])
                            return ins
                        S.op("pe", fn, reads=[f"mrow{kc % 2}", "identb"], writes=[ptk])
                        S.op("act", lambda e, ptb=ptb, kc=kc: e.activation(out=maskT[:, kc * 4:(kc + 1) * 4, :],
                                                                          in_=ptb[:, :CH].rearrange("p (b q) -> p b q", q=128), func=AF.Identity),
                             reads=[ptk], writes=[km])
                    S.op("act", lambda e: e.activation(out=qsq, in_=qT, func=AF.Square), reads=[kq], writes=["qsq"])
                    S.op("dve", lambda e: e.memset(qn2, 0.0), writes=["qn2"])
                    for hf in range(2):
                        ps, pk = next_ps(psA, "ps")
                        mm_group(ps[:, :CH], pk, CH, [(onesb[:], qsq[:, hf * 4:(hf + 1) * 4, :].rearrange("p h q -> p (h q)"))], ["onesb", "qsq"])
                        S.op("dve", lambda e, ps=ps: e.tensor_reduce(out=cntt, in_=ps[:, :CH], axis=mybir.AxisListType.X, op=ALU.max), reads=[pk], writes=["cntt"])
                        S.op("dve", lambda e: e.tensor_tensor(out=qn2, in0=qn2, in1=cntt, op=ALU.max), reads=["qn2", "cntt"], writes=["qn2"])
                    S.op("dve", lambda e: e.tensor_tensor(out=negC, in0=qn2, in1=kn2, op=ALU.mult), reads=["qn2", "kn2"], writes=[kn])
                    S.op("act", lambda e: e.activation(out=negC, in_=negC, func=AF.Sqrt), reads=[kn], writes=[kn])
                    S.op("dve", lambda e: e.tensor_scalar(out=negC, in0=negC, scalar1=-1.0, scalar2=None, op0=ALU.mult), reads=[kn], writes=[kn])
                def att_C(qb):
                    bq = qb % 2
                    qT, maskT, negC = qT2[bq], maskT2[bq], negC2[bq]
                    kq, km, kn = f"qT{bq}", f"maskT{bq}", f"negC{bq}"
                    t0 = qb * 128
                    nkb = (NBLK - 1) + qb
                    nch = -(-nkb // (CH // 128))
                    S_ = nch * CH
                    nblk_proc = nch * (CH // 128)
                    for kvh in range(2):
                        pso, psok = psB[0], "pb0"
                        pss, pssk = psB[1], "pb1"
                        for kc in range(nch):
                            i = load_kv_chunk(kc, ("kv",))
                            for b4 in range(CH // 128):
                                kb = kc * 4 + b4
                                pl, plk = next_ps(psA, "ps")
                                mm_group(pl[:, :512], plk, 512, [(KTc[i][:, kvh, b4 * 128:(b4 + 1) * 128],
                                                                   qT[:, kvh * 4:(kvh + 1) * 4, :].rearrange("p h q -> p (h q)"))], [f"KTc{i}", kq])
                                et = e_t[kb % 2]; pm = pmt[kb % 2]
                                S.op("act", lambda e, pl=pl, et=et: e.activation(out=et, in_=pl[:, :512].rearrange("p (h q) -> p h q", h=4), func=AF.Exp,
                                                                               bias=negC), reads=[plk, kn], writes=[f"e_t{kb % 2}"])
                                S.op("pool", lambda e, et=et, pm=pm, kb=kb: e.tensor_tensor(out=pm, in0=et, in1=maskT[:, kb:kb + 1, :].to_broadcast([128, 4, 128]),
                                                                                          op=ALU.mult), reads=[f"e_t{kb % 2}", km], writes=[f"pm{kb % 2}"])
                                first, last = (kb == 0), (kb == nblk_proc - 1)
                                pmf = pm.rearrange("p h q -> p (h q)")
                                S.op("pe", lambda e, i=i, b4=b4, pmf=pmf, first=first, last=last, kvh=kvh: e.matmul(
                                    pso[:, :512], lhsT=Vc[i][:, b4, kvh * 128:(kvh + 1) * 128], rhs=pmf, start=first, stop=last),
                                    reads=[f"Vc{i}", f"pm{kb % 2}"], writes=[psok])
                                S.op("pe", lambda e, pmf=pmf, first=first, last=last: e.matmul(pss[:, :512], lhsT=onesb[:], rhs=pmf, start=first, stop=last),
                                     reads=["onesb", f"pm{kb % 2}"], writes=[pssk])
                        S.op("dve", lambda e: e.tensor_scalar(out=rec, in0=pss[:, :512], scalar1=1e-30, scalar2=None, op0=ALU.max), reads=[pssk], writes=["rec"])
                        S.op("dve", lambda e: e.reciprocal(out=rec, in_=rec), reads=["rec"], writes=["rec"])
                        S.op("dve", lambda e, kvh=kvh: e.tensor_tensor(out=oT[:, kvh * 4:(kvh + 1) * 4, :], in0=pso[:, :512].rearrange("p (h q) -> p h q", h=4),
                                                                      in1=rec.rearrange("p (h q) -> p h q", h=4), op=ALU.mult), reads=[psok, "rec"], writes=["oT"])
                    for c in range(KC):
                        ps, pk = next_ps(psA, "ps")
                        linear_chunk(dsa_w_out[jd], KC, c * 128, oT, ["oT"], 128, ps, pk)
                        z_from_ps(ps, pk, c, t0, 128, 2)
                    postnorm(t0, 128, 0)

                for qb in range(NBLK):
                    att_A(qb)
                    if qb >= 1:
                        att_C(qb - 1)
                    att_B(qb)
                att_C(NBLK - 1)
            ffn_phase(li)

        def ffn_phase(li):
            new_phase()
            aT = [carve([128, 2 + WMAX]) for _ in range(2)]
            cvt = carve([128, WMAX]); gt = carve([128, WMAX]); guT = carve([128, FC, WMAX], BF16)
            for jc in range(3):
                slow_vec(cw[:, jc, :], vec_pk(ffn_conv_w[li, jc]))
            slow_vec(cb[:], vec_pk(ffn_conv_b[li]))
            S.op("pool", lambda e: e.memset(halo[:], 0.0), writes=["halo"])
            if SMP:
                aS = carve([128, FC, NSMP]); uS = carve([128, FC, NSMP]); cvS = carve([128, FC, NSMP]); t2S = carve([128, FC, NSMP])
                Pst = carve([128, FC, 2 * NSMP])
                sstg = carve([128, DFF])
                S.dma(lambda e: e.dma_start(out=sstg[:2 * NSMP, :], in_=sconv[li].rearrange("s j f -> (s j) f")), writes=["sstg"])
                for c0 in range(0, FC, 4):
                    pt, ptk = next_ps(psT, "pt")
                    cs = list(range(c0, min(c0 + 4, FC)))

                    def fnp(e, cs=cs, pt=pt):
                        for c in cs:
                            ins = e.transpose(out=pt[:, (c - cs[0]) * 128:(c - cs[0]) * 128 + 2 * NSMP], in_=sstg[:2 * NSMP, c * 128:(c + 1) * 128],
                                              identity=ident[:2 * NSMP, :2 * NSMP])
                        return ins
                    S.op("pe", fnp, reads=["sstg", "ident"], writes=[ptk])
                    for c in cs:
                        S.op("dve", lambda e, c=c, cs=cs, pt=pt: e.tensor_copy(out=Pst[:, c, :], in_=pt[:, (c - cs[0]) * 128:(c - cs[0]) * 128 + 2 * NSMP]),
                             reads=[ptk], writes=["Pst"])
                S.dma(lambda e: e.dma_start(out=convs[li, :, 0, :], in_=sconv[li, :, 1, :]))

            def ffn_tile(ti, t0, W):
                smp = SMP and ti == 0
                Wx = W + (NSMP if smp else 0)
                modulate(t0, W, 3, 4)
                if smp:
                    modulate_s(3, 4)
                for fc in range(FC):
                    pa, pak = next_ps(psA, "ps")
                    linear_chunk(ffn_w_up[li], KC, fc * 128, hT, ["hT"], Wx, pa, pak)
                    pu, puk = next_ps(psB, "pb")
                    linear_chunk(ffn_w_up[li], KC, DFF + fc * 128, hT, ["hT"], Wx, pu, puk)
                    if smp:
                        S.op("act", lambda e, pa=pa, fc=fc: e.activation(out=aS[:, fc, :], in_=pa[:, 128:128 + NSMP], func=AF.Identity), reads=[pak], writes=["aS"])
                        S.op("act", lambda e, pu=pu, fc=fc: e.activation(out=uS[:, fc, :], in_=pu[:, 128:128 + NSMP], func=AF.Identity), reads=[puk], writes=["uS"])
                    ai = fc % 2
                    a_t = aT[ai]
                    S.op("pool", lambda e, a_t=a_t, fc=fc: e.tensor_copy(out=a_t[:, 0:2], in_=halo[:, fc, :]), reads=["halo"], writes=[f"aT{ai}"])
                    S.op("act", lambda e, a_t=a_t, pa=pa: e.activation(out=a_t[:, 2:2 + W], in_=pa[:, :W], func=AF.Identity),
                         reads=[pak], writes=[f"aT{ai}"])
                    S.op("pool", lambda e, a_t=a_t, fc=fc: e.tensor_copy(out=halo[:, fc, :], in_=a_t[:, W:W + 2]), reads=[f"aT{ai}"], writes=["halo"])
                    S.op("dve", lambda e, a_t=a_t, fc=fc: e.tensor_scalar(out=cvt[:, :W], in0=a_t[:, 0:W], scalar1=cw[:, 0, fc:fc + 1],
                                                                          scalar2=cb[:, fc:fc + 1], op0=ALU.mult, op1=ALU.add),
                         reads=[f"aT{ai}", "cw", "cb"], writes=["cvt"])
                    for jc in (1, 2):
                        S.op("dve", lambda e, a_t=a_t, fc=fc, jc=jc: e.scalar_tensor_tensor(
                            out=cvt[:, :W], in0=a_t[:, jc:jc + W], scalar=cw[:, jc, fc:fc + 1], in1=cvt[:, :W], op0=ALU.mult, op1=ALU.add),
                            reads=[f"aT{ai}", "cw", "cvt"], writes=["cvt"])
                    S.op("act", lambda e: e.activation(out=gt[:, :W], in_=cvt[:, :W], func=AF.Gelu_apprx_tanh), reads=["cvt"], writes=["gt"])
                    S.op("dve", lambda e, pu=pu, fc=fc: e.tensor_tensor(out=guT[:, fc, :W], in0=gt[:, :W], in1=pu[:, :W], op=ALU.mult),
                         reads=["gt", puk], writes=["guT"])
                if ti == 0:
                    S.op("dve", lambda e: e.tensor_scalar(out=halo[:], in0=halo[:], scalar1=rolet[:, 1:2], scalar2=None, op0=ALU.mult),
                         reads=["halo", "rolet"], writes=["halo"])
                if smp:
                    Pv = Pst.rearrange("p f (s j) -> p f s j", j=2)
                    bc = lambda col: col.unsqueeze(2).to_broadcast([128, FC, NSMP])
                    S.op("dve", lambda e: e.tensor_tensor(out=cvS, in0=Pv[:, :, :, 0], in1=bc(cw[:, 0, :]), op=ALU.mult), reads=["Pst", "cw"], writes=["cvS"])
                    S.op("dve", lambda e: e.tensor_tensor(out=t2S, in0=Pv[:, :, :, 1], in1=bc(cw[:, 1, :]), op=ALU.mult), reads=["Pst", "cw"], writes=["t2S"])
                    S.op("dve", lambda e: e.tensor_tensor(out=cvS, in0=cvS, in1=t2S, op=ALU.add), reads=["cvS", "t2S"], writes=["cvS"])
                    S.op("dve", lambda e: e.tensor_tensor(out=t2S, in0=aS, in1=bc(cw[:, 2, :]), op=ALU.mult), reads=["aS", "cw"], writes=["t2S"])
                    S.op("dve", lambda e: e.tensor_tensor(out=cvS, in0=cvS, in1=t2S, op=ALU.add), reads=["cvS", "t2S"], writes=["cvS"])
                    S.op("dve", lambda e: e.tensor_tensor(out=cvS, in0=cvS, in1=bc(cb[:, :]), op=ALU.add), reads=["cvS", "cb"], writes=["cvS"])
                    S.op("act", lambda e: e.activation(out=cvS, in_=cvS, func=AF.Gelu_apprx_tanh), reads=["cvS"], writes=["cvS"])
                    S.op("dve", lambda e: e.tensor_tensor(out=guT[:, :, 128:128 + NSMP], in0=cvS, in1=uS, op=ALU.mult), reads=["cvS", "uS"], writes=["guT"])
                    for c0 in range(0, FC, 4):
                        pt, ptk = next_ps(psT, "pt")
                        cs = list(range(c0, min(c0 + 4, FC)))

                        def fna(e, cs=cs, pt=pt):
                            for c in cs:
                                ins = e.transpose(out=pt[:NSMP, (c - cs[0]) * 128:(c - cs[0] + 1) * 128], in_=aS[:, c, :], identity=ident[:])
                            return ins
                        S.op("pe", fna, reads=["aS", "ident"], writes=[ptk])
                        S.op("dve", lambda e, cs=cs, pt=pt: e.tensor_copy(out=sstg[:NSMP, cs[0] * 128:(cs[-1] + 1) * 128], in_=pt[:NSMP, :len(cs) * 128]),
                             reads=[ptk], writes=["sstg"])
                    S.dma(lambda e: e.dma_start(out=convs[li, :, 1, :], in_=sstg[:NSMP, :]), reads=["sstg"])
                for c in range(KC):
                    ps, pk = next_ps(psA, "ps")
                    linear_chunk(ffn_w_down[li], FC, c * 128, guT, ["guT"], Wx, ps, pk)
                    z_from_ps(ps, pk, c, t0, W, 5)
                    if smp:
                        zs_from_ps(ps, pk, c, 5)
                postnorm(t0, W, 1)
                if smp:
                    postnorm(0, NSMP, 1, samples=True)
            for ti, (t0, W) in enumerate(tiles):
                ffn_tile(ti, t0, W)
            for jr in range(2):
                S.dma(lambda e, li=li, jr=jr: e.dma_start(out=vec_pk(convp[li, jr]), in_=halo[:, :, jr],
                                                          allow_slow_non_contiguous=True), reads=["halo"])

        for li in range(n_layers):
            do_layer(li)
        new_phase()
        if SMP:
            for c0 in (0, 4):
                pt, ptk = next_ps(psT, "pt")

                def fny(e, c0=c0, pt=pt):
                    for c in range(c0, c0 + 4):
                        ins = e.transpose(out=pt[:NSMP, (c - c0) * 128:(c - c0 + 1) * 128], in_=xsT[:, c, :], identity=ident[:])
                    return ins
                S.op("pe", fny, reads=["xs", "ident"], writes=[ptk])
                S.op("dve", lambda e, c0=c0, pt=pt: e.tensor_copy(out=tokt[:NSMP, c0 * 128:(c0 + 4) * 128], in_=pt[:NSMP, :512]), reads=[ptk], writes=["tokt"])
            S.dma(lambda e: e.dma_start(out=ys[:, :], in_=tokt[:NSMP, :]), reads=["tokt"])
        for b in range(NBLK):
            tk = [tt for tt, ww in tiles if tt <= b * 128 < tt + ww][0]
            for c0 in range(0, KC, 4):
                pt, ptk = next_ps(psT, "pt")

                def fn(e, b=b, c0=c0, pt=pt):
                    for c in range(c0, c0 + 4):
                        ins = e.transpose(out=pt[:, (c - c0) * 128:(c - c0 + 1) * 128], in_=xres[:, c, b * 128:(b + 1) * 128], identity=ident[:])
                    return ins
                S.op("pe", fn, reads=[f"x{tk}", "ident"], writes=[ptk])
                S.op("dve", lambda e, c0=c0, pt=pt: e.tensor_copy(out=tokt[:, c0 * 128:(c0 + 4) * 128], in_=pt[:, :512]), reads=[ptk], writes=["tokt"])
            S.dma(lambda e, b=b: e.dma_start(out=y_loc[b * 128:(b + 1) * 128, :], in_=tokt[:]), reads=["tokt"])

        S.emit(st)
    return nc


_WEIGHT_KEYS = ["w_ada", "b_ada", "ln_g", "ln_b", "sgu_w_in", "sgu_b_in", "sgu_norm_g", "sgu_norm_b", "sgu_w_s", "sgu_b_s",
                "sgu_w_out", "dsa_w_in", "dsa_w_out", "ffn_w_up", "ffn_conv_w", "ffn_conv_b", "ffn_w_down"]
_NC_CACHE = {}


def kernel(**inp):
    f32 = lambda a: np.ascontiguousarray(np.asarray(a), dtype=np.float32)
    x_prompt = f32(inp["x_prompt"]); c_prompt = f32(inp["c_prompt"]); c_sample = f32(inp["c_sample"])
    B, T, _ = x_prompt.shape
    half = T // 2
    n_cores = 2 * B
    if "nc" not in _NC_CACHE:
        _NC_CACHE["nc"] = build_program(NBLK=1 + half // 128, n_layers=DEPTH)
    nc = _NC_CACHE["nc"]
    weights = {k: f32(inp[k]) for k in _WEIGHT_KEYS}
    x_sample = f32(inp["x_sample"]); state_conv = f32(inp["state_conv"])
    page_table = np.ascontiguousarray(np.asarray(inp["page_table"]), dtype=np.int32)
    ck_, cv_, ci_ = f32(inp["cache_k"]), f32(inp["cache_v"]), f32(inp["cache_kidx"])
    shared = {}
    for i in range(2):
        shared[f"cache_k{i}"] = ck_[i].reshape(-1, 256); shared[f"cache_v{i}"] = cv_[i].reshape(-1, 256); shared[f"cache_ki{i}"] = ci_[i].reshape(-1, 64)
    in_maps = []
    for c in range(n_cores):
        seq, role = c // 2, c % 2
        if role == 0:
            xloc = np.concatenate([np.zeros((128, D), np.float32), x_prompt[seq, :half]], axis=0)
        else:
            xloc = x_prompt[seq, half - 128:]
        m = dict(weights)
        m.update(shared)
        sl_s = slice(NSMP * c, NSMP * (c + 1))
        m["xs_in"] = np.ascontiguousarray(x_sample[sl_s, 0, :])
        m["sconv"] = np.ascontiguousarray(state_conv[:, sl_s])
        m["ptab"] = np.ascontiguousarray(page_table[sl_s])
        m["xloc"] = np.ascontiguousarray(xloc)
        m["call"] = np.ascontiguousarray(np.concatenate([c_prompt[seq:seq + 1], c_sample[NSMP * c:NSMP * (c + 1)]], axis=0))
        m["role"] = np.tile(np.array([[float(half * role), float(role)]], np.float32), (128, 1))
        in_maps.append(m)
    res = run_bass_kernel_spmd(nc, in_maps, core_ids=list(range(n_cores))).results

    y_prompt = np.zeros((B, T, D), np.float32)
    new_conv_prompt = np.zeros((DEPTH, B, 2, DFF), np.float32)
    new_k_prompt = np.zeros((2, B, T, 2, 128), np.float32); new_v_prompt = np.zeros((2, B, T, 2, 128), np.float32)
    new_kidx_prompt = np.zeros((2, B, T, 64), np.float32)
    for c in range(n_cores):
        seq, role = c // 2, c % 2
        sl = slice(role * half, (role + 1) * half)
        y_prompt[seq, sl] = res[c]["y_loc"][128:]
        new_k_prompt[:, seq, sl] = res[c]["knew"].reshape(2, half, 2, 128)
        new_v_prompt[:, seq, sl] = res[c]["vnew"].reshape(2, half, 2, 128)
        new_kidx_prompt[:, seq, sl] = res[c]["kinew"]
        if role == 1:
            new_conv_prompt[:, seq] = res[c]["convp"]
    DB = c_sample.shape[0]
    y_sample = np.zeros((DB, 1, D), np.float32)
    new_k_sample = np.zeros((2, DB, 1, 2, 128), np.float32); new_v_sample = np.zeros((2, DB, 1, 2, 128), np.float32)
    new_kidx_sample = np.zeros((2, DB, 1, 64), np.float32)
    new_sgu_v_sample = np.zeros((2, DB, 1, D), np.float32)
    new_conv_sample = np.zeros((DEPTH, DB, 2, DFF), np.float32)
    for c in range(n_cores):
        sl_s = slice(NSMP * c, NSMP * (c + 1))
        y_sample[sl_s, 0] = res[c]["ys"]
        new_k_sample[:, sl_s, 0] = res[c]["ksn"].reshape(2, NSMP, 2, 128)
        new_v_sample[:, sl_s, 0] = res[c]["vsn"].reshape(2, NSMP, 2, 128)
        new_kidx_sample[:, sl_s, 0] = res[c]["kisn"]
        new_sgu_v_sample[:, sl_s, 0] = res[c]["sguv"]
        new_conv_sample[:, sl_s] = res[c]["convs"]
    return (y_prompt, y_sample, new_k_prompt, new_v_prompt, new_kidx_prompt, new_k_sample, new_v_sample, new_kidx_sample,
            new_sgu_v_sample, new_conv_prompt, new_conv_sample)


def extra_sample_inputs(d):
    ck, cv, ci = d["cache_k"], d["cache_v"], d["cache_ki"]
    out = {"ptab": np.ascontiguousarray(d["pt"].astype(np.int32))}
    for i in range(2):
        out[f"cache_k{i}"] = np.ascontiguousarray(ck[i].reshape(-1, 256)); out[f"cache_v{i}"] = np.ascontiguousarray(cv[i].reshape(-1, 256))
        out[f"cache_ki{i}"] = np.ascontiguousarray(ci[i].reshape(-1, 64))
    return out
```

```python
import contextlib
import numpy as np
import concourse.bass as bass
import concourse.mybir as mybir
from concourse.bass_utils import run_bass_kernel_spmd

F32 = mybir.dt.float32
BF16 = mybir.dt.bfloat16
I32 = mybir.dt.int32
AF = mybir.ActivationFunctionType
ALU = mybir.AluOpType

D = 1024
KC = 8
DFF = 2816
FC = 22
DEPTH = 4
ALPHA = (2 * DEPTH) ** 0.25
LN_EPS = 1e-5
EPS_A = LN_EPS / (ALPHA * ALPHA)
NSMP = 16
N_CORES = 8

ENG_ATTR = {"pe": "tensor", "act": "scalar", "dve": "vector", "pool": "gpsimd", "sp": "sync"}


class Sched:
    def __init__(self, nc, n_dma_ch=10, same_engine_wait=True):
        self.nc = nc
        self.engs = list(ENG_ATTR)
        self.ops = []
        self.last_w = {}
        self.readers = {}
        self.n_dma_ch = n_dma_ch
        self.same_engine_wait = same_engine_wait
        self.ch_next = {e: 0 for e in self.engs}
        self.ch_last = {e: [None] * n_dma_ch for e in self.engs}
        self.last_op = {e: None for e in self.engs}
        self.pending_bar = {e: set() for e in self.engs}
        self.cc_eng = "pool"

    def cc(self, fn, reads=(), writes=()):
        return self.op("pool", fn, reads, writes, dma=True, cc=True)

    def _needs_wait(self, prod, cons_eng):
        if prod["dma"]:
            return True
        if prod["eng"] != cons_eng:
            return True
        if cons_eng == "pe":
            return False
        return self.same_engine_wait

    def op(self, eng, fn, reads=(), writes=(), dma=False, cc=False):
        idx = len(self.ops)
        deps = set()
        for k in list(reads) + list(writes):
            if k in self.last_w:
                deps.add(self.last_w[k])
        for k in writes:
            deps.update(self.readers.get(k, ()))
        if self.pending_bar[eng]:
            deps.update(self.pending_bar[eng])
            self.pending_bar[eng] = set()
        o = dict(eng=eng, fn=fn, deps=deps, dma=dma, signal=False, ch=None, inc=(1 if cc else 16))
        if dma:
            if cc:
                c = self.n_dma_ch - 1
            else:
                nfree = self.n_dma_ch - (1 if self.cc_eng == eng else 0)
                c = self.ch_next[eng]
                self.ch_next[eng] = (c + 1) % nfree
            o["ch"] = c
            o["prev"] = self.ch_last[eng][c]
            self.ch_last[eng][c] = idx
            o["signal"] = True
        else:
            self.last_op[eng] = idx
        self.ops.append(o)
        for k in reads:
            self.readers.setdefault(k, []).append(idx)
        for k in writes:
            self.last_w[k] = idx
            self.readers[k] = []
        return idx

    def dma(self, fn, reads=(), writes=(), q="sp"):
        return self.op(q, fn, reads, writes, dma=True)

    def barrier(self):
        b = set()
        for e in self.engs:
            if self.last_op[e] is not None:
                b.add(self.last_op[e])
            for c in self.ch_last[e]:
                if c is not None:
                    b.add(c)
        for e in self.engs:
            self.pending_bar[e] = set(b)

    def emit(self, stack):
        nc, ops = self.nc, self.ops
        for o in ops:
            for d in o["deps"]:
                if self._needs_wait(ops[d], o["eng"]):
                    ops[d]["signal"] = True
        used = [e for e in self.engs if any(o["eng"] == e for o in ops)]
        sems = {e: stack.enter_context(nc.semaphore(f"sem_{e}")) for e in used}
        chs = {}
        for e in used:
            if any(o["dma"] and o["eng"] == e for o in ops):
                chs[e] = [stack.enter_context(nc.semaphore(f"dch_{e}_{i}")) for i in range(self.n_dma_ch)]
        ticket = {e: 0 for e in used}
        chcnt = {e: [0] * self.n_dma_ch for e in used}
        for o in ops:
            e = o["eng"]
            if o["dma"]:
                chcnt[e][o["ch"]] += o["inc"]
                o["sig"] = (f"dch_{e}_{o['ch']}", chs[e][o["ch"]], chcnt[e][o["ch"]])
            elif o["signal"]:
                ticket[e] += 1
                o["sig"] = (f"sem_{e}", sems[e], ticket[e])
            else:
                o["sig"] = None
        per_eng = {e: [i for i, o in enumerate(ops) if o["eng"] == e] for e in used}
        waited = {e: {} for e in used}
        self.n_wait = 0
        block = stack.enter_context(nc.Block())

        def make(e):
            def body(engh):
                for i in per_eng[e]:
                    o = ops[i]
                    need = {}
                    dl = set(o["deps"])
                    if o["dma"] and o["prev"] is not None:
                        dl.add(o["prev"])
                    for d in dl:
                        p = ops[d]
                        if not (o["dma"] and d == o.get("prev")) and not self._needs_wait(p, e):
                            continue
                        name, sem, val = p["sig"]
                        if need.get(name, (None, 0))[1] < val:
                            need[name] = (sem, val)
                    for name, (sem, val) in need.items():
                        if waited[e].get(name, 0) < val:
                            engh.wait_ge(sem, val)
                            waited[e][name] = val
                            self.n_wait += 1
                    ins = o["fn"](engh)
                    if o["sig"] is not None:
                        if o["dma"] and o["inc"] == 1:
                            ins.then_inc(o["sig"][1])
                        else:
                            ins.then_inc(o["sig"][1], 16 if o["dma"] else 1)
                if e in chs:
                    for c in range(self.n_dma_ch):
                        if chcnt[e][c] > 0:
                            engh.wait_ge(chs[e][c], chcnt[e][c])
            return body

        for e in used:
            getattr(block, ENG_ATTR[e])(make(e))


def vec_pk(ap1d, p=128):
    return ap1d.rearrange("(kc p) -> p kc", p=p)


def build_program(NBLK=17, n_layers=4, with_samples=True, KEEP=256, NIT=20, n_cores=N_CORES, dbg_stop=99, NPHYS=2560, NITS=22):
    NT = NBLK * 128
    HALF = NT - 128
    NKEY = 2 * HALF
    CH = 512
    BIG = 30000.0
    U8 = mybir.dt.uint8
    tiles = [(0, 128)]
    t = 128
    while t < NT:
        w = min(256, NT - t)
        tiles.append((t, w))
        t += w
    WMAX = max(max(w for _, w in tiles), 128 + NSMP)

    nc = bass.Bass("TRN2", target_bir_lowering=False)
    dt_in = lambda name, shape, dt=F32: nc.dram_tensor(name, list(shape), dt, kind="ExternalInput").ap()
    dt_out = lambda name, shape, dt=F32: nc.dram_tensor(name, list(shape), dt, kind="ExternalOutput").ap()

    xloc = dt_in("xloc", [NT, D])
    call = dt_in("call", [1 + NSMP, D])
    role = dt_in("role", [128, 2])
    w_ada = dt_in("w_ada", [DEPTH, D, 6 * D]); b_ada = dt_in("b_ada", [DEPTH, 6 * D])
    ln_g = dt_in("ln_g", [DEPTH, 2, D]); ln_b = dt_in("ln_b", [DEPTH, 2, D])
    sgu_w_in = dt_in("sgu_w_in", [2, D, 2 * D]); sgu_b_in = dt_in("sgu_b_in", [2, 2 * D])
    sgu_norm_g = dt_in("sgu_norm_g", [2, D]); sgu_norm_b = dt_in("sgu_norm_b", [2, D])
    sgu_w_s = dt_in("sgu_w_s", [2, 8, 128, 128]); sgu_b_s = dt_in("sgu_b_s", [2, 8, 128])
    sgu_w_out = dt_in("sgu_w_out", [2, D, D])
    ffn_w_up = dt_in("ffn_w_up", [DEPTH, D, 2 * DFF]); ffn_conv_w = dt_in("ffn_conv_w", [DEPTH, 3, DFF])
    ffn_conv_b = dt_in("ffn_conv_b", [DEPTH, DFF]); ffn_w_down = dt_in("ffn_w_down", [DEPTH, DFF, D])

    dsa_w_in = dt_in("dsa_w_in", [2, D, 2120]); dsa_w_out = dt_in("dsa_w_out", [2, D, D])
    knew = dt_out("knew", [2, HALF, 256]); vnew = dt_out("vnew", [2, HALF, 256]); kinew = dt_out("kinew", [2, HALF, 64])
    VSEG = min(2 * HALF, 2048)
    NVS = (2 * HALF) // VSEG
    SEGW = [HALF, HALF] + [VSEG] * NVS + [HALF]
    NSEG = len(SEGW)
    bounce = [[nc.dram_tensor(f"bounce{i}_{g}", [128, w], BF16, kind="Internal").ap() for g, w in enumerate(SEGW)] for i in range(2)]
    gath = [[nc.dram_tensor(f"gath{i}_{g}", [256, w], BF16, kind="Internal").ap() for g, w in enumerate(SEGW)] for i in range(2)]
    SMP = with_samples
    WS = NSMP if SMP else 0
    xs_in = dt_in("xs_in", [NSMP, D]); sconv = dt_in("sconv", [DEPTH, NSMP, 2, DFF])
    NPG = 16
    KEEP_S = 256
    cache_k = [dt_in(f"cache_k{i}", [NPHYS * 128, 256]) for i in range(2)]; cache_v = [dt_in(f"cache_v{i}", [NPHYS * 128, 256]) for i in range(2)]
    cache_ki = [dt_in(f"cache_ki{i}", [NPHYS * 128, 64]) for i in range(2)]; ptab = dt_in("ptab", [NSMP, NPG], I32)
    ksn = dt_out("ksn", [2, NSMP, 256]); vsn = dt_out("vsn", [2, NSMP, 256]); kisn = dt_out("kisn", [2, NSMP, 64])
    ys = dt_out("ys", [NSMP, D]); sguv = dt_out("sguv", [2, NSMP, D]); convs = dt_out("convs", [DEPTH, NSMP, 2, DFF])
    wup_s = nc.dram_tensor("wup_s", [DEPTH, 2 * FC, 128, KC * 128], BF16, kind="Internal").ap()
    wdn_s = nc.dram_tensor("wdn_s", [DEPTH, KC, 2, 128, 11 * 128], BF16, kind="Internal").ap()
    y_loc = dt_out("y_loc", [NT, D])
    convp = dt_out("convp", [DEPTH, 2, DFF])

    st = contextlib.ExitStack()
    with st:
        SB = lambda n, s, d=F32: st.enter_context(nc.sbuf_tensor(n, list(s), d))
        PS = lambda n: st.enter_context(nc.psum_tensor(n, [128, 512], F32))
        S = Sched(nc)

        xres = SB("xres", [128, KC, NT])
        ident = SB("ident", [128, 128]); ones_f = SB("ones_f", [128, 128])
        tri01 = SB("tri01", [128, 128])
        rolet = SB("rolet", [128, 2])
        cT = SB("cT", [128, KC, 1 + NSMP])
        modp = SB("modp", [128, 48])
        modp1 = SB("modp1", [128, 48])
        lng = SB("lng", [128, 2, KC]); lnb = SB("lnb", [128, 2, KC])
        xsT = SB("xsT", [128, KC, NSMP]); zs = SB("zs", [128, KC, NSMP]); mods1 = SB("mods1", [128, 48, NSMP])
        KTn = SB("KTn", [128, 2, NSMP], BF16); kiTn = SB("kiTn", [128, NSMP], BF16); Vn = SB("Vn", [NSMP, 256], BF16)
        idx_all = SB("idx_all", [128, NSMP * NPG], I32); piota_p = SB("piota_p", [128, 1])
        vTs = SB("vTs", [128, KC, NSMP]); w00c = SB("w00c", [128, 8]); bs0c = SB("bs0c", [128, 8]); tmp16 = SB("tmp16", [128, KC, NSMP])
        hT = SB("hT", [128, KC, WMAX], BF16)
        z = SB("z", [128, KC, WMAX])
        zsq = [SB(f"zsq{i}", [128, WMAX]) for i in range(2)]
        m_t = SB("m_t", [128, WMAX]); v_t = SB("v_t", [128, WMAX]); r_t = SB("r_t", [128, WMAX])
        NWB, WK = 3, 11
        wst = [SB(f"wst{i}", [128, WK, 128]) for i in range(NWB)]
        wbf = [SB(f"wbf{i}", [128, WK, 128], BF16) for i in range(NWB)]
        tokt = SB("tokt", [128, D])
        psA = [PS(f"psA{i}") for i in range(2)]
        psB = [PS(f"psB{i}") for i in range(2)]
        psT = [PS(f"psT{i}") for i in range(2)]
        psS = [PS(f"psS{i}") for i in range(2)]

        cnt = {"w": 0, "ps": 0, "pb": 0, "pt": 0, "zs": 0, "pq": 0}
        PAIRS = [[2 * i, 2 * i + 1] for i in range(n_cores // 2)]

        def slow_vec(dst, src):
            S.dma(lambda e: e.dma_start(out=dst, in_=src, allow_slow_non_contiguous=True), writes=[dst.tensor.name])

        def load_wchunk(w_ap2d, kcn, c0, ncol=128, k0=0):
            i = cnt["w"] % NWB
            cnt["w"] += 1
            src = w_ap2d[k0 * 128:(k0 + kcn) * 128, c0:c0 + ncol].rearrange("(kc p) n -> p kc n", p=128)
            S.dma(lambda e: e.dma_start(out=wst[i][:, :kcn, :ncol], in_=src), writes=[f"wst{i}"])
            if cnt["w"] % 2 == 0:
                S.op("act", lambda e: e.activation(out=wbf[i][:, :kcn, :ncol], in_=wst[i][:, :kcn, :ncol], func=AF.Identity),
                     reads=[f"wst{i}"], writes=[f"wbf{i}"])
            else:
                S.op("dve", lambda e: e.tensor_copy(out=wbf[i][:, :kcn, :ncol], in_=wst[i][:, :kcn, :ncol]),
                     reads=[f"wst{i}"], writes=[f"wbf{i}"])
            return wbf[i], f"wbf{i}"

        def mm_group(ps, pskey, W, pairs, reads):
            def fn(e):
                n = len(pairs)
                for j, (l, r) in enumerate(pairs):
                    ins = e.matmul(ps, lhsT=l, rhs=r, start=(j == 0), stop=(j == n - 1))
                return ins
            S.op("pe", fn, reads=reads, writes=[pskey])

        def linear_chunk(w_ap2d, kcn, c0, src, srckeys, W, ps, pskey, off=0):
            pairs, keys = [], []
            for k0 in range(0, kcn, WK):
                kn = min(WK, kcn - k0)
                wt, wkey = load_wchunk(w_ap2d, kn, c0, k0=k0)
                pairs += [(wt[:, k, :], src[:, k0 + k, off:off + W]) for k in range(kn)]
                keys.append(wkey)
            mm_group(ps[:, :W], pskey, W, pairs, keys + srckeys)

        def load_schunk(src2d, kn, skey):
            i = cnt["w"] % NWB
            cnt["w"] += 1
            S.dma(lambda e: e.dma_start(out=wbf[i][:, :kn, :], in_=src2d.rearrange("p (k n) -> p k n", k=kn)), reads=[skey], writes=[f"wbf{i}"])
            return wbf[i], f"wbf{i}"

        def linear_schunk(blocks, skey, src, srckeys, W, ps, pskey):
            pairs, keys = [], []
            for blk_ap, kn, k0 in blocks:
                wt, wkey = load_schunk(blk_ap, kn, skey)
                pairs += [(wt[:, k, :], src[:, k0 + k, :W]) for k in range(kn)]
                keys.append(wkey)
            mm_group(ps[:, :W], pskey, W, pairs, keys + srckeys)

        def next_ps(lst, name):
            i = cnt[name] % 2
            cnt[name] += 1
            return lst[i], "%s%d" % ({"pq": "psS"}.get(name, name), i)

        def tile_key(t):
            return "x%d" % [tt for tt, ww in tiles if tt <= t < tt + ww][0]

        def modulate(t0, W, jshift, jscale):
            xk = tile_key(t0)
            for k in range(KC):
                S.op("act", lambda e, k=k: e.activation(out=hT[:, k, :W], in_=xres[:, k, t0:t0 + W], func=AF.Identity,
                                                        scale=modp1[:, jscale * 8 + k:jscale * 8 + k + 1],
                                                        bias=modp[:, jshift * 8 + k:jshift * 8 + k + 1]),
                     reads=[xk, "modp", "modp1"], writes=["hT"])

        def postnorm(t0, W, sub, samples=False):
            if samples:
                return _postnorm(zs, "zs", NSMP, sub, lambda k: xsT[:, k, :], "xs")
            return _postnorm(z, "z", W, sub, lambda k: xres[:, k, t0:t0 + W], tile_key(t0))

        def _postnorm(z, zk, W, sub, xout, xk):
            s1, s1k = psS[0], "psS0"
            s2, s2k = psS[1], "psS1"
            for k in range(KC):
                i = cnt["zs"] % 2
                cnt["zs"] += 1
                S.op("act", lambda e, k=k, i=i: e.activation(out=zsq[i][:, :W], in_=z[:, k, :W], func=AF.Square),
                     reads=[zk], writes=[f"zsq{i}"])
                S.op("pe", lambda e, k=k: e.matmul(s1[:, :W], lhsT=ones_f[:], rhs=z[:, k, :W], start=(k == 0), stop=(k == KC - 1)),
                     reads=[zk, "ones_f"], writes=[s1k])
                S.op("pe", lambda e, k=k, i=i: e.matmul(s2[:, :W], lhsT=ones_f[:], rhs=zsq[i][:, :W], start=(k == 0), stop=(k == KC - 1)),
                     reads=[f"zsq{i}", "ones_f"], writes=[s2k])
            S.op("act", lambda e: e.activation(out=m_t[:, :W], in_=s1[:, :W], func=AF.Identity, scale=1.0 / D), reads=[s1k], writes=["m_t"])
            S.op("dve", lambda e: e.tensor_tensor(out=v_t[:, :W], in0=m_t[:, :W], in1=m_t[:, :W], op=ALU.mult), reads=["m_t"], writes=["v_t"])
            S.op("dve", lambda e: e.scalar_tensor_tensor(out=v_t[:, :W], in0=s2[:, :W], scalar=1.0 / D, in1=v_t[:, :W],
                                                         op0=ALU.mult, op1=ALU.subtract), reads=[s2k, "v_t"], writes=["v_t"])
            S.op("act", lambda e: e.activation(out=r_t[:, :W], in_=v_t[:, :W], func=AF.Sqrt, bias=epsA[:, 0:1]), reads=["v_t", "epsA"], writes=["r_t"])
            S.op("dve", lambda e: e.reciprocal(out=r_t[:, :W], in_=r_t[:, :W]), reads=["r_t"], writes=["r_t"])
            for k in range(KC):
                S.op("dve", lambda e, k=k: e.tensor_tensor(out=z[:, k, :W], in0=z[:, k, :W], in1=m_t[:, :W], op=ALU.subtract),
                     reads=[zk, "m_t"], writes=[zk])
                S.op("dve", lambda e, k=k: e.tensor_tensor(out=z[:, k, :W], in0=z[:, k, :W], in1=r_t[:, :W], op=ALU.mult),
                     reads=[zk, "r_t"], writes=[zk])
                S.op("act", lambda e, k=k: e.activation(out=xout(k), in_=z[:, k, :W], func=AF.Identity,
                                                        scale=lng[:, sub, k:k + 1], bias=lnb[:, sub, k:k + 1]),
                     reads=[zk, "lng", "lnb"], writes=[xk])

        def z_from_ps(ps, pskey, c, t0, W, jgate):
            S.op("dve", lambda e: e.scalar_tensor_tensor(out=z[:, c, :W], in0=ps[:, :W], scalar=modp1[:, jgate * 8 + c:jgate * 8 + c + 1],
                                                         in1=xres[:, c, t0:t0 + W], op0=ALU.mult, op1=ALU.add),
                 reads=[pskey, "modp1", tile_key(t0)], writes=["z"])

        def modulate_s(jshift, jscale):
            S.op("dve", lambda e: e.tensor_tensor(out=tmp16[:], in0=xsT[:], in1=mods1[:, jscale * 8:jscale * 8 + 8, :], op=ALU.mult),
                 reads=["xs", "mods1"], writes=["tmp16"])
            S.op("dve", lambda e: e.tensor_tensor(out=hT[:, :, 128:128 + NSMP], in0=tmp16[:], in1=modall[:, jshift * 8:jshift * 8 + 8, 1:1 + NSMP],
                                                  op=ALU.add), reads=["tmp16", "modall"], writes=["hT"])

        def zs_from_ps(ps, pskey, c, jgate, off=128):
            S.op("dve", lambda e: e.tensor_tensor(out=zs[:, c, :], in0=ps[:, off:off + NSMP], in1=mods1[:, jgate * 8 + c, :], op=ALU.mult),
                 reads=[pskey, "mods1"], writes=["zs"])
            S.op("dve", lambda e: e.tensor_tensor(out=zs[:, c, :], in0=zs[:, c, :], in1=xsT[:, c, :], op=ALU.add), reads=["zs", "xs"], writes=["zs"])

        epsA = SB("epsA", [128, 1]); epsL = SB("epsL", [128, 1]); onesb = SB("onesb", [128, 128], BF16)
        S.op("pool", lambda e: e.memset(epsA[:], EPS_A), writes=["epsA"])
        S.op("pool", lambda e: e.memset(epsL[:], LN_EPS), writes=["epsL"])
        S.op("pool", lambda e: e.memset(ones_f[:], 1.0), writes=["ones_f"])
        identb = SB("identb", [128, 128], BF16)
        S.op("pool", lambda e: e.memset(ident[:], 1.0), writes=["ident"])
        S.op("pool", lambda e: e.affine_select(out=ident[:], in_=ident[:], pattern=[[1, 128]], compare_op=ALU.is_equal,
                                               fill=0.0, base=0, channel_multiplier=-1), reads=["ident"], writes=["ident"])
        S.op("pool", lambda e: e.memset(tri01[:], 1.0), writes=["tri01"])
        S.op("pool", lambda e: e.affine_select(out=tri01[:], in_=tri01[:], pattern=[[1, 128]], compare_op=ALU.is_ge,
                                               fill=0.0, base=0, channel_multiplier=-1), reads=["tri01"], writes=["tri01"])
        S.dma(lambda e: e.dma_start(out=rolet[:], in_=role[:, :]), writes=["rolet"])
        S.op("pool", lambda e: e.tensor_copy(out=identb[:], in_=ident[:]), reads=["ident"], writes=["identb"])
        S.op("pool", lambda e: e.tensor_copy(out=onesb[:], in_=ones_f[:]), reads=["ones_f"], writes=["onesb"])

        def transpose_rows(src_tile, nrows, ncols_chunks, consume):
            for c0 in range(0, ncols_chunks, 4):
                pt, ptk = next_ps(psT, "pt")
                cs = list(range(c0, min(c0 + 4, ncols_chunks)))

                def fn(e, cs=cs, pt=pt):
                    for c in cs:
                        ins = e.transpose(out=pt[:, (c - cs[0]) * 128:(c - cs[0]) * 128 + nrows],
                                          in_=src_tile[:nrows, c * 128:(c + 1) * 128], identity=ident[:nrows, :nrows])
                    return ins
                S.op("pe", fn, reads=["tokt", "ident"], writes=[ptk])
                for c in cs:
                    consume(c, pt[:, (c - cs[0]) * 128:(c - cs[0]) * 128 + nrows], ptk)

        S.dma(lambda e: e.dma_start(out=tokt[:1 + NSMP, :], in_=call[:, :]), writes=["tokt"])
        transpose_rows(tokt, 1 + NSMP, KC,
                       lambda c, p, k: S.op("act", lambda e: e.activation(out=cT[:, c, :], in_=p, func=AF.Silu), reads=[k], writes=["cT"]))

        S.op("pool", lambda e: e.iota(piota_p[:], pattern=[[0, 1]], base=0, channel_multiplier=1, allow_small_or_imprecise_dtypes=True), writes=["piota_p"])
        if SMP:
            ptf = SB("ptf", [128, NSMP * NPG])
            S.dma(lambda e: e.dma_start(out=idx_all[:], in_=ptab.rearrange("s g -> (s g)").partition_broadcast(128)), writes=["idx_all"])
            S.op("dve", lambda e: e.tensor_copy(out=ptf[:], in_=idx_all[:]), reads=["idx_all"], writes=["ptf"])
            S.op("dve", lambda e: e.tensor_scalar(out=ptf[:], in0=ptf[:], scalar1=128.0, scalar2=piota_p[:, 0:1], op0=ALU.mult, op1=ALU.add),
                 reads=["ptf", "piota_p"], writes=["ptf"])
            S.op("dve", lambda e: e.tensor_copy(out=idx_all[:], in_=ptf[:]), reads=["ptf"], writes=["idx_all"])
            S.dma(lambda e: e.dma_start(out=tokt[:NSMP, :], in_=xs_in[:, :]), writes=["tokt"])
            transpose_rows(tokt, NSMP, KC,
                           lambda c, p, k: S.op("dve", lambda e: e.tensor_copy(out=xsT[:, c, :], in_=p), reads=[k], writes=["xs"]))
        for b in range(NBLK):
            S.dma(lambda e, b=b: e.dma_start(out=tokt[:], in_=xloc[b * 128:(b + 1) * 128, :]), writes=["tokt"])
            tk = [tt for tt, ww in tiles if tt <= b * 128 < tt + ww][0]
            transpose_rows(tokt, 128, KC,
                           lambda c, p, k, b=b, tk=tk: S.op("dve", lambda e: e.tensor_copy(out=xres[:, c, b * 128:(b + 1) * 128], in_=p),
                                                            reads=[k], writes=[f"x{tk}"]))

        bada_t = SB("bada_t", [128, 48])
        modall = SB("modall", [128, 48, 1 + NSMP])
        halo = SB("halo", [128, FC, 2])
        cw = SB("cw", [128, 3, FC]); cb = SB("cb", [128, FC])
        b_u = SB("b_u", [128, KC]); ngp = SB("ngp", [128, KC]); nbp = SB("nbp", [128, KC])
        bst = SB("bst", [128, 2, 6]); mv = SB("mv", [128, 2]); rs_t = SB("rs_t", [128, 1])

        ARENA_F32 = 19712
        arena = SB("arena", [128, ARENA_F32])
        ar = {"off": 0, "phase": 0}

        def new_phase():
            S.barrier()
            ar["off"] = 0
            ar["phase"] += 1

        def carve(shape, dt=F32):
            n = 1
            for d_ in shape[1:]:
                n *= d_
            nf = n if dt == F32 else (n + 1) // 2
            a = arena[:, ar["off"]:ar["off"] + nf]
            ar["off"] += nf
            assert ar["off"] <= ARENA_F32, ("arena overflow", ar["off"])
            if dt != F32:
                a = a.bitcast(dt)
            if len(shape) == 3:
                a = a.rearrange("p (a b) -> p a b", a=shape[1])
            return a

        def load_resident(dst, dkey, w_ap2d, c0, ncols):
            for cc in range(0, ncols, 128):
                wt, wkey = load_wchunk(w_ap2d, KC, c0 + cc)
                S.op("act", lambda e, cc=cc, wt=wt: e.activation(out=dst[:, :, cc:cc + 128], in_=wt[:, :KC, :], func=AF.Identity),
                     reads=[wkey], writes=[dkey])

        def do_layer(li):
            j = li // 2
            slow_vec(bada_t[:], vec_pk(b_ada[li]))
            slow_vec(lng[:, 0, :], vec_pk(ln_g[li, 0])); slow_vec(lng[:, 1, :], vec_pk(ln_g[li, 1]))
            slow_vec(lnb[:, 0, :], vec_pk(ln_b[li, 0])); slow_vec(lnb[:, 1, :], vec_pk(ln_b[li, 1]))
            for jj in range(48):
                i = cnt["w"] % NWB
                cnt["w"] += 1
                S.dma(lambda e, i=i, jj=jj: e.dma_start(out=wst[i][:, :KC, :],
                                                         in_=w_ada[li][:, jj * 128:(jj + 1) * 128].rearrange("(kc p) n -> p kc n", p=128)),
                      writes=[f"wst{i}"])
                ps, pk = next_ps(psA, "ps")
                mm_group(ps[:, :1 + NSMP], pk, 1 + NSMP, [(wst[i][:, k, :], cT[:, k, :]) for k in range(KC)], [f"wst{i}", "cT"])
                S.op("act", lambda e, ps=ps, jj=jj: e.activation(out=modall[:, jj, :], in_=ps[:, :1 + NSMP], func=AF.Identity,
                                                                 bias=bada_t[:, jj:jj + 1]), reads=[pk, "bada_t"], writes=["modall"])
            S.op("dve", lambda e: e.tensor_copy(out=modp[:], in_=modall[:, :, 0]), reads=["modall"], writes=["modp"])
            S.op("dve", lambda e: e.tensor_scalar(out=modp1[:], in0=modp[:], scalar1=1.0, scalar2=None, op0=ALU.add),
                 reads=["modp"], writes=["modp1"])
            for jg in (2, 5):
                S.op("dve", lambda e, jg=jg: e.tensor_scalar(out=modp1[:, jg * 8:jg * 8 + 8], in0=modp1[:, jg * 8:jg * 8 + 8],
                                                             scalar1=1.0 / ALPHA, scalar2=None, op0=ALU.mult),
                     reads=["modp1"], writes=["modp1"])

            if SMP:
                S.op("dve", lambda e: e.tensor_scalar(out=mods1[:], in0=modall[:, :, 1:1 + NSMP], scalar1=1.0, scalar2=None, op0=ALU.add),
                     reads=["modall"], writes=["mods1"])
                for jg in (2, 5):
                    S.op("dve", lambda e, jg=jg: e.tensor_scalar(out=mods1[:, jg * 8:jg * 8 + 8, :], in0=mods1[:, jg * 8:jg * 8 + 8, :],
                                                                 scalar1=1.0 / ALPHA, scalar2=None, op0=ALU.mult), reads=["mods1"], writes=["mods1"])
            if li % 2 == 0:
                new_phase()
                w_u = carve([128, KC, D], BF16); w_v = carve([128, KC, D], BF16); w_o = carve([128, KC, D], BF16)
                bvb = carve([128, D]); WsT = carve([128, 8, 128], BF16); C2 = carve([128, 8, 128])
                uT = carve([128, KC, WMAX]); umT = carve([128, KC, WMAX], BF16)
                vtok = carve([128, D]); vhat = carve([128, D], BF16); mix = carve([128, 128])
                load_resident(w_u, "w_u", sgu_w_in[j], 0, D)
                load_resident(w_v, "w_v", sgu_w_in[j], D, D)
                load_resident(w_o, "w_o", sgu_w_out[j], 0, D)
                slow_vec(b_u[:], vec_pk(sgu_b_in[j, 0:D]))
                slow_vec(ngp[:], vec_pk(sgu_norm_g[j])); slow_vec(nbp[:], vec_pk(sgu_norm_b[j]))
                S.dma(lambda e: e.dma_start(out=bvb, in_=sgu_b_in[j, D:2 * D].partition_broadcast(128)), writes=["bvb"])
                S.dma(lambda e: e.dma_start(out=C2, in_=sgu_b_s[j].partition_broadcast(128)), writes=["C2"])
                for g in range(8):
                    S.dma(lambda e, g=g: e.dma_start(out=tokt[:, g * 128:(g + 1) * 128], in_=sgu_w_s[j, g]), writes=["tokt"])
                transpose_rows(tokt, 128, 8,
                               lambda c, p, k: S.op("dve", lambda e: e.tensor_tensor(out=WsT[:, c, :], in0=p, in1=tri01[:], op=ALU.mult),
                                                    reads=[k, "tri01"], writes=["WsT"]))
                S.op("pool", lambda e: e.tensor_copy(out=onesb[:], in_=ones_f[:]), reads=["ones_f"], writes=["onesb"])
                for g in range(8):
                    ps, pk = next_ps(psA, "ps")
                    mm_group(ps[:, :128], pk, 128, [(onesb[:], WsT[:, g, :])], ["onesb", "WsT"])
                    S.op("dve", lambda e, g=g, ps=ps: e.scalar_tensor_tensor(out=C2[:, g, :], in0=ps[:, :128], scalar=nbp[:, g:g + 1],
                                                                             in1=C2[:, g, :], op0=ALU.mult, op1=ALU.add),
                         reads=[pk, "nbp", "C2"], writes=["C2"])
                if SMP:
                    S.dma(lambda e: e.dma_start(out=w00c[:], in_=sgu_w_s[j, :, 0, 0].partition_broadcast(128), allow_slow_non_contiguous=True), writes=["w00c"])
                    S.dma(lambda e: e.dma_start(out=bs0c[:], in_=sgu_b_s[j, :, 0].partition_broadcast(128), allow_slow_non_contiguous=True), writes=["bs0c"])

                def sgu_tile(t0, W):
                    smp = SMP and t0 == 0
                    Wx = W + (NSMP if smp else 0)
                    modulate(t0, W, 0, 1)
                    if smp:
                        modulate_s(0, 1)
                    for c in range(KC):
                        ps, pk = next_ps(psA, "ps")
                        mm_group(ps[:, :Wx], pk, Wx, [(w_u[:, k, c * 128:(c + 1) * 128], hT[:, k, :Wx]) for k in range(KC)], ["w_u", "hT"])
                        S.op("act", lambda e, c=c, ps=ps: e.activation(out=uT[:, c, :Wx], in_=ps[:, :Wx], func=AF.Gelu_apprx_tanh,
                                                                      bias=b_u[:, c:c + 1]), reads=[pk, "b_u"], writes=["uT"])
                    if smp:
                        for half in range(2):
                            ps, pk = next_ps(psB, "pb")
                            mm_group(ps[:NSMP, :512], pk, 512,
                                     [(hT[:, k, 128:128 + NSMP], w_v[:, k, half * 512:(half + 1) * 512]) for k in range(KC)], ["w_v", "hT"])
                            S.op("dve", lambda e, ps=ps, half=half: e.tensor_tensor(out=vtok[:NSMP, half * 512:(half + 1) * 512], in0=ps[:NSMP, :512],
                                                                                    in1=bvb[:NSMP, half * 512:(half + 1) * 512], op=ALU.add),
                                 reads=[pk, "bvb"], writes=["vtok"])
                        S.op("act", lambda e: e.activation(out=vtok[:NSMP, :], in_=vtok[:NSMP, :], func=AF.Gelu_apprx_tanh), reads=["vtok"], writes=["vtok"])
                        for half in range(2):
                            S.op("dve", lambda e, half=half: e.bn_stats(out=bst[:NSMP, half, :], in_=vtok[:NSMP, half * 512:(half + 1) * 512]),
                                 reads=["vtok"], writes=["bst"])
                        S.op("dve", lambda e: e.bn_aggr(out=mv[:NSMP], in_=bst[:NSMP]), reads=["bst"], writes=["mv"])
                        S.op("act", lambda e: e.activation(out=rs_t[:NSMP], in_=mv[:NSMP, 1:2], func=AF.Sqrt, bias=epsL[:NSMP, 0:1]),
                             reads=["mv", "epsL"], writes=["rs_t"])
                        S.op("dve", lambda e: e.reciprocal(out=rs_t[:NSMP], in_=rs_t[:NSMP]), reads=["rs_t"], writes=["rs_t"])
                        S.op("dve", lambda e: e.tensor_scalar(out=vtok[:NSMP, :], in0=vtok[:NSMP, :], scalar1=mv[:NSMP, 0:1], scalar2=rs_t[:NSMP, 0:1],
                                                              op0=ALU.subtract, op1=ALU.mult), reads=["vtok", "mv", "rs_t"], writes=["vtok"])
                        for c0 in (0, 4):
                            pt, ptk = next_ps(psT, "pt")

                            def fn(e, c0=c0, pt=pt):
                                for c in range(c0, c0 + 4):
                                    ins = e.transpose(out=pt[:, (c - c0) * 128:(c - c0) * 128 + NSMP], in_=vtok[:NSMP, c * 128:(c + 1) * 128],
                                                      identity=ident[:NSMP, :NSMP])
                                return ins
                            S.op("pe", fn, reads=["vtok", "ident"], writes=[ptk])
                            for c in range(c0, c0 + 4):
                                S.op("act", lambda e, c=c, c0=c0, pt=pt: e.activation(out=vTs[:, c, :], in_=pt[:, (c - c0) * 128:(c - c0) * 128 + NSMP],
                                                                                    func=AF.Identity, scale=ngp[:, c:c + 1], bias=nbp[:, c:c + 1]),
                                     reads=[ptk, "ngp", "nbp"], writes=["vTs"])
                        for c0 in (0, 4):
                            pt, ptk = next_ps(psT, "pt")

                            def fn2(e, c0=c0, pt=pt):
                                for c in range(c0, c0 + 4):
                                    ins = e.transpose(out=pt[:NSMP, (c - c0) * 128:(c - c0 + 1) * 128], in_=vTs[:, c, :], identity=ident[:])
                                return ins
                            S.op("pe", fn2, reads=["vTs", "ident"], writes=[ptk])
                            S.op("dve", lambda e, c0=c0, pt=pt: e.tensor_copy(out=tokt[:NSMP, c0 * 128:(c0 + 4) * 128], in_=pt[:NSMP, :512]),
                                 reads=[ptk], writes=["tokt"])
                        S.dma(lambda e: e.dma_start(out=sguv[j], in_=tokt[:NSMP, :]), reads=["tokt"])
                        for g in range(8):
                            S.op("dve", lambda e, g=g: e.tensor_scalar(out=tmp16[:, g, :], in0=vTs[:, g, :], scalar1=w00c[:, g:g + 1], scalar2=bs0c[:, g:g + 1],
                                                                      op0=ALU.mult, op1=ALU.add), reads=["vTs", "w00c", "bs0c"], writes=["tmp16"])
                        S.op("dve", lambda e: e.tensor_tensor(out=umT[:, :, 128:128 + NSMP], in0=tmp16[:], in1=uT[:, :, 128:128 + NSMP], op=ALU.mult),
                             reads=["tmp16", "uT"], writes=["umT"])
                    for bb in range(W // 128):
                        for half in range(2):
                            ps, pk = next_ps(psB, "pb")
                            mm_group(ps[:, :512], pk, 512,
                                     [(hT[:, k, bb * 128:(bb + 1) * 128], w_v[:, k, half * 512:(half + 1) * 512]) for k in range(KC)], ["w_v", "hT"])
                            S.op("dve", lambda e, ps=ps, half=half: e.tensor_tensor(out=vtok[:, half * 512:(half + 1) * 512], in0=ps[:, :512],
                                                                                    in1=bvb[:, half * 512:(half + 1) * 512], op=ALU.add),
                                 reads=[pk, "bvb"], writes=["vtok"])
                        S.op("act", lambda e: e.activation(out=vtok, in_=vtok, func=AF.Gelu_apprx_tanh), reads=["vtok"], writes=["vtok"])
                        for half in range(2):
                            S.op("dve", lambda e, half=half: e.bn_stats(out=bst[:, half, :], in_=vtok[:, half * 512:(half + 1) * 512]),
                                 reads=["vtok"], writes=["bst"])
                        S.op("dve", lambda e: e.bn_aggr(out=mv[:], in_=bst[:]), reads=["bst"], writes=["mv"])
                        S.op("act", lambda e: e.activation(out=rs_t[:], in_=mv[:, 1:2], func=AF.Sqrt, bias=epsL[:, 0:1]),
                             reads=["mv", "epsL"], writes=["rs_t"])
                        S.op("dve", lambda e: e.reciprocal(out=rs_t[:], in_=rs_t[:]), reads=["rs_t"], writes=["rs_t"])
                        S.op("dve", lambda e: e.tensor_scalar(out=vhat, in0=vtok, scalar1=mv[:, 0:1], scalar2=rs_t[:, 0:1],
                                                              op0=ALU.subtract, op1=ALU.mult), reads=["vtok", "mv", "rs_t"], writes=["vhat"])
                        for gh in range(2):
                            ps, pk = next_ps(psB, "pb")

                            def fn(e, ps=ps, gh=gh):
                                for g4 in range(4):
                                    g = gh * 4 + g4
                                    ins = e.matmul(ps[:, g4 * 128:(g4 + 1) * 128], lhsT=vhat[:, g * 128:(g + 1) * 128], rhs=WsT[:, g, :],
                                                   start=True, stop=True)
                                return ins
                            S.op("pe", fn, reads=["vhat", "WsT"], writes=[pk])
                            for g4 in range(4):
                                g = gh * 4 + g4
                                S.op("dve", lambda e, ps=ps, g=g, g4=g4: e.scalar_tensor_tensor(
                                    out=mix, in0=ps[:, g4 * 128:(g4 + 1) * 128], scalar=ngp[:, g:g + 1], in1=C2[:, g, :],
                                    op0=ALU.mult, op1=ALU.add), reads=[pk, "ngp", "C2"], writes=["mix"])
                                S.op("dve", lambda e, g=g, bb=bb: e.tensor_tensor(out=umT[:, g, bb * 128:(bb + 1) * 128], in0=mix,
                                                                                  in1=uT[:, g, bb * 128:(bb + 1) * 128], op=ALU.mult),
                                     reads=["mix", "uT"], writes=["umT"])
                    for c in range(KC):
                        ps, pk = next_ps(psA, "ps")
                        mm_group(ps[:, :Wx], pk, Wx, [(w_o[:, k, c * 128:(c + 1) * 128], umT[:, k, :Wx]) for k in range(KC)], ["w_o", "umT"])
                        z_from_ps(ps, pk, c, t0, W, 2)
                        if smp:
                            zs_from_ps(ps, pk, c, 2)
                    postnorm(t0, W, 0)
                    if smp:
                        postnorm(0, NSMP, 0, samples=True)
                for (t0, W) in tiles:
                    sgu_tile(t0, W)
            else:

                jd = j
                assert HALF % CH == 0
                WIN = dsa_w_in[jd]
                new_phase()
                wkv = carve([128, KC, 512], BF16); wki = carve([128, KC, 64], BF16)
                KTl = carve([128, 2, NT], BF16); kiTl = carve([128, NT], BF16); Vl = carve([128, NBLK, 256], BF16)
                kvst = [carve([128, 576]) for _ in range(2)]
                load_resident(wkv, "wkv", WIN, 1024, 512)
                wt, wkey = load_wchunk(WIN, KC, 2048, ncol=64)
                S.op("pool", lambda e, wt=wt: e.tensor_copy(out=wki, in_=wt[:, :KC, :64]), reads=[wkey], writes=["wki"])
                wki2 = carve([128, KC, 128], BF16)
                for hh in range(2):
                    S.op("pool", lambda e, hh=hh: e.tensor_copy(out=wki2[:, :, hh * 64:(hh + 1) * 64], in_=wki), reads=["wki"], writes=["wki2"])
                S.op("pool", lambda e: e.memset(kiTl, 0.0), writes=["kiTl"])
                def dsap_tile(t0, W):
                    smp = SMP and t0 == 0
                    Wx = W + (NSMP if smp else 0)
                    modulate(t0, W, 0, 1)
                    if smp:
                        modulate_s(0, 1)
                    for c in range(2):
                        ps, pk = next_ps(psA, "ps")
                        linear_chunk(WIN, KC, 1024 + c * 128, hT, ["hT"], Wx, ps, pk)
                        S.op("act", lambda e, ps=ps, c=c: e.activation(out=KTl[:, c, t0:t0 + W], in_=ps[:, :W], func=AF.Identity),
                             reads=[pk], writes=["KTl"])
                        if smp:
                            S.op("act", lambda e, ps=ps, c=c: e.activation(out=KTn[:, c, :], in_=ps[:, 128:128 + NSMP], func=AF.Identity), reads=[pk], writes=["KTn"])
                    ps, pk = next_ps(psA, "ps")
                    wt, wkey = load_wchunk(WIN, KC, 2048, ncol=64)
                    mm_group(ps[:64, :W], pk, W, [(wt[:, k, :64], hT[:, k, :W]) for k in range(KC)], [wkey, "hT"])
                    S.op("act", lambda e, ps=ps: e.activation(out=kiTl[:64, t0:t0 + W], in_=ps[:64, :W], func=AF.Identity),
                         reads=[pk], writes=["kiTl"])
                    if smp:
                        ps, pk = next_ps(psA, "ps")
                        mm_group(ps[:, :NSMP], pk, NSMP, [(wki2[:, k, :], hT[:, k, 128:128 + NSMP]) for k in range(KC)], ["wki2", "hT"])
                        S.op("act", lambda e, ps=ps: e.activation(out=kiTn[:], in_=ps[:, :NSMP], func=AF.Identity), reads=[pk], writes=["kiTn"])
                        stg = kvst[0]
                        ps, pk = next_ps(psB, "pb")
                        mm_group(ps[:NSMP, :512], pk, 512, [(hT[:, k, 128:128 + NSMP], wkv[:, k, :]) for k in range(KC)], ["wkv", "hT"])
                        S.op("act", lambda e, ps=ps: e.activation(out=stg[:NSMP, 0:512], in_=ps[:NSMP, :512], func=AF.Identity), reads=[pk], writes=["kvst0"])
                        ps2, pk2 = next_ps(psB, "pb")
                        mm_group(ps2[:NSMP, :64], pk2, 64, [(hT[:, k, 128:128 + NSMP], wki[:, k, :]) for k in range(KC)], ["wki", "hT"])
                        S.op("dve", lambda e, ps2=ps2: e.tensor_copy(out=stg[:NSMP, 512:576], in_=ps2[:NSMP, :64]), reads=[pk2], writes=["kvst0"])
                        S.op("pool", lambda e: e.tensor_copy(out=Vn[:], in_=stg[:NSMP, 256:512]), reads=["kvst0"], writes=["Vn"])
                        S.dma(lambda e: e.dma_start(out=ksn[jd], in_=stg[:NSMP, 0:256]), reads=["kvst0"])
                        S.dma(lambda e: e.dma_start(out=vsn[jd], in_=stg[:NSMP, 256:512]), reads=["kvst0"])
                        S.dma(lambda e: e.dma_start(out=kisn[jd], in_=stg[:NSMP, 512:576]), reads=["kvst0"])
                    for bb in range(W // 128):
                        blk = t0 // 128 + bb
                        stg = kvst[blk % 2]; sk = f"kvst{blk % 2}"
                        ps, pk = next_ps(psB, "pb")
                        mm_group(ps[:, :512], pk, 512, [(hT[:, k, bb * 128:(bb + 1) * 128], wkv[:, k, :]) for k in range(KC)], ["wkv", "hT"])
                        S.op("act", lambda e, ps=ps, stg=stg: e.activation(out=stg[:, 0:512], in_=ps[:, :512], func=AF.Identity),
                             reads=[pk], writes=[sk])
                        ps2, pk2 = next_ps(psB, "pb")
                        mm_group(ps2[:, :64], pk2, 64, [(hT[:, k, bb * 128:(bb + 1) * 128], wki[:, k, :]) for k in range(KC)], ["wki", "hT"])
                        S.op("dve", lambda e, ps2=ps2, stg=stg: e.tensor_copy(out=stg[:, 512:576], in_=ps2[:, :64]), reads=[pk2], writes=[sk])
                        S.op("pool", lambda e, stg=stg, blk=blk: e.tensor_copy(out=Vl[:, blk, :], in_=stg[:, 256:512]), reads=[sk], writes=["Vl"])
                        if blk >= 1:
                            r0 = (blk - 1) * 128
                            S.dma(lambda e, stg=stg, r0=r0: e.dma_start(out=knew[jd, r0:r0 + 128, :], in_=stg[:, 0:256]), reads=[sk])
                            S.dma(lambda e, stg=stg, r0=r0: e.dma_start(out=vnew[jd, r0:r0 + 128, :], in_=stg[:, 256:512]), reads=[sk])
                            S.dma(lambda e, stg=stg, r0=r0: e.dma_start(out=kinew[jd, r0:r0 + 128, :], in_=stg[:, 512:576]), reads=[sk])
                for (t0, W) in tiles:
                    dsap_tile(t0, W)
                if dbg_stop <= 1:
                    return ffn_phase(li)
                bn, gt_ = bounce[jd], gath[jd]
                for c in range(2):
                    S.dma(lambda e, c=c: e.dma_start(out=bn[c][:, :], in_=KTl[:, c, 128:NT]), reads=["KTl"], writes=[f"bounce{jd}_{c}"])
                bpv = VSEG // 256
                for g in range(NVS):
                    S.dma(lambda e, g=g: e.dma_start(out=bn[2 + g].rearrange("p (b v) -> p b v", v=256), in_=Vl[:, 1 + g * bpv:1 + (g + 1) * bpv, :]),
                          reads=["Vl"], writes=[f"bounce{jd}_{2 + g}"])
                S.dma(lambda e: e.dma_start(out=bn[NSEG - 1][:, :], in_=kiTl[:, 128:NT]), reads=["kiTl"], writes=[f"bounce{jd}_{NSEG - 1}"])
                for g in range(NSEG):
                    S.cc(lambda e, g=g: e.collective_compute("AllGather", ALU.bypass, replica_groups=PAIRS, ins=[bn[g][:, :]], outs=[gt_[g][:, :]]),
                         reads=[f"bounce{jd}_{g}"], writes=[f"gath{jd}_{g}"])
                gk = [f"gath{jd}_{g}" for g in range(NSEG)]

                if dbg_stop <= 2:
                    return ffn_phase(li)

                if SMP:
                    new_phase()
                    NK1 = NPG + 1
                    KTs2 = [carve([128, 2, NK1 * 128], BF16) for _ in range(2)]; kiTs2 = [carve([128, NK1 * 128], BF16) for _ in range(2)]
                    Vs2 = [carve([128, NK1, 256], BF16) for _ in range(2)]
                    Kst = [carve([128, 256]) for _ in range(2)]; Vst = [carve([128, 256]) for _ in range(2)]; kist = [carve([128, 128]) for _ in range(2)]
                    qTs = carve([128, 8, NSMP], BF16); qiTs = carve([128, 4, NSMP], BF16); oTs = carve([128, 8, NSMP], BF16)
                    wwi_s = carve([128, KC, 8], BF16); E_all = carve([128, NSMP, 128])
                    rS = carve([128, NK1, 8]); eS = carve([128, NK1, 8]); pTs = carve([128, NK1, 8], BF16)
                    scT = carve([128, NK1]); junk17 = carve([128, NK1]); mk17 = carve([128, NK1]); nb16 = carve([128, NK1])
                    ss = carve([128, 32])
                    wit, witp, wib, cntp, lo_s, w0_s, mid_s, gew_s, negM, rec8, m11 = (
                        ss[:, 0:8], ss[:, 8:16], ss[:, 16:24], ss[:, 24:25], ss[:, 25:26], ss[:, 26:27], ss[:, 27:28], ss[:, 28:29],
                        ss[:, 29:30], ss[:, 16:24], ss[:, 30:31])
                    rec8 = carve([128, 8])
                    wt, wkey = load_wchunk(WIN, KC, 2112, ncol=8)
                    S.op("pool", lambda e, wt=wt: e.tensor_copy(out=wwi_s, in_=wt[:, :KC, :8]), reads=[wkey], writes=["wwi_s"])
                    for bi in range(2):
                        S.op("pool", lambda e, bi=bi: e.memset(KTs2[bi], 0.0), writes=[f"KTs{bi}"])
                        S.op("pool", lambda e, bi=bi: e.memset(kiTs2[bi], 0.0), writes=[f"kiTs{bi}"])
                        S.op("pool", lambda e, bi=bi: e.memset(Vs2[bi], 0.0), writes=[f"Vs{bi}"])
                    S.op("pool", lambda e: e.memset(nb16, 0.0), writes=["nb16"])
                    S.op("pool", lambda e: e.memset(nb16[:, NPG:NPG + 1], -BIG), writes=["nb16"])
                    S.op("pool", lambda e: e.affine_select(out=nb16[:, NPG:NPG + 1], in_=nb16[:, NPG:NPG + 1], pattern=[[0, 1]], compare_op=ALU.is_gt,
                                                           fill=0.0, base=0, channel_multiplier=1), reads=["nb16"], writes=["nb16"])
                    S.op("dve", lambda e: e.tensor_copy(out=E_all[:NSMP], in_=ident[:NSMP, :NSMP].unsqueeze(2).to_broadcast([NSMP, NSMP, 128])),
                         reads=["ident"], writes=["E_all"])
                    modulate_s(0, 1)
                    for h in range(8):
                        ps, pk = next_ps(psA, "ps")
                        linear_chunk(WIN, KC, h * 128, hT, ["hT"], NSMP, ps, pk, off=128)
                        S.op("act", lambda e, ps=ps, h=h: e.activation(out=qTs[:, h, :], in_=ps[:, :NSMP], func=AF.Identity, scale=128 ** -0.5), reads=[pk], writes=["qTs"])
                    for c in range(4):
                        ps, pk = next_ps(psA, "ps")
                        linear_chunk(WIN, KC, 1536 + c * 128, hT, ["hT"], NSMP, ps, pk, off=128)
                        S.op("act", lambda e, ps=ps, c=c: e.activation(out=qiTs[:, c, :], in_=ps[:, :NSMP], func=AF.Identity), reads=[pk], writes=["qiTs"])
                    ps, pk = next_ps(psA, "ps")
                    mm_group(ps[:NSMP, :8], pk, 8, [(hT[:, k, 128:128 + NSMP], wwi_s[:, k, :]) for k in range(KC)], ["wwi_s", "hT"])
                    S.op("act", lambda e, ps=ps: e.activation(out=wit[:NSMP], in_=ps[:NSMP, :8], func=AF.Identity, scale=(8 ** -0.5) * (64 ** -0.5)), reads=[pk], writes=["wit"])
                    S.op("dve", lambda e: e.tensor_copy(out=witp[:NSMP, 0:4], in_=wit[:NSMP, 0:8:2]), reads=["wit"], writes=["witp"])
                    S.op("dve", lambda e: e.tensor_copy(out=witp[:NSMP, 4:8], in_=wit[:NSMP, 1:8:2]), reads=["wit"], writes=["witp"])
                    CK = cache_k[jd]; CV = cache_v[jd]; CI = cache_ki[jd]

                    def sample_attend(si):
                        bi = si % 2
                        KTs, kiTs, Vs = KTs2[bi], kiTs2[bi], Vs2[bi]
                        kKT, kki, kV = f"KTs{bi}", f"kiTs{bi}", f"Vs{bi}"
                        for pg in range(NPG):
                            i = pg % 2
                            icol = idx_all[:, si * NPG + pg:si * NPG + pg + 1]
                            S.dma(lambda e, i=i, icol=icol: e.indirect_dma_start(out=Kst[i], out_offset=None, in_=CK[:, :],
                                                                                in_offset=bass.IndirectOffsetOnAxis(ap=icol, axis=0)),
                                  reads=["idx_all"], writes=[f"Kst{i}"], q="pool")
                            if dbg_stop <= 4.1:
                                continue
                            S.dma(lambda e, i=i, icol=icol: e.indirect_dma_start(out=Vst[i], out_offset=None, in_=CV[:, :],
                                                                                in_offset=bass.IndirectOffsetOnAxis(ap=icol, axis=0)),
                                  reads=["idx_all"], writes=[f"Vst{i}"], q="pool")
                            S.dma(lambda e, i=i, icol=icol: e.indirect_dma_start(out=kist[i][:, 0:64], out_offset=None, in_=CI[:, :],
                                                                                in_offset=bass.IndirectOffsetOnAxis(ap=icol, axis=0)),
                                  reads=["idx_all"], writes=[f"kist{i}"], q="pool")
                            if dbg_stop <= 4.2:
                                continue
                            S.op("dve", lambda e, i=i: e.tensor_copy(out=kist[i][:, 64:128], in_=kist[i][:, 0:64]), reads=[f"kist{i}"], writes=[f"kist{i}"])
                            S.op("act", lambda e, i=i, pg=pg: e.activation(out=Vs[:, pg, :], in_=Vst[i], func=AF.Identity), reads=[f"Vst{i}"], writes=[kV])
                            if dbg_stop <= 4.3:
                                continue
                            pt, ptk = next_ps(psT, "pt")

                            def fnt(e, i=i, pt=pt):
                                e.transpose(out=pt[:, 0:128], in_=Kst[i][:, 0:128], identity=ident[:])
                                e.transpose(out=pt[:, 128:256], in_=Kst[i][:, 128:256], identity=ident[:])
                                return e.transpose(out=pt[:, 256:384], in_=kist[i][:, :], identity=ident[:])
                            S.op("pe", fnt, reads=[f"Kst{i}", f"kist{i}", "ident"], writes=[ptk])
                            if dbg_stop <= 4.4:
                                continue
                            S.op("act", lambda e, pt=pt, pg=pg: e.activation(out=KTs[:, :, pg * 128:(pg + 1) * 128],
                                                                             in_=pt[:, 0:256].rearrange("p (c s) -> p c s", c=2), func=AF.Identity),
                                 reads=[ptk], writes=[kKT])
                            if dbg_stop <= 4.45:
                                continue
                            S.op("act", lambda e, pt=pt, pg=pg: e.activation(out=kiTs[:, pg * 128:(pg + 1) * 128], in_=pt[:, 256:384], func=AF.Identity),
                                 reads=[ptk], writes=[kki])
                        if dbg_stop <= 4.5:
                            return
                        S.op("dve", lambda e: e.tensor_copy(out=KTs[:, :, NPG * 128:NPG * 128 + 1], in_=KTn[:, :, si:si + 1]), reads=["KTn"], writes=[kKT])
                        S.op("dve", lambda e: e.tensor_copy(out=kiTs[:, NPG * 128:NPG * 128 + 1], in_=kiTn[:, si:si + 1]), reads=["kiTn"], writes=[kki])
                        S.dma(lambda e: e.dma_start(out=Vs[0:1, NPG, :], in_=Vn[si:si + 1, :]), reads=["Vn"], writes=[kV])
                        if dbg_stop <= 5.1:
                            return
                        pse, pek = next_ps(psA, "ps")
                        pso_, pok = next_ps(psA, "ps")
                        pscs = [pse[:, :NK1 * 4].rearrange("p (g h) -> p g h", h=4), pso_[:, :NK1 * 4].rearrange("p (g h) -> p g h", h=4)]
                        for hh in range(2):
                            def fns(e, hh=hh):
                                pb = hh * 64
                                for pg in range(NK1):
                                    ins = e.matmul(pscs[hh][:, pg, :], lhsT=kiTs[pb:pb + 64, pg * 128:(pg + 1) * 128], rhs=qiTs[pb:pb + 64, :, si],
                                                   start=True, stop=True)
                                return ins
                            S.op("pe", fns, reads=[kki, "qiTs"], writes=[(pek, pok)[hh]])
                            S.op("act", lambda e, hh=hh: e.activation(out=rS[:, :, hh * 4:(hh + 1) * 4], in_=pscs[hh], func=AF.Relu),
                                 reads=[(pek, pok)[hh]], writes=["rS"])
                        if dbg_stop <= 5.2:
                            return
                        ps2, pk2 = next_ps(psA, "ps")
                        mm_group(ps2[:, :8], pk2, 8, [(E_all[:NSMP, si, :], witp[:NSMP, :])], ["E_all", "witp"])
                        S.op("dve", lambda e, ps2=ps2: e.tensor_copy(out=wib, in_=ps2[:, :8]), reads=[pk2], writes=["wib"])
                        S.op("dve", lambda e: e.tensor_tensor(out=rS, in0=rS, in1=wib.unsqueeze(1).to_broadcast([128, NK1, 8]), op=ALU.mult),
                             reads=["rS", "wib"], writes=["rS"])
                        S.op("dve", lambda e: e.tensor_reduce(out=scT, in_=rS, axis=mybir.AxisListType.X, op=ALU.add), reads=["rS"], writes=["scT"])
                        if dbg_stop <= 5.3:
                            return
                        S.op("act", lambda e: e.activation(out=junk17, in_=scT, func=AF.Square, accum_out=cntp), reads=["scT"], writes=["junk17", "cntp"])
                        pq, pqk = next_ps(psS, "pq")
                        mm_group(pq[:, :1], pqk, 1, [(ones_f[:], cntp)], ["ones_f", "cntp"])
                        S.op("act", lambda e, pq=pq: e.activation(out=w0_s, in_=pq[:, :1], func=AF.Sqrt), reads=[pqk], writes=["w0_s"])
                        S.op("dve", lambda e: e.tensor_scalar(out=lo_s, in0=w0_s, scalar1=-1.0, scalar2=-1.0, op0=ALU.mult, op1=ALU.add), reads=["w0_s"], writes=["lo_s"])
                        S.op("dve", lambda e: e.tensor_scalar(out=w0_s, in0=w0_s, scalar1=2.0, scalar2=2.0, op0=ALU.mult, op1=ALU.add), reads=["w0_s"], writes=["w0_s"])
                        S.op("dve", lambda e: e.tensor_tensor(out=scT, in0=scT, in1=nb16, op=ALU.add), reads=["scT", "nb16"], writes=["scT"])
                        if dbg_stop <= 5.4:
                            return
                        for it in range(NITS):
                            ck = 0.5 ** (it + 1)
                            S.op("dve", lambda e, ck=ck: e.scalar_tensor_tensor(out=mid_s, in0=w0_s, scalar=ck, in1=lo_s, op0=ALU.mult, op1=ALU.add),
                                 reads=["w0_s", "lo_s"], writes=["mid_s"])
                            S.op("dve", lambda e: e.tensor_scalar(out=junk17, in0=scT, scalar1=mid_s, scalar2=0.0, op0=ALU.is_ge, op1=ALU.add, accum_out=cntp),
                                 reads=["scT", "mid_s"], writes=["junk17", "cntp"])
                            pq, pqk = next_ps(psS, "pq")
                            mm_group(pq[:, :1], pqk, 1, [(ones_f[:], cntp)], ["ones_f", "cntp"])
                            S.op("dve", lambda e, pq=pq: e.tensor_scalar(out=gew_s, in0=pq[:, :1], scalar1=KEEP_S - 0.5, scalar2=w0_s, op0=ALU.is_ge, op1=ALU.mult),
                                 reads=[pqk, "w0_s"], writes=["gew_s"])
                            S.op("dve", lambda e, ck=ck: e.scalar_tensor_tensor(out=lo_s, in0=gew_s, scalar=ck, in1=lo_s, op0=ALU.mult, op1=ALU.add),
                                 reads=["gew_s", "lo_s"], writes=["lo_s"])
                        S.op("dve", lambda e: e.tensor_scalar(out=mk17, in0=scT, scalar1=lo_s, scalar2=None, op0=ALU.is_ge), reads=["scT", "lo_s"], writes=["mk17"])
                        if dbg_stop <= 5:
                            return
                        ps, pk = next_ps(psA, "ps")
                        psl = ps[:, :NK1 * 8].rearrange("p (g h) -> p g h", h=8)

                        def fnl(e, psl=psl):
                            for pg in range(NK1):
                                for kvh in range(2):
                                    ins = e.matmul(psl[:, pg, kvh * 4:(kvh + 1) * 4], lhsT=KTs[:, kvh, pg * 128:(pg + 1) * 128], rhs=qTs[:, kvh * 4:(kvh + 1) * 4, si],
                                                   start=True, stop=True)
                            return ins
                        S.op("pe", fnl, reads=[kKT, "qTs"], writes=[pk])
                        S.op("dve", lambda e, ps=ps: e.tensor_reduce(out=cntp, in_=ps[:, :NK1 * 8], axis=mybir.AxisListType.X, op=ALU.max), reads=[pk], writes=["cntp"])
                        pt, ptk = next_ps(psT, "pt")
                        S.op("pe", lambda e, pt=pt: e.transpose(out=pt[:1, 0:128], in_=cntp, identity=ident[:]), reads=["cntp", "ident"], writes=[ptk])
                        S.op("dve", lambda e, pt=pt: e.tensor_reduce(out=m11[0:1], in_=pt[:1, 0:128], axis=mybir.AxisListType.X, op=ALU.max), reads=[ptk], writes=["m11"])
                        pq, pqk = next_ps(psS, "pq")
                        mm_group(pq[:, :1], pqk, 1, [(ones_f[0:1, :], m11[0:1])], ["ones_f", "m11"])
                        S.op("dve", lambda e, pq=pq: e.tensor_scalar(out=negM, in0=pq[:, :1], scalar1=-1.0, scalar2=None, op0=ALU.mult), reads=[pqk], writes=["negM"])
                        S.op("act", lambda e, psl=psl: e.activation(out=eS, in_=psl, func=AF.Exp, bias=negM), reads=[pk, "negM"], writes=["eS"])
                        S.op("dve", lambda e: e.tensor_tensor(out=pTs, in0=eS, in1=mk17.unsqueeze(2).to_broadcast([128, NK1, 8]), op=ALU.mult),
                             reads=["eS", "mk17"], writes=["pTs"])
                        if dbg_stop <= 6:
                            return
                        pso, psok = psB[0], "pb0"
                        pss, pssk = psB[1], "pb1"

                        def fno(e):
                            for kvh in range(2):
                                for pg in range(NK1):
                                    ins = e.matmul(pso[:, kvh * 4:(kvh + 1) * 4], lhsT=Vs[:, pg, kvh * 128:(kvh + 1) * 128], rhs=pTs[:, pg, kvh * 4:(kvh + 1) * 4],
                                                   start=(pg == 0), stop=(pg == NK1 - 1))
                            return ins
                        S.op("pe", fno, reads=[kV, "pTs"], writes=[psok])

                        def fnsum(e):
                            for pg in range(NK1):
                                ins = e.matmul(pss[:, 0:8], lhsT=onesb[:], rhs=pTs[:, pg, :], start=(pg == 0), stop=(pg == NK1 - 1))
                            return ins
                        S.op("pe", fnsum, reads=["onesb", "pTs"], writes=[pssk])
                        S.op("dve", lambda e: e.tensor_scalar(out=rec8, in0=pss[:, 0:8], scalar1=1e-30, scalar2=None, op0=ALU.max), reads=[pssk], writes=["rec8"])
                        S.op("dve", lambda e: e.reciprocal(out=rec8, in_=rec8), reads=["rec8"], writes=["rec8"])
                        S.op("dve", lambda e: e.tensor_tensor(out=oTs[:, :, si], in0=pso[:, 0:8], in1=rec8, op=ALU.mult), reads=[psok, "rec8"], writes=["oTs"])
                    for si in range(NSMP if dbg_stop >= 99 else (0 if dbg_stop <= 3 else 1)):
                        sample_attend(si)
                    for c in range(KC):
                        ps, pk = next_ps(psA, "ps")
                        linear_chunk(dsa_w_out[jd], KC, c * 128, oTs, ["oTs"], NSMP, ps, pk)
                        zs_from_ps(ps, pk, c, 2, off=0)
                    postnorm(0, NSMP, 0, samples=True)
                new_phase()
                NKBM = 2 * (NBLK - 1)
                wwi = carve([128, KC, 8], BF16)
                scores = carve([128, NKEY]); junk = carve([128, NKEY], U8); maskT2 = [carve([128, NKBM, 128], BF16) for _ in range(2)]
                qT2 = [carve([128, 8, 128], BF16) for _ in range(2)]; qiT = carve([128, 4, 128], BF16); oT = carve([128, 8, 128], BF16)
                negC2 = [carve([128, 1]) for _ in range(2)]
                KTc = [carve([128, 2, CH], BF16) for _ in range(2)]; Vc = [carve([128, CH // 128, 256], BF16) for _ in range(2)]
                kic = [carve([128, CH], BF16) for _ in range(2)]
                rbuf = [carve([128, CH]) for _ in range(2)]; mrow = [carve([128, CH], BF16) for _ in range(2)]
                e_t = [carve([128, 4, 128], BF16) for _ in range(2)]; pmt = [carve([128, 4, 128], BF16) for _ in range(2)]
                cbt = carve([128, CH]); kposb = carve([128, CH]); rec = carve([128, CH]); qsq = carve([128, 8, 128], BF16)
                sm = carve([128, 32])
                wi_t, qpos, lo, w0, mid, cntt, gew, qn2, kn2, negC, mins, qrel, piota = (
                    sm[:, 0:8], sm[:, 8:9], sm[:, 9:10], sm[:, 10:11], sm[:, 11:12], sm[:, 12:13], sm[:, 13:14], sm[:, 14:15],
                    sm[:, 15:16], sm[:, 16:17], sm[:, 17:25], sm[:, 25:26], sm[:, 26:27])
                wt, wkey = load_wchunk(WIN, KC, 2112, ncol=8)
                S.op("pool", lambda e, wt=wt: e.tensor_copy(out=wwi, in_=wt[:, :KC, :8]), reads=[wkey], writes=["wwi"])
                S.op("pool", lambda e: e.iota(kposb, pattern=[[1, CH]], base=0, channel_multiplier=0, allow_small_or_imprecise_dtypes=True),
                     writes=["kposb"])
                S.op("pool", lambda e: e.iota(piota, pattern=[[0, 1]], base=0, channel_multiplier=1, allow_small_or_imprecise_dtypes=True),
                     writes=["piota"])
                ccnt = {"k": 0}

                def load_kv_chunk(kc, want):
                    i = ccnt["k"] % 2
                    ccnt["k"] += 1
                    r, cl = divmod(kc * CH, HALF)
                    rr = slice(r * 128, (r + 1) * 128)
                    if "ki" in want:
                        for hh in range(2):
                            S.dma(lambda e, i=i, hh=hh, r=r, cl=cl: e.dma_start(out=kic[i][hh * 64:(hh + 1) * 64, :],
                                                                               in_=gt_[NSEG - 1][r * 128:r * 128 + 64, cl:cl + CH]),
                                  reads=[gk[NSEG - 1]], writes=[f"kic{i}"])
                    if "kv" in want:
                        for c in range(2):
                            S.dma(lambda e, i=i, c=c, rr=rr, cl=cl: e.dma_start(out=KTc[i][:, c, :], in_=gt_[c][rr, cl:cl + CH]),
                                  reads=[gk[c]], writes=[f"KTc{i}"])
                        vg, vo = divmod((cl // 128) * 256, VSEG)
                        S.dma(lambda e, i=i, rr=rr, vg=vg, vo=vo: e.dma_start(
                            out=Vc[i], in_=gt_[2 + vg][rr, vo:vo + (CH // 128) * 256].rearrange("p (b v) -> p b v", v=256)),
                            reads=[gk[2 + vg]], writes=[f"Vc{i}"])
                    return i

                S.op("dve", lambda e: e.memset(kn2, 0.0), writes=["kn2"])
                for kc in range(NKEY // CH):
                    i = load_kv_chunk(kc, ("kv",))
                    for c in range(2):
                        S.op("act", lambda e, i=i, c=c: e.activation(out=mrow[c], in_=KTc[i][:, c, :], func=AF.Square), reads=[f"KTc{i}"], writes=[f"mrow{c}"])
                        ps, pk = next_ps(psA, "ps")
                        mm_group(ps[:, :CH], pk, CH, [(onesb[:], mrow[c])], ["onesb", f"mrow{c}"])
                        S.op("dve", lambda e, ps=ps: e.tensor_reduce(out=cntt, in_=ps[:, :CH], axis=mybir.AxisListType.X, op=ALU.max), reads=[pk], writes=["cntt"])
                        S.op("dve", lambda e: e.tensor_tensor(out=kn2, in0=kn2, in1=cntt, op=ALU.max), reads=["kn2", "cntt"], writes=["kn2"])

                def att_A(qb):
                    bq = qb % 2
                    qT, maskT, negC = qT2[bq], maskT2[bq], negC2[bq]
                    kq, km, kn = f"qT{bq}", f"maskT{bq}", f"negC{bq}"
                    t0 = qb * 128
                    nkb = (NBLK - 1) + qb
                    nch = -(-nkb // (CH // 128))
                    S_ = nch * CH
                    modulate(t0, 128, 0, 1)
                    for h in range(8):
                        ps, pk = next_ps(psA, "ps")
                        linear_chunk(WIN, KC, h * 128, hT, ["hT"], 128, ps, pk)
                        S.op("act", lambda e, ps=ps, h=h: e.activation(out=qT[:, h, :], in_=ps[:, :128], func=AF.Identity, scale=128 ** -0.5),
                             reads=[pk], writes=[kq])
                    for c in range(4):
                        ps, pk = next_ps(psA, "ps")
                        linear_chunk(WIN, KC, 1536 + c * 128, hT, ["hT"], 128, ps, pk)
                        S.op("act", lambda e, ps=ps, c=c: e.activation(out=qiT[:, c, :], in_=ps[:, :128], func=AF.Identity), reads=[pk], writes=["qiT"])
                    ps, pk = next_ps(psA, "ps")
                    mm_group(ps[:, :8], pk, 8, [(hT[:, k, :128], wwi[:, k, :]) for k in range(KC)], ["wwi", "hT"])
                    S.op("act", lambda e, ps=ps: e.activation(out=wi_t, in_=ps[:, :8], func=AF.Identity, scale=(8 ** -0.5) * (64 ** -0.5)),
                         reads=[pk], writes=["wi_t"])
                    S.op("dve", lambda e, qb=qb: e.tensor_scalar(out=qpos, in0=piota, scalar1=rolet[:, 0:1], scalar2=float((qb - 1) * 128),
                                                                op0=ALU.add, op1=ALU.add), reads=["piota", "rolet"], writes=["qpos"])
                    for kc in range(nch):
                        i = load_kv_chunk(kc, ("ki",))
                        sc = scores[:, kc * CH:(kc + 1) * CH]
                        for h in range(8):
                            ps, pk = next_ps(psA, "ps")
                            pb = (h % 2) * 64
                            mm_group(ps[:, :CH], pk, CH, [(qiT[pb:pb + 64, h // 2, :], kic[i][pb:pb + 64, :])], ["qiT", f"kic{i}"])
                            rb = rbuf[h % 2]
                            S.op("act", lambda e, ps=ps, rb=rb: e.activation(out=rb, in_=ps[:, :CH], func=AF.Relu), reads=[pk], writes=[f"rbuf{h % 2}"])
                            if h == 0:
                                S.op("dve", lambda e, rb=rb, sc=sc: e.tensor_scalar(out=sc, in0=rb, scalar1=wi_t[:, 0:1], scalar2=None, op0=ALU.mult),
                                     reads=[f"rbuf{h % 2}", "wi_t"], writes=["scores"])
                            else:
                                S.op("dve", lambda e, rb=rb, sc=sc, h=h: e.scalar_tensor_tensor(out=sc, in0=rb, scalar=wi_t[:, h:h + 1], in1=sc,
                                                                                              op0=ALU.mult, op1=ALU.add),
                                     reads=[f"rbuf{h % 2}", "wi_t", "scores"], writes=["scores"])
                        S.op("dve", lambda e, sc=sc, kc=kc: e.tensor_reduce(out=mins[:, kc:kc + 1], in_=sc, axis=mybir.AxisListType.X, op=ALU.min),
                             reads=["scores"], writes=["mins"])
                        S.op("dve", lambda e, kc=kc: e.tensor_scalar(out=qrel, in0=qpos, scalar1=float(-kc * CH), scalar2=None, op0=ALU.add),
                             reads=["qpos"], writes=["qrel"])
                        S.op("dve", lambda e: e.tensor_scalar(out=cbt, in0=kposb, scalar1=qrel, scalar2=-BIG, op0=ALU.is_gt, op1=ALU.mult),
                             reads=["kposb", "qrel"], writes=["cbt"])
                        S.op("dve", lambda e, sc=sc: e.tensor_tensor(out=sc, in0=sc, in1=cbt, op=ALU.add), reads=["scores", "cbt"], writes=["scores"])
                    S.op("dve", lambda e, nch=nch: e.tensor_reduce(out=lo, in_=mins[:, :nch], axis=mybir.AxisListType.X, op=ALU.min), reads=["mins"], writes=["lo"])
                    S.op("dve", lambda e: e.tensor_scalar(out=lo, in0=lo, scalar1=-1.0, scalar2=None, op0=ALU.add), reads=["lo"], writes=["lo"])
                    S.op("dve", lambda e, S_=S_: e.tensor_reduce(out=w0, in_=scores[:, :S_], axis=mybir.AxisListType.X, op=ALU.max), reads=["scores"], writes=["w0"])
                    S.op("dve", lambda e: e.scalar_tensor_tensor(out=w0, in0=w0, scalar=1.0, in1=lo, op0=ALU.add, op1=ALU.subtract), reads=["w0", "lo"], writes=["w0"])
                    S.op("dve", lambda e: e.tensor_scalar(out=w0, in0=w0, scalar1=1.0, scalar2=None, op0=ALU.max), reads=["w0"], writes=["w0"])
                    for it in range(NIT):
                        ck = 0.5 ** (it + 1)
                        S.op("dve", lambda e, ck=ck: e.scalar_tensor_tensor(out=mid, in0=w0, scalar=ck, in1=lo, op0=ALU.mult, op1=ALU.add),
                             reads=["w0", "lo"], writes=["mid"])
                        S.op("dve", lambda e, S_=S_: e.tensor_scalar(out=junk[:, :S_], in0=scores[:, :S_], scalar1=mid, scalar2=0.0,
                                                                     op0=ALU.is_ge, op1=ALU.add, accum_out=cntt),
                             reads=["scores", "mid"], writes=["junk", "cntt"])
                        S.op("dve", lambda e: e.tensor_scalar(out=gew, in0=cntt, scalar1=KEEP - 0.5, scalar2=w0, op0=ALU.is_ge, op1=ALU.mult),
                             reads=["cntt", "w0"], writes=["gew"])
                        S.op("dve", lambda e, ck=ck: e.scalar_tensor_tensor(out=lo, in0=gew, scalar=ck, in1=lo, op0=ALU.mult, op1=ALU.add),
                             reads=["gew", "lo"], writes=["lo"])
                def att_B(qb):
                    bq = qb % 2
                    qT, maskT, negC = qT2[bq], maskT2[bq], negC2[bq]
                    kq, km, kn = f"qT{bq}", f"maskT{bq}", f"negC{bq}"
                    t0 = qb * 128
                    nkb = (NBLK - 1) + qb
                    nch = -(-nkb // (CH // 128))
                    S_ = nch * CH
                    for kc in range(nch):
                        mr = mrow[kc % 2]
                        S.op("dve", lambda e, mr=mr, kc=kc: e.tensor_scalar(out=mr, in0=scores[:, kc * CH:(kc + 1) * CH], scalar1=lo, scalar2=None, op0=ALU.is_ge),
                             reads=["scores", "lo"], writes=[f"mrow{kc % 2}"])
                        pt, ptk = next_ps(psT, "pt")
                        ptb = pt[:, :].bitcast(BF16)

                        def fn(e, mr=mr, ptb=ptb):
                            for b4 in range(CH // 128):
                                ins = e.transpose(out=ptb[:, b4 * 128:(b4 + 1) * 128], in_=mr[:, b4 * 128:(b4 + 1) * 128], identity=identb[:])
                            return ins
                        S.op("pe", fn, reads=[f"mrow{kc % 2}", "identb"], writes=[ptk])
                        S.op("act", lambda e, ptb=ptb, kc=kc: e.activation(out=maskT[:, kc * 4:(kc + 1) * 4, :],
                                                                          in_=ptb[:, :CH].rearrange("p (b q) -> p b q", q=128), func=AF.Identity),
                             reads=[ptk], writes=[km])
                    S.op("act", lambda e: e.activation(out=qsq, in_=qT, func=AF.Square), reads=[kq], writes=["qsq"])
                    S.op("dve", lambda e: e.memset(qn2, 0.0), writes=["qn2"])
                    for hf in range(2):
                        ps, pk = next_ps(psA, "ps")
                        mm_group(ps[:, :CH], pk, CH, [(onesb[:], qsq[:, hf * 4:(hf + 1) * 4, :].rearrange("p h q -> p (h q)"))], ["onesb", "qsq"])
                        S.op("dve", lambda e, ps=ps: e.tensor_reduce(out=cntt, in_=ps[:, :CH], axis=mybir.AxisListType.X, op=ALU.max), reads=[pk], writes=["cntt"])
                        S.op("dve", lambda e: e.tensor_tensor(out=qn2, in0=qn2, in1=cntt, op=ALU.max), reads=["qn2", "cntt"], writes=["qn2"])
                    S.op("dve", lambda e: e.tensor_tensor(out=negC, in0=qn2, in1=kn2, op=ALU.mult), reads=["qn2", "kn2"], writes=[kn])
                    S.op("act", lambda e: e.activation(out=negC, in_=negC, func=AF.Sqrt), reads=[kn], writes=[kn])
                    S.op("dve", lambda e: e.tensor_scalar(out=negC, in0=negC, scalar1=-1.0, scalar2=None, op0=ALU.mult), reads=[kn], writes=[kn])
                def att_C(qb):
                    bq = qb % 2
                    qT, maskT, negC = qT2[bq], maskT2[bq], negC2[bq]
                    kq, km, kn = f"qT{bq}", f"maskT{bq}", f"negC{bq}"
                    t0 = qb * 128
                    nkb = (NBLK - 1) + qb
                    nch = -(-nkb // (CH // 128))
                    S_ = nch * CH
                    nblk_proc = nch * (CH // 128)
                    for kvh in range(2):
                        pso, psok = psB[0], "pb0"
                        pss, pssk = psB[1], "pb1"
                        for kc in range(nch):
                            i = load_kv_chunk(kc, ("kv",))
                            for b4 in range(CH // 128):
                                kb = kc * 4 + b4
                                pl, plk = next_ps(psA, "ps")
                                mm_group(pl[:, :512], plk, 512, [(KTc[i][:, kvh, b4 * 128:(b4 + 1) * 128],
                                                                   qT[:, kvh * 4:(kvh + 1) * 4, :].rearrange("p h q -> p (h q)"))], [f"KTc{i}", kq])
                                et = e_t[kb % 2]; pm = pmt[kb % 2]
                                S.op("act", lambda e, pl=pl, et=et: e.activation(out=et, in_=pl[:, :512].rearrange("p (h q) -> p h q", h=4), func=AF.Exp,
                                                                               bias=negC), reads=[plk, kn], writes=[f"e_t{kb % 2}"])
                                S.op("pool", lambda e, et=et, pm=pm, kb=kb: e.tensor_tensor(out=pm, in0=et, in1=maskT[:, kb:kb + 1, :].to_broadcast([128, 4, 128]),
                                                                                          op=ALU.mult), reads=[f"e_t{kb % 2}", km], writes=[f"pm{kb % 2}"])
                                first, last = (kb == 0), (kb == nblk_proc - 1)
                                pmf = pm.rearrange("p h q -> p (h q)")
                                S.op("pe", lambda e, i=i, b4=b4, pmf=pmf, first=first, last=last, kvh=kvh: e.matmul(
                                    pso[:, :512], lhsT=Vc[i][:, b4, kvh * 128:(kvh + 1) * 128], rhs=pmf, start=first, stop=last),
                                    reads=[f"Vc{i}", f"pm{kb % 2}"], writes=[psok])
                                S.op("pe", lambda e, pmf=pmf, first=first, last=last: e.matmul(pss[:, :512], lhsT=onesb[:], rhs=pmf, start=first, stop=last),
                                     reads=["onesb", f"pm{kb % 2}"], writes=[pssk])
                        S.op("dve", lambda e: e.tensor_scalar(out=rec, in0=pss[:, :512], scalar1=1e-30, scalar2=None, op0=ALU.max), reads=[pssk], writes=["rec"])
                        S.op("dve", lambda e: e.reciprocal(out=rec, in_=rec), reads=["rec"], writes=["rec"])
                        S.op("dve", lambda e, kvh=kvh: e.tensor_tensor(out=oT[:, kvh * 4:(kvh + 1) * 4, :], in0=pso[:, :512].rearrange("p (h q) -> p h q", h=4),
                                                                      in1=rec.rearrange("p (h q) -> p h q", h=4), op=ALU.mult), reads=[psok, "rec"], writes=["oT"])
                    for c in range(KC):
                        ps, pk = next_ps(psA, "ps")
                        linear_chunk(dsa_w_out[jd], KC, c * 128, oT, ["oT"], 128, ps, pk)
                        z_from_ps(ps, pk, c, t0, 128, 2)
                    postnorm(t0, 128, 0)

                for qb in range(NBLK):
                    att_A(qb)
                    if qb >= 1:
                        att_C(qb - 1)
                    att_B(qb)
                att_C(NBLK - 1)
            ffn_phase(li)

        def ffn_phase(li):
            new_phase()
            aT = [carve([128, 2 + WMAX]) for _ in range(2)]
            cvt = carve([128, WMAX]); gt = carve([128, WMAX]); guT = carve([128, FC, WMAX], BF16)
            for jc in range(3):
                slow_vec(cw[:, jc, :], vec_pk(ffn_conv_w[li, jc]))
            slow_vec(cb[:], vec_pk(ffn_conv_b[li]))
            S.op("pool", lambda e: e.memset(halo[:], 0.0), writes=["halo"])
            if SMP:
                aS = carve([128, FC, NSMP]); uS = carve([128, FC, NSMP]); cvS = carve([128, FC, NSMP]); t2S = carve([128, FC, NSMP])
                Pst = carve([128, FC, 2 * NSMP])
                sstg = carve([128, DFF])
                S.dma(lambda e: e.dma_start(out=sstg[:2 * NSMP, :], in_=sconv[li].rearrange("s j f -> (s j) f")), writes=["sstg"])
                for c0 in range(0, FC, 4):
                    pt, ptk = next_ps(psT, "pt")
                    cs = list(range(c0, min(c0 + 4, FC)))

                    def fnp(e, cs=cs, pt=pt):
                        for c in cs:
                            ins = e.transpose(out=pt[:, (c - cs[0]) * 128:(c - cs[0]) * 128 + 2 * NSMP], in_=sstg[:2 * NSMP, c * 128:(c + 1) * 128],
                                              identity=ident[:2 * NSMP, :2 * NSMP])
                        return ins
                    S.op("pe", fnp, reads=["sstg", "ident"], writes=[ptk])
                    for c in cs:
                        S.op("dve", lambda e, c=c, cs=cs, pt=pt: e.tensor_copy(out=Pst[:, c, :], in_=pt[:, (c - cs[0]) * 128:(c - cs[0]) * 128 + 2 * NSMP]),
                             reads=[ptk], writes=["Pst"])
                S.dma(lambda e: e.dma_start(out=convs[li, :, 0, :], in_=sconv[li, :, 1, :]))

            def ffn_tile(ti, t0, W):
                smp = SMP and ti == 0
                Wx = W + (NSMP if smp else 0)
                modulate(t0, W, 3, 4)
                if smp:
                    modulate_s(3, 4)
                for fc in range(FC):
                    pa, pak = next_ps(psA, "ps")
                    linear_schunk([(wup_s[li, fc], KC, 0)], f"wsc{li}", hT, ["hT"], Wx, pa, pak)
                    pu, puk = next_ps(psB, "pb")
                    linear_schunk([(wup_s[li, FC + fc], KC, 0)], f"wsc{li}", hT, ["hT"], Wx, pu, puk)
                    if smp:
                        S.op("act", lambda e, pa=pa, fc=fc: e.activation(out=aS[:, fc, :], in_=pa[:, 128:128 + NSMP], func=AF.Identity), reads=[pak], writes=["aS"])
                        S.op("act", lambda e, pu=pu, fc=fc: e.activation(out=uS[:, fc, :], in_=pu[:, 128:128 + NSMP], func=AF.Identity), reads=[puk], writes=["uS"])
                    ai = fc % 2
                    a_t = aT[ai]
                    S.op("pool", lambda e, a_t=a_t, fc=fc: e.tensor_copy(out=a_t[:, 0:2], in_=halo[:, fc, :]), reads=["halo"], writes=[f"aT{ai}"])
                    S.op("act", lambda e, a_t=a_t, pa=pa: e.activation(out=a_t[:, 2:2 + W], in_=pa[:, :W], func=AF.Identity),
                         reads=[pak], writes=[f"aT{ai}"])
                    S.op("pool", lambda e, a_t=a_t, fc=fc: e.tensor_copy(out=halo[:, fc, :], in_=a_t[:, W:W + 2]), reads=[f"aT{ai}"], writes=["halo"])
                    S.op("dve", lambda e, a_t=a_t, fc=fc: e.tensor_scalar(out=cvt[:, :W], in0=a_t[:, 0:W], scalar1=cw[:, 0, fc:fc + 1],
                                                                          scalar2=cb[:, fc:fc + 1], op0=ALU.mult, op1=ALU.add),
                         reads=[f"aT{ai}", "cw", "cb"], writes=["cvt"])
                    for jc in (1, 2):
                        S.op("dve", lambda e, a_t=a_t, fc=fc, jc=jc: e.scalar_tensor_tensor(
                            out=cvt[:, :W], in0=a_t[:, jc:jc + W], scalar=cw[:, jc, fc:fc + 1], in1=cvt[:, :W], op0=ALU.mult, op1=ALU.add),
                            reads=[f"aT{ai}", "cw", "cvt"], writes=["cvt"])
                    S.op("act", lambda e: e.activation(out=gt[:, :W], in_=cvt[:, :W], func=AF.Gelu_apprx_tanh), reads=["cvt"], writes=["gt"])
                    S.op("dve", lambda e, pu=pu, fc=fc: e.tensor_tensor(out=guT[:, fc, :W], in0=gt[:, :W], in1=pu[:, :W], op=ALU.mult),
                         reads=["gt", puk], writes=["guT"])
                if ti == 0:
                    S.op("dve", lambda e: e.tensor_scalar(out=halo[:], in0=halo[:], scalar1=rolet[:, 1:2], scalar2=None, op0=ALU.mult),
                         reads=["halo", "rolet"], writes=["halo"])
                if smp:
                    Pv = Pst.rearrange("p f (s j) -> p f s j", j=2)
                    bc = lambda col: col.unsqueeze(2).to_broadcast([128, FC, NSMP])
                    S.op("dve", lambda e: e.tensor_tensor(out=cvS, in0=Pv[:, :, :, 0], in1=bc(cw[:, 0, :]), op=ALU.mult), reads=["Pst", "cw"], writes=["cvS"])
                    S.op("dve", lambda e: e.tensor_tensor(out=t2S, in0=Pv[:, :, :, 1], in1=bc(cw[:, 1, :]), op=ALU.mult), reads=["Pst", "cw"], writes=["t2S"])
                    S.op("dve", lambda e: e.tensor_tensor(out=cvS, in0=cvS, in1=t2S, op=ALU.add), reads=["cvS", "t2S"], writes=["cvS"])
                    S.op("dve", lambda e: e.tensor_tensor(out=t2S, in0=aS, in1=bc(cw[:, 2, :]), op=ALU.mult), reads=["aS", "cw"], writes=["t2S"])
                    S.op("dve", lambda e: e.tensor_tensor(out=cvS, in0=cvS, in1=t2S, op=ALU.add), reads=["cvS", "t2S"], writes=["cvS"])
                    S.op("dve", lambda e: e.tensor_tensor(out=cvS, in0=cvS, in1=bc(cb[:, :]), op=ALU.add), reads=["cvS", "cb"], writes=["cvS"])
                    S.op("act", lambda e: e.activation(out=cvS, in_=cvS, func=AF.Gelu_apprx_tanh), reads=["cvS"], writes=["cvS"])
                    S.op("dve", lambda e: e.tensor_tensor(out=guT[:, :, 128:128 + NSMP], in0=cvS, in1=uS, op=ALU.mult), reads=["cvS", "uS"], writes=["guT"])
                    for c0 in range(0, FC, 4):
                        pt, ptk = next_ps(psT, "pt")
                        cs = list(range(c0, min(c0 + 4, FC)))

                        def fna(e, cs=cs, pt=pt):
                            for c in cs:
                                ins = e.transpose(out=pt[:NSMP, (c - cs[0]) * 128:(c - cs[0] + 1) * 128], in_=aS[:, c, :], identity=ident[:])
                            return ins
                        S.op("pe", fna, reads=["aS", "ident"], writes=[ptk])
                        S.op("dve", lambda e, cs=cs, pt=pt: e.tensor_copy(out=sstg[:NSMP, cs[0] * 128:(cs[-1] + 1) * 128], in_=pt[:NSMP, :len(cs) * 128]),
                             reads=[ptk], writes=["sstg"])
                    S.dma(lambda e: e.dma_start(out=convs[li, :, 1, :], in_=sstg[:NSMP, :]), reads=["sstg"])
                for c in range(KC):
                    ps, pk = next_ps(psA, "ps")
                    linear_schunk([(wdn_s[li, c, 0], 11, 0), (wdn_s[li, c, 1], 11, 11)], f"wsc{li}", guT, ["guT"], Wx, ps, pk)
                    z_from_ps(ps, pk, c, t0, W, 5)
                    if smp:
                        zs_from_ps(ps, pk, c, 5)
                postnorm(t0, W, 1)
                if smp:
                    postnorm(0, NSMP, 1, samples=True)
            for ti, (t0, W) in enumerate(tiles):
                ffn_tile(ti, t0, W)
            for jr in range(2):
                S.dma(lambda e, li=li, jr=jr: e.dma_start(out=vec_pk(convp[li, jr]), in_=halo[:, :, jr],
                                                          allow_slow_non_contiguous=True), reads=["halo"])

        for li in range(n_layers):
            for c in range(2 * FC):
                wt, wkey = load_wchunk(ffn_w_up[li], KC, c * 128)
                S.dma(lambda e, li=li, c=c, wt=wt: e.dma_start(out=wup_s[li, c].rearrange("p (k n) -> p k n", k=KC), in_=wt[:, :KC, :]),
                      reads=[wkey], writes=[f"wsc{li}"])
            for c in range(KC):
                for hf in range(2):
                    wt, wkey = load_wchunk(ffn_w_down[li], 11, c * 128, k0=hf * 11)
                    S.dma(lambda e, li=li, c=c, hf=hf, wt=wt: e.dma_start(out=wdn_s[li, c, hf].rearrange("p (k n) -> p k n", k=11), in_=wt[:, :11, :]),
                          reads=[wkey], writes=[f"wsc{li}"])
        for li in range(n_layers):
            do_layer(li)
        new_phase()
        if SMP:
            for c0 in (0, 4):
                pt, ptk = next_ps(psT, "pt")

                def fny(e, c0=c0, pt=pt):
                    for c in range(c0, c0 + 4):
                        ins = e.transpose(out=pt[:NSMP, (c - c0) * 128:(c - c0 + 1) * 128], in_=xsT[:, c, :], identity=ident[:])
                    return ins
                S.op("pe", fny, reads=["xs", "ident"], writes=[ptk])
                S.op("dve", lambda e, c0=c0, pt=pt: e.tensor_copy(out=tokt[:NSMP, c0 * 128:(c0 + 4) * 128], in_=pt[:NSMP, :512]), reads=[ptk], writes=["tokt"])
            S.dma(lambda e: e.dma_start(out=ys[:, :], in_=tokt[:NSMP, :]), reads=["tokt"])
        for b in range(NBLK):
            tk = [tt for tt, ww in tiles if tt <= b * 128 < tt + ww][0]
            for c0 in range(0, KC, 4):
                pt, ptk = next_ps(psT, "pt")

                def fn(e, b=b, c0=c0, pt=pt):
                    for c in range(c0, c0 + 4):
                        ins = e.transpose(out=pt[:, (c - c0) * 128:(c - c0 + 1) * 128], in_=xres[:, c, b * 128:(b + 1) * 128], identity=ident[:])
                    return ins
                S.op("pe", fn, reads=[f"x{tk}", "ident"], writes=[ptk])
                S.op("dve", lambda e, c0=c0, pt=pt: e.tensor_copy(out=tokt[:, c0 * 128:(c0 + 4) * 128], in_=pt[:, :512]), reads=[ptk], writes=["tokt"])
            S.dma(lambda e, b=b: e.dma_start(out=y_loc[b * 128:(b + 1) * 128, :], in_=tokt[:]), reads=["tokt"])

        S.emit(st)
    return nc


_WEIGHT_KEYS = ["w_ada", "b_ada", "ln_g", "ln_b", "sgu_w_in", "sgu_b_in", "sgu_norm_g", "sgu_norm_b", "sgu_w_s", "sgu_b_s",
                "sgu_w_out", "dsa_w_in", "dsa_w_out", "ffn_w_up", "ffn_conv_w", "ffn_conv_b", "ffn_w_down"]
_NC_CACHE = {}


def kernel(**inp):
    f32 = lambda a: np.ascontiguousarray(np.asarray(a), dtype=np.float32)
    x_prompt = f32(inp["x_prompt"]); c_prompt = f32(inp["c_prompt"]); c_sample = f32(inp["c_sample"])
    B, T, _ = x_prompt.shape
    half = T // 2
    n_cores = 2 * B
    if "nc" not in _NC_CACHE:
        _NC_CACHE["nc"] = build_program(NBLK=1 + half // 128, n_layers=DEPTH)
    nc = _NC_CACHE["nc"]
    weights = {k: f32(inp[k]) for k in _WEIGHT_KEYS}
    x_sample = f32(inp["x_sample"]); state_conv = f32(inp["state_conv"])
    page_table = np.ascontiguousarray(np.asarray(inp["page_table"]), dtype=np.int32)
    ck_, cv_, ci_ = f32(inp["cache_k"]), f32(inp["cache_v"]), f32(inp["cache_kidx"])
    shared = {}
    for i in range(2):
        shared[f"cache_k{i}"] = ck_[i].reshape(-1, 256); shared[f"cache_v{i}"] = cv_[i].reshape(-1, 256); shared[f"cache_ki{i}"] = ci_[i].reshape(-1, 64)
    in_maps = []
    for c in range(n_cores):
        seq, role = c // 2, c % 2
        if role == 0:
            xloc = np.concatenate([np.zeros((128, D), np.float32), x_prompt[seq, :half]], axis=0)
        else:
            xloc = x_prompt[seq, half - 128:]
        m = dict(weights)
        m.update(shared)
        sl_s = slice(NSMP * c, NSMP * (c + 1))
        m["xs_in"] = np.ascontiguousarray(x_sample[sl_s, 0, :])
        m["sconv"] = np.ascontiguousarray(state_conv[:, sl_s])
        m["ptab"] = np.ascontiguousarray(page_table[sl_s])
        m["xloc"] = np.ascontiguousarray(xloc)
        m["call"] = np.ascontiguousarray(np.concatenate([c_prompt[seq:seq + 1], c_sample[NSMP * c:NSMP * (c + 1)]], axis=0))
        m["role"] = np.tile(np.array([[float(half * role), float(role)]], np.float32), (128, 1))
        in_maps.append(m)
    res = run_bass_kernel_spmd(nc, in_maps, core_ids=list(range(n_cores))).results

    y_prompt = np.zeros((B, T, D), np.float32)
    new_conv_prompt = np.zeros((DEPTH, B, 2, DFF), np.float32)
    new_k_prompt = np.zeros((2, B, T, 2, 128), np.float32); new_v_prompt = np.zeros((2, B, T, 2, 128), np.float32)
    new_kidx_prompt = np.zeros((2, B, T, 64), np.float32)
    for c in range(n_cores):
        seq, role = c // 2, c % 2
        sl = slice(role * half, (role + 1) * half)
        y_prompt[seq, sl] = res[c]["y_loc"][128:]
        new_k_prompt[:, seq, sl] = res[c]["knew"].reshape(2, half, 2, 128)
        new_v_prompt[:, seq, sl] = res[c]["vnew"].reshape(2, half, 2, 128)
        new_kidx_prompt[:, seq, sl] = res[c]["kinew"]
        if role == 1:
            new_conv_prompt[:, seq] = res[c]["convp"]
    DB = c_sample.shape[0]
    y_sample = np.zeros((DB, 1, D), np.float32)
    new_k_sample = np.zeros((2, DB, 1, 2, 128), np.float32); new_v_sample = np.zeros((2, DB, 1, 2, 128), np.float32)
    new_kidx_sample = np.zeros((2, DB, 1, 64), np.float32)
    new_sgu_v_sample = np.zeros((2, DB, 1, D), np.float32)
    new_conv_sample = np.zeros((DEPTH, DB, 2, DFF), np.float32)
    for c in range(n_cores):
        sl_s = slice(NSMP * c, NSMP * (c + 1))
        y_sample[sl_s, 0] = res[c]["ys"]
        new_k_sample[:, sl_s, 0] = res[c]["ksn"].reshape(2, NSMP, 2, 128)
        new_v_sample[:, sl_s, 0] = res[c]["vsn"].reshape(2, NSMP, 2, 128)
        new_kidx_sample[:, sl_s, 0] = res[c]["kisn"]
        new_sgu_v_sample[:, sl_s, 0] = res[c]["sguv"]
        new_conv_sample[:, sl_s] = res[c]["convs"]
    return (y_prompt, y_sample, new_k_prompt, new_v_prompt, new_kidx_prompt, new_k_sample, new_v_sample, new_kidx_sample,
            new_sgu_v_sample, new_conv_prompt, new_conv_sample)


def extra_sample_inputs(d):
    ck, cv, ci = d["cache_k"], d["cache_v"], d["cache_ki"]
    out = {"ptab": np.ascontiguousarray(d["pt"].astype(np.int32))}
    for i in range(2):
        out[f"cache_k{i}"] = np.ascontiguousarray(ck[i].reshape(-1, 256)); out[f"cache_v{i}"] = np.ascontiguousarray(cv[i].reshape(-1, 256))
        out[f"cache_ki{i}"] = np.ascontiguousarray(ci[i].reshape(-1, 64))
    return out
```
